# Optimizing a Trainium2 kernel written in Bass

```python
import jax, jax.numpy as jnp
from jax import lax
import numpy as np

D_MODEL = 1024
BATCH = 16
SEQ = 256
DEPTH = 2
DEC_BATCH = 8
DEC_SEQ = 1024
PAST_LEN = 256

GRID_W = 64
HEAD_DIM = 64
SCALE = HEAD_DIM ** -0.5
Q_BLOCK = 128
A_HEADS = 8
A_KV_HEADS = 2
A_WINDOW = 128
A_BLOCK = 128
A_Q = A_HEADS * HEAD_DIM
A_KV = A_KV_HEADS * HEAD_DIM
A_COLS = A_Q + 2 * A_KV
B_HEADS = 8
B_W = B_HEADS * HEAD_DIM
B_COLS = 3 * B_W
NA_ROWS = 8
NA_COLS = 16
NA_COL_BLOCK = 16
NA_COL_SPAN = NA_COL_BLOCK + NA_COLS
C_HEADS = 8
C_WIDTH = C_HEADS * HEAD_DIM
DECAY_RANK = 64
ICLR_RANK = 64
GATE_RANK = 128
C_COLS = 3 * C_WIDTH + 2 * DECAY_RANK + 2 * ICLR_RANK + GATE_RANK
SHIFT_WIDTH = 3
GN_EPS = 64e-5
GATE_COLS = 3 * D_MODEL
IN_COLS = A_COLS + B_COLS + C_COLS + GATE_COLS
D_FF = 2816
N_MOD = 9
ROPE_BASE = 10000.0
LN_EPS = 1e-5
NEG_INF = -1e30
ALPHA = (2 * DEPTH) ** 0.25
BETA = (8 * DEPTH) ** -0.25

kernel_name = 'hybrid_dit_window_natten_rwkv7_step'


def split_points(sizes):
    return np.cumsum(sizes)[:-1].tolist()


def split_heads(t, n):
    return t.reshape(t.shape[:-1] + (n, HEAD_DIM))


def layer_norm(x, g, b):
    x32 = x.astype(jnp.float32)
    mu = x32.mean(-1, keepdims=True)
    var = jnp.mean(jnp.square(x32 - mu), -1, keepdims=True)
    return ((x32 - mu) * lax.rsqrt(var + LN_EPS) * g + b).astype(x.dtype)


def adaln_mods(c, w, b):
    return (jax.nn.silu(c) @ w + b).reshape(c.shape[0], N_MOD, D_MODEL)


def swiglu(h, w_in, w_out):
    a, g = jnp.split(h @ w_in, 2, axis=-1)
    return (jax.nn.silu(a) * g) @ w_out


def short_conv(z, w):
    L = z.shape[1]
    pad = SHIFT_WIDTH // 2
    zp = jnp.pad(z, ((0, 0), (pad, pad), (0, 0)))
    return sum(zp[:, i:i + L] * w[i] for i in range(SHIFT_WIDTH))


def rope_2d(x):
    L = x.shape[1]
    t = jnp.arange(L)
    n_freq = HEAD_DIM // 4
    inv = ROPE_BASE ** (-jnp.arange(n_freq, dtype=jnp.float32) / n_freq)
    rows = (t // GRID_W).astype(jnp.float32)
    cols = (t % GRID_W).astype(jnp.float32)
    ang = jnp.concatenate([rows[:, None] * inv, cols[:, None] * inv], axis=-1)[:, None, :]
    cos, sin = jnp.cos(ang).astype(x.dtype), jnp.sin(ang).astype(x.dtype)
    x1, x2 = x[..., :HEAD_DIM // 2], x[..., HEAD_DIM // 2:]
    return jnp.concatenate([x1 * cos - x2 * sin, x1 * sin + x2 * cos], axis=-1)


def project_inputs(h, w_in, w_shift):
    z = h @ w_in
    za, zb, zc, zg = jnp.split(z, split_points((A_COLS, B_COLS, C_COLS, GATE_COLS)), axis=-1)
    qa, ka, va = jnp.split(za, split_points((A_Q, A_KV, A_KV)), axis=-1)
    qb, kb, vb = jnp.split(zb, 3, axis=-1)
    heads_a = (split_heads(qa, A_HEADS), split_heads(ka, A_KV_HEADS), split_heads(va, A_KV_HEADS))
    heads_b = (split_heads(qb, B_HEADS), split_heads(kb, B_HEADS), split_heads(vb, B_HEADS))
    return heads_a, heads_b, short_conv(zc, w_shift), zg


def context_attention(q, k, v, sink):
    B, Lc, H, _ = q.shape
    KVh = k.shape[2]
    G = H // KVh
    nq = Lc // Q_BLOCK
    qb = jnp.moveaxis(q.reshape(B, nq, Q_BLOCK, KVh, G, HEAD_DIM), 1, 0)

    def attend(qblk):
        s = jnp.einsum('bqkgd,bckd->bkgqc', qblk, k).astype(jnp.float32) * SCALE
        if sink is not None:
            col = jnp.broadcast_to(sink.astype(jnp.float32).reshape(KVh, G)[None, :, :, None, None], s.shape[:-1] + (1,))
            s = jnp.concatenate([s, col], axis=-1)
        p = jax.nn.softmax(s, axis=-1)[..., :Lc]
        return jnp.einsum('bkgqc,bckd->bqkgd', p.astype(v.dtype), v)

    o = lax.map(attend, qb)
    return jnp.moveaxis(o, 0, 1).reshape(B, Lc, H * HEAD_DIM)


def window_attention(q, k, v, ck, cv, sink):
    B, L = q.shape[:2]
    nb = L // A_BLOCK
    G = A_HEADS // A_KV_HEADS
    qb = q.reshape(B, nb, A_BLOCK, A_KV_HEADS, G, HEAD_DIM)
    pad = ((0, 0), (A_BLOCK, A_BLOCK), (0, 0), (0, 0))

    def band(t):
        tp = jnp.pad(t, pad).reshape(B, nb + 2, A_BLOCK, A_KV_HEADS, HEAD_DIM)
        return jnp.concatenate([tp[:, :-2], tp[:, 1:-1], tp[:, 2:]], axis=2)

    kb, vb = band(k), band(v)
    s_loc = jnp.einsum('bnqkgd,bnjkd->bnkgqj', qb, kb).astype(jnp.float32) * SCALE
    s_ctx = jnp.einsum('bnqkgd,bckd->bnkgqc', qb, ck).astype(jnp.float32) * SCALE
    qi = jnp.arange(A_BLOCK)
    kj = jnp.arange(3 * A_BLOCK)
    rel = kj[None, :] - A_BLOCK - qi[:, None]
    kpos = jnp.arange(nb)[:, None] * A_BLOCK - A_BLOCK + kj[None, :]
    valid = (jnp.abs(rel) <= A_WINDOW)[None] & ((kpos >= 0) & (kpos < L))[:, None, :]
    s_loc = jnp.where(valid[None, :, None, None], s_loc, NEG_INF)
    col = jnp.broadcast_to(sink.astype(jnp.float32).reshape(A_KV_HEADS, G)[None, None, :, :, None, None], s_loc.shape[:-1] + (1,))
    p = jax.nn.softmax(jnp.concatenate([s_loc, s_ctx, col], axis=-1), axis=-1)
    nk = 3 * A_BLOCK
    p_loc = p[..., :nk].astype(v.dtype)
    p_ctx = p[..., nk:nk + ck.shape[1]].astype(v.dtype)
    o = jnp.einsum('bnkgqj,bnjkd->bnqkgd', p_loc, vb) + jnp.einsum('bnkgqc,bckd->bnqkgd', p_ctx, cv)
    return o.reshape(B, L, A_Q)


def neighbourhood_attention(q, k, v, ck, cv, rpb):
    B, L, H, _ = q.shape
    rows = L // GRID_W
    kr = min(NA_ROWS, rows)
    n_cb = GRID_W // NA_COL_BLOCK
    r = jnp.arange(rows)
    rs = jnp.clip(r - kr // 2, 0, rows - kr)
    row_idx = rs[:, None] + jnp.arange(kr)[None, :]
    cb_start = jnp.clip(jnp.arange(n_cb) * NA_COL_BLOCK - NA_COLS // 2, 0, GRID_W - NA_COL_SPAN)
    col_idx = cb_start[:, None] + jnp.arange(NA_COL_SPAN)[None, :]
    ri, ci = row_idx[:, None, :, None], col_idx[None, :, None, :]
    kg = k.reshape(B, rows, GRID_W, H, HEAD_DIM)[:, ri, ci]
    vg = v.reshape(B, rows, GRID_W, H, HEAD_DIM)[:, ri, ci]
    qb = q.reshape(B, rows, n_cb, NA_COL_BLOCK, H, HEAD_DIM)
    s_loc = jnp.einsum('brnqhd,brnijhd->bhrnqij', qb, kg).astype(jnp.float32) * SCALE
    qcol = jnp.arange(n_cb)[:, None] * NA_COL_BLOCK + jnp.arange(NA_COL_BLOCK)[None, :]
    cs = jnp.clip(qcol - NA_COLS // 2, 0, GRID_W - NA_COLS)
    kcol = col_idx[:, None, :]
    valid = (kcol >= cs[..., None]) & (kcol < cs[..., None] + NA_COLS)
    dr_idx = row_idx - r[:, None] + NA_ROWS - 1
    dc_idx = jnp.clip(kcol - qcol[..., None] + NA_COLS - 1, 0, 2 * NA_COLS - 2)
    bias = rpb[:, dr_idx[:, None, None, :, None], dc_idx[None, :, :, None, :]].astype(jnp.float32)
    s_loc = jnp.where(valid[None, None, None, :, :, None, :], s_loc + bias[None], NEG_INF)
    s_loc = s_loc.reshape(B, H, rows, n_cb, NA_COL_BLOCK, kr * NA_COL_SPAN)
    s_ctx = jnp.einsum('brnqhd,bchd->bhrnqc', qb, ck).astype(jnp.float32) * SCALE
    p = jax.nn.softmax(jnp.concatenate([s_loc, s_ctx], axis=-1), axis=-1)
    nk = kr * NA_COL_SPAN
    p_loc, p_ctx = p[..., :nk].astype(v.dtype), p[..., nk:].astype(v.dtype)
    vg = vg.reshape(B, rows, n_cb, nk, H, HEAD_DIM)
    o = jnp.einsum('bhrnqi,brnihd->brnqhd', p_loc, vg) + jnp.einsum('bhrnqc,bchd->brnqhd', p_ctx, cv)
    return o.reshape(B, L, B_W)


def rwkv_scan(r, decay, k, v, kk, a, s0, reverse):
    def step(S, inp):
        r_t, w_t, k_t, v_t, kk_t, a_t = inp
        sa = jnp.einsum('bhvk,bhk->bhv', S, -kk_t)
        S = S * w_t[:, :, None, :] + sa[..., None] * (kk_t * a_t)[:, :, None, :] + v_t[..., None] * k_t[:, :, None, :]
        return S, jnp.einsum('bhvk,bhk->bhv', S, r_t)

    xs = tuple(jnp.swapaxes(t, 0, 1) for t in (r, decay, k, v, kk, a))
    S, ys = lax.scan(step, s0, xs, reverse=reverse)
    return jnp.swapaxes(ys, 0, 1), S


def rwkv_mix(zc, lp, s0):
    B, L, _ = zc.shape
    f32 = jnp.float32
    sizes = (C_WIDTH, C_WIDTH, C_WIDTH, DECAY_RANK, DECAY_RANK, ICLR_RANK, ICLR_RANK, GATE_RANK)
    r, k, v, wlo_f, wlo_b, alo_f, alo_b, glo = jnp.split(zc, split_points(sizes), axis=-1)
    g = jax.nn.sigmoid(glo) @ lp['gate_up']
    kk = split_heads((k * lp['k_k']).astype(f32), C_HEADS)
    kk = kk / jnp.maximum(jnp.sqrt(jnp.sum(kk * kk, axis=-1, keepdims=True)), 1e-12)
    r32 = split_heads(r.astype(f32), C_HEADS)
    v32 = split_heads(v.astype(f32), C_HEADS)
    y = jnp.zeros_like(r32)
    bonus = jnp.zeros_like(r32)
    finals = []
    for d, (wlo, alo) in enumerate(((wlo_f, alo_f), (wlo_b, alo_b))):
        logw = -jax.nn.softplus(-(lp['decay_w0'][d] + jnp.tanh(wlo) @ lp['decay_up'][d]).astype(f32)) - 0.5
        decay = jnp.exp(-jnp.exp(logw))
        a = jax.nn.sigmoid((lp['iclr_a0'][d] + alo @ lp['iclr_up'][d]).astype(f32))
        k_d = k.astype(f32) * (1.0 + (a - 1.0) * lp['k_a'].astype(f32))
        decay, a, k_d = split_heads(decay, C_HEADS), split_heads(a, C_HEADS), split_heads(k_d, C_HEADS)
        y_d, s_d = rwkv_scan(r32, decay, k_d, v32, kk, a, s0[:, d].astype(f32), reverse=(d == 1))
        y = y + y_d
        bonus = bonus + jnp.sum(r32 * k_d * lp['r_k'].astype(f32), axis=-1, keepdims=True) * v32
        finals.append(s_d)
    mu = y.mean(-1, keepdims=True)
    var = jnp.mean(jnp.square(y - mu), -1, keepdims=True)
    yn = ((y - mu) * lax.rsqrt(var + GN_EPS)).reshape(B, L, C_WIDTH) * lp['gn_g'] + lp['gn_b']
    out = (yn + bonus.reshape(B, L, C_WIDTH)) * g
    return out.astype(zc.dtype), jnp.stack(finals, axis=1)


def merge_branches(oa, ob, oc, zg, lp):
    ga, gb, gc = jnp.split(jax.nn.sigmoid(zg), 3, axis=-1)
    merged = ga * (oa @ lp['proj_a']) + gb * (ob @ lp['proj_b']) + gc * (oc @ lp['proj_c'])
    return merged @ lp['w_out']


def mixer_context(h, lp):
    (qa, ka, va), (qb, kb, vb), zc, zg = project_inputs(h, lp['w_in'], lp['w_shift'])
    oa = context_attention(qa, ka, va, lp['attn_sink'])
    ob = context_attention(qb, kb, vb, None)
    s0 = jnp.zeros((h.shape[0], 2, C_HEADS, HEAD_DIM, HEAD_DIM), jnp.float32)
    oc, s_ctx = rwkv_mix(zc, lp, s0)
    return merge_branches(oa, ob, oc, zg, lp), (ka, va, kb, vb, s_ctx)


def mixer_latent(h, lp, ck_a, cv_a, ck_b, cv_b, s_ctx):
    (qa, ka, va), (qb, kb, vb), zc, zg = project_inputs(h, lp['w_in'], lp['w_shift'])
    oa = window_attention(rope_2d(qa), rope_2d(ka), va, ck_a, cv_a, lp['attn_sink'])
    ob = neighbourhood_attention(qb, kb, vb, ck_b, cv_b, lp['na_rpb'])
    oc, _ = rwkv_mix(zc, lp, s_ctx)
    return merge_branches(oa, ob, oc, zg, lp), ()


def run_layer(x, mod, lp, mixer):
    m = [mod[:, None, i] for i in range(N_MOD)]
    h = x * (1.0 + m[1]) + m[0]
    x = layer_norm(ALPHA * x + 0.5 * m[2] * swiglu(h, lp['ffn1_w_in'], lp['ffn1_w_out']), lp['ln_g'][0], lp['ln_b'][0])
    h = x * (1.0 + m[4]) + m[3]
    o, extras = mixer(h)
    x = layer_norm(ALPHA * x + m[5] * o, lp['ln_g'][1], lp['ln_b'][1])
    h = x * (1.0 + m[7]) + m[6]
    x = layer_norm(ALPHA * x + 0.5 * m[8] * swiglu(h, lp['ffn2_w_in'], lp['ffn2_w_out']), lp['ln_g'][2], lp['ln_b'][2])
    return x, extras


def setup_inputs(seed: int = 0) -> dict:
    key = jax.random.key(seed)
    ks = iter(jax.random.split(key, 48))

    def nrm(shape, scale=1.0):
        return jax.random.normal(next(ks), shape, jnp.float32) * scale

    d = DEPTH
    taps = jnp.array([0.25, 0.5, 0.25], jnp.float32)[None, :, None]
    return {
        'x_prompt': nrm((BATCH, SEQ, D_MODEL)),
        'x_sample': nrm((DEC_BATCH, DEC_SEQ, D_MODEL)),
        'cache_attn_k': nrm((DEC_BATCH, DEPTH, PAST_LEN, A_KV_HEADS, HEAD_DIM)),
        'cache_attn_v': nrm((DEC_BATCH, DEPTH, PAST_LEN, A_KV_HEADS, HEAD_DIM)),
        'cache_na_k': nrm((DEC_BATCH, DEPTH, PAST_LEN, B_HEADS, HEAD_DIM)),
        'cache_na_v': nrm((DEC_BATCH, DEPTH, PAST_LEN, B_HEADS, HEAD_DIM)),
        'state_rwkv': nrm((DEC_BATCH, DEPTH, 2, C_HEADS, HEAD_DIM, HEAD_DIM), 0.5),
        'c': nrm((DEC_BATCH, D_MODEL)),
        'c_ctx': nrm((D_MODEL,)),
        'w_ada': nrm((d, D_MODEL, N_MOD * D_MODEL), 0.5 * D_MODEL ** -0.5),
        'b_ada': nrm((d, N_MOD * D_MODEL), 0.02),
        'ffn1_w_in': nrm((d, D_MODEL, 2 * D_FF), D_MODEL ** -0.5),
        'ffn1_w_out': nrm((d, D_FF, D_MODEL), BETA * D_FF ** -0.5),
        'ffn2_w_in': nrm((d, D_MODEL, 2 * D_FF), D_MODEL ** -0.5),
        'ffn2_w_out': nrm((d, D_FF, D_MODEL), BETA * D_FF ** -0.5),
        'w_in': nrm((d, D_MODEL, IN_COLS), D_MODEL ** -0.5),
        'w_shift': taps + nrm((d, SHIFT_WIDTH, C_COLS), 0.05),
        'attn_sink': nrm((d, A_HEADS), 0.5),
        'na_rpb': nrm((d, B_HEADS, 2 * NA_ROWS - 1, 2 * NA_COLS - 1), 0.1),
        'decay_w0': jax.random.uniform(next(ks), (d, 2, C_WIDTH), jnp.float32, -6.0, -1.0),
        'decay_up': nrm((d, 2, DECAY_RANK, C_WIDTH), 0.1 * DECAY_RANK ** -0.5),
        'iclr_a0': nrm((d, 2, C_WIDTH), 0.1),
        'iclr_up': nrm((d, 2, ICLR_RANK, C_WIDTH), 0.1 * ICLR_RANK ** -0.5),
        'gate_up': nrm((d, GATE_RANK, C_WIDTH), GATE_RANK ** -0.5),
        'k_k': 0.85 + nrm((d, C_WIDTH), 0.02),
        'k_a': 1.0 + nrm((d, C_WIDTH), 0.02),
        'r_k': nrm((d, C_HEADS, HEAD_DIM), 0.1),
        'gn_g': 1.0 + nrm((d, C_WIDTH), 0.02),
        'gn_b': nrm((d, C_WIDTH), 0.02),
        'proj_a': nrm((d, A_Q, D_MODEL), A_Q ** -0.5),
        'proj_b': nrm((d, B_W, D_MODEL), B_W ** -0.5),
        'proj_c': nrm((d, C_WIDTH, D_MODEL), C_WIDTH ** -0.5),
        'w_out': nrm((d, D_MODEL, D_MODEL), BETA * D_MODEL ** -0.5),
        'ln_g': 1.0 + nrm((d, 3, D_MODEL), 0.02),
        'ln_b': nrm((d, 3, D_MODEL), 0.02),
    }


def reference(x_prompt, x_sample, cache_attn_k, cache_attn_v, cache_na_k, cache_na_v, state_rwkv, c, c_ctx,
              w_ada, b_ada, ffn1_w_in, ffn1_w_out, ffn2_w_in, ffn2_w_out, w_in, w_shift, attn_sink, na_rpb,
              decay_w0, decay_up, iclr_a0, iclr_up, gate_up, k_k, k_a, r_k, gn_g, gn_b,
              proj_a, proj_b, proj_c, w_out, ln_g, ln_b):
    y_p, y_s = x_prompt, x_sample
    new_ka, new_va, new_kb, new_vb, new_st = [], [], [], [], []
    for l in range(DEPTH):
        lp = dict(ffn1_w_in=ffn1_w_in[l], ffn1_w_out=ffn1_w_out[l], ffn2_w_in=ffn2_w_in[l], ffn2_w_out=ffn2_w_out[l],
                  w_in=w_in[l], w_shift=w_shift[l], attn_sink=attn_sink[l], na_rpb=na_rpb[l],
                  decay_w0=decay_w0[l], decay_up=decay_up[l], iclr_a0=iclr_a0[l], iclr_up=iclr_up[l],
                  gate_up=gate_up[l], k_k=k_k[l], k_a=k_a[l], r_k=r_k[l], gn_g=gn_g[l], gn_b=gn_b[l],
                  proj_a=proj_a[l], proj_b=proj_b[l], proj_c=proj_c[l], w_out=w_out[l], ln_g=ln_g[l], ln_b=ln_b[l])
        mod_ctx = adaln_mods(c_ctx[None, :], w_ada[l], b_ada[l])
        mod_lat = adaln_mods(c, w_ada[l], b_ada[l])
        y_p, (ka, va, kb, vb, st) = run_layer(y_p, mod_ctx, lp, lambda h: mixer_context(h, lp))
        new_ka.append(ka)
        new_va.append(va)
        new_kb.append(kb)
        new_vb.append(vb)
        new_st.append(st)
        y_s, _ = run_layer(y_s, mod_lat, lp, lambda h: mixer_latent(h, lp, cache_attn_k[:, l], cache_attn_v[:, l],
                                                                       cache_na_k[:, l], cache_na_v[:, l], state_rwkv[:, l]))
    new_attn_k = jnp.stack(new_ka, axis=1)
    new_attn_v = jnp.stack(new_va, axis=1)
    new_na_k = jnp.stack(new_kb, axis=1)
    new_na_v = jnp.stack(new_vb, axis=1)
    new_rwkv_state = jnp.stack(new_st, axis=1)
    return (y_p, y_s, new_attn_k, new_attn_v, new_na_k, new_na_v, new_rwkv_state)
```

```python
import numpy as np
from contextlib import ExitStack
import concourse.bass as bass
import concourse.mybir as mybir
from concourse.bass_utils import run_bass_kernel_spmd

F32 = mybir.dt.float32
BF16 = mybir.dt.bfloat16
AF = mybir.ActivationFunctionType
ALU = mybir.AluOpType
AX = mybir.AxisListType

D = 1024
NCH = 8
DEPTH = 2
DFF = 2816
NJ = 22
NTOK = 1536
TILES = [(0, 512), (512, 512), (1024, 512)]
ALPHA = (2 * DEPTH) ** 0.25
LN_EPS = 1e-5
HD = 64
SCALE = HD ** -0.5
IN_COLS = 7296
A_OFF = 0
B_OFF = 768
C_OFF = 2304
G_OFF = 4224
GN_EPS = 64e-5


class Trk:
    SEM_LIMIT = 60000

    def __init__(self, nc, es):
        self.nc = nc
        self.es = es
        self.eng = {}
        for name, obj in (("pe", nc.tensor), ("act", nc.scalar), ("dve", nc.vector),
                          ("pool", nc.gpsimd), ("sp", nc.sync)):
            sem = es.enter_context(nc.semaphore("sem_" + name))
            self.eng[name] = dict(obj=obj, sem=sem, cnt=0, waited={}, id=name, name=name, epoch=0, total=0)
        self.dsems = [es.enter_context(nc.semaphore("dsem%d" % i)) for i in range(40)]
        self.dcnt = [0] * len(self.dsems)
        self.drr = 0
        self.last_w = {}
        self.readers = {}
        self.n_wait = 0

    def _wait(self, e, ev):
        sem, val, sid = ev
        if e["waited"].get(sid, 0) < val:
            e["obj"].wait_ge(sem, val)
            e["waited"][sid] = val
            self.n_wait += 1

    def _deps(self, e, reads, writes, same=True):
        evs = []
        for r in reads:
            if r in self.last_w:
                evs.append(self.last_w[r])
        for w in writes:
            if w in self.last_w:
                evs.append(self.last_w[w])
            evs.extend(self.readers.get(w, ()))
        for ev in evs:
            if (not same) and ev[2].split("_")[0] == e["name"]:
                continue
            self._wait(e, ev)

    def _commit(self, ev, reads, writes):
        for w in writes:
            self.last_w[w] = ev
            self.readers[w] = []
        for r in reads:
            if r in writes:
                continue
            lst = self.readers.setdefault(r, [])
            lst[:] = [x for x in lst if x[2] != ev[2]]
            lst.append(ev)

    def op(self, ename, fn, reads=(), writes=(), same=True):
        e = self.eng[ename]
        if ename != "pe":
            psr = [r for r in reads if r.startswith("ps") and r[2:].isdigit() and r not in writes]
            if psr:
                writes = list(writes) + psr
        self._deps(e, reads, writes, same)
        if e["cnt"] >= self.SEM_LIMIT:
            e["epoch"] += 1
            e["sem"] = self.es.enter_context(self.nc.semaphore("sem_%s_%d" % (e["name"], e["epoch"])))
            e["id"] = "%s_%d" % (e["name"], e["epoch"])
            e["cnt"] = 0
        inst = fn(e["obj"])
        e["cnt"] += 1
        e["total"] += 1
        inst.then_inc(e["sem"], 1)
        ev = (e["sem"], e["cnt"], e["id"])
        self._commit(ev, reads, writes)
        return ev

    def dma(self, qname, out, in_, reads=(), writes=()):
        e = self.eng[qname]
        self._deps(e, reads, writes, True)
        i = self.drr
        self.drr = (self.drr + 1) % len(self.dsems)
        outs = out if isinstance(out, (list, tuple)) else [out]
        ins = in_ if isinstance(in_, (list, tuple)) else [in_]
        for o_, i_ in zip(outs, ins):
            inst = e["obj"].dma_start(out=o_, in_=i_)
            self.dcnt[i] += 16
            inst.then_inc(self.dsems[i], 16)
        ev = (self.dsems[i], self.dcnt[i], "d%d" % i)
        self._commit(ev, reads, writes)
        return ev

    def barrier(self):
        evs = [(e["sem"], e["cnt"], e["id"]) for e in self.eng.values() if e["cnt"] > 0]
        evs += [(self.dsems[i], self.dcnt[i], "d%d" % i) for i in range(len(self.dsems)) if self.dcnt[i] > 0]
        for e in self.eng.values():
            for ev in evs:
                self._wait(e, ev)

    def wait_all(self, ename):
        e = self.eng[ename]
        for k, ev in list(self.last_w.items()):
            self._wait(e, ev)
        for k, lst in list(self.readers.items()):
            for ev in lst:
                self._wait(e, ev)


def host_consts():
    c = {}
    c["ident"] = np.eye(128, dtype=np.float32)
    bo = np.zeros((128, 128), np.float32)
    bo[:64, :64] = 1.0
    bo[64:, 64:] = 1.0
    c["blockones"] = bo
    c["ones"] = np.ones((128, 128), np.float32)
    return c


class Builder:
    def __init__(self, dbg=None, nlayers=DEPTH, stop_after=None):
        self.dbg = dbg or []
        self.nlayers = nlayers
        self.stop_after = stop_after
        self.nc = bass.Bass("TRN2", target_bir_lowering=False)
        self.es = ExitStack()
        self.dram_in = {}
        self.dram_out = {}

    def din(self, name, shape, dt=F32):
        t = self.nc.dram_tensor(name, list(shape), dt, kind="ExternalInput").ap()
        self.dram_in[name] = t
        return t

    def dout(self, name, shape, dt=F32):
        t = self.nc.dram_tensor(name, list(shape), dt, kind="ExternalOutput").ap()
        self.dram_out[name] = t
        return t

    def sb(self, name, shape, dt=F32, es=None):
        self._uid = getattr(self, "_uid", 0) + 1
        return (es or self.es).enter_context(self.nc.sbuf_tensor("%s_%d" % (name, self._uid), list(shape), dt))

    def build(self):
        nc, es = self.nc, self.es
        with es:
            self.T = Trk(nc, es)
            self._declare()
            self._globals()
            for l in range(self.nlayers):
                self._layer(l)
                if self.stop_after is not None and self.stop_after[0] == l:
                    break
            self._finish()
        return nc

    def _declare(self):
        L = DEPTH
        self.x_in = self.din("x_tok", [NTOK, D])
        self.cvec = self.din("cvec", [2, D])
        self.w_ada = self.din("w_ada", [L, D, 9 * D])
        self.b_ada = self.din("b_ada", [L, 9 * D])
        self.ffn_w_in = [self.din("ffn1_w_in", [L, D, 2 * DFF]), self.din("ffn2_w_in", [L, D, 2 * DFF])]
        self.ffn_w_out = [self.din("ffn1_w_out", [L, DFF, D]), self.din("ffn2_w_out", [L, DFF, D])]
        self.ln_g = self.din("ln_g", [L, 3, D])
        self.ln_b = self.din("ln_b", [L, 3, D])
        self.c_ones = self.din("c_ones", [128, 128])
        self.c_ident = self.din("c_ident", [128, 128])
        self.c_blockones = self.din("c_blockones", [128, 128])
        self.y_out = self.dout("y_tok", [NTOK, D])
        for name, shape in self.dbg:
            self.dout(name, shape)

    def _globals(self):
        nc, T = self.nc, self.T
        self.xT = self.sb("xT", [128, NCH, NTOK], F32)
        self.hT = self.sb("hT", [128, NCH, NTOK], BF16)
        self.ps = [self.es.enter_context(nc.psum_tensor("ps%d" % i, [128, 512], F32)) for i in range(8)]
        self.NSLOT = 3
        self.wring = self.sb("wring", [128, self.NSLOT, 4096], BF16)
        self.wslot = 0
        self.ones_bf = self.sb("ones_bf", [128, 128], BF16)
        self.ident_f = self.sb("ident_f", [128, 128], F32)
        self.bones_bf = self.sb("bones_bf", [128, 128], BF16)
        self.scT = self.sb("scT", [128, NCH, 2], BF16)
        self.cT = self.sb("cT", [128, NCH, 2], F32)
        self.modT = self.sb("modT", [128, 2, 9, NCH], F32)
        self.badaT = self.sb("badaT", [128, 9, NCH], F32)
        self.lngT = self.sb("lngT", [128, 3, NCH], F32)
        self.lnbT = self.sb("lnbT", [128, 3, NCH], F32)
        self.gsT = self.sb("gsT", [128, 2, NCH], F32)
        self.ghT = self.sb("ghT", [128, 2, NCH], F32)
        self.bhT = self.sb("bhT", [128, 2, NCH], F32)
        self.s1T = self.sb("s1T", [128, 2, NCH], F32)
        self.eps_t = self.sb("eps_t", [128, 1], F32)

        T.dma("pool", self.ones_bf[:], self.c_ones, writes=["ones_bf"])
        T.dma("pool", self.bones_bf[:], self.c_blockones, writes=["bones_bf"])
        T.dma("sp", self.ident_f[:], self.c_ident, writes=["ident_f"])
        T.op("dve", lambda e: e.memset(self.eps_t[:], LN_EPS / (ALPHA * ALPHA)), writes=["eps_t"])
        self._load_x()
        self.vecstage = self.sb("vecstage", [72, 128], F32)
        self._load_vecT(self.cvec.rearrange("w (c p) -> (w c) p", p=128), 16, "cT_raw")
        T.op("dve", lambda e: e.tensor_copy(out=self.cT[:], in_=self.ps[7][:, 0:16].rearrange("p (w c) -> p c w", w=2)),
             reads=["ps7"], writes=["cT"])
        T.op("act", lambda e: e.activation(out=self.scT[:], in_=self.cT[:], func=AF.Silu),
             reads=["cT"], writes=["scT"])

    def _load_x(self):
        nc, T = self.nc, self.T
        with ExitStack() as les:
            xs = [self.sb("xstage%d" % i, [128, D], F32, es=les) for i in range(2)]
            for tt in range(NTOK // 128):
                s = xs[tt % 2]
                key = "xstage%d" % (tt % 2)
                T.dma("sp", s[:], self.x_in[tt * 128:(tt + 1) * 128, :], writes=[key])
                for half in range(2):
                    bank = self.ps[(tt * 2 + half) % 4]
                    bkey = "ps%d" % ((tt * 2 + half) % 4)
                    for q in range(4):
                        c = half * 4 + q
                        T.op("pe", lambda e, c=c, q=q, bank=bank, s=s: e.transpose(
                            bank[:, q * 128:(q + 1) * 128], s[:, c * 128:(c + 1) * 128], self.ident_f[:]),
                            reads=[key, "ident_f"], writes=[bkey] if q in (0, 3) else [], same=False)
                    T.op("dve" if half == 0 else "act",
                         (lambda e, bank=bank, half=half, tt=tt: e.tensor_copy(
                             out=self.xT[:, half * 4:(half + 1) * 4, tt * 128:(tt + 1) * 128],
                             in_=bank[:].rearrange("p (q t) -> p q t", q=4))) if half == 0 else
                         (lambda e, bank=bank, half=half, tt=tt: e.activation(
                             out=self.xT[:, half * 4:(half + 1) * 4, tt * 128:(tt + 1) * 128],
                             in_=bank[:].rearrange("p (q t) -> p q t", q=4), func=AF.Copy)),
                         reads=[bkey], writes=["xT"])
            self.T.barrier()


    def _load_vecT(self, src_rows, nrows, tag):
        T = self.T
        T.dma("sp", self.vecstage[0:nrows, :], src_rows, writes=["vecstage"])
        T.op("pe", lambda e: e.transpose(self.ps[7][:, 0:nrows], self.vecstage[0:nrows, :], self.ident_f[0:nrows, 0:nrows]),
             reads=["vecstage", "ident_f"], writes=["ps7"], same=False)

    def wload(self, view_fn, src_ap, split=None):
        s = self.wslot
        self.wslot = (self.wslot + 1) % self.NSLOT
        key = "wring%d" % s
        dst = view_fn(self.wring[:, s, :])
        if split:
            self.T.dma("pool", [dst[:, :, a, :] for a in range(split)], [src_ap[:, :, a, :] for a in range(split)], writes=[key])
        else:
            self.T.dma("pool", dst, src_ap, writes=[key])
        return dst, key


    def mmg(self, out_ap, okey, terms, reads):
        T = self.T
        n = len(terms)
        for i, (lt, rh) in enumerate(terms):
            T.op("pe", lambda e, lt=lt, rh=rh, i=i: e.matmul(out_ap, lt, rh, start=(i == 0), stop=(i == n - 1)),
                 reads=reads, writes=[okey] if (i == 0 or i == n - 1) else [], same=False)

    def _layer(self, l):
        self._mods(l)
        self._ffn(l, 0)
        if self.stop_after == (l, 0):
            return
        if hasattr(self, "_mixer"):
            self._mixer(l)
            if self.stop_after == (l, 1):
                return
        self._ffn(l, 1)

    def _mods(self, l):
        nc, T = self.nc, self.T
        self._load_vecT(self.b_ada[l].rearrange("(ic p) -> ic p", p=128), 72, "bada")
        T.op("dve", lambda e: e.tensor_copy(out=self.badaT[:].rearrange("p i c -> p (i c)"), in_=self.ps[7][:, 0:72]),
             reads=["ps7"], writes=["badaT"])
        self._load_vecT(self.ln_g[l].rearrange("s (c p) -> (s c) p", p=128), 24, "lng")
        T.op("dve", lambda e: e.tensor_copy(out=self.lngT[:].rearrange("p s c -> p (s c)"), in_=self.ps[7][:, 0:24]),
             reads=["ps7"], writes=["lngT"])
        self._load_vecT(self.ln_b[l].rearrange("s (c p) -> (s c) p", p=128), 24, "lnb")
        T.op("dve", lambda e: e.tensor_copy(out=self.lnbT[:].rearrange("p s c -> p (s c)"), in_=self.ps[7][:, 0:24]),
             reads=["ps7"], writes=["lnbT"])
        wv = self.w_ada[l].rearrange("(kc p) n -> p kc n", p=128)
        for piece in range(18):
            dst, key = self.wload(lambda s: s.rearrange("p (kc n) -> p kc n", kc=NCH), wv[:, :, piece * 512:(piece + 1) * 512])
            bank = self.ps[piece % 2]
            bkey = "ps%d" % (piece % 2)
            for q in range(4):
                for kc in range(NCH):
                    T.op("pe", lambda e, q=q, kc=kc, bank=bank, dst=dst: e.matmul(
                        bank[:, q * 2:q * 2 + 2], dst[:, kc, q * 128:(q + 1) * 128], self.scT[:, kc, :],
                        start=(kc == 0), stop=(kc == NCH - 1)),
                        reads=[key, "scT"], writes=[bkey] if ((q == 0 and kc == 0) or (q == 3 and kc == NCH - 1)) else [], same=False)
            i, c0 = divmod(piece * 4, NCH)
            T.op("dve", lambda e, bank=bank, i=i, c0=c0: e.tensor_tensor(
                out=self.modT[:, :, i, c0:c0 + 4],
                in0=bank[:, 0:8].rearrange("p (q w) -> p w q", w=2),
                in1=self.badaT[:, i, c0:c0 + 4].unsqueeze(1).broadcast_to([128, 2, 4]), op=ALU.add),
                reads=[bkey, "badaT"], writes=["modT"])
        T.op("dve", lambda e: e.tensor_scalar_add(out=self.s1T[:], in0=self.modT[:, :, 1, :], scalar1=1.0),
             reads=["modT"], writes=["s1T"])
        self._modulate_all(self.s1T, lambda w: self.modT[:, w, 0, :])

    def _modulate_all(self, scaleT, shift_fn):
        T = self.T
        for c in range(NCH):
            for (w, t0, n) in ((0, 0, 512), (1, 512, 1024)):
                T.op("act", lambda e, c=c, w=w, t0=t0, n=n: e.activation(
                    out=self.hT[:, c, t0:t0 + n], in_=self.xT[:, c, t0:t0 + n], func=AF.Identity,
                    scale=scaleT[:, w, c:c + 1], bias=shift_fn(w)[:, c:c + 1]),
                    reads=["xT", "modT", "s1T", "ghT", "bhT"], writes=["hT"])

    def _sub_scalars(self, l, sub, gate_i, gate_mul, nxt):
        T = self.T
        T.op("dve", lambda e: e.tensor_scalar_mul(out=self.gsT[:], in0=self.modT[:, :, gate_i, :], scalar1=gate_mul / ALPHA),
             reads=["modT"], writes=["gsT"])
        if nxt is not None:
            sh_i, sc_i = nxt
            T.op("dve", lambda e: e.tensor_scalar_add(out=self.ghT[:], in0=self.modT[:, :, sc_i, :], scalar1=1.0),
                 reads=["modT"], writes=["ghT"])
            T.op("dve", lambda e: e.tensor_tensor(out=self.bhT[:], in0=self.ghT[:],
                                                  in1=self.lnbT[:, sub, :].unsqueeze(1).broadcast_to([128, 2, NCH]), op=ALU.mult),
                 reads=["ghT", "lnbT"], writes=["bhT"])
            T.op("dve", lambda e: e.tensor_tensor(out=self.bhT[:], in0=self.bhT[:], in1=self.modT[:, :, sh_i, :], op=ALU.add),
                 reads=["modT"], writes=["bhT"])
            T.op("dve", lambda e: e.tensor_tensor(out=self.ghT[:], in0=self.ghT[:],
                                                  in1=self.lngT[:, sub, :].unsqueeze(1).broadcast_to([128, 2, NCH]), op=ALU.mult),
                 reads=["lngT"], writes=["ghT"])

    def _ln_tile(self, l, sub, ti, has_next, les):
        T = self.T
        t0, n = TILES[ti]
        w = 0 if ti == 0 else 1
        mean, rstd, tmp = self.ln_mean, self.ln_rstd, self.ln_tmp
        T.op("act", lambda e: e.activation(out=mean[:], in_=self.ps[6][:], func=AF.Copy, scale=1.0 / D),
             reads=["ps6"], writes=["ln_mean"])
        T.op("dve", lambda e: e.tensor_tensor(out=tmp[:], in0=mean[:], in1=mean[:], op=ALU.mult),
             reads=["ln_mean"], writes=["ln_tmp"])
        T.op("dve", lambda e: e.scalar_tensor_tensor(out=rstd[:], in0=self.ps[7][:], scalar=1.0 / D, in1=tmp[:],
                                                     op0=ALU.mult, op1=ALU.subtract),
             reads=["ps7", "ln_tmp"], writes=["ln_rstd"])
        T.op("act", lambda e: e.activation(out=rstd[:], in_=rstd[:], func=AF.Sqrt, bias=self.eps_t[:, 0:1], scale=1.0),
             reads=["ln_rstd", "eps_t"], writes=["ln_rstd"])
        T.op("dve", lambda e: e.reciprocal(out=rstd[:], in_=rstd[:]), reads=["ln_rstd"], writes=["ln_rstd"])
        for c in range(NCH):
            tb = self.ln_t[c % 2]
            tk = "ln_t%d" % (c % 2)
            T.op("dve", lambda e, c=c, tb=tb: e.tensor_tensor(out=tb[:], in0=self.xT[:, c, t0:t0 + n], in1=mean[:], op=ALU.subtract),
                 reads=["xT", "ln_mean"], writes=[tk])
            T.op("pool", lambda e, tb=tb: e.tensor_tensor(out=tb[:], in0=tb[:], in1=rstd[:], op=ALU.mult),
                 reads=["ln_rstd", tk], writes=[tk])
            T.op("act", lambda e, c=c, tb=tb: e.activation(out=self.xT[:, c, t0:t0 + n], in_=tb[:], func=AF.Identity,
                                                           scale=self.lngT[:, sub, c:c + 1], bias=self.lnbT[:, sub, c:c + 1]),
                 reads=[tk, "lngT", "lnbT"], writes=["xT"])
            if has_next:
                T.op("dve", lambda e, c=c, tb=tb: e.tensor_scalar(out=self.hT[:, c, t0:t0 + n], in0=tb[:],
                                                                  scalar1=self.ghT[:, w, c:c + 1], scalar2=self.bhT[:, w, c:c + 1],
                                                                  op0=ALU.mult, op1=ALU.add),
                     reads=[tk, "ghT", "bhT"], writes=["hT"])

    def _ffn(self, l, which):
        nc, T = self.nc, self.T
        sub = 0 if which == 0 else 2
        gate_i = 2 if which == 0 else 8
        nxt = (3, 4) if which == 0 else None
        has_next = which == 0
        self._sub_scalars(l, sub, gate_i, 0.5, nxt)
        w_in = self.ffn_w_in[which][l].rearrange("(kc p) (ag n) -> p kc ag n", p=128, ag=2)
        w_out = self.ffn_w_out[which][l].rearrange("(j p) n -> p j n", p=128)
        with ExitStack() as les:
            uT = self.sb("uT", [128, NJ, NTOK], BF16, es=les)
            sl = [self.sb("silu%d" % i, [128, 512], F32, es=les) for i in range(2)]
            self._ln_alloc(les)
            cnt = 0
            for jp in range(NJ // 2):
                dst, key = self.wload(lambda s_: s_.rearrange("p (ag kc n) -> p kc ag n", kc=NCH, ag=2),
                                      w_in[:, :, :, jp * 256:(jp + 1) * 256], split=2)
                for jj in range(2):
                    j = jp * 2 + jj
                    for ti, (t0, n) in enumerate(TILES):
                        ba, bg = self.ps[(cnt % 2) * 2], self.ps[(cnt % 2) * 2 + 1]
                        ka, kg = "ps%d" % ((cnt % 2) * 2), "ps%d" % ((cnt % 2) * 2 + 1)
                        for (bank, bk, ag) in ((ba, ka, 0), (bg, kg, 1)):
                            self.mmg(bank[:, 0:n], bk,
                                     [(dst[:, kc, ag, jj * 128:(jj + 1) * 128], self.hT[:, kc, t0:t0 + n]) for kc in range(NCH)],
                                     reads=[key, "hT"])
                        sb_ = sl[cnt % 2]
                        sk = "silu%d" % (cnt % 2)
                        T.op("act", lambda e, ba=ba, sb_=sb_, n=n: e.activation(out=sb_[:, 0:n], in_=ba[:, 0:n], func=AF.Silu),
                             reads=[ka], writes=[sk])
                        T.op("dve", lambda e, bg=bg, sb_=sb_, j=j, t0=t0, n=n: e.tensor_tensor(
                            out=uT[:, j, t0:t0 + n], in0=bg[:, 0:n], in1=sb_[:, 0:n], op=ALU.mult),
                            reads=[kg, sk], writes=["uT"])
                        cnt += 1
            for mp in range(4):
                pcs = []
                for jh in range(2):
                    pcs.append(self.wload(lambda s_: s_[:, 0:11 * 256].rearrange("p (j n) -> p j n", j=11),
                                          w_out[:, jh * 11:(jh + 1) * 11, mp * 256:(mp + 1) * 256]))
                for ti, (t0, n) in enumerate(TILES):
                    for mm_ in range(2):
                        c = mp * 2 + mm_
                        bi = (ti * 2 + mm_) % 6
                        bank, bk = self.ps[bi], "ps%d" % bi
                        self.mmg(bank[:, 0:n], bk,
                                 [(pcs[j // 11][0][:, j % 11, mm_ * 128:(mm_ + 1) * 128], uT[:, j, t0:t0 + n]) for j in range(NJ)],
                                 reads=[pcs[0][1], pcs[1][1], "uT"])
                        w = 0 if ti == 0 else 1
                        T.op("dve", lambda e, bank=bank, c=c, t0=t0, n=n, w=w: e.scalar_tensor_tensor(
                            out=self.xT[:, c, t0:t0 + n], in0=bank[:, 0:n], scalar=self.gsT[:, w, c:c + 1],
                            in1=self.xT[:, c, t0:t0 + n], op0=ALU.mult, op1=ALU.add),
                            reads=[bk, "gsT", "xT"], writes=["xT"])
            self._ln_all(l, sub, has_next)
            T.barrier()

    def _ln_alloc(self, les):
        self.ln_zb = [self.sb("ln_zb%d" % i, [128, 512], BF16, es=les) for i in range(2)]
        self.ln_zq = [self.sb("ln_zq%d" % i, [128, 512], BF16, es=les) for i in range(2)]
        self.ln_t = [self.sb("ln_t%d" % i, [128, 512], F32, es=les) for i in range(2)]
        self.ln_mean = self.sb("ln_mean", [128, 512], F32, es=les)
        self.ln_rstd = self.sb("ln_rstd", [128, 512], F32, es=les)
        self.ln_tmp = self.sb("ln_tmp", [128, 512], F32, es=les)

    def _ln_all(self, l, sub, has_next):
        T = self.T
        for ti, (t0, n) in enumerate(TILES):
            for c in range(NCH):
                zb = self.ln_zb[c % 2]
                zq = self.ln_zq[c % 2]
                kb, kq = "ln_zb%d" % (c % 2), "ln_zq%d" % (c % 2)
                T.op("act", lambda e, c=c, zb=zb: e.activation(out=zb[:], in_=self.xT[:, c, t0:t0 + n], func=AF.Copy),
                     reads=["xT"], writes=[kb])
                T.op("pool", lambda e, c=c, zq=zq: e.tensor_tensor(out=zq[:], in0=self.xT[:, c, t0:t0 + n],
                                                                   in1=self.xT[:, c, t0:t0 + n], op=ALU.mult),
                     reads=["xT"], writes=[kq])
                T.op("pe", lambda e, c=c, zb=zb: e.matmul(self.ps[6][:], self.ones_bf[:], zb[:], start=(c == 0), stop=(c == NCH - 1)),
                     reads=[kb, "ones_bf"], writes=["ps6"], same=False)
                T.op("pe", lambda e, c=c, zq=zq: e.matmul(self.ps[7][:], self.ones_bf[:], zq[:], start=(c == 0), stop=(c == NCH - 1)),
                     reads=[kq, "ones_bf"], writes=["ps7"], same=False)
            self._ln_tile(l, sub, ti, has_next, None)

    def _finish(self):
        nc, T = self.nc, self.T
        with ExitStack() as les:
            ys = [self.sb("ystage%d" % i, [128, D], F32, es=les) for i in range(2)]
            import os
            for tt in range(int(os.environ.get("DBG_NT", NTOK // 128))):
                s = ys[tt % 2]
                key = "ystage%d" % (tt % 2)
                for half in range(2):
                    bi = (tt * 2 + half) % 4
                    bank, bkey = self.ps[bi], "ps%d" % bi
                    for q in range(0 if os.environ.get("DBG_NOTR") else 4):
                        c = half * 4 + q
                        T.op("pe", lambda e, c=c, q=q, bank=bank, tt=tt: e.transpose(
                            bank[:, q * 128:(q + 1) * 128], self.xT[:, c, tt * 128:(tt + 1) * 128], self.ident_f[:]),
                            reads=["xT", "ident_f"], writes=[bkey] if q in (0, 3) else [], same=False)
                    if half == 0 or os.environ.get("DBG_NOACT"):
                        T.op("dve", lambda e, bank=bank, s=s, half=half: e.tensor_copy(out=s[:, half * 512:(half + 1) * 512], in_=bank[:]),
                             reads=[bkey], writes=[key if half == 0 else key + "h"])
                    else:
                        T.op("act", lambda e, bank=bank, s=s: e.activation(out=s[:, 512:1024], in_=bank[:], func=AF.Copy),
                             reads=[bkey], writes=[key + "h"])
                T.dma("sp", self.y_out[tt * 128:(tt + 1) * 128, :], s[:], reads=[key, key + "h"], writes=["y_out%d" % tt])
            T.wait_all("sp")


def _prep_inputs(inputs):
    consts = host_consts()
    shared = {}
    for k in ("w_ada", "b_ada", "ffn1_w_in", "ffn1_w_out", "ffn2_w_in", "ffn2_w_out", "ln_g", "ln_b"):
        shared[k] = np.ascontiguousarray(inputs[k], dtype=np.float32)
    shared["c_ones"] = consts["ones"]
    shared["c_ident"] = consts["ident"]
    shared["c_blockones"] = consts["blockones"]
    maps = []
    for core in range(8):
        m = dict(shared)
        xp = inputs["x_prompt"][2 * core:2 * core + 2].reshape(512, D)
        xs = inputs["x_sample"][core]
        m["x_tok"] = np.ascontiguousarray(np.concatenate([xp, xs], axis=0), dtype=np.float32)
        m["cvec"] = np.ascontiguousarray(np.stack([inputs["c_ctx"], inputs["c"][core]], axis=0), dtype=np.float32)
        maps.append(m)
    return maps


PROMPT_SEQS = [(0, 256), (256, 256)]
SAMPLE = (512, 1024)
NA_R = {0: (0, 5), 1: (0, 7), 2: (0, 9), 3: (0, 11), 4: (5, 15), 5: (7, 15), 6: (9, 15), 7: (11, 15)}
NVA, NVB = 12, 11


def _na_tables_idx():
    flat = np.zeros((128, NVA + NVB, 64), np.int64)
    mask = np.zeros((128, NVA + NVB, 64), np.float32)
    qc = np.arange(64)
    cs = np.clip(qc - 8, 0, 48)
    for half in range(2):
        for kc in range(64):
            p = half * 64 + kc
            colv = (kc >= cs) & (kc < cs + 16)
            dc = np.clip(kc - qc + 15, 0, 30)
            for v in range(NVA + NVB):
                idx = (v - 6) if v < NVA else (v - NVA - 3)
                dr = half - idx + 7
                rowv = True
                if v < NVA and idx == 5 and half == 0:
                    rowv = False
                if v >= NVA and idx == -3 and half == 1:
                    rowv = False
                drc = min(max(dr, 0), 14)
                flat[p, v] = drc * 31 + dc
                mask[p, v] = (colv & rowv).astype(np.float32)
    return flat, mask


def _rope_tables():
    t = np.arange(1024)
    n_freq = 16
    inv = 10000.0 ** (-np.arange(n_freq, dtype=np.float32) / n_freq)
    rows = (t // 64).astype(np.float32)
    cols = (t % 64).astype(np.float32)
    ang = np.concatenate([rows[:, None] * inv, cols[:, None] * inv], axis=-1)
    cos32, sin32 = np.cos(ang).astype(np.float32), np.sin(ang).astype(np.float32)
    cosT = np.zeros((128, 1024), np.float32)
    sinT = np.zeros((128, 1024), np.float32)
    for p in range(128):
        d = p % 64
        cosT[p] = cos32[:, d % 32]
        sinT[p] = sin32[:, d % 32] * (-1.0 if d < 32 else 1.0)
    return cosT, sinT


def _mixer_consts():
    c = {}
    a = np.arange(128)
    c["c_mlow"] = (a[None, :] <= a[:, None]).astype(np.float32)
    c["c_mup"] = (a[:, None] <= a[None, :]).astype(np.float32)
    pm = np.zeros((128, 128), np.float32)
    for m in range(128):
        k = (m & 64) | ((m + 32) & 63)
        pm[k, m] = 1.0
    c["c_swap"] = pm
    c["c_ropecos"], c["c_ropesin"] = _rope_tables()
    return c


def _mx_declare(self):
    L = DEPTH
    self.w_in = self.din("w_in", [L, D, IN_COLS])
    self.proj = [self.din("proj_a", [L, 512, D]), self.din("proj_b", [L, 512, D]), self.din("proj_c", [L, 512, D])]
    self.w_out = self.din("w_out", [L, D, D])
    self.sink = self.din("attn_sink", [L, 8])
    self.cak = self.din("cache_attn_k", [L, 256, 128])
    self.cav = self.din("cache_attn_v", [L, 256, 128])
    self.cbk = self.din("cache_na_k", [L, 256, 512])
    self.cbv = self.din("cache_na_v", [L, 256, 512])
    self.natab = self.din("na_tab", [L, 8, 128, (NVA + NVB) * 64])
    for nm in ("c_mlow", "c_mup", "c_swap"):
        setattr(self, nm, self.din(nm, [128, 128]))
    self.c_ropecos = self.din("c_ropecos", [128, 1024])
    self.c_ropesin = self.din("c_ropesin", [128, 1024])
    self.o_nak = self.dout("o_nak", [2, L, 256, 128])
    self.o_nav = self.dout("o_nav", [2, L, 256, 128])
    self.o_nbk = self.dout("o_nbk", [2, L, 256, 512])
    self.o_nbv = self.dout("o_nbv", [2, L, 256, 512])


def _mx_globals(self):
    T = self.T
    self.mlow = self.sb("mlow", [128, 128], BF16)
    self.mup = self.sb("mup", [128, 128], BF16)
    self.swapm = self.sb("swapm", [128, 128], BF16)
    T.dma("pool", self.mlow[:], self.c_mlow, writes=["mlow"])
    T.dma("pool", self.mup[:], self.c_mup, writes=["mup"])
    T.dma("pool", self.swapm[:], self.c_swap, writes=["swapm"])


def _mixer(self, l):
    T = self.T
    self._sub_scalars(l, 1, 5, 1.0, (6, 7))
    with ExitStack() as mes:
        self.mergedT = self.sb("mergedT", [128, NCH, NTOK], BF16, es=mes)
        self.sgb = [self.sb("sgb%d" % i, [128, 512], F32, es=mes) for i in range(2)]
        self.mtmp = [self.sb("mtmp%d" % i, [128, 512], F32, es=mes) for i in range(2)]
        self._attn(l, 0)
        self._attn(l, 1)
        if hasattr(self, "_rwkv"):
            self._rwkv(l)
        self._mix_out(l)
        T.barrier()


def _merge_branch(self, l, g, o_chunks, okey, prow0, t0, ntok, first):
    T = self.T
    nk = len(o_chunks)
    wg = self.w_in[l].rearrange("(kc p) n -> p kc n", p=128)
    wp = self.proj[g][l].rearrange("(kc p) n -> p kc n", p=128)
    k0 = prow0 // 128
    tiles = [(t0 + a, min(512, ntok - a)) for a in range(0, ntok, 512)]
    cnt = getattr(self, "_mb_cnt", 0)
    for m in range(NCH):
        s = self.wslot
        self.wslot = (self.wslot + 1) % self.NSLOT
        key = "wring%d" % s
        gv = self.wring[:, s, 0:1024].rearrange("p (kc n) -> p kc n", kc=NCH)
        pv = self.wring[:, s, 1024:1024 + nk * 128].rearrange("p (kc n) -> p kc n", kc=nk)
        gc0 = G_OFF + g * 1024 + m * 128
        T.dma("pool", [gv, pv], [wg[:, :, gc0:gc0 + 128], wp[:, k0:k0 + nk, m * 128:(m + 1) * 128]], writes=[key])
        for (tt0, n) in tiles:
            ba, bp = self.ps[(cnt % 2) * 2], self.ps[(cnt % 2) * 2 + 1]
            ka, kp = "ps%d" % ((cnt % 2) * 2), "ps%d" % ((cnt % 2) * 2 + 1)
            self.mmg(ba[:, 0:n], ka, [(gv[:, kc, :], self.hT[:, kc, tt0:tt0 + n]) for kc in range(NCH)], reads=[key, "hT"])
            self.mmg(bp[:, 0:n], kp, [(pv[:, kc, :], o_chunks[kc](tt0, n)) for kc in range(nk)], reads=[key, okey])
            sg = self.sgb[cnt % 2]
            sk = "sgb%d" % (cnt % 2)
            T.op("act", lambda e, ba=ba, sg=sg, n=n: e.activation(out=sg[:, 0:n], in_=ba[:, 0:n], func=AF.Sigmoid),
                 reads=[ka], writes=[sk])
            if first:
                T.op("dve", lambda e, bp=bp, sg=sg, m=m, tt0=tt0, n=n: e.tensor_tensor(
                    out=self.mergedT[:, m, tt0:tt0 + n], in0=bp[:, 0:n], in1=sg[:, 0:n], op=ALU.mult),
                    reads=[kp, sk], writes=["mergedT"])
            else:
                mt = self.mtmp[cnt % 2]
                mk = "mtmp%d" % (cnt % 2)
                T.op("dve", lambda e, bp=bp, sg=sg, mt=mt, n=n: e.tensor_tensor(out=mt[:, 0:n], in0=bp[:, 0:n], in1=sg[:, 0:n], op=ALU.mult),
                     reads=[kp, sk], writes=[mk])
                T.op("pool", lambda e, mt=mt, m=m, tt0=tt0, n=n: e.tensor_tensor(
                    out=self.mergedT[:, m, tt0:tt0 + n], in0=self.mergedT[:, m, tt0:tt0 + n], in1=mt[:, 0:n], op=ALU.add),
                    reads=[mk, "mergedT"], writes=["mergedT"])
            cnt += 1
    self._mb_cnt = cnt


def _mix_out(self, l):
    T = self.T
    wv = self.w_out[l].rearrange("(kc p) n -> p kc n", p=128)
    with ExitStack() as les:
        self._ln_alloc(les)
        cnt = 0
        for piece in range(2):
            dst, key = self.wload(lambda s_: s_.rearrange("p (kc n) -> p kc n", kc=NCH), wv[:, :, piece * 512:(piece + 1) * 512])
            for ti, (t0, n) in enumerate(TILES):
                w = 0 if ti == 0 else 1
                for q in range(4):
                    c = piece * 4 + q
                    bank, bk = self.ps[cnt % 4], "ps%d" % (cnt % 4)
                    self.mmg(bank[:, 0:n], bk, [(dst[:, kc, q * 128:(q + 1) * 128], self.mergedT[:, kc, t0:t0 + n]) for kc in range(NCH)],
                             reads=[key, "mergedT"])
                    T.op("dve", lambda e, bank=bank, c=c, t0=t0, n=n, w=w: e.scalar_tensor_tensor(
                        out=self.xT[:, c, t0:t0 + n], in0=bank[:, 0:n], scalar=self.gsT[:, w, c:c + 1],
                        in1=self.xT[:, c, t0:t0 + n], op0=ALU.mult, op1=ALU.add),
                        reads=[bk, "gsT", "xT"], writes=["xT"])
                    cnt += 1
        self._ln_all(l, 1, True)
        T.barrier()


def _attn_head(self, specs, q_fn, ncols, out_ap, rows, sink_ap, okey):
    T = self.T
    hc = self._ah_cnt
    self._ah_cnt += 1
    O, Ok = self.ps[2 + (hc % 2) * 2], "ps%d" % (2 + (hc % 2) * 2)
    Dn, Dk = self.ps[3 + (hc % 2) * 2], "ps%d" % (3 + (hc % 2) * 2)
    r0, r1 = rows
    n = len(specs)
    pend = None
    for idx in range(n + 1):
        if idx < n:
            sp = specs[idx]
            sc = self._as_cnt
            self._as_cnt += 1
            sbk, sk = self.ps[sc % 2], "ps%d" % (sc % 2)
            w = sp["c1"] - sp["c0"]
            self.mmg(sbk[:, 0:w], sk, [(sp["kT"], q_fn(sp["c0"], sp["c1"]))], reads=sp["keys"])
            pt, pk = self.ptb[sc % 4], "ptb%d" % (sc % 4)
            T.op("act", lambda e, sbk=sbk, pt=pt, w=w: e.activation(out=pt[:, 0:w], in_=sbk[:, 0:w], func=AF.Exp, scale=SCALE),
                 reads=[sk], writes=[pk])
            for (a, b, mk_ap, mkey) in sp.get("masks", ()):
                T.op("dve", lambda e, pt=pt, a=a, b=b, mk_ap=mk_ap: e.tensor_tensor(out=pt[:, a:b], in0=pt[:, a:b], in1=mk_ap, op=ALU.mult),
                     reads=[pk, mkey], writes=[pk])
            cur = (sp, pt, pk, w, idx)
        else:
            cur = None
        if pend is not None:
            sp, pt, pk, w, i = pend
            first, last = (i == 0), (i == n - 1)
            for (bank, bk, lt) in ((O, Ok, sp["v"]), (Dn, Dk, self.ones_bf[:])):
                T.op("pe", lambda e, bank=bank, lt=lt, pt=pt, w=w, sp=sp, first=first, last=last: e.matmul(
                    bank[:, sp["c0"]:sp["c1"]], lt, pt[:, 0:w], start=first, stop=last),
                    reads=[pk, "ones_bf"] + sp["keys"], writes=[bk] if (first or last) else [], same=False)
        pend = cur
    rc, rk = self.rcb[hc % 2], "rcb%d" % (hc % 2)
    import os
    if "dbg_misc" in self.dram_out and os.environ.get("DBG_HEAD") and int(os.environ["DBG_HEAD"]) == hc and not getattr(self, "_dbg_done", False):
        self._dbg_done = True
        T.op("dve", lambda e: e.tensor_copy(out=self.mtmp[0][:, 0:ncols], in_=Dn[:, 0:ncols]), reads=[Dk], writes=["mtmp0"])
        T.dma("sp", self.dram_out["dbg_misc"][:, 1024:1024 + ncols], self.mtmp[0][:, 0:ncols], reads=["mtmp0"], writes=["dbgm2"])
        T.op("dve", lambda e: e.tensor_copy(out=self.mtmp[1][:, 0:ncols], in_=O[:, 0:ncols]), reads=[Ok], writes=["mtmp1"])
        T.dma("sp", self.dram_out["dbg_misc"][:, 1536:1536 + ncols], self.mtmp[1][:, 0:ncols], reads=["mtmp1"], writes=["dbgm3"])
    if sink_ap is not None:
        T.op("dve", lambda e: e.tensor_scalar(out=rc[r0:r1, 0:ncols], in0=Dn[r0:r1, 0:ncols], scalar1=sink_ap, scalar2=None, op0=ALU.add),
             reads=[Dk, "esink"], writes=[rk])
        T.op("dve", lambda e: e.reciprocal(out=rc[r0:r1, 0:ncols], in_=rc[r0:r1, 0:ncols]), reads=[rk], writes=[rk])
    else:
        T.op("dve", lambda e: e.reciprocal(out=rc[r0:r1, 0:ncols], in_=Dn[r0:r1, 0:ncols]), reads=[Dk], writes=[rk])
    T.op("dve", lambda e: e.tensor_tensor(out=out_ap, in0=O[r0:r1, 0:ncols], in1=rc[r0:r1, 0:ncols], op=ALU.mult),
         reads=[Ok, rk], writes=[okey])


for _f in (_mx_declare, _mx_globals, _mixer, _merge_branch, _mix_out, _attn_head):
    setattr(Builder, _f.__name__, _f)


def _attn(self, l, which):
    nc, T = self.nc, self.T
    A = (which == 0)
    nkc = 2 if A else 4
    qoff = A_OFF if A else B_OFF
    wv = self.w_in[l].rearrange("(kc p) n -> p kc n", p=128)
    self._ah_cnt = 0
    self._as_cnt = 0
    with ExitStack() as aes:
        qT = self.sb("qT", [128, 4, NTOK], BF16, es=aes)
        kT = self.sb("kT", [128, nkc, NTOK], BF16, es=aes)
        VW = 256 if A else 512
        vtm = self.sb("vtm", [128, 12, VW], BF16, es=aes)
        ckT = self.sb("ckT", [128, nkc, 256], BF16, es=aes)
        cv = self.sb("cv", [128, 2, VW], BF16, es=aes)
        oT = self.sb("oT", [128, 4, NTOK], BF16, es=aes)
        self.ptb = [self.sb("ptb%d" % i, [128, 512], BF16, es=aes) for i in range(4)]
        self.rcb = [self.sb("rcb%d" % i, [128, 512], F32, es=aes) for i in range(2)]
        ostg = [self.sb("ostg%d" % i, [128, 512], F32, es=aes) for i in range(2)]
        if A:
            rcos = self.sb("rcos", [128, 1024], BF16, es=aes)
            rsin = self.sb("rsin", [128, 1024], BF16, es=aes)
            esink = self.sb("esink", [128, 8], F32, es=aes)
            T.dma("pool", rcos[:], self.c_ropecos, writes=["rcos"])
            T.dma("pool", rsin[:], self.c_ropesin, writes=["rsin"])
            T.dma("sp", esink[:], self.sink[l].partition_broadcast(128), writes=["esink"])
            T.op("act", lambda e: e.activation(out=esink[:], in_=esink[:], func=AF.Exp), reads=["esink"], writes=["esink"])
        else:
            etab = self.sb("etab", [128, (NVA + NVB) * 64], BF16, es=aes)
        pcnt = [0]

        def bankof():
            i = pcnt[0] % 2
            pcnt[0] += 1
            return self.ps[i], "ps%d" % i

        def evac(i, out_ap, in_ap, reads, writes):
            if i % 2 == 0:
                T.op("act", lambda e: e.activation(out=out_ap, in_=in_ap, func=AF.Copy), reads=reads, writes=writes)
            else:
                T.op("dve", lambda e: e.tensor_copy(out=out_ap, in_=in_ap), reads=reads, writes=writes)

        dst, key = self.wload(lambda s_: s_.rearrange("p (kc n) -> p kc n", kc=NCH), wv[:, :, qoff:qoff + 512])
        for c in range(4):
            for (t0, n) in TILES:
                bank, bk = bankof()
                self.mmg(bank[:, 0:n], bk, [(dst[:, kc, c * 128:(c + 1) * 128], self.hT[:, kc, t0:t0 + n]) for kc in range(NCH)], reads=[key, "hT"])
                evac(pcnt[0], qT[:, c, t0:t0 + n], bank[:, 0:n], [bk], ["qT"])
        import os
        stopat = os.environ.get("DBG_STOP", "")
        if stopat == "q":
            T.op("dve", lambda e: e.memset(oT[:], 0.0), writes=["oT"])
            self._merge_branch(l, which, [(lambda t0, n, c=c: oT[:, c, t0:t0 + n]) for c in range(4)], "oT", 0, 0, NTOK, first=A)
            return
        if A:
            s = self.wslot
            self.wslot = (self.wslot + 1) % self.NSLOT
            key = "wring%d" % s
            dst = self.wring[:, s, 0:NCH * 256].rearrange("p (kc kv dup d) -> p kc kv dup d", kc=NCH, kv=2, dup=2)
            T.dma("pool", [dst[:, :, kv, dup, :] for kv in range(2) for dup in range(2)],
                  [wv[:, :, 512 + kv * 64:512 + (kv + 1) * 64] for kv in range(2) for dup in range(2)], writes=[key])
            kw = lambda kc, c: dst[:, kc, c, :, :]
        else:
            dst, key = self.wload(lambda s_: s_.rearrange("p (kc n) -> p kc n", kc=NCH), wv[:, :, B_OFF + 512:B_OFF + 1024])
            kw = lambda kc, c: dst[:, kc, c * 128:(c + 1) * 128]
        for c in range(nkc):
            for (t0, n) in TILES:
                bank, bk = bankof()
                self.mmg(bank[:, 0:n], bk, [(kw(kc, c), self.hT[:, kc, t0:t0 + n]) for kc in range(NCH)], reads=[key, "hT"])
                evac(pcnt[0], kT[:, c, t0:t0 + n], bank[:, 0:n], [bk], ["kT"])
        if stopat == "k":
            T.op("dve", lambda e: e.memset(oT[:], 0.0), writes=["oT"])
            self._merge_branch(l, which, [(lambda t0, n, c=c: oT[:, c, t0:t0 + n]) for c in range(4)], "oT", 0, 0, NTOK, first=A)
            return
        ocnt = [0]

        def out_rows(dram_ap, st, width, bank):
            if os.environ.get("DBG_NOOUTROWS"):
                return
            og, ogk = ostg[ocnt[0] % 2], "ostg%d" % (ocnt[0] % 2)
            ocnt[0] += 1
            T.op("dve", lambda e: e.tensor_copy(out=og[:, 0:width], in_=bank), reads=[bk_cur[0]], writes=[ogk])
            sq, tl = st // 2, (st % 2) * 128
            if os.environ.get("DBG_NOOUTDMA"):
                return
            T.dma("sp", dram_ap[sq, l, tl:tl + 128, :], og[:, 0:width], reads=[ogk], writes=["outrows%d" % ocnt[0]])

        bk_cur = [None]
        if A:
            dst, key = self.wload(lambda s_: s_[:, 0:NCH * 256].rearrange("p (kc n) -> p kc n", kc=NCH), wv[:, :, 512:768])
            for st in range(12):
                bank, bk = bankof()
                bk_cur[0] = bk
                self.mmg(bank[:, 0:256], bk, [(self.hT[:, kc, st * 128:(st + 1) * 128], dst[:, kc, :]) for kc in range(NCH)], reads=[key, "hT"])
                if st < 4:
                    out_rows(self.o_nak, st, 128, bank[:, 0:128])
                    out_rows(self.o_nav, st, 128, bank[:, 128:256])
                for dup in range(2):
                    T.op("act", lambda e, bank=bank, st=st, dup=dup: e.activation(
                        out=vtm[:, st, :].rearrange("p (kv dup d) -> p kv dup d", kv=2, dup=2)[:, :, dup, :],
                        in_=bank[:, 128:256].rearrange("p (kv d) -> p kv d", kv=2), func=AF.Copy),
                        reads=[bk], writes=["vtm"])
        else:
            dstk, keyk = self.wload(lambda s_: s_.rearrange("p (kc n) -> p kc n", kc=NCH), wv[:, :, B_OFF + 512:B_OFF + 1024])
            for st in range(4):
                bank, bk = bankof()
                bk_cur[0] = bk
                self.mmg(bank[:, 0:512], bk, [(self.hT[:, kc, st * 128:(st + 1) * 128], dstk[:, kc, :]) for kc in range(NCH)], reads=[keyk, "hT"])
                out_rows(self.o_nbk, st, 512, bank[:, 0:512])
            dst, key = self.wload(lambda s_: s_.rearrange("p (kc n) -> p kc n", kc=NCH), wv[:, :, B_OFF + 1024:B_OFF + 1536])
            for st in range(12):
                bank, bk = bankof()
                bk_cur[0] = bk
                self.mmg(bank[:, 0:512], bk, [(self.hT[:, kc, st * 128:(st + 1) * 128], dst[:, kc, :]) for kc in range(NCH)], reads=[key, "hT"])
                if st < 4:
                    out_rows(self.o_nbv, st, 512, bank[:, 0:512])
                T.op("act", lambda e, bank=bank, st=st: e.activation(out=vtm[:, st, :], in_=bank[:, 0:512], func=AF.Copy),
                     reads=[bk], writes=["vtm"])
        import os
        if os.environ.get("DBG_NOCACHE"):
            pass
        elif A:
            for ct in range(2):
                T.dma("pool", [cv[:, ct, :].rearrange("p (kv dup d) -> p kv dup d", kv=2, dup=2)[:, :, dup, :] for dup in range(2)],
                      [self.cav[l, ct * 128:(ct + 1) * 128, :].rearrange("t (kv d) -> t kv d", kv=2) for dup in range(2)], writes=["cv"])
                og, ogk = ostg[ct], "ostg%d" % ct
                T.dma("sp", [og[:, 0:256].rearrange("p (kv dup d) -> p kv dup d", kv=2, dup=2)[:, :, dup, :] for dup in range(2)],
                      [self.cak[l, ct * 128:(ct + 1) * 128, :].rearrange("t (kv d) -> t kv d", kv=2) for dup in range(2)], writes=[ogk])
                for kv in range(2):
                    bank, bk = self.ps[6 + kv], "ps%d" % (6 + kv)
                    T.op("pe", lambda e, bank=bank, og=og, kv=kv: e.transpose(bank[:, 0:128], og[:, kv * 128:(kv + 1) * 128], self.ident_f[:]),
                         reads=[ogk, "ident_f"], writes=[bk], same=False)
                    T.op("dve", lambda e, bank=bank, kv=kv, ct=ct: e.tensor_copy(out=ckT[:, kv, ct * 128:(ct + 1) * 128], in_=bank[:, 0:128]),
                         reads=[bk], writes=["ckT"])
        else:
            for ct in range(2):
                T.dma("pool", cv[:, ct, :], self.cbv[l, ct * 128:(ct + 1) * 128, :], writes=["cv"])
                og, ogk = ostg[ct], "ostg%d" % ct
                T.dma("sp", og[:], self.cbk[l, ct * 128:(ct + 1) * 128, :], writes=[ogk])
                bank, bk = self.ps[6 + ct], "ps%d" % (6 + ct)
                for c in range(4):
                    T.op("pe", lambda e, bank=bank, og=og, c=c: e.transpose(bank[:, c * 128:(c + 1) * 128], og[:, c * 128:(c + 1) * 128], self.ident_f[:]),
                         reads=[ogk, "ident_f"], writes=[bk] if c in (0, 3) else [], same=False)
                T.op("dve", lambda e, bank=bank, ct=ct: e.tensor_copy(out=ckT[:, :, ct * 128:(ct + 1) * 128],
                                                                      in_=bank[:].rearrange("p (c t) -> p c t", c=4)),
                     reads=[bk], writes=["ckT"])
        if "dbg_misc" in self.dram_out and l == 0 and A and os.environ.get("DBG_DUMPM"):
            T.op("dve", lambda e: e.tensor_copy(out=self.rcb[0][:, 0:128], in_=self.mlow[:]), reads=["mlow"], writes=["rcb0"])
            T.op("dve", lambda e: e.tensor_copy(out=self.rcb[0][:, 128:256], in_=self.mup[:]), reads=["mup"], writes=["rcb0"])
            T.op("dve", lambda e: e.tensor_copy(out=self.rcb[0][:, 256:384], in_=self.swapm[:]), reads=["swapm"], writes=["rcb0"])
            T.op("dve", lambda e: e.tensor_copy(out=self.rcb[0][:, 384:512], in_=rcos[:, 0:128]), reads=["rcos"], writes=["rcb0"])
            T.dma("sp", self.dram_out["dbg_misc"][:, 0:512], self.rcb[0][:], reads=["rcb0"], writes=["dbgm0"])
        elif "dbg_misc" in self.dram_out and l == 0 and A:
            T.op("dve", lambda e: e.tensor_copy(out=self.rcb[0][:], in_=ckT[:].rearrange("p a b -> p (a b)")), reads=["ckT"], writes=["rcb0"])
            T.dma("sp", self.dram_out["dbg_misc"][:, 0:512], self.rcb[0][:], reads=["rcb0"], writes=["dbgm0"])
            T.op("dve", lambda e: e.tensor_copy(out=self.rcb[1][:], in_=cv[:].rearrange("p a b -> p (a b)")), reads=["cv"], writes=["rcb1"])
            T.dma("sp", self.dram_out["dbg_misc"][:, 512:1024], self.rcb[1][:], reads=["rcb1"], writes=["dbgm1"])
        import os
        if A and not os.environ.get("DBG_NOROPE"):
            rc = 0
            for (arr, akey, nchunk) in ((qT, "qT", 4), (kT, "kT", 2)):
                for c in range(nchunk):
                    for qt in range(2):
                        t0 = 512 + qt * 512
                        bank, bk = self.ps[6 + rc % 2], "ps%d" % (6 + rc % 2)
                        rc += 1
                        x = arr[:, c, t0:t0 + 512]
                        self.mmg(bank[:], bk, [(self.swapm[:], x)], reads=[akey, "swapm"])
                        T.op("dve", lambda e, x=x, qt=qt: e.tensor_tensor(out=self.mtmp[0][:], in0=x, in1=rcos[:, qt * 512:(qt + 1) * 512], op=ALU.mult),
                             reads=[akey, "rcos"], writes=["mtmp0"])
                        T.op("dve", lambda e, bank=bank, qt=qt: e.tensor_tensor(out=self.mtmp[1][:], in0=bank[:], in1=rsin[:, qt * 512:(qt + 1) * 512], op=ALU.mult),
                             reads=[bk, "rsin"], writes=["mtmp1"])
                        T.op("pool", lambda e, x=x: e.tensor_tensor(out=x, in0=self.mtmp[0][:], in1=self.mtmp[1][:], op=ALU.add),
                             reads=["mtmp0", "mtmp1"], writes=[akey])
        import os
        if os.environ.get("DBG_NOHEADS"):
            T.op("dve", lambda e: e.memset(oT[:], 0.0), writes=["oT"])
        for hp in range(0 if os.environ.get("DBG_NOHEADS") else 4):
            for par in range(2):
                h = hp * 2 + par
                if not A:
                    T.dma("pool", etab[:], self.natab[l, h], writes=["etab"])
                    T.op("act", lambda e: e.activation(out=etab[:], in_=etab[:], func=AF.Exp), reads=["etab"], writes=["etab"])
                rows = (par * 64, par * 64 + 64)
                kc_ = (h // 4) if A else hp
                vsl = (lambda a: a[:, kc_ * 128:(kc_ + 1) * 128])
                sink_ap = esink[rows[0]:rows[1], h:h + 1] if A else None
                for sq in range(2):
                    b0 = sq * 256
                    specs = [dict(kT=kT[rows[0]:rows[1], kc_, b0 + kt * 128:b0 + (kt + 1) * 128], v=vsl(vtm[:, sq * 2 + kt, :]),
                                  c0=0, c1=256, keys=["kT", "vtm", "qT"]) for kt in range(2)]
                    self._attn_head(specs, lambda c0, c1, b0=b0: qT[rows[0]:rows[1], hp, b0 + c0:b0 + c1], 256,
                                    oT[rows[0]:rows[1], hp, b0:b0 + 256], rows, sink_ap, "oT")
                for qt in range(2):
                    b0 = 512 + qt * 512
                    specs = [dict(kT=ckT[rows[0]:rows[1], kc_, ct * 128:(ct + 1) * 128], v=vsl(cv[:, ct, :]), c0=0, c1=512,
                                  keys=["ckT", "cv", "qT"]) for ct in range(2)]
                    if A and os.environ.get("DBG_ANOLOCAL"):
                        pass
                    elif A:
                        for j in range(4 * qt - 1, 4 * qt + 5):
                            if j < 0 or j > 7:
                                continue
                            ilo, ihi = max(j - 1, 4 * qt), min(j + 1, 4 * qt + 3)
                            masks = []
                            for i in range(ilo, ihi + 1):
                                a = (i - ilo) * 128
                                if i == j + 1:
                                    masks.append((a, a + 128, self.mlow[:], "mlow"))
                                elif i == j - 1:
                                    masks.append((a, a + 128, self.mup[:], "mup"))
                            specs.append(dict(kT=kT[rows[0]:rows[1], kc_, 512 + j * 128:512 + (j + 1) * 128], v=vsl(vtm[:, 4 + j, :]),
                                              c0=(ilo - 4 * qt) * 128, c1=(ihi - 4 * qt + 1) * 128, masks=masks, keys=["kT", "vtm", "qT"]))
                    else:
                        for j in range(8):
                            ra, rb = NA_R[j]
                            lo, hi = max(ra, 8 * qt), min(rb, 8 * qt + 7)
                            if lo > hi:
                                continue
                            c0, c1 = (lo - 8 * qt) * 64, (hi - 8 * qt + 1) * 64
                            v0 = (lo - 2 * j + 6) if j <= 3 else (NVA + lo - 2 * j + 3)
                            masks = [(0, c1 - c0, etab[:, v0 * 64:v0 * 64 + (c1 - c0)], "etab")]
                            specs.append(dict(kT=kT[rows[0]:rows[1], kc_, 512 + j * 128:512 + (j + 1) * 128], v=vsl(vtm[:, 4 + j, :]),
                                              c0=c0, c1=c1, masks=masks, keys=["kT", "vtm", "qT"]))
                    self._attn_head(specs, lambda c0, c1, b0=b0: qT[rows[0]:rows[1], hp, b0 + c0:b0 + c1], 512,
                                    oT[rows[0]:rows[1], hp, b0:b0 + 512], rows, sink_ap, "oT")
        if "dbg_oT" in self.dram_out and l == 0:
            for c in range(4):
                for (t0, n) in TILES:
                    og, ogk = ostg[c % 2], "ostg%d" % (c % 2)
                    T.op("dve", lambda e, og=og, c=c, t0=t0, n=n: e.tensor_copy(out=og[:, 0:n], in_=oT[:, c, t0:t0 + n]), reads=["oT"], writes=[ogk])
                    T.dma("sp", self.dram_out["dbg_oT"][which, c, :, t0:t0 + n], og[:, 0:n], reads=[ogk], writes=["dbgo%d_%d_%d" % (which, c, t0)])
        self._merge_branch(l, which, [(lambda t0, n, c=c: oT[:, c, t0:t0 + n]) for c in range(4)], "oT", 0, 0, NTOK, first=A)
        T.barrier()


Builder._attn = _attn
_old_declare = Builder._declare
_old_globals = Builder._globals


def _declare2(self):
    _old_declare(self)
    self._mx_declare()


def _globals2(self):
    _old_globals(self)
    self._mx_globals()


Builder._declare = _declare2
Builder._globals = _globals2


def _prep_inputs2(inputs):
    maps = _prep_inputs(inputs)
    mc = _mixer_consts()
    flat, mask = _na_tables_idx()
    flat = np.where(mask > 0, flat, 15 * 31)
    rpb = np.asarray(inputs["na_rpb"], np.float32).reshape(DEPTH, 8, 15 * 31)
    rpb = np.concatenate([rpb, np.full((DEPTH, 8, 1), -1.0e4, np.float32)], axis=-1)
    na_tab = rpb[:, :, flat.reshape(-1)].reshape(DEPTH, 8, 128, (NVA + NVB) * 64)
    shared = dict(mc)
    shared["na_tab"] = np.ascontiguousarray(na_tab)
    for k in ("w_in", "proj_a", "proj_b", "proj_c", "w_out", "attn_sink"):
        shared[k] = np.ascontiguousarray(inputs[k], dtype=np.float32)
    for core, m in enumerate(maps):
        m.update(shared)
        m["cache_attn_k"] = np.ascontiguousarray(inputs["cache_attn_k"][core].reshape(DEPTH, 256, 128))
        m["cache_attn_v"] = np.ascontiguousarray(inputs["cache_attn_v"][core].reshape(DEPTH, 256, 128))
        m["cache_na_k"] = np.ascontiguousarray(inputs["cache_na_k"][core].reshape(DEPTH, 256, 512))
        m["cache_na_v"] = np.ascontiguousarray(inputs["cache_na_v"][core].reshape(DEPTH, 256, 512))
    return maps


def _rw_declare(self):
    L = DEPTH
    self.w_shift = self.din("w_shift", [L, 3, 1920])
    self.decay_w0 = self.din("decay_w0", [L, 2, 512])
    self.decay_up = self.din("decay_up", [L, 128, 512])
    self.iclr_a0 = self.din("iclr_a0", [L, 2, 512])
    self.iclr_up = self.din("iclr_up", [L, 128, 512])
    self.gate_up = self.din("gate_up", [L, 128, 512])
    self.vec512 = {k: self.din(k, [L, 512]) for k in ("k_k", "k_a", "r_k", "gn_g", "gn_b")}
    self.st_in = self.din("state_rwkv", [L, 2, 8, 64, 64])
    self.o_nst = self.dout("o_nst", [2, L, 2, 8, 64, 64])
    self.c_identh = self.din("c_identh", [128, 64])


def _rw_layer_consts(self, l, es):
    T = self.T
    R = {}
    R["shT"] = self.sb("shT", [128, 3, 15], F32, es=es)
    self._load_vecT(self.w_shift[l].rearrange("s (c p) -> (s c) p", p=128), 45, "sh")
    T.op("dve", lambda e: e.tensor_copy(out=R["shT"][:].rearrange("p s c -> p (s c)"), in_=self.ps[7][:, 0:45]), reads=["ps7"], writes=["shT"])
    R["vecs"] = self.sb("rwvecs", [128, 36], F32, es=es)
    srcs = [self.decay_w0[l].rearrange("d (c p) -> (d c) p", p=128), self.iclr_a0[l].rearrange("d (c p) -> (d c) p", p=128)]
    srcs += [self.vec512[k][l].rearrange("(c p) -> c p", p=128) for k in ("k_k", "k_a", "r_k", "gn_g", "gn_b")]
    off = 0
    for sap, nr in zip(srcs, (8, 8, 4, 4, 4, 4, 4)):
        self._load_vecT(sap, nr, "rwv")
        T.op("dve", lambda e, off=off, nr=nr: e.tensor_copy(out=R["vecs"][:, off:off + nr], in_=self.ps[7][:, 0:nr]), reads=["ps7"], writes=["rwvecs"])
        off += nr
    R["omka"] = self.sb("omka", [128, 4], F32, es=es)
    T.op("dve", lambda e: e.tensor_scalar(out=R["omka"][:], in0=R["vecs"][:, 20:24], scalar1=-1.0, scalar2=1.0, op0=ALU.mult, op1=ALU.add),
         reads=["rwvecs"], writes=["omka"])
    R["dup"] = self.sb("dupw", [128, 512], BF16, es=es)
    R["iup"] = self.sb("iupw", [128, 512], BF16, es=es)
    R["gup"] = self.sb("gupw", [128, 512], BF16, es=es)
    T.dma("pool", R["dup"][:], self.decay_up[l], writes=["dupw"])
    T.dma("pool", R["iup"][:], self.iclr_up[l], writes=["iupw"])
    T.dma("pool", R["gup"][:], self.gate_up[l], writes=["gupw"])
    R["identh"] = self.sb("identh", [128, 64], BF16, es=es)
    T.dma("pool", R["identh"][:], self.c_identh, writes=["identh"])
    return R


def _rwkv(self, l):
    T = self.T
    with ExitStack() as res_:
        R = self._rw_layer_consts(l, res_)
        units = [dict(segs=[(0, 256), (256, 256)], pairs=[0, 1, 2, 3], sample=False),
                 dict(segs=[(512, 1024)], pairs=[0, 1], sample=True),
                 dict(segs=[(512, 1024)], pairs=[2, 3], sample=True)]
        for u in units:
            self._rw_unit(l, R, u)
        T.barrier()


def _rw_unit(self, l, R, u):
    nc, T = self.nc, self.T
    segs, pairs, sample = u["segs"], u["pairs"], u["sample"]
    nseg, npair = len(segs), len(pairs)
    t00 = segs[0][0]
    L = segs[0][1]
    TU = nseg * L
    tiles = [(t00 + a, 512) for a in range(0, TU, 512)]
    G = npair * nseg
    wv = self.w_in[l].rearrange("(kc p) n -> p kc n", p=128)
    vec = R["vecs"]
    with ExitStack() as ues:
        kdT = [self.sb("kdT%d" % d, [128, npair, TU], BF16, es=ues) for d in range(2)]
        kapT = self.sb("kapT", [128, npair, TU], BF16, es=ues)
        rT = self.sb("rT", [128, npair, TU], BF16, es=ues)
        vT = self.sb("vT", [128, npair, TU], BF16, es=ues)
        B = {}
        pc = [0]

        def pbank():
            i = pc[0] % 2
            pc[0] += 1
            return self.ps[i], "ps%d" % i

        def conv_chunk(ci, out_ap, okey, post=None):
            zraw, tmp = B["zraw"], B["tmp"]
            c0 = C_OFF + ci * 128
            s = self.wslot
            self.wslot = (self.wslot + 1) % self.NSLOT
            key = "wring%d" % s
            dst = self.wring[:, s, 0:1024].rearrange("p (kc n) -> p kc n", kc=NCH)
            T.dma("pool", dst, wv[:, :, c0:c0 + 128], writes=[key])
            for ti, (tt0, n) in enumerate(tiles):
                bank, bk = pbank()
                self.mmg(bank[:, 0:n], bk, [(dst[:, kc_, :], self.hT[:, kc_, tt0:tt0 + n]) for kc_ in range(NCH)], reads=[key, "hT"])
                T.op("act", lambda e, bank=bank, ti=ti, n=n: e.activation(out=zraw[:, ti * 512:ti * 512 + n], in_=bank[:, 0:n], func=AF.Copy),
                     reads=[bk], writes=["zraw"])
            sh = R["shT"]
            T.op("dve", lambda e: e.tensor_scalar(out=tmp[:], in0=zraw[:], scalar1=sh[:, 1, ci:ci + 1], scalar2=None, op0=ALU.mult),
                 reads=["zraw", "shT"], writes=["rwtmp"])
            for si in range(nseg):
                a, b = si * L, (si + 1) * L
                T.op("dve", lambda e, a=a, b=b: e.scalar_tensor_tensor(out=tmp[:, a + 1:b], in0=zraw[:, a:b - 1], scalar=sh[:, 0, ci:ci + 1],
                                                                       in1=tmp[:, a + 1:b], op0=ALU.mult, op1=ALU.add),
                     reads=["zraw", "shT", "rwtmp"], writes=["rwtmp"])
                T.op("dve", lambda e, a=a, b=b: e.scalar_tensor_tensor(out=tmp[:, a:b - 1], in0=zraw[:, a + 1:b], scalar=sh[:, 2, ci:ci + 1],
                                                                       in1=tmp[:, a:b - 1], op0=ALU.mult, op1=ALU.add),
                     reads=["zraw", "shT", "rwtmp"], writes=["rwtmp"])
            T.op("act", lambda e: e.activation(out=out_ap, in_=tmp[:], func=(post or AF.Copy)), reads=["rwtmp"], writes=[okey])
        yT = self.sb("yT", [128, npair, TU], F32, es=ues)
        T.op("pool", lambda e: e.memset(yT[:], 0.0), writes=["yT"])
        wsc = ExitStack()
        wT = [self.sb("wT%d" % d, [128, npair, TU], F32, es=wsc) for d in range(2)]
        bpT = [self.sb("bpT%d" % d, [128, npair, TU], BF16, es=wsc) for d in range(2)]
        with ExitStack() as pes:
            zraw = self.sb("zraw", [128, TU], F32, es=pes)
            kc = self.sb("kcv", [128, TU], F32, es=pes)
            av = zraw
            tmp = self.sb("rwtmp", [128, TU], F32, es=pes)
            tmpb = self.sb("rwtmpb", [128, TU], BF16, es=pes)
            twlo = self.sb("twlo", [128, TU], BF16, es=pes)
            talo = self.sb("talo", [128, TU], BF16, es=pes)
            B["zraw"], B["tmp"] = zraw, tmp
            conv_chunk(12, twlo[:], "twlo", AF.Tanh)
            conv_chunk(13, talo[:], "talo")
            for qi, p in enumerate(pairs):
                conv_chunk(p, rT[:, qi, :], "rT")
                conv_chunk(8 + p, vT[:, qi, :], "vT")
                conv_chunk(4 + p, kc[:], "kcv")
                T.op("dve", lambda e, p=p: e.tensor_scalar(out=av[:], in0=kc[:], scalar1=vec[:, 16 + p:17 + p], scalar2=None, op0=ALU.mult),
                     reads=["kcv", "rwvecs"], writes=["zraw"])
                T.op("pool", lambda e: e.tensor_tensor(out=tmpb[:], in0=av[:], in1=av[:], op=ALU.mult), reads=["zraw"], writes=["rwtmpb"])
                for ti in range(TU // 512):
                    bank, bk = pbank()
                    sl = slice(ti * 512, (ti + 1) * 512)
                    self.mmg(bank[:], bk, [(self.bones_bf[:], tmpb[:, sl])], reads=["bones_bf", "rwtmpb"])
                    T.op("act", lambda e, bank=bank, sl=sl: e.activation(out=tmp[:, sl], in_=bank[:], func=AF.Sqrt), reads=[bk], writes=["rwtmp"])
                T.op("dve", lambda e: e.tensor_scalar(out=tmp[:], in0=tmp[:], scalar1=1e-12, scalar2=None, op0=ALU.max), reads=["rwtmp"], writes=["rwtmp"])
                T.op("dve", lambda e: e.reciprocal(out=tmp[:], in_=tmp[:]), reads=["rwtmp"], writes=["rwtmp"])
                T.op("dve", lambda e, qi=qi: e.tensor_tensor(out=kapT[:, qi, :], in0=av[:], in1=tmp[:], op=ALU.mult), reads=["zraw", "rwtmp"], writes=["kapT"])
                for d in range(2):
                    hs = slice(d * 64, (d + 1) * 64)
                    for ti in range(TU // 512):
                        sl = slice(ti * 512, (ti + 1) * 512)
                        bank, bk = pbank()
                        self.mmg(bank[:], bk, [(R["dup"][hs, p * 128:(p + 1) * 128], twlo[hs, sl])], reads=["dupw", "twlo"])
                        T.op("act", lambda e, bank=bank, sl=sl, d=d, p=p: e.activation(out=tmp[:, sl], in_=bank[:], func=AF.Sigmoid,
                                                                                      bias=vec[:, d * 4 + p:d * 4 + p + 1], scale=1.0),
                             reads=[bk, "rwvecs"], writes=["rwtmp"])
                        bank2, bk2 = pbank()
                        self.mmg(bank2[:], bk2, [(R["iup"][hs, p * 128:(p + 1) * 128], talo[hs, sl])], reads=["iupw", "talo"])
                        T.op("act", lambda e, bank2=bank2, sl=sl, d=d, p=p: e.activation(out=av[:, sl], in_=bank2[:], func=AF.Sigmoid,
                                                                                        bias=vec[:, 8 + d * 4 + p:8 + d * 4 + p + 1], scale=1.0),
                             reads=[bk2, "rwvecs"], writes=["zraw"])
                    T.op("act", lambda e, d=d, qi=qi: e.activation(out=wT[d][:, qi, :], in_=tmp[:], func=AF.Exp, scale=-float(np.exp(-0.5))),
                         reads=["rwtmp"], writes=["wT%d" % d])
                    T.op("dve", lambda e, d=d, qi=qi: e.scalar_tensor_tensor(out=bpT[d][:, qi, :], in0=kapT[:, qi, :], scalar=-1.0, in1=av[:],
                                                                             op0=ALU.mult, op1=ALU.mult),
                         reads=["kapT", "zraw"], writes=["bpT%d" % d])
                    T.op("dve", lambda e, p=p: e.tensor_scalar(out=tmp[:], in0=av[:], scalar1=vec[:, 20 + p:21 + p], scalar2=R["omka"][:, p:p + 1],
                                                               op0=ALU.mult, op1=ALU.add),
                         reads=["zraw", "rwvecs", "omka"], writes=["rwtmp"])
                    T.op("dve", lambda e, d=d, qi=qi: e.tensor_tensor(out=kdT[d][:, qi, :], in0=kc[:], in1=tmp[:], op=ALU.mult),
                         reads=["kcv", "rwtmp"], writes=["kdT%d" % d])
            T.barrier()
        with ExitStack() as ses:
            H = [self.sb("H%d" % d, [128, npair, nseg, 64], F32, es=ses) for d in range(2)]
            Hk = [self.sb("Hk%d" % d, [128, npair, nseg, 64], BF16, es=ses) for d in range(2)]
            Hc = [self.sb("Hc%d" % d, [128, npair, nseg, 64], BF16, es=ses) for d in range(2)]
            Vd = [self.sb("Vd%d" % d, [128, npair, nseg, 64], BF16, es=ses) for d in range(2)]
            U = [self.sb("U%d" % d, [128, npair, nseg, 64], F32, es=ses) for d in range(2)]
            KV = [self.sb("KV%d" % d, [128, npair, nseg, 64], F32, es=ses) for d in range(2)]
            stg = self.sb("ststg", [64, 128], F32, es=ses)
            for d in range(2):
                if not sample:
                    T.op("pool", lambda e, d=d: e.memset(H[d][:], 0.0), writes=["H%d" % d])
                else:
                    for qi, p in enumerate(pairs):
                        T.dma("sp", stg[:].rearrange("v (h k) -> v h k", h=2), self.st_in[l, d, 2 * p:2 * p + 2].rearrange("h v k -> v h k"), writes=["ststg"])
                        T.op("pe", lambda e: e.transpose(self.ps[6][:, 0:64], stg[:], self.ident_f[0:64, 0:64]), reads=["ststg", "ident_f"], writes=["ps6"], same=False)
                        T.op("dve", lambda e, d=d, qi=qi: e.tensor_copy(out=H[d][:, qi, 0, :], in_=self.ps[6][:, 0:64]), reads=["ps6"], writes=["H%d" % d])
            NB = 512 // G
            SA = [self.ps[0], self.ps[1]]
            VB = [self.ps[2], self.ps[3]]
            YP = [self.ps[4], self.ps[5]]
            ypv = [YP[d][:, 0:G * NB].rearrange("p (q s n) -> p q s n", q=npair, s=nseg) for d in range(2)]
            sh4 = [128, npair, nseg, 64]

            def col(arr, tt):
                return arr[:].rearrange("p q (s t) -> p q s t", s=nseg)[:, :, :, tt].unsqueeze(3).broadcast_to(sh4)

            idb = R["identh"][:].unsqueeze(1).unsqueeze(1).broadcast_to(sh4)
            fl = lambda a: a[:].rearrange("p q s v -> p (q s v)")
            for i in range(L):
                for d in range(2):
                    tt = i if d == 0 else L - 1 - i
                    hk, sak, vbk, ypk = "H%d" % d, "ps%d" % d, "ps%d" % (2 + d), "ps%d" % (4 + d)
                    T.op("pool", lambda e, d=d, tt=tt: e.tensor_tensor(out=Vd[d][:], in0=idb, in1=col(vT, tt), op=ALU.mult),
                         reads=["vT", "identh"], writes=["Vd%d" % d])
                    T.op("pe", lambda e, d=d: e.matmul(VB[d][:, 0:G * 64], self.bones_bf[:], fl(Vd[d]), start=True, stop=True),
                         reads=["Vd%d" % d, "bones_bf"], writes=[vbk], same=False)
                    T.op("pool", lambda e, d=d, tt=tt: e.tensor_tensor(out=Hk[d][:], in0=H[d][:], in1=col(kapT, tt), op=ALU.mult),
                         reads=[hk, "kapT"], writes=["Hk%d" % d])
                    T.op("pe", lambda e, d=d: e.matmul(SA[d][:, 0:G * 64], self.bones_bf[:], fl(Hk[d]), start=True, stop=True),
                         reads=["Hk%d" % d, "bones_bf"], writes=[sak], same=False)
                    T.op("dve", lambda e, d=d, tt=tt: e.tensor_tensor(out=KV[d][:], in0=VB[d][:, 0:G * 64].rearrange("p (q s v) -> p q s v", q=npair, s=nseg),
                                                                      in1=col(kdT[d], tt), op=ALU.mult),
                         reads=[vbk, "kdT%d" % d], writes=["KV%d" % d])
                    T.op("dve", lambda e, d=d, tt=tt: e.tensor_tensor(out=U[d][:], in0=SA[d][:, 0:G * 64].rearrange("p (q s v) -> p q s v", q=npair, s=nseg),
                                                                      in1=col(bpT[d], tt), op=ALU.mult),
                         reads=[sak, "bpT%d" % d], writes=["U%d" % d])
                    T.op("dve", lambda e, d=d, tt=tt: e.tensor_tensor(out=H[d][:], in0=H[d][:], in1=col(wT[d], tt), op=ALU.mult),
                         reads=[hk, "wT%d" % d, "Hk%d" % d], writes=[hk])
                    T.op("dve", lambda e, d=d: e.tensor_tensor(out=H[d][:], in0=H[d][:], in1=U[d][:], op=ALU.add), reads=[hk, "U%d" % d], writes=[hk])
                    T.op("dve", lambda e, d=d: e.tensor_tensor(out=H[d][:], in0=H[d][:], in1=KV[d][:], op=ALU.add), reads=[hk, "KV%d" % d], writes=[hk])
                    T.op("act", lambda e, d=d: e.activation(out=Hc[d][:], in_=H[d][:], func=AF.Copy), reads=[hk], writes=["Hc%d" % d])
                    cidx = (i % NB) if d == 0 else (NB - 1 - (i % NB))
                    first_in_blk = (i % NB == 0)
                    cnt_in = 0
                    for qi in range(npair):
                        for si in range(nseg):
                            for par in range(2):
                                hs = slice(par * 64, (par + 1) * 64)
                                T.op("pe", lambda e, d=d, qi=qi, si=si, hs=hs, tt=tt, cidx=cidx: e.matmul(
                                    ypv[d][hs, qi, si, cidx:cidx + 1], Hc[d][hs, qi, si, :], rT[hs, qi, si * L + tt:si * L + tt + 1], start=True, stop=True),
                                    reads=["Hc%d" % d, "rT"], writes=[ypk], same=False)
                    if (i % NB == NB - 1) or i == L - 1:
                        i0 = (i // NB) * NB
                        nb = i - i0 + 1
                        for si in range(nseg):
                            if d == 0:
                                ta, ca = si * L + i0, 0
                            else:
                                ta, ca = si * L + (L - 1 - i), NB - nb
                            T.op("dve", lambda e, d=d, si=si, ta=ta, ca=ca, nb=nb: e.tensor_tensor(
                                out=yT[:, :, ta:ta + nb], in0=ypv[d][:, :, si, ca:ca + nb], in1=yT[:, :, ta:ta + nb], op=ALU.add),
                                reads=[ypk, "yT"], writes=["yT"])
            if not sample:
                for d in range(2):
                    for qi, p in enumerate(pairs):
                        for si in range(nseg):
                            T.op("pe", lambda e, d=d, qi=qi, si=si: e.transpose(self.ps[6][0:64, 0:128], H[d][:, qi, si, :], self.ident_f[:]),
                                 reads=["H%d" % d, "ident_f"], writes=["ps6"], same=False)
                            T.op("dve", lambda e: e.tensor_copy(out=stg[:], in_=self.ps[6][0:64, 0:128]), reads=["ps6"], writes=["ststg"])
                            T.dma("sp", self.o_nst[si, l, d, 2 * p:2 * p + 2].rearrange("h v k -> v h k"), stg[:].rearrange("v (h k) -> v h k", h=2),
                                  reads=["ststg"], writes=["nst%d_%d_%d_%d" % (l, d, p, si)])
            T.barrier()
        wsc.close()
        with ExitStack() as fes:
            yb = self.sb("ybf", [128, 512], BF16, es=fes)
            ysq = self.sb("ysq", [128, 512], BF16, es=fes)
            mean = self.sb("gmean", [128, 512], F32, es=fes)
            rstd = self.sb("grstd", [128, 512], F32, es=fes)
            tq = self.sb("gtq", [128, 512], F32, es=fes)
            gneps = self.sb("gneps", [128, 1], F32, es=fes)
            T.op("dve", lambda e: e.memset(gneps[:], GN_EPS), writes=["gneps"])
            gT = self.sb("gT", [128, npair, TU], BF16, es=fes)
            cbT = self.sb("cbT", [128, npair, TU], BF16, es=fes)
            zraw = self.sb("zraw", [128, TU], F32, es=fes)
            tmp = self.sb("rwtmp", [128, TU], F32, es=fes)
            tmpb = self.sb("rwtmpb", [128, TU], BF16, es=fes)
            B["zraw"], B["tmp"] = zraw, tmp
            conv_chunk(14, tmpb[:], "rwtmpb", AF.Sigmoid)
            for qi, p in enumerate(pairs):
                for ti in range(TU // 512):
                    bank, bk = pbank()
                    self.mmg(bank[:], bk, [(R["gup"][:, p * 128:(p + 1) * 128], tmpb[:, ti * 512:(ti + 1) * 512])], reads=["gupw", "rwtmpb"])
                    T.op("act", lambda e, bank=bank, qi=qi, ti=ti: e.activation(out=gT[:, qi, ti * 512:(ti + 1) * 512], in_=bank[:], func=AF.Copy),
                         reads=[bk], writes=["gT"])
            for qi, p in enumerate(pairs):
                for d in range(2):
                    T.op("dve", lambda e, d=d, qi=qi, p=p: e.scalar_tensor_tensor(out=tmpb[:], in0=kdT[d][:, qi, :], scalar=vec[:, 24 + p:25 + p],
                                                                                  in1=rT[:, qi, :], op0=ALU.mult, op1=ALU.mult),
                         reads=["kdT%d" % d, "rT", "rwvecs"], writes=["rwtmpb"])
                    for ti in range(TU // 512):
                        sl = slice(ti * 512, (ti + 1) * 512)
                        bank, bk = pbank()
                        self.mmg(bank[:], bk, [(self.bones_bf[:], tmpb[:, sl])], reads=["bones_bf", "rwtmpb"])
                        if d == 0:
                            T.op("act", lambda e, bank=bank, sl=sl, qi=qi: e.activation(out=cbT[:, qi, sl], in_=bank[:], func=AF.Copy),
                                 reads=[bk], writes=["cbT"])
                        else:
                            T.op("dve", lambda e, bank=bank, sl=sl, qi=qi: e.tensor_tensor(out=cbT[:, qi, sl], in0=bank[:], in1=cbT[:, qi, sl], op=ALU.add),
                                 reads=[bk, "cbT"], writes=["cbT"])
            ocT = kapT
            for qi, p in enumerate(pairs):
                for ti in range(TU // 512):
                    sl = slice(ti * 512, (ti + 1) * 512)
                    y = yT[:, qi, sl]
                    T.op("act", lambda e, y=y: e.activation(out=yb[:], in_=y, func=AF.Copy), reads=["yT"], writes=["ybf"])
                    T.op("pool", lambda e, y=y: e.tensor_tensor(out=ysq[:], in0=y, in1=y, op=ALU.mult), reads=["yT"], writes=["ysq"])
                    self.mmg(self.ps[0][:], "ps0", [(self.bones_bf[:], yb[:])], reads=["bones_bf", "ybf"])
                    self.mmg(self.ps[1][:], "ps1", [(self.bones_bf[:], ysq[:])], reads=["bones_bf", "ysq"])
                    T.op("act", lambda e: e.activation(out=mean[:], in_=self.ps[0][:], func=AF.Copy, scale=1.0 / 64), reads=["ps0"], writes=["gmean"])
                    T.op("dve", lambda e: e.tensor_tensor(out=tq[:], in0=mean[:], in1=mean[:], op=ALU.mult), reads=["gmean"], writes=["gtq"])
                    T.op("dve", lambda e: e.scalar_tensor_tensor(out=rstd[:], in0=self.ps[1][:], scalar=1.0 / 64, in1=tq[:], op0=ALU.mult, op1=ALU.subtract),
                         reads=["ps1", "gtq"], writes=["grstd"])
                    T.op("act", lambda e: e.activation(out=rstd[:], in_=rstd[:], func=AF.Sqrt, bias=gneps[:, 0:1], scale=1.0), reads=["grstd", "gneps"], writes=["grstd"])
                    T.op("dve", lambda e: e.reciprocal(out=rstd[:], in_=rstd[:]), reads=["grstd"], writes=["grstd"])
                    T.op("dve", lambda e, y=y: e.tensor_tensor(out=tq[:], in0=y, in1=mean[:], op=ALU.subtract), reads=["yT", "gmean"], writes=["gtq"])
                    T.op("dve", lambda e: e.tensor_tensor(out=tq[:], in0=tq[:], in1=rstd[:], op=ALU.mult), reads=["gtq", "grstd"], writes=["gtq"])
                    T.op("dve", lambda e, p=p: e.tensor_scalar(out=tq[:], in0=tq[:], scalar1=vec[:, 28 + p:29 + p], scalar2=vec[:, 32 + p:33 + p], op0=ALU.mult, op1=ALU.add),
                         reads=["gtq", "rwvecs"], writes=["gtq"])
                    T.op("pool", lambda e, qi=qi, sl=sl: e.tensor_tensor(out=mean[:], in0=cbT[:, qi, sl], in1=vT[:, qi, sl], op=ALU.mult),
                         reads=["cbT", "vT"], writes=["gmean"])
                    T.op("dve", lambda e: e.tensor_tensor(out=tq[:], in0=tq[:], in1=mean[:], op=ALU.add), reads=["gtq", "gmean"], writes=["gtq"])
                    T.op("dve", lambda e, qi=qi, sl=sl: e.tensor_tensor(out=ocT[:, qi, sl], in0=tq[:], in1=gT[:, qi, sl], op=ALU.mult),
                         reads=["gtq", "gT"], writes=["kapT"])
            self._merge_branch(l, 2, [(lambda tt0, n, qi=qi: ocT[:, qi, tt0 - t00:tt0 - t00 + n]) for qi in range(npair)], "kapT",
                               pairs[0] * 128, t00, TU, first=False)
            T.barrier()


for _f in (_rw_declare, _rw_layer_consts, _rwkv, _rw_unit):
    setattr(Builder, _f.__name__, _f)
_old_declare3 = Builder._declare


def _declare3(self):
    _old_declare3(self)
    self._rw_declare()


Builder._declare = _declare3


def _prep_inputs3(inputs):
    maps = _prep_inputs2(inputs)
    shared = {}
    for k in ("w_shift", "decay_w0", "iclr_a0", "gate_up", "k_k", "k_a", "gn_g", "gn_b"):
        shared[k] = np.ascontiguousarray(inputs[k], dtype=np.float32)
    shared["r_k"] = np.ascontiguousarray(np.asarray(inputs["r_k"], np.float32).reshape(DEPTH, 512))
    shared["decay_up"] = np.ascontiguousarray(np.asarray(inputs["decay_up"], np.float32).reshape(DEPTH, 128, 512))
    shared["iclr_up"] = np.ascontiguousarray(np.asarray(inputs["iclr_up"], np.float32).reshape(DEPTH, 128, 512))
    idh = np.zeros((128, 64), np.float32)
    idh[np.arange(128), np.arange(128) % 64] = 1.0
    shared["c_identh"] = idh
    for core, m in enumerate(maps):
        m.update(shared)
        m["state_rwkv"] = np.ascontiguousarray(inputs["state_rwkv"][core], dtype=np.float32)
    return maps


def kernel(**inputs):
    b = Builder()
    nc = b.build()
    maps = _prep_inputs3(inputs)
    maps = [{k: v for k, v in m.items() if k in b.dram_in} for m in maps]
    res = run_bass_kernel_spmd(nc, maps, core_ids=list(range(8)))
    outs = res.results
    f32 = np.float32
    y_p = np.stack([outs[c]["y_tok"][:512].reshape(2, 256, D) for c in range(8)], 0).reshape(16, 256, D).astype(f32)
    y_s = np.stack([outs[c]["y_tok"][512:] for c in range(8)], 0).astype(f32)
    nak = np.concatenate([outs[c]["o_nak"] for c in range(8)], 0).reshape(16, DEPTH, 256, 2, 64).astype(f32)
    nav = np.concatenate([outs[c]["o_nav"] for c in range(8)], 0).reshape(16, DEPTH, 256, 2, 64).astype(f32)
    nbk = np.concatenate([outs[c]["o_nbk"] for c in range(8)], 0).reshape(16, DEPTH, 256, 8, 64).astype(f32)
    nbv = np.concatenate([outs[c]["o_nbv"] for c in range(8)], 0).reshape(16, DEPTH, 256, 8, 64).astype(f32)
    nst = np.concatenate([outs[c]["o_nst"] for c in range(8)], 0).reshape(16, DEPTH, 2, 8, 64, 64).astype(f32)
    return (y_p, y_s, nak, nav, nbk, nbv, nst)
```

```python
import numpy as np
from contextlib import ExitStack
import concourse.bass as bass
import concourse.mybir as mybir
from concourse.bass_utils import run_bass_kernel_spmd

F32 = mybir.dt.float32
BF16 = mybir.dt.bfloat16
AF = mybir.ActivationFunctionType
ALU = mybir.AluOpType
AX = mybir.AxisListType

D = 1024
NCH = 8
DEPTH = 2
DFF = 2816
NJ = 22
NTOK = 1536
TILES = [(0, 512), (512, 512), (1024, 512)]
ALPHA = (2 * DEPTH) ** 0.25
LN_EPS = 1e-5
HD = 64
SCALE = HD ** -0.5
IN_COLS = 7296
A_OFF = 0
B_OFF = 768
C_OFF = 2304
G_OFF = 4224
GN_EPS = 64e-5


class Trk:
    SEM_LIMIT = 60000

    def __init__(self, nc, es):
        self.nc = nc
        self.es = es
        self.eng = {}
        for name, obj in (("pe", nc.tensor), ("act", nc.scalar), ("dve", nc.vector),
                          ("pool", nc.gpsimd), ("sp", nc.sync)):
            sem = es.enter_context(nc.semaphore("sem_" + name))
            self.eng[name] = dict(obj=obj, sem=sem, cnt=0, waited={}, id=name, name=name, epoch=0, total=0)
        self.dsems = [es.enter_context(nc.semaphore("dsem%d" % i)) for i in range(40)]
        self.dcnt = [0] * len(self.dsems)
        self.drr = 0
        self.last_w = {}
        self.readers = {}
        self.n_wait = 0

    def _wait(self, e, ev):
        sem, val, sid = ev
        if e["waited"].get(sid, 0) < val:
            e["obj"].wait_ge(sem, val)
            e["waited"][sid] = val
            self.n_wait += 1

    def _deps(self, e, reads, writes, same=True):
        evs = []
        for r in reads:
            if r in self.last_w:
                evs.append(self.last_w[r])
        for w in writes:
            if w in self.last_w:
                evs.append(self.last_w[w])
            evs.extend(self.readers.get(w, ()))
        for ev in evs:
            if (not same) and ev[2].split("_")[0] == e["name"]:
                continue
            self._wait(e, ev)

    def _commit(self, ev, reads, writes):
        for w in writes:
            self.last_w[w] = ev
            self.readers[w] = []
        for r in reads:
            if r in writes:
                continue
            lst = self.readers.setdefault(r, [])
            lst[:] = [x for x in lst if x[2] != ev[2]]
            lst.append(ev)

    def op(self, ename, fn, reads=(), writes=(), same=True):
        e = self.eng[ename]
        if ename != "pe":
            psr = [r for r in reads if r.startswith("ps") and r[2:].isdigit() and r not in writes]
            if psr:
                writes = list(writes) + psr
        self._deps(e, reads, writes, same)
        if e["cnt"] >= self.SEM_LIMIT:
            e["epoch"] += 1
            e["sem"] = self.es.enter_context(self.nc.semaphore("sem_%s_%d" % (e["name"], e["epoch"])))
            e["id"] = "%s_%d" % (e["name"], e["epoch"])
            e["cnt"] = 0
        inst = fn(e["obj"])
        e["cnt"] += 1
        e["total"] += 1
        inst.then_inc(e["sem"], 1)
        ev = (e["sem"], e["cnt"], e["id"])
        self._commit(ev, reads, writes)
        return ev

    def dma(self, qname, out, in_, reads=(), writes=()):
        e = self.eng[qname]
        self._deps(e, reads, writes, True)
        i = self.drr
        self.drr = (self.drr + 1) % len(self.dsems)
        outs = out if isinstance(out, (list, tuple)) else [out]
        ins = in_ if isinstance(in_, (list, tuple)) else [in_]
        for o_, i_ in zip(outs, ins):
            inst = e["obj"].dma_start(out=o_, in_=i_)
            self.dcnt[i] += 16
            inst.then_inc(self.dsems[i], 16)
        ev = (self.dsems[i], self.dcnt[i], "d%d" % i)
        self._commit(ev, reads, writes)
        return ev

    def barrier(self):
        evs = [(e["sem"], e["cnt"], e["id"]) for e in self.eng.values() if e["cnt"] > 0]
        evs += [(self.dsems[i], self.dcnt[i], "d%d" % i) for i in range(len(self.dsems)) if self.dcnt[i] > 0]
        for e in self.eng.values():
            for ev in evs:
                self._wait(e, ev)

    def wait_all(self, ename):
        e = self.eng[ename]
        for k, ev in list(self.last_w.items()):
            self._wait(e, ev)
        for k, lst in list(self.readers.items()):
            for ev in lst:
                self._wait(e, ev)


def host_consts():
    c = {}
    c["ident"] = np.eye(128, dtype=np.float32)
    bo = np.zeros((128, 128), np.float32)
    bo[:64, :64] = 1.0
    bo[64:, 64:] = 1.0
    c["blockones"] = bo
    c["ones"] = np.ones((128, 128), np.float32)
    return c


class Builder:
    def __init__(self, dbg=None, nlayers=DEPTH, stop_after=None):
        self.dbg = dbg or []
        self.nlayers = nlayers
        self.stop_after = stop_after
        self.nc = bass.Bass("TRN2", target_bir_lowering=False)
        self.es = ExitStack()
        self.dram_in = {}
        self.dram_out = {}

    def din(self, name, shape, dt=F32):
        t = self.nc.dram_tensor(name, list(shape), dt, kind="ExternalInput").ap()
        self.dram_in[name] = t
        return t

    def dout(self, name, shape, dt=F32):
        t = self.nc.dram_tensor(name, list(shape), dt, kind="ExternalOutput").ap()
        self.dram_out[name] = t
        return t

    def sb(self, name, shape, dt=F32, es=None):
        self._uid = getattr(self, "_uid", 0) + 1
        return (es or self.es).enter_context(self.nc.sbuf_tensor("%s_%d" % (name, self._uid), list(shape), dt))

    def build(self):
        nc, es = self.nc, self.es
        with es:
            self.T = Trk(nc, es)
            self._declare()
            self._globals()
            for l in range(self.nlayers):
                self._layer(l)
                if self.stop_after is not None and self.stop_after[0] == l:
                    break
            self._finish()
        return nc

    def _declare(self):
        L = DEPTH
        self.x_in = self.din("x_tok", [NTOK, D])
        self.cvec = self.din("cvec", [2, D])
        self.w_ada = self.din("w_ada", [L, D, 9 * D])
        self.b_ada = self.din("b_ada", [L, 9 * D])
        self.ffn_w_in = [self.din("ffn1_w_in", [L, D, 2 * DFF]), self.din("ffn2_w_in", [L, D, 2 * DFF])]
        self.ffn_w_out = [self.din("ffn1_w_out", [L, DFF, D]), self.din("ffn2_w_out", [L, DFF, D])]
        self.ln_g = self.din("ln_g", [L, 3, D])
        self.ln_b = self.din("ln_b", [L, 3, D])
        self.c_ones = self.din("c_ones", [128, 128])
        self.c_ident = self.din("c_ident", [128, 128])
        self.c_blockones = self.din("c_blockones", [128, 128])
        self.y_out = self.dout("y_tok", [NTOK, D])
        for name, shape in self.dbg:
            self.dout(name, shape)

    def _globals(self):
        nc, T = self.nc, self.T
        self.xT = self.sb("xT", [128, NCH, NTOK], F32)
        self.hT = self.sb("hT", [128, NCH, NTOK], BF16)
        self.ps = [self.es.enter_context(nc.psum_tensor("ps%d" % i, [128, 512], F32)) for i in range(8)]
        self.NSLOT = 3
        self.wring = self.sb("wring", [128, self.NSLOT, 4096], BF16)
        self.wslot = 0
        self.ones_bf = self.sb("ones_bf", [128, 128], BF16)
        self.ident_f = self.sb("ident_f", [128, 128], F32)
        self.bones_bf = self.sb("bones_bf", [128, 128], BF16)
        self.scT = self.sb("scT", [128, NCH, 2], BF16)
        self.cT = self.sb("cT", [128, NCH, 2], F32)
        self.modT = self.sb("modT", [128, 2, 9, NCH], F32)
        self.badaT = self.sb("badaT", [128, 9, NCH], F32)
        self.lngT = self.sb("lngT", [128, 3, NCH], F32)
        self.lnbT = self.sb("lnbT", [128, 3, NCH], F32)
        self.gsT = self.sb("gsT", [128, 2, NCH], F32)
        self.ghT = self.sb("ghT", [128, 2, NCH], F32)
        self.bhT = self.sb("bhT", [128, 2, NCH], F32)
        self.s1T = self.sb("s1T", [128, 2, NCH], F32)
        self.eps_t = self.sb("eps_t", [128, 1], F32)

        T.dma("pool", self.ones_bf[:], self.c_ones, writes=["ones_bf"])
        T.dma("pool", self.bones_bf[:], self.c_blockones, writes=["bones_bf"])
        T.dma("sp", self.ident_f[:], self.c_ident, writes=["ident_f"])
        T.op("dve", lambda e: e.memset(self.eps_t[:], LN_EPS / (ALPHA * ALPHA)), writes=["eps_t"])
        self._load_x()
        self.vecstage = self.sb("vecstage", [72, 128], F32)
        self._load_vecT(self.cvec.rearrange("w (c p) -> (w c) p", p=128), 16, "cT_raw")
        T.op("dve", lambda e: e.tensor_copy(out=self.cT[:], in_=self.ps[7][:, 0:16].rearrange("p (w c) -> p c w", w=2)),
             reads=["ps7"], writes=["cT"])
        T.op("act", lambda e: e.activation(out=self.scT[:], in_=self.cT[:], func=AF.Silu),
             reads=["cT"], writes=["scT"])

    def _load_x(self):
        nc, T = self.nc, self.T
        with ExitStack() as les:
            xs = [self.sb("xstage%d" % i, [128, D], F32, es=les) for i in range(2)]
            for tt in range(NTOK // 128):
                s = xs[tt % 2]
                key = "xstage%d" % (tt % 2)
                T.dma("sp", s[:], self.x_in[tt * 128:(tt + 1) * 128, :], writes=[key])
                for half in range(2):
                    bank = self.ps[(tt * 2 + half) % 4]
                    bkey = "ps%d" % ((tt * 2 + half) % 4)
                    for q in range(4):
                        c = half * 4 + q
                        T.op("pe", lambda e, c=c, q=q, bank=bank, s=s: e.transpose(
                            bank[:, q * 128:(q + 1) * 128], s[:, c * 128:(c + 1) * 128], self.ident_f[:]),
                            reads=[key, "ident_f"], writes=[bkey] if q in (0, 3) else [], same=False)
                    T.op("dve" if half == 0 else "act",
                         (lambda e, bank=bank, half=half, tt=tt: e.tensor_copy(
                             out=self.xT[:, half * 4:(half + 1) * 4, tt * 128:(tt + 1) * 128],
                             in_=bank[:].rearrange("p (q t) -> p q t", q=4))) if half == 0 else
                         (lambda e, bank=bank, half=half, tt=tt: e.activation(
                             out=self.xT[:, half * 4:(half + 1) * 4, tt * 128:(tt + 1) * 128],
                             in_=bank[:].rearrange("p (q t) -> p q t", q=4), func=AF.Copy)),
                         reads=[bkey], writes=["xT"])
            self.T.barrier()


    def _load_vecT(self, src_rows, nrows, tag):
        T = self.T
        T.dma("sp", self.vecstage[0:nrows, :], src_rows, writes=["vecstage"])
        T.op("pe", lambda e: e.transpose(self.ps[7][:, 0:nrows], self.vecstage[0:nrows, :], self.ident_f[0:nrows, 0:nrows]),
             reads=["vecstage", "ident_f"], writes=["ps7"], same=False)

    def wload(self, view_fn, src_ap, split=None):
        s = self.wslot
        self.wslot = (self.wslot + 1) % self.NSLOT
        key = "wring%d" % s
        dst = view_fn(self.wring[:, s, :])
        if split:
            self.T.dma("pool", [dst[:, :, a, :] for a in range(split)], [src_ap[:, :, a, :] for a in range(split)], writes=[key])
        else:
            self.T.dma("pool", dst, src_ap, writes=[key])
        return dst, key


    def mmg(self, out_ap, okey, terms, reads):
        T = self.T
        n = len(terms)
        for i, (lt, rh) in enumerate(terms):
            T.op("pe", lambda e, lt=lt, rh=rh, i=i: e.matmul(out_ap, lt, rh, start=(i == 0), stop=(i == n - 1)),
                 reads=reads, writes=[okey] if (i == 0 or i == n - 1) else [], same=False)

    def _layer(self, l):
        self._mods(l)
        self._ffn(l, 0)
        if self.stop_after == (l, 0):
            return
        if hasattr(self, "_mixer"):
            self._mixer(l)
            if self.stop_after == (l, 1):
                return
        self._ffn(l, 1)

    def _mods(self, l):
        nc, T = self.nc, self.T
        self._load_vecT(self.b_ada[l].rearrange("(ic p) -> ic p", p=128), 72, "bada")
        T.op("dve", lambda e: e.tensor_copy(out=self.badaT[:].rearrange("p i c -> p (i c)"), in_=self.ps[7][:, 0:72]),
             reads=["ps7"], writes=["badaT"])
        self._load_vecT(self.ln_g[l].rearrange("s (c p) -> (s c) p", p=128), 24, "lng")
        T.op("dve", lambda e: e.tensor_copy(out=self.lngT[:].rearrange("p s c -> p (s c)"), in_=self.ps[7][:, 0:24]),
             reads=["ps7"], writes=["lngT"])
        self._load_vecT(self.ln_b[l].rearrange("s (c p) -> (s c) p", p=128), 24, "lnb")
        T.op("dve", lambda e: e.tensor_copy(out=self.lnbT[:].rearrange("p s c -> p (s c)"), in_=self.ps[7][:, 0:24]),
             reads=["ps7"], writes=["lnbT"])
        wv = self.w_ada[l].rearrange("(kc p) n -> p kc n", p=128)
        for piece in range(18):
            dst, key = self.wload(lambda s: s.rearrange("p (kc n) -> p kc n", kc=NCH), wv[:, :, piece * 512:(piece + 1) * 512])
            bank = self.ps[piece % 2]
            bkey = "ps%d" % (piece % 2)
            for q in range(4):
                for kc in range(NCH):
                    T.op("pe", lambda e, q=q, kc=kc, bank=bank, dst=dst: e.matmul(
                        bank[:, q * 2:q * 2 + 2], dst[:, kc, q * 128:(q + 1) * 128], self.scT[:, kc, :],
                        start=(kc == 0), stop=(kc == NCH - 1)),
                        reads=[key, "scT"], writes=[bkey] if ((q == 0 and kc == 0) or (q == 3 and kc == NCH - 1)) else [], same=False)
            i, c0 = divmod(piece * 4, NCH)
            T.op("dve", lambda e, bank=bank, i=i, c0=c0: e.tensor_tensor(
                out=self.modT[:, :, i, c0:c0 + 4],
                in0=bank[:, 0:8].rearrange("p (q w) -> p w q", w=2),
                in1=self.badaT[:, i, c0:c0 + 4].unsqueeze(1).broadcast_to([128, 2, 4]), op=ALU.add),
                reads=[bkey, "badaT"], writes=["modT"])
        T.op("dve", lambda e: e.tensor_scalar_add(out=self.s1T[:], in0=self.modT[:, :, 1, :], scalar1=1.0),
             reads=["modT"], writes=["s1T"])
        self._modulate_all(self.s1T, lambda w: self.modT[:, w, 0, :])

    def _modulate_all(self, scaleT, shift_fn):
        T = self.T
        for c in range(NCH):
            for (w, t0, n) in ((0, 0, 512), (1, 512, 1024)):
                T.op("act", lambda e, c=c, w=w, t0=t0, n=n: e.activation(
                    out=self.hT[:, c, t0:t0 + n], in_=self.xT[:, c, t0:t0 + n], func=AF.Identity,
                    scale=scaleT[:, w, c:c + 1], bias=shift_fn(w)[:, c:c + 1]),
                    reads=["xT", "modT", "s1T", "ghT", "bhT"], writes=["hT"])

    def _sub_scalars(self, l, sub, gate_i, gate_mul, nxt):
        T = self.T
        T.op("dve", lambda e: e.tensor_scalar_mul(out=self.gsT[:], in0=self.modT[:, :, gate_i, :], scalar1=gate_mul / ALPHA),
             reads=["modT"], writes=["gsT"])
        if nxt is not None:
            sh_i, sc_i = nxt
            T.op("dve", lambda e: e.tensor_scalar_add(out=self.ghT[:], in0=self.modT[:, :, sc_i, :], scalar1=1.0),
                 reads=["modT"], writes=["ghT"])
            T.op("dve", lambda e: e.tensor_tensor(out=self.bhT[:], in0=self.ghT[:],
                                                  in1=self.lnbT[:, sub, :].unsqueeze(1).broadcast_to([128, 2, NCH]), op=ALU.mult),
                 reads=["ghT", "lnbT"], writes=["bhT"])
            T.op("dve", lambda e: e.tensor_tensor(out=self.bhT[:], in0=self.bhT[:], in1=self.modT[:, :, sh_i, :], op=ALU.add),
                 reads=["modT"], writes=["bhT"])
            T.op("dve", lambda e: e.tensor_tensor(out=self.ghT[:], in0=self.ghT[:],
                                                  in1=self.lngT[:, sub, :].unsqueeze(1).broadcast_to([128, 2, NCH]), op=ALU.mult),
                 reads=["lngT"], writes=["ghT"])

    def _ln_tile(self, l, sub, ti, has_next, les):
        T = self.T
        t0, n = TILES[ti]
        w = 0 if ti == 0 else 1
        mean, rstd, tmp = self.ln_mean, self.ln_rstd, self.ln_tmp
        T.op("act", lambda e: e.activation(out=mean[:], in_=self.ps[6][:], func=AF.Copy, scale=1.0 / D),
             reads=["ps6"], writes=["ln_mean"])
        T.op("dve", lambda e: e.tensor_tensor(out=tmp[:], in0=mean[:], in1=mean[:], op=ALU.mult),
             reads=["ln_mean"], writes=["ln_tmp"])
        T.op("dve", lambda e: e.scalar_tensor_tensor(out=rstd[:], in0=self.ps[7][:], scalar=1.0 / D, in1=tmp[:],
                                                     op0=ALU.mult, op1=ALU.subtract),
             reads=["ps7", "ln_tmp"], writes=["ln_rstd"])
        T.op("act", lambda e: e.activation(out=rstd[:], in_=rstd[:], func=AF.Sqrt, bias=self.eps_t[:, 0:1], scale=1.0),
             reads=["ln_rstd", "eps_t"], writes=["ln_rstd"])
        T.op("dve", lambda e: e.reciprocal(out=rstd[:], in_=rstd[:]), reads=["ln_rstd"], writes=["ln_rstd"])
        for c in range(NCH):
            tb = self.ln_t[c % 2]
            tk = "ln_t%d" % (c % 2)
            T.op("dve", lambda e, c=c, tb=tb: e.tensor_tensor(out=tb[:], in0=self.xT[:, c, t0:t0 + n], in1=mean[:], op=ALU.subtract),
                 reads=["xT", "ln_mean"], writes=[tk])
            T.op("pool", lambda e, tb=tb: e.tensor_tensor(out=tb[:], in0=tb[:], in1=rstd[:], op=ALU.mult),
                 reads=["ln_rstd", tk], writes=[tk])
            T.op("act", lambda e, c=c, tb=tb: e.activation(out=self.xT[:, c, t0:t0 + n], in_=tb[:], func=AF.Identity,
                                                           scale=self.lngT[:, sub, c:c + 1], bias=self.lnbT[:, sub, c:c + 1]),
                 reads=[tk, "lngT", "lnbT"], writes=["xT"])
            if has_next:
                T.op("dve", lambda e, c=c, tb=tb: e.tensor_scalar(out=self.hT[:, c, t0:t0 + n], in0=tb[:],
                                                                  scalar1=self.ghT[:, w, c:c + 1], scalar2=self.bhT[:, w, c:c + 1],
                                                                  op0=ALU.mult, op1=ALU.add),
                     reads=[tk, "ghT", "bhT"], writes=["hT"])

    def _ffn(self, l, which):
        nc, T = self.nc, self.T
        sub = 0 if which == 0 else 2
        gate_i = 2 if which == 0 else 8
        nxt = (3, 4) if which == 0 else None
        has_next = which == 0
        self._sub_scalars(l, sub, gate_i, 0.5, nxt)
        w_in = self.ffn_w_in[which][l].rearrange("(kc p) (ag n) -> p kc ag n", p=128, ag=2)
        w_out = self.ffn_w_out[which][l].rearrange("(j p) n -> p j n", p=128)
        with ExitStack() as les:
            uT = self.sb("uT", [128, NJ, NTOK], BF16, es=les)
            sl = [self.sb("silu%d" % i, [128, 512], F32, es=les) for i in range(2)]
            self._ln_alloc(les)
            cnt = 0
            for jp in range(NJ // 2):
                dst, key = self.wload(lambda s_: s_.rearrange("p (ag kc n) -> p kc ag n", kc=NCH, ag=2),
                                      w_in[:, :, :, jp * 256:(jp + 1) * 256], split=2)
                for jj in range(2):
                    j = jp * 2 + jj
                    for ti, (t0, n) in enumerate(TILES):
                        ba, bg = self.ps[(cnt % 2) * 2], self.ps[(cnt % 2) * 2 + 1]
                        ka, kg = "ps%d" % ((cnt % 2) * 2), "ps%d" % ((cnt % 2) * 2 + 1)
                        for (bank, bk, ag) in ((ba, ka, 0), (bg, kg, 1)):
                            self.mmg(bank[:, 0:n], bk,
                                     [(dst[:, kc, ag, jj * 128:(jj + 1) * 128], self.hT[:, kc, t0:t0 + n]) for kc in range(NCH)],
                                     reads=[key, "hT"])
                        sb_ = sl[cnt % 2]
                        sk = "silu%d" % (cnt % 2)
                        T.op("act", lambda e, ba=ba, sb_=sb_, n=n: e.activation(out=sb_[:, 0:n], in_=ba[:, 0:n], func=AF.Silu),
                             reads=[ka], writes=[sk])
                        T.op("dve", lambda e, bg=bg, sb_=sb_, j=j, t0=t0, n=n: e.tensor_tensor(
                            out=uT[:, j, t0:t0 + n], in0=bg[:, 0:n], in1=sb_[:, 0:n], op=ALU.mult),
                            reads=[kg, sk], writes=["uT"])
                        cnt += 1
            for mp in range(4):
                pcs = []
                for jh in range(2):
                    pcs.append(self.wload(lambda s_: s_[:, 0:11 * 256].rearrange("p (j n) -> p j n", j=11),
                                          w_out[:, jh * 11:(jh + 1) * 11, mp * 256:(mp + 1) * 256]))
                for ti, (t0, n) in enumerate(TILES):
                    for mm_ in range(2):
                        c = mp * 2 + mm_
                        bi = (ti * 2 + mm_) % 6
                        bank, bk = self.ps[bi], "ps%d" % bi
                        self.mmg(bank[:, 0:n], bk,
                                 [(pcs[j // 11][0][:, j % 11, mm_ * 128:(mm_ + 1) * 128], uT[:, j, t0:t0 + n]) for j in range(NJ)],
                                 reads=[pcs[0][1], pcs[1][1], "uT"])
                        w = 0 if ti == 0 else 1
                        T.op("dve", lambda e, bank=bank, c=c, t0=t0, n=n, w=w: e.scalar_tensor_tensor(
                            out=self.xT[:, c, t0:t0 + n], in0=bank[:, 0:n], scalar=self.gsT[:, w, c:c + 1],
                            in1=self.xT[:, c, t0:t0 + n], op0=ALU.mult, op1=ALU.add),
                            reads=[bk, "gsT", "xT"], writes=["xT"])
            self._ln_all(l, sub, has_next)
            T.barrier()

    def _ln_alloc(self, les):
        self.ln_zb = [self.sb("ln_zb%d" % i, [128, 512], BF16, es=les) for i in range(2)]
        self.ln_zq = [self.sb("ln_zq%d" % i, [128, 512], BF16, es=les) for i in range(2)]
        self.ln_t = [self.sb("ln_t%d" % i, [128, 512], F32, es=les) for i in range(2)]
        self.ln_mean = self.sb("ln_mean", [128, 512], F32, es=les)
        self.ln_rstd = self.sb("ln_rstd", [128, 512], F32, es=les)
        self.ln_tmp = self.sb("ln_tmp", [128, 512], F32, es=les)

    def _ln_all(self, l, sub, has_next):
        T = self.T
        for ti, (t0, n) in enumerate(TILES):
            for c in range(NCH):
                zb = self.ln_zb[c % 2]
                zq = self.ln_zq[c % 2]
                kb, kq = "ln_zb%d" % (c % 2), "ln_zq%d" % (c % 2)
                T.op("act", lambda e, c=c, zb=zb: e.activation(out=zb[:], in_=self.xT[:, c, t0:t0 + n], func=AF.Copy),
                     reads=["xT"], writes=[kb])
                T.op("pool", lambda e, c=c, zq=zq: e.tensor_tensor(out=zq[:], in0=self.xT[:, c, t0:t0 + n],
                                                                   in1=self.xT[:, c, t0:t0 + n], op=ALU.mult),
                     reads=["xT"], writes=[kq])
                T.op("pe", lambda e, c=c, zb=zb: e.matmul(self.ps[6][:], self.ones_bf[:], zb[:], start=(c == 0), stop=(c == NCH - 1)),
                     reads=[kb, "ones_bf"], writes=["ps6"], same=False)
                T.op("pe", lambda e, c=c, zq=zq: e.matmul(self.ps[7][:], self.ones_bf[:], zq[:], start=(c == 0), stop=(c == NCH - 1)),
                     reads=[kq, "ones_bf"], writes=["ps7"], same=False)
            self._ln_tile(l, sub, ti, has_next, None)

    def _finish(self):
        nc, T = self.nc, self.T
        with ExitStack() as les:
            ys = [self.sb("ystage%d" % i, [128, D], F32, es=les) for i in range(2)]
            import os
            for tt in range(int(os.environ.get("DBG_NT", NTOK // 128))):
                s = ys[tt % 2]
                key = "ystage%d" % (tt % 2)
                for half in range(2):
                    bi = (tt * 2 + half) % 4
                    bank, bkey = self.ps[bi], "ps%d" % bi
                    for q in range(0 if os.environ.get("DBG_NOTR") else 4):
                        c = half * 4 + q
                        T.op("pe", lambda e, c=c, q=q, bank=bank, tt=tt: e.transpose(
                            bank[:, q * 128:(q + 1) * 128], self.xT[:, c, tt * 128:(tt + 1) * 128], self.ident_f[:]),
                            reads=["xT", "ident_f"], writes=[bkey] if q in (0, 3) else [], same=False)
                    if half == 0 or os.environ.get("DBG_NOACT"):
                        T.op("dve", lambda e, bank=bank, s=s, half=half: e.tensor_copy(out=s[:, half * 512:(half + 1) * 512], in_=bank[:]),
                             reads=[bkey], writes=[key if half == 0 else key + "h"])
                    else:
                        T.op("act", lambda e, bank=bank, s=s: e.activation(out=s[:, 512:1024], in_=bank[:], func=AF.Copy),
                             reads=[bkey], writes=[key + "h"])
                T.dma("sp", self.y_out[tt * 128:(tt + 1) * 128, :], s[:], reads=[key, key + "h"], writes=["y_out%d" % tt])
            T.wait_all("sp")


def _prep_inputs(inputs):
    consts = host_consts()
    shared = {}
    for k in ("w_ada", "b_ada", "ffn1_w_in", "ffn1_w_out", "ffn2_w_in", "ffn2_w_out", "ln_g", "ln_b"):
        shared[k] = np.ascontiguousarray(inputs[k], dtype=np.float32)
    shared["c_ones"] = consts["ones"]
    shared["c_ident"] = consts["ident"]
    shared["c_blockones"] = consts["blockones"]
    maps = []
    for core in range(8):
        m = dict(shared)
        xp = inputs["x_prompt"][2 * core:2 * core + 2].reshape(512, D)
        xs = inputs["x_sample"][core]
        m["x_tok"] = np.ascontiguousarray(np.concatenate([xp, xs], axis=0), dtype=np.float32)
        m["cvec"] = np.ascontiguousarray(np.stack([inputs["c_ctx"], inputs["c"][core]], axis=0), dtype=np.float32)
        maps.append(m)
    return maps


PROMPT_SEQS = [(0, 256), (256, 256)]
SAMPLE = (512, 1024)
NA_R = {0: (0, 5), 1: (0, 7), 2: (0, 9), 3: (0, 11), 4: (5, 15), 5: (7, 15), 6: (9, 15), 7: (11, 15)}
NVA, NVB = 12, 11


def _na_tables_idx():
    flat = np.zeros((128, NVA + NVB, 64), np.int64)
    mask = np.zeros((128, NVA + NVB, 64), np.float32)
    qc = np.arange(64)
    cs = np.clip(qc - 8, 0, 48)
    for half in range(2):
        for kc in range(64):
            p = half * 64 + kc
            colv = (kc >= cs) & (kc < cs + 16)
            dc = np.clip(kc - qc + 15, 0, 30)
            for v in range(NVA + NVB):
                idx = (v - 6) if v < NVA else (v - NVA - 3)
                dr = half - idx + 7
                rowv = True
                if v < NVA and idx == 5 and half == 0:
                    rowv = False
                if v >= NVA and idx == -3 and half == 1:
                    rowv = False
                drc = min(max(dr, 0), 14)
                flat[p, v] = drc * 31 + dc
                mask[p, v] = (colv & rowv).astype(np.float32)
    return flat, mask


def _rope_tables():
    t = np.arange(1024)
    n_freq = 16
    inv = 10000.0 ** (-np.arange(n_freq, dtype=np.float32) / n_freq)
    rows = (t // 64).astype(np.float32)
    cols = (t % 64).astype(np.float32)
    ang = np.concatenate([rows[:, None] * inv, cols[:, None] * inv], axis=-1)
    cos32, sin32 = np.cos(ang).astype(np.float32), np.sin(ang).astype(np.float32)
    cosT = np.zeros((128, 1024), np.float32)
    sinT = np.zeros((128, 1024), np.float32)
    for p in range(128):
        d = p % 64
        cosT[p] = cos32[:, d % 32]
        sinT[p] = sin32[:, d % 32] * (-1.0 if d < 32 else 1.0)
    return cosT, sinT


def _mixer_consts():
    c = {}
    a = np.arange(128)
    c["c_mlow"] = (a[None, :] <= a[:, None]).astype(np.float32)
    c["c_mup"] = (a[:, None] <= a[None, :]).astype(np.float32)
    pm = np.zeros((128, 128), np.float32)
    for m in range(128):
        k = (m & 64) | ((m + 32) & 63)
        pm[k, m] = 1.0
    c["c_swap"] = pm
    c["c_ropecos"], c["c_ropesin"] = _rope_tables()
    return c


def _mx_declare(self):
    L = DEPTH
    self.w_in = self.din("w_in", [L, D, IN_COLS])
    self.proj = [self.din("proj_a", [L, 512, D]), self.din("proj_b", [L, 512, D]), self.din("proj_c", [L, 512, D])]
    self.w_out = self.din("w_out", [L, D, D])
    self.sink = self.din("attn_sink", [L, 8])
    self.cak = self.din("cache_attn_k", [L, 256, 128])
    self.cav = self.din("cache_attn_v", [L, 256, 128])
    self.cbk = self.din("cache_na_k", [L, 256, 512])
    self.cbv = self.din("cache_na_v", [L, 256, 512])
    self.natab = self.din("na_tab", [L, 8, 128, (NVA + NVB) * 64])
    for nm in ("c_mlow", "c_mup", "c_swap"):
        setattr(self, nm, self.din(nm, [128, 128]))
    self.c_ropecos = self.din("c_ropecos", [128, 1024])
    self.c_ropesin = self.din("c_ropesin", [128, 1024])
    self.o_nak = self.dout("o_nak", [2, L, 256, 128])
    self.o_nav = self.dout("o_nav", [2, L, 256, 128])
    self.o_nbk = self.dout("o_nbk", [2, L, 256, 512])
    self.o_nbv = self.dout("o_nbv", [2, L, 256, 512])


def _mx_globals(self):
    T = self.T
    self.mlow = self.sb("mlow", [128, 128], BF16)
    self.mup = self.sb("mup", [128, 128], BF16)
    self.swapm = self.sb("swapm", [128, 128], BF16)
    T.dma("pool", self.mlow[:], self.c_mlow, writes=["mlow"])
    T.dma("pool", self.mup[:], self.c_mup, writes=["mup"])
    T.dma("pool", self.swapm[:], self.c_swap, writes=["swapm"])


def _mixer(self, l):
    T = self.T
    self._sub_scalars(l, 1, 5, 1.0, (6, 7))
    with ExitStack() as mes:
        self.mergedT = self.sb("mergedT", [128, NCH, NTOK], BF16, es=mes)
        self.sgb = [self.sb("sgb%d" % i, [128, 512], F32, es=mes) for i in range(2)]
        self.mtmp = [self.sb("mtmp%d" % i, [128, 512], F32, es=mes) for i in range(2)]
        self._attn(l, 0)
        self._attn(l, 1)
        if hasattr(self, "_rwkv"):
            self._rwkv(l)
        self._mix_out(l)
        T.barrier()


def _merge_branch(self, l, g, o_chunks, okey, prow0, t0, ntok, first):
    T = self.T
    nk = len(o_chunks)
    wg = self.w_in[l].rearrange("(kc p) n -> p kc n", p=128)
    wp = self.proj[g][l].rearrange("(kc p) n -> p kc n", p=128)
    k0 = prow0 // 128
    tiles = [(t0 + a, min(512, ntok - a)) for a in range(0, ntok, 512)]
    cnt = getattr(self, "_mb_cnt", 0)
    for m in range(NCH):
        s = self.wslot
        self.wslot = (self.wslot + 1) % self.NSLOT
        key = "wring%d" % s
        gv = self.wring[:, s, 0:1024].rearrange("p (kc n) -> p kc n", kc=NCH)
        pv = self.wring[:, s, 1024:1024 + nk * 128].rearrange("p (kc n) -> p kc n", kc=nk)
        gc0 = G_OFF + g * 1024 + m * 128
        T.dma("pool", [gv, pv], [wg[:, :, gc0:gc0 + 128], wp[:, k0:k0 + nk, m * 128:(m + 1) * 128]], writes=[key])
        for (tt0, n) in tiles:
            ba, bp = self.ps[(cnt % 2) * 2], self.ps[(cnt % 2) * 2 + 1]
            ka, kp = "ps%d" % ((cnt % 2) * 2), "ps%d" % ((cnt % 2) * 2 + 1)
            self.mmg(ba[:, 0:n], ka, [(gv[:, kc, :], self.hT[:, kc, tt0:tt0 + n]) for kc in range(NCH)], reads=[key, "hT"])
            self.mmg(bp[:, 0:n], kp, [(pv[:, kc, :], o_chunks[kc](tt0, n)) for kc in range(nk)], reads=[key, okey])
            sg = self.sgb[cnt % 2]
            sk = "sgb%d" % (cnt % 2)
            T.op("act", lambda e, ba=ba, sg=sg, n=n: e.activation(out=sg[:, 0:n], in_=ba[:, 0:n], func=AF.Sigmoid),
                 reads=[ka], writes=[sk])
            if first:
                T.op("dve", lambda e, bp=bp, sg=sg, m=m, tt0=tt0, n=n: e.tensor_tensor(
                    out=self.mergedT[:, m, tt0:tt0 + n], in0=bp[:, 0:n], in1=sg[:, 0:n], op=ALU.mult),
                    reads=[kp, sk], writes=["mergedT"])
            else:
                mt = self.mtmp[cnt % 2]
                mk = "mtmp%d" % (cnt % 2)
                T.op("dve", lambda e, bp=bp, sg=sg, mt=mt, n=n: e.tensor_tensor(out=mt[:, 0:n], in0=bp[:, 0:n], in1=sg[:, 0:n], op=ALU.mult),
                     reads=[kp, sk], writes=[mk])
                T.op("pool", lambda e, mt=mt, m=m, tt0=tt0, n=n: e.tensor_tensor(
                    out=self.mergedT[:, m, tt0:tt0 + n], in0=self.mergedT[:, m, tt0:tt0 + n], in1=mt[:, 0:n], op=ALU.add),
                    reads=[mk, "mergedT"], writes=["mergedT"])
            cnt += 1
    self._mb_cnt = cnt


def _mix_out(self, l):
    T = self.T
    wv = self.w_out[l].rearrange("(kc p) n -> p kc n", p=128)
    with ExitStack() as les:
        self._ln_alloc(les)
        cnt = 0
        for piece in range(2):
            dst, key = self.wload(lambda s_: s_.rearrange("p (kc n) -> p kc n", kc=NCH), wv[:, :, piece * 512:(piece + 1) * 512])
            for ti, (t0, n) in enumerate(TILES):
                w = 0 if ti == 0 else 1
                for q in range(4):
                    c = piece * 4 + q
                    bank, bk = self.ps[cnt % 4], "ps%d" % (cnt % 4)
                    self.mmg(bank[:, 0:n], bk, [(dst[:, kc, q * 128:(q + 1) * 128], self.mergedT[:, kc, t0:t0 + n]) for kc in range(NCH)],
                             reads=[key, "mergedT"])
                    T.op("dve", lambda e, bank=bank, c=c, t0=t0, n=n, w=w: e.scalar_tensor_tensor(
                        out=self.xT[:, c, t0:t0 + n], in0=bank[:, 0:n], scalar=self.gsT[:, w, c:c + 1],
                        in1=self.xT[:, c, t0:t0 + n], op0=ALU.mult, op1=ALU.add),
                        reads=[bk, "gsT", "xT"], writes=["xT"])
                    cnt += 1
        self._ln_all(l, 1, True)
        T.barrier()


def _attn_head(self, specs, q_fn, ncols, out_ap, rows, sink_ap, okey):
    T = self.T
    hc = self._ah_cnt
    self._ah_cnt += 1
    O, Ok = self.ps[2 + (hc % 2) * 2], "ps%d" % (2 + (hc % 2) * 2)
    Dn, Dk = self.ps[3 + (hc % 2) * 2], "ps%d" % (3 + (hc % 2) * 2)
    r0, r1 = rows
    n = len(specs)
    pend = None
    for idx in range(n + 1):
        if idx < n:
            sp = specs[idx]
            sc = self._as_cnt
            self._as_cnt += 1
            sbk, sk = self.ps[sc % 2], "ps%d" % (sc % 2)
            w = sp["c1"] - sp["c0"]
            self.mmg(sbk[:, 0:w], sk, [(sp["kT"], q_fn(sp["c0"], sp["c1"]))], reads=sp["keys"])
            pt, pk = self.ptb[sc % 4], "ptb%d" % (sc % 4)
            T.op("act", lambda e, sbk=sbk, pt=pt, w=w: e.activation(out=pt[:, 0:w], in_=sbk[:, 0:w], func=AF.Exp, scale=SCALE),
                 reads=[sk], writes=[pk])
            for (a, b, mk_ap, mkey) in sp.get("masks", ()):
                T.op("dve", lambda e, pt=pt, a=a, b=b, mk_ap=mk_ap: e.tensor_tensor(out=pt[:, a:b], in0=pt[:, a:b], in1=mk_ap, op=ALU.mult),
                     reads=[pk, mkey], writes=[pk])
            cur = (sp, pt, pk, w, idx)
        else:
            cur = None
        if pend is not None:
            sp, pt, pk, w, i = pend
            first, last = (i == 0), (i == n - 1)
            for (bank, bk, lt) in ((O, Ok, sp["v"]), (Dn, Dk, self.ones_bf[:])):
                T.op("pe", lambda e, bank=bank, lt=lt, pt=pt, w=w, sp=sp, first=first, last=last: e.matmul(
                    bank[:, sp["c0"]:sp["c1"]], lt, pt[:, 0:w], start=first, stop=last),
                    reads=[pk, "ones_bf"] + sp["keys"], writes=[bk] if (first or last) else [], same=False)
        pend = cur
    rc, rk = self.rcb[hc % 2], "rcb%d" % (hc % 2)
    import os
    if "dbg_misc" in self.dram_out and os.environ.get("DBG_HEAD") and int(os.environ["DBG_HEAD"]) == hc and not getattr(self, "_dbg_done", False):
        self._dbg_done = True
        T.op("dve", lambda e: e.tensor_copy(out=self.mtmp[0][:, 0:ncols], in_=Dn[:, 0:ncols]), reads=[Dk], writes=["mtmp0"])
        T.dma("sp", self.dram_out["dbg_misc"][:, 1024:1024 + ncols], self.mtmp[0][:, 0:ncols], reads=["mtmp0"], writes=["dbgm2"])
        T.op("dve", lambda e: e.tensor_copy(out=self.mtmp[1][:, 0:ncols], in_=O[:, 0:ncols]), reads=[Ok], writes=["mtmp1"])
        T.dma("sp", self.dram_out["dbg_misc"][:, 1536:1536 + ncols], self.mtmp[1][:, 0:ncols], reads=["mtmp1"], writes=["dbgm3"])
    if sink_ap is not None:
        T.op("dve", lambda e: e.tensor_scalar(out=rc[r0:r1, 0:ncols], in0=Dn[r0:r1, 0:ncols], scalar1=sink_ap, scalar2=None, op0=ALU.add),
             reads=[Dk, "esink"], writes=[rk])
        T.op("dve", lambda e: e.reciprocal(out=rc[r0:r1, 0:ncols], in_=rc[r0:r1, 0:ncols]), reads=[rk], writes=[rk])
    else:
        T.op("dve", lambda e: e.reciprocal(out=rc[r0:r1, 0:ncols], in_=Dn[r0:r1, 0:ncols]), reads=[Dk], writes=[rk])
    T.op("dve", lambda e: e.tensor_tensor(out=out_ap, in0=O[r0:r1, 0:ncols], in1=rc[r0:r1, 0:ncols], op=ALU.mult),
         reads=[Ok, rk], writes=[okey])


for _f in (_mx_declare, _mx_globals, _mixer, _merge_branch, _mix_out, _attn_head):
    setattr(Builder, _f.__name__, _f)


def _attn(self, l, which):
    nc, T = self.nc, self.T
    A = (which == 0)
    nkc = 2 if A else 4
    qoff = A_OFF if A else B_OFF
    wv = self.w_in[l].rearrange("(kc p) n -> p kc n", p=128)
    self._ah_cnt = 0
    self._as_cnt = 0
    with ExitStack() as aes:
        qT = self.sb("qT", [128, 4, NTOK], BF16, es=aes)
        kT = self.sb("kT", [128, nkc, NTOK], BF16, es=aes)
        VW = 256 if A else 512
        vtm = self.sb("vtm", [128, 12, VW], BF16, es=aes)
        ckT = self.sb("ckT", [128, nkc, 256], BF16, es=aes)
        cv = self.sb("cv", [128, 2, VW], BF16, es=aes)
        oT = self.sb("oT", [128, 4, NTOK], BF16, es=aes)
        self.ptb = [self.sb("ptb%d" % i, [128, 512], BF16, es=aes) for i in range(4)]
        self.rcb = [self.sb("rcb%d" % i, [128, 512], F32, es=aes) for i in range(2)]
        ostg = [self.sb("ostg%d" % i, [128, 512], F32, es=aes) for i in range(2)]
        if A:
            rcos = self.sb("rcos", [128, 1024], BF16, es=aes)
            rsin = self.sb("rsin", [128, 1024], BF16, es=aes)
            esink = self.sb("esink", [128, 8], F32, es=aes)
            T.dma("pool", rcos[:], self.c_ropecos, writes=["rcos"])
            T.dma("pool", rsin[:], self.c_ropesin, writes=["rsin"])
            T.dma("sp", esink[:], self.sink[l].partition_broadcast(128), writes=["esink"])
            T.op("act", lambda e: e.activation(out=esink[:], in_=esink[:], func=AF.Exp), reads=["esink"], writes=["esink"])
        else:
            etab = self.sb("etab", [128, (NVA + NVB) * 64], BF16, es=aes)
        pcnt = [0]

        def bankof():
            i = pcnt[0] % 2
            pcnt[0] += 1
            return self.ps[i], "ps%d" % i

        def evac(i, out_ap, in_ap, reads, writes):
            if i % 2 == 0:
                T.op("act", lambda e: e.activation(out=out_ap, in_=in_ap, func=AF.Copy), reads=reads, writes=writes)
            else:
                T.op("dve", lambda e: e.tensor_copy(out=out_ap, in_=in_ap), reads=reads, writes=writes)

        dst, key = self.wload(lambda s_: s_.rearrange("p (kc n) -> p kc n", kc=NCH), wv[:, :, qoff:qoff + 512])
        for c in range(4):
            for (t0, n) in TILES:
                bank, bk = bankof()
                self.mmg(bank[:, 0:n], bk, [(dst[:, kc, c * 128:(c + 1) * 128], self.hT[:, kc, t0:t0 + n]) for kc in range(NCH)], reads=[key, "hT"])
                evac(pcnt[0], qT[:, c, t0:t0 + n], bank[:, 0:n], [bk], ["qT"])
        import os
        stopat = os.environ.get("DBG_STOP", "")
        if stopat == "q":
            T.op("dve", lambda e: e.memset(oT[:], 0.0), writes=["oT"])
            self._merge_branch(l, which, [(lambda t0, n, c=c: oT[:, c, t0:t0 + n]) for c in range(4)], "oT", 0, 0, NTOK, first=A)
            return
        if A:
            s = self.wslot
            self.wslot = (self.wslot + 1) % self.NSLOT
            key = "wring%d" % s
            dst = self.wring[:, s, 0:NCH * 256].rearrange("p (kc kv dup d) -> p kc kv dup d", kc=NCH, kv=2, dup=2)
            T.dma("pool", [dst[:, :, kv, dup, :] for kv in range(2) for dup in range(2)],
                  [wv[:, :, 512 + kv * 64:512 + (kv + 1) * 64] for kv in range(2) for dup in range(2)], writes=[key])
            kw = lambda kc, c: dst[:, kc, c, :, :]
        else:
            dst, key = self.wload(lambda s_: s_.rearrange("p (kc n) -> p kc n", kc=NCH), wv[:, :, B_OFF + 512:B_OFF + 1024])
            kw = lambda kc, c: dst[:, kc, c * 128:(c + 1) * 128]
        for c in range(nkc):
            for (t0, n) in TILES:
                bank, bk = bankof()
                self.mmg(bank[:, 0:n], bk, [(kw(kc, c), self.hT[:, kc, t0:t0 + n]) for kc in range(NCH)], reads=[key, "hT"])
                evac(pcnt[0], kT[:, c, t0:t0 + n], bank[:, 0:n], [bk], ["kT"])
        if stopat == "k":
            T.op("dve", lambda e: e.memset(oT[:], 0.0), writes=["oT"])
            self._merge_branch(l, which, [(lambda t0, n, c=c: oT[:, c, t0:t0 + n]) for c in range(4)], "oT", 0, 0, NTOK, first=A)
            return
        ocnt = [0]

        def out_rows(dram_ap, st, width, bank):
            if os.environ.get("DBG_NOOUTROWS"):
                return
            og, ogk = ostg[ocnt[0] % 2], "ostg%d" % (ocnt[0] % 2)
            ocnt[0] += 1
            T.op("dve", lambda e: e.tensor_copy(out=og[:, 0:width], in_=bank), reads=[bk_cur[0]], writes=[ogk])
            sq, tl = st // 2, (st % 2) * 128
            if os.environ.get("DBG_NOOUTDMA"):
                return
            T.dma("sp", dram_ap[sq, l, tl:tl + 128, :], og[:, 0:width], reads=[ogk], writes=["outrows%d" % ocnt[0]])

        bk_cur = [None]
        if A:
            dst, key = self.wload(lambda s_: s_[:, 0:NCH * 256].rearrange("p (kc n) -> p kc n", kc=NCH), wv[:, :, 512:768])
            for st in range(12):
                bank, bk = bankof()
                bk_cur[0] = bk
                self.mmg(bank[:, 0:256], bk, [(self.hT[:, kc, st * 128:(st + 1) * 128], dst[:, kc, :]) for kc in range(NCH)], reads=[key, "hT"])
                if st < 4:
                    out_rows(self.o_nak, st, 128, bank[:, 0:128])
                    out_rows(self.o_nav, st, 128, bank[:, 128:256])
                for dup in range(2):
                    T.op("act", lambda e, bank=bank, st=st, dup=dup: e.activation(
                        out=vtm[:, st, :].rearrange("p (kv dup d) -> p kv dup d", kv=2, dup=2)[:, :, dup, :],
                        in_=bank[:, 128:256].rearrange("p (kv d) -> p kv d", kv=2), func=AF.Copy),
                        reads=[bk], writes=["vtm"])
        else:
            dstk, keyk = self.wload(lambda s_: s_.rearrange("p (kc n) -> p kc n", kc=NCH), wv[:, :, B_OFF + 512:B_OFF + 1024])
            for st in range(4):
                bank, bk = bankof()
                bk_cur[0] = bk
                self.mmg(bank[:, 0:512], bk, [(self.hT[:, kc, st * 128:(st + 1) * 128], dstk[:, kc, :]) for kc in range(NCH)], reads=[keyk, "hT"])
                out_rows(self.o_nbk, st, 512, bank[:, 0:512])
            dst, key = self.wload(lambda s_: s_.rearrange("p (kc n) -> p kc n", kc=NCH), wv[:, :, B_OFF + 1024:B_OFF + 1536])
            for st in range(12):
                bank, bk = bankof()
                bk_cur[0] = bk
                self.mmg(bank[:, 0:512], bk, [(self.hT[:, kc, st * 128:(st + 1) * 128], dst[:, kc, :]) for kc in range(NCH)], reads=[key, "hT"])
                if st < 4:
                    out_rows(self.o_nbv, st, 512, bank[:, 0:512])
                T.op("act", lambda e, bank=bank, st=st: e.activation(out=vtm[:, st, :], in_=bank[:, 0:512], func=AF.Copy),
                     reads=[bk], writes=["vtm"])
        import os
        if os.environ.get("DBG_NOCACHE"):
            pass
        elif A:
            for ct in range(2):
                T.dma("pool", [cv[:, ct, :].rearrange("p (kv dup d) -> p kv dup d", kv=2, dup=2)[:, :, dup, :] for dup in range(2)],
                      [self.cav[l, ct * 128:(ct + 1) * 128, :].rearrange("t (kv d) -> t kv d", kv=2) for dup in range(2)], writes=["cv"])
                og, ogk = ostg[ct], "ostg%d" % ct
                T.dma("sp", [og[:, 0:256].rearrange("p (kv dup d) -> p kv dup d", kv=2, dup=2)[:, :, dup, :] for dup in range(2)],
                      [self.cak[l, ct * 128:(ct + 1) * 128, :].rearrange("t (kv d) -> t kv d", kv=2) for dup in range(2)], writes=[ogk])
                for kv in range(2):
                    bank, bk = self.ps[6 + kv], "ps%d" % (6 + kv)
                    T.op("pe", lambda e, bank=bank, og=og, kv=kv: e.transpose(bank[:, 0:128], og[:, kv * 128:(kv + 1) * 128], self.ident_f[:]),
                         reads=[ogk, "ident_f"], writes=[bk], same=False)
                    T.op("dve", lambda e, bank=bank, kv=kv, ct=ct: e.tensor_copy(out=ckT[:, kv, ct * 128:(ct + 1) * 128], in_=bank[:, 0:128]),
                         reads=[bk], writes=["ckT"])
        else:
            for ct in range(2):
                T.dma("pool", cv[:, ct, :], self.cbv[l, ct * 128:(ct + 1) * 128, :], writes=["cv"])
                og, ogk = ostg[ct], "ostg%d" % ct
                T.dma("sp", og[:], self.cbk[l, ct * 128:(ct + 1) * 128, :], writes=[ogk])
                bank, bk = self.ps[6 + ct], "ps%d" % (6 + ct)
                for c in range(4):
                    T.op("pe", lambda e, bank=bank, og=og, c=c: e.transpose(bank[:, c * 128:(c + 1) * 128], og[:, c * 128:(c + 1) * 128], self.ident_f[:]),
                         reads=[ogk, "ident_f"], writes=[bk] if c in (0, 3) else [], same=False)
                T.op("dve", lambda e, bank=bank, ct=ct: e.tensor_copy(out=ckT[:, :, ct * 128:(ct + 1) * 128],
                                                                      in_=bank[:].rearrange("p (c t) -> p c t", c=4)),
                     reads=[bk], writes=["ckT"])
        if "dbg_misc" in self.dram_out and l == 0 and A and os.environ.get("DBG_DUMPM"):
            T.op("dve", lambda e: e.tensor_copy(out=self.rcb[0][:, 0:128], in_=self.mlow[:]), reads=["mlow"], writes=["rcb0"])
            T.op("dve", lambda e: e.tensor_copy(out=self.rcb[0][:, 128:256], in_=self.mup[:]), reads=["mup"], writes=["rcb0"])
            T.op("dve", lambda e: e.tensor_copy(out=self.rcb[0][:, 256:384], in_=self.swapm[:]), reads=["swapm"], writes=["rcb0"])
            T.op("dve", lambda e: e.tensor_copy(out=self.rcb[0][:, 384:512], in_=rcos[:, 0:128]), reads=["rcos"], writes=["rcb0"])
            T.dma("sp", self.dram_out["dbg_misc"][:, 0:512], self.rcb[0][:], reads=["rcb0"], writes=["dbgm0"])
        elif "dbg_misc" in self.dram_out and l == 0 and A:
            T.op("dve", lambda e: e.tensor_copy(out=self.rcb[0][:], in_=ckT[:].rearrange("p a b -> p (a b)")), reads=["ckT"], writes=["rcb0"])
            T.dma("sp", self.dram_out["dbg_misc"][:, 0:512], self.rcb[0][:], reads=["rcb0"], writes=["dbgm0"])
            T.op("dve", lambda e: e.tensor_copy(out=self.rcb[1][:], in_=cv[:].rearrange("p a b -> p (a b)")), reads=["cv"], writes=["rcb1"])
            T.dma("sp", self.dram_out["dbg_misc"][:, 512:1024], self.rcb[1][:], reads=["rcb1"], writes=["dbgm1"])
        import os
        if A and not os.environ.get("DBG_NOROPE"):
            rc = 0
            for (arr, akey, nchunk) in ((qT, "qT", 4), (kT, "kT", 2)):
                for c in range(nchunk):
                    for qt in range(2):
                        t0 = 512 + qt * 512
                        bank, bk = self.ps[6 + rc % 2], "ps%d" % (6 + rc % 2)
                        rc += 1
                        x = arr[:, c, t0:t0 + 512]
                        self.mmg(bank[:], bk, [(self.swapm[:], x)], reads=[akey, "swapm"])
                        T.op("dve", lambda e, x=x, qt=qt: e.tensor_tensor(out=self.mtmp[0][:], in0=x, in1=rcos[:, qt * 512:(qt + 1) * 512], op=ALU.mult),
                             reads=[akey, "rcos"], writes=["mtmp0"])
                        T.op("dve", lambda e, bank=bank, qt=qt: e.tensor_tensor(out=self.mtmp[1][:], in0=bank[:], in1=rsin[:, qt * 512:(qt + 1) * 512], op=ALU.mult),
                             reads=[bk, "rsin"], writes=["mtmp1"])
                        T.op("pool", lambda e, x=x: e.tensor_tensor(out=x, in0=self.mtmp[0][:], in1=self.mtmp[1][:], op=ALU.add),
                             reads=["mtmp0", "mtmp1"], writes=[akey])
        import os
        if os.environ.get("DBG_NOHEADS"):
            T.op("dve", lambda e: e.memset(oT[:], 0.0), writes=["oT"])
        for hp in range(0 if os.environ.get("DBG_NOHEADS") else 4):
            for par in range(2):
                h = hp * 2 + par
                if not A:
                    T.dma("pool", etab[:], self.natab[l, h], writes=["etab"])
                    T.op("act", lambda e: e.activation(out=etab[:], in_=etab[:], func=AF.Exp), reads=["etab"], writes=["etab"])
                rows = (par * 64, par * 64 + 64)
                kc_ = (h // 4) if A else hp
                vsl = (lambda a: a[:, kc_ * 128:(kc_ + 1) * 128])
                sink_ap = esink[rows[0]:rows[1], h:h + 1] if A else None
                for sq in range(2):
                    b0 = sq * 256
                    specs = [dict(kT=kT[rows[0]:rows[1], kc_, b0 + kt * 128:b0 + (kt + 1) * 128], v=vsl(vtm[:, sq * 2 + kt, :]),
                                  c0=0, c1=256, keys=["kT", "vtm", "qT"]) for kt in range(2)]
                    self._attn_head(specs, lambda c0, c1, b0=b0: qT[rows[0]:rows[1], hp, b0 + c0:b0 + c1], 256,
                                    oT[rows[0]:rows[1], hp, b0:b0 + 256], rows, sink_ap, "oT")
                for qt in range(2):
                    b0 = 512 + qt * 512
                    specs = [dict(kT=ckT[rows[0]:rows[1], kc_, ct * 128:(ct + 1) * 128], v=vsl(cv[:, ct, :]), c0=0, c1=512,
                                  keys=["ckT", "cv", "qT"]) for ct in range(2)]
                    if A and os.environ.get("DBG_ANOLOCAL"):
                        pass
                    elif A:
                        for j in range(4 * qt - 1, 4 * qt + 5):
                            if j < 0 or j > 7:
                                continue
                            ilo, ihi = max(j - 1, 4 * qt), min(j + 1, 4 * qt + 3)
                            masks = []
                            for i in range(ilo, ihi + 1):
                                a = (i - ilo) * 128
                                if i == j + 1:
                                    masks.append((a, a + 128, self.mlow[:], "mlow"))
                                elif i == j - 1:
                                    masks.append((a, a + 128, self.mup[:], "mup"))
                            specs.append(dict(kT=kT[rows[0]:rows[1], kc_, 512 + j * 128:512 + (j + 1) * 128], v=vsl(vtm[:, 4 + j, :]),
                                              c0=(ilo - 4 * qt) * 128, c1=(ihi - 4 * qt + 1) * 128, masks=masks, keys=["kT", "vtm", "qT"]))
                    else:
                        for j in range(8):
                            ra, rb = NA_R[j]
                            lo, hi = max(ra, 8 * qt), min(rb, 8 * qt + 7)
                            if lo > hi:
                                continue
                            c0, c1 = (lo - 8 * qt) * 64, (hi - 8 * qt + 1) * 64
                            v0 = (lo - 2 * j + 6) if j <= 3 else (NVA + lo - 2 * j + 3)
                            masks = [(0, c1 - c0, etab[:, v0 * 64:v0 * 64 + (c1 - c0)], "etab")]
                            specs.append(dict(kT=kT[rows[0]:rows[1], kc_, 512 + j * 128:512 + (j + 1) * 128], v=vsl(vtm[:, 4 + j, :]),
                                              c0=c0, c1=c1, masks=masks, keys=["kT", "vtm", "qT"]))
                    self._attn_head(specs, lambda c0, c1, b0=b0: qT[rows[0]:rows[1], hp, b0 + c0:b0 + c1], 512,
                                    oT[rows[0]:rows[1], hp, b0:b0 + 512], rows, sink_ap, "oT")
        if "dbg_oT" in self.dram_out and l == 0:
            for c in range(4):
                for (t0, n) in TILES:
                    og, ogk = ostg[c % 2], "ostg%d" % (c % 2)
                    T.op("dve", lambda e, og=og, c=c, t0=t0, n=n: e.tensor_copy(out=og[:, 0:n], in_=oT[:, c, t0:t0 + n]), reads=["oT"], writes=[ogk])
                    T.dma("sp", self.dram_out["dbg_oT"][which, c, :, t0:t0 + n], og[:, 0:n], reads=[ogk], writes=["dbgo%d_%d_%d" % (which, c, t0)])
        self._merge_branch(l, which, [(lambda t0, n, c=c: oT[:, c, t0:t0 + n]) for c in range(4)], "oT", 0, 0, NTOK, first=A)
        T.barrier()


Builder._attn = _attn
_old_declare = Builder._declare
_old_globals = Builder._globals


def _declare2(self):
    _old_declare(self)
    self._mx_declare()


def _globals2(self):
    _old_globals(self)
    self._mx_globals()


Builder._declare = _declare2
Builder._globals = _globals2


def _prep_inputs2(inputs):
    maps = _prep_inputs(inputs)
    mc = _mixer_consts()
    flat, mask = _na_tables_idx()
    flat = np.where(mask > 0, flat, 15 * 31)
    rpb = np.asarray(inputs["na_rpb"], np.float32).reshape(DEPTH, 8, 15 * 31)
    rpb = np.concatenate([rpb, np.full((DEPTH, 8, 1), -1.0e4, np.float32)], axis=-1)
    na_tab = rpb[:, :, flat.reshape(-1)].reshape(DEPTH, 8, 128, (NVA + NVB) * 64)
    shared = dict(mc)
    shared["na_tab"] = np.ascontiguousarray(na_tab)
    for k in ("w_in", "proj_a", "proj_b", "proj_c", "w_out", "attn_sink"):
        shared[k] = np.ascontiguousarray(inputs[k], dtype=np.float32)
    for core, m in enumerate(maps):
        m.update(shared)
        m["cache_attn_k"] = np.ascontiguousarray(inputs["cache_attn_k"][core].reshape(DEPTH, 256, 128))
        m["cache_attn_v"] = np.ascontiguousarray(inputs["cache_attn_v"][core].reshape(DEPTH, 256, 128))
        m["cache_na_k"] = np.ascontiguousarray(inputs["cache_na_k"][core].reshape(DEPTH, 256, 512))
        m["cache_na_v"] = np.ascontiguousarray(inputs["cache_na_v"][core].reshape(DEPTH, 256, 512))
    return maps


def _rw_declare(self):
    L = DEPTH
    self.w_shift = self.din("w_shift", [L, 3, 1920])
    self.decay_w0 = self.din("decay_w0", [L, 2, 512])
    self.decay_up = self.din("decay_up", [L, 128, 512])
    self.iclr_a0 = self.din("iclr_a0", [L, 2, 512])
    self.iclr_up = self.din("iclr_up", [L, 128, 512])
    self.gate_up = self.din("gate_up", [L, 128, 512])
    self.vec512 = {k: self.din(k, [L, 512]) for k in ("k_k", "k_a", "r_k", "gn_g", "gn_b")}
    self.st_in = self.din("state_rwkv", [L, 2, 8, 64, 64])
    self.o_nst = self.dout("o_nst", [2, L, 2, 8, 64, 64])
    self.c_identh = self.din("c_identh", [128, 64])


def _rw_layer_consts(self, l, es):
    T = self.T
    R = {}
    R["shT"] = self.sb("shT", [128, 3, 15], F32, es=es)
    self._load_vecT(self.w_shift[l].rearrange("s (c p) -> (s c) p", p=128), 45, "sh")
    T.op("dve", lambda e: e.tensor_copy(out=R["shT"][:].rearrange("p s c -> p (s c)"), in_=self.ps[7][:, 0:45]), reads=["ps7"], writes=["shT"])
    R["vecs"] = self.sb("rwvecs", [128, 36], F32, es=es)
    srcs = [self.decay_w0[l].rearrange("d (c p) -> (d c) p", p=128), self.iclr_a0[l].rearrange("d (c p) -> (d c) p", p=128)]
    srcs += [self.vec512[k][l].rearrange("(c p) -> c p", p=128) for k in ("k_k", "k_a", "r_k", "gn_g", "gn_b")]
    off = 0
    for sap, nr in zip(srcs, (8, 8, 4, 4, 4, 4, 4)):
        self._load_vecT(sap, nr, "rwv")
        T.op("dve", lambda e, off=off, nr=nr: e.tensor_copy(out=R["vecs"][:, off:off + nr], in_=self.ps[7][:, 0:nr]), reads=["ps7"], writes=["rwvecs"])
        off += nr
    R["omka"] = self.sb("omka", [128, 4], F32, es=es)
    T.op("dve", lambda e: e.tensor_scalar(out=R["omka"][:], in0=R["vecs"][:, 20:24], scalar1=-1.0, scalar2=1.0, op0=ALU.mult, op1=ALU.add),
         reads=["rwvecs"], writes=["omka"])
    R["dup"] = self.sb("dupw", [128, 512], BF16, es=es)
    R["iup"] = self.sb("iupw", [128, 512], BF16, es=es)
    R["gup"] = self.sb("gupw", [128, 512], BF16, es=es)
    T.dma("pool", R["dup"][:], self.decay_up[l], writes=["dupw"])
    T.dma("pool", R["iup"][:], self.iclr_up[l], writes=["iupw"])
    T.dma("pool", R["gup"][:], self.gate_up[l], writes=["gupw"])
    R["identh"] = self.sb("identh", [128, 64], BF16, es=es)
    T.dma("pool", R["identh"][:], self.c_identh, writes=["identh"])
    return R


def _rwkv(self, l):
    T = self.T
    with ExitStack() as res_:
        R = self._rw_layer_consts(l, res_)
        units = [dict(segs=[(0, 256), (256, 256)], pairs=[0, 1, 2, 3], sample=False),
                 dict(segs=[(512, 1024)], pairs=[0, 1], sample=True),
                 dict(segs=[(512, 1024)], pairs=[2, 3], sample=True)]
        for u in units:
            self._rw_unit(l, R, u)
        T.barrier()


def _rw_unit(self, l, R, u):
    nc, T = self.nc, self.T
    segs, pairs, sample = u["segs"], u["pairs"], u["sample"]
    nseg, npair = len(segs), len(pairs)
    t00 = segs[0][0]
    L = segs[0][1]
    TU = nseg * L
    tiles = [(t00 + a, 512) for a in range(0, TU, 512)]
    G = npair * nseg
    wv = self.w_in[l].rearrange("(kc p) n -> p kc n", p=128)
    vec = R["vecs"]
    with ExitStack() as ues:
        kdT = [self.sb("kdT%d" % d, [128, npair, TU], BF16, es=ues) for d in range(2)]
        kapT = self.sb("kapT", [128, npair, TU], BF16, es=ues)
        rT = self.sb("rT", [128, npair, TU], BF16, es=ues)
        vT = self.sb("vT", [128, npair, TU], BF16, es=ues)
        B = {}
        pc = [0]

        def pbank():
            i = pc[0] % 2
            pc[0] += 1
            return self.ps[i], "ps%d" % i

        def conv_chunk(ci, out_ap, okey, post=None):
            zraw, tmp = B["zraw"], B["tmp"]
            c0 = C_OFF + ci * 128
            s = self.wslot
            self.wslot = (self.wslot + 1) % self.NSLOT
            key = "wring%d" % s
            dst = self.wring[:, s, 0:1024].rearrange("p (kc n) -> p kc n", kc=NCH)
            T.dma("pool", dst, wv[:, :, c0:c0 + 128], writes=[key])
            for ti, (tt0, n) in enumerate(tiles):
                bank, bk = pbank()
                self.mmg(bank[:, 0:n], bk, [(dst[:, kc_, :], self.hT[:, kc_, tt0:tt0 + n]) for kc_ in range(NCH)], reads=[key, "hT"])
                T.op("act", lambda e, bank=bank, ti=ti, n=n: e.activation(out=zraw[:, ti * 512:ti * 512 + n], in_=bank[:, 0:n], func=AF.Copy),
                     reads=[bk], writes=["zraw"])
            sh = R["shT"]
            T.op("dve", lambda e: e.tensor_scalar(out=tmp[:], in0=zraw[:], scalar1=sh[:, 1, ci:ci + 1], scalar2=None, op0=ALU.mult),
                 reads=["zraw", "shT"], writes=["rwtmp"])
            for si in range(nseg):
                a, b = si * L, (si + 1) * L
                T.op("dve", lambda e, a=a, b=b: e.scalar_tensor_tensor(out=tmp[:, a + 1:b], in0=zraw[:, a:b - 1], scalar=sh[:, 0, ci:ci + 1],
                                                                       in1=tmp[:, a + 1:b], op0=ALU.mult, op1=ALU.add),
                     reads=["zraw", "shT", "rwtmp"], writes=["rwtmp"])
                T.op("dve", lambda e, a=a, b=b: e.scalar_tensor_tensor(out=tmp[:, a:b - 1], in0=zraw[:, a + 1:b], scalar=sh[:, 2, ci:ci + 1],
                                                                       in1=tmp[:, a:b - 1], op0=ALU.mult, op1=ALU.add),
                     reads=["zraw", "shT", "rwtmp"], writes=["rwtmp"])
            T.op("act", lambda e: e.activation(out=out_ap, in_=tmp[:], func=(post or AF.Copy)), reads=["rwtmp"], writes=[okey])
        yT = self.sb("yT", [128, npair, TU], F32, es=ues)
        T.op("pool", lambda e: e.memset(yT[:], 0.0), writes=["yT"])
        wsc = ExitStack()
        wT = [self.sb("wT%d" % d, [128, npair, TU], F32, es=wsc) for d in range(2)]
        bpT = [self.sb("bpT%d" % d, [128, npair, TU], BF16, es=wsc) for d in range(2)]
        with ExitStack() as pes:
            zraw = self.sb("zraw", [128, TU], F32, es=pes)
            kc = self.sb("kcv", [128, TU], F32, es=pes)
            av = zraw
            tmp = self.sb("rwtmp", [128, TU], F32, es=pes)
            tmpb = self.sb("rwtmpb", [128, TU], BF16, es=pes)
            twlo = self.sb("twlo", [128, TU], BF16, es=pes)
            talo = self.sb("talo", [128, TU], BF16, es=pes)
            B["zraw"], B["tmp"] = zraw, tmp
            conv_chunk(12, twlo[:], "twlo", AF.Tanh)
            conv_chunk(13, talo[:], "talo")
            for qi, p in enumerate(pairs):
                conv_chunk(p, rT[:, qi, :], "rT")
                conv_chunk(8 + p, vT[:, qi, :], "vT")
                conv_chunk(4 + p, kc[:], "kcv")
                T.op("dve", lambda e, p=p: e.tensor_scalar(out=av[:], in0=kc[:], scalar1=vec[:, 16 + p:17 + p], scalar2=None, op0=ALU.mult),
                     reads=["kcv", "rwvecs"], writes=["zraw"])
                T.op("pool", lambda e: e.tensor_tensor(out=tmpb[:], in0=av[:], in1=av[:], op=ALU.mult), reads=["zraw"], writes=["rwtmpb"])
                for ti in range(TU // 512):
                    bank, bk = pbank()
                    sl = slice(ti * 512, (ti + 1) * 512)
                    self.mmg(bank[:], bk, [(self.bones_bf[:], tmpb[:, sl])], reads=["bones_bf", "rwtmpb"])
                    T.op("act", lambda e, bank=bank, sl=sl: e.activation(out=tmp[:, sl], in_=bank[:], func=AF.Sqrt), reads=[bk], writes=["rwtmp"])
                T.op("dve", lambda e: e.tensor_scalar(out=tmp[:], in0=tmp[:], scalar1=1e-12, scalar2=None, op0=ALU.max), reads=["rwtmp"], writes=["rwtmp"])
                T.op("dve", lambda e: e.reciprocal(out=tmp[:], in_=tmp[:]), reads=["rwtmp"], writes=["rwtmp"])
                T.op("dve", lambda e, qi=qi: e.tensor_tensor(out=kapT[:, qi, :], in0=av[:], in1=tmp[:], op=ALU.mult), reads=["zraw", "rwtmp"], writes=["kapT"])
                for d in range(2):
                    hs = slice(d * 64, (d + 1) * 64)
                    for ti in range(TU // 512):
                        sl = slice(ti * 512, (ti + 1) * 512)
                        bank, bk = pbank()
                        self.mmg(bank[:], bk, [(R["dup"][hs, p * 128:(p + 1) * 128], twlo[hs, sl])], reads=["dupw", "twlo"])
                        T.op("act", lambda e, bank=bank, sl=sl, d=d, p=p: e.activation(out=tmp[:, sl], in_=bank[:], func=AF.Sigmoid,
                                                                                      bias=vec[:, d * 4 + p:d * 4 + p + 1], scale=1.0),
                             reads=[bk, "rwvecs"], writes=["rwtmp"])
                        bank2, bk2 = pbank()
                        self.mmg(bank2[:], bk2, [(R["iup"][hs, p * 128:(p + 1) * 128], talo[hs, sl])], reads=["iupw", "talo"])
                        T.op("act", lambda e, bank2=bank2, sl=sl, d=d, p=p: e.activation(out=av[:, sl], in_=bank2[:], func=AF.Sigmoid,
                                                                                        bias=vec[:, 8 + d * 4 + p:8 + d * 4 + p + 1], scale=1.0),
                             reads=[bk2, "rwvecs"], writes=["zraw"])
                    T.op("act", lambda e, d=d, qi=qi: e.activation(out=wT[d][:, qi, :], in_=tmp[:], func=AF.Exp, scale=-float(np.exp(-0.5))),
                         reads=["rwtmp"], writes=["wT%d" % d])
                    T.op("dve", lambda e, d=d, qi=qi: e.scalar_tensor_tensor(out=bpT[d][:, qi, :], in0=kapT[:, qi, :], scalar=-1.0, in1=av[:],
                                                                             op0=ALU.mult, op1=ALU.mult),
                         reads=["kapT", "zraw"], writes=["bpT%d" % d])
                    T.op("dve", lambda e, p=p: e.tensor_scalar(out=tmp[:], in0=av[:], scalar1=vec[:, 20 + p:21 + p], scalar2=R["omka"][:, p:p + 1],
                                                               op0=ALU.mult, op1=ALU.add),
                         reads=["zraw", "rwvecs", "omka"], writes=["rwtmp"])
                    T.op("dve", lambda e, d=d, qi=qi: e.tensor_tensor(out=kdT[d][:, qi, :], in0=kc[:], in1=tmp[:], op=ALU.mult),
                         reads=["kcv", "rwtmp"], writes=["kdT%d" % d])
            T.barrier()
        with ExitStack() as ses:
            H = [self.sb("H%d" % d, [128, npair, nseg, 64], F32, es=ses) for d in range(2)]
            Hk = [self.sb("Hk%d" % d, [128, npair, nseg, 64], BF16, es=ses) for d in range(2)]
            Hc = [self.sb("Hc%d" % d, [128, npair, nseg, 64], BF16, es=ses) for d in range(2)]
            Vd = [self.sb("Vd%d" % d, [128, npair, nseg, 64], BF16, es=ses) for d in range(2)]
            KV = [self.sb("KV%d" % d, [128, npair, nseg, 64], F32, es=ses) for d in range(2)]
            stg = self.sb("ststg", [64, 128], F32, es=ses)
            for d in range(2):
                if not sample:
                    T.op("pool", lambda e, d=d: e.memset(H[d][:], 0.0), writes=["H%d" % d])
                else:
                    for qi, p in enumerate(pairs):
                        T.dma("sp", stg[:].rearrange("v (h k) -> v h k", h=2), self.st_in[l, d, 2 * p:2 * p + 2].rearrange("h v k -> v h k"), writes=["ststg"])
                        T.op("pe", lambda e: e.transpose(self.ps[6][:, 0:64], stg[:], self.ident_f[0:64, 0:64]), reads=["ststg", "ident_f"], writes=["ps6"], same=False)
                        T.op("dve", lambda e, d=d, qi=qi: e.tensor_copy(out=H[d][:, qi, 0, :], in_=self.ps[6][:, 0:64]), reads=["ps6"], writes=["H%d" % d])
            NB = 512 // G
            SA = [self.ps[0], self.ps[1]]
            VB = [self.ps[2], self.ps[3]]
            YP = [self.ps[4], self.ps[5]]
            ypv = [YP[d][:, 0:G * NB].rearrange("p (q s n) -> p q s n", q=npair, s=nseg) for d in range(2)]
            sh4 = [128, npair, nseg, 64]

            def col(arr, tt):
                return arr[:].rearrange("p q (s t) -> p q s t", s=nseg)[:, :, :, tt].unsqueeze(3).broadcast_to(sh4)

            idb = R["identh"][:].unsqueeze(1).unsqueeze(1).broadcast_to(sh4)
            fl = lambda a: a[:].rearrange("p q s v -> p (q s v)")
            X = [self.sb("X%d" % d, sh4, F32, es=ses) for d in range(2)]

            def emit_y(i, d):
                tt = i if d == 0 else L - 1 - i
                ypk = "ps%d" % (4 + d)
                cidx = (i % NB) if d == 0 else (NB - 1 - (i % NB))
                for qi in range(npair):
                    for si in range(nseg):
                        for par in range(2):
                            hs = slice(par * 64, (par + 1) * 64)
                            T.op("pe", lambda e, d=d, qi=qi, si=si, hs=hs, tt=tt, cidx=cidx: e.matmul(
                                ypv[d][hs, qi, si, cidx:cidx + 1], Hc[d][hs, qi, si, :], rT[hs, qi, si * L + tt:si * L + tt + 1], start=True, stop=True),
                                reads=["Hc%d" % d, "rT"], writes=[ypk], same=False)
                if (i % NB == NB - 1) or i == L - 1:
                    i0 = (i // NB) * NB
                    nb = i - i0 + 1
                    for si in range(nseg):
                        if d == 0:
                            ta, ca = si * L + i0, 0
                        else:
                            ta, ca = si * L + (L - 1 - i), NB - nb
                        T.op("dve", lambda e, d=d, si=si, ta=ta, ca=ca, nb=nb: e.tensor_tensor(
                            out=yT[:, :, ta:ta + nb], in0=ypv[d][:, :, si, ca:ca + nb], in1=yT[:, :, ta:ta + nb], op=ALU.add),
                            reads=[ypk, "yT"], writes=["yT"])

            for i in range(L):
                for d in range(2):
                    tt = i if d == 0 else L - 1 - i
                    hk, sak, vbk = "H%d" % d, "ps%d" % d, "ps%d" % (2 + d)
                    T.op("pool", lambda e, d=d, tt=tt: e.tensor_tensor(out=Vd[d][:], in0=idb, in1=col(vT, tt), op=ALU.mult),
                         reads=["vT", "identh"], writes=["Vd%d" % d])
                    T.op("pe", lambda e, d=d: e.matmul(VB[d][:, 0:G * 64], self.bones_bf[:], fl(Vd[d]), start=True, stop=True),
                         reads=["Vd%d" % d, "bones_bf"], writes=[vbk], same=False)
                    T.op("pool", lambda e, d=d, tt=tt: e.tensor_tensor(out=Hk[d][:], in0=H[d][:], in1=col(kapT, tt), op=ALU.mult),
                         reads=[hk, "kapT"], writes=["Hk%d" % d])
                    T.op("pe", lambda e, d=d: e.matmul(SA[d][:, 0:G * 64], self.bones_bf[:], fl(Hk[d]), start=True, stop=True),
                         reads=["Hk%d" % d, "bones_bf"], writes=[sak], same=False)
                    if i > 0:
                        emit_y(i - 1, d)
                    T.op("dve", lambda e, d=d, tt=tt: e.tensor_tensor(out=KV[d][:], in0=VB[d][:, 0:G * 64].rearrange("p (q s v) -> p q s v", q=npair, s=nseg),
                                                                      in1=col(kdT[d], tt), op=ALU.mult),
                         reads=[vbk, "kdT%d" % d], writes=["KV%d" % d])
                    T.op("dve", lambda e, d=d, tt=tt: e.tensor_tensor(out=X[d][:], in0=H[d][:], in1=col(wT[d], tt), op=ALU.mult),
                         reads=[hk, "wT%d" % d], writes=["X%d" % d])
                    T.op("dve", lambda e, d=d: e.tensor_tensor(out=X[d][:], in0=X[d][:], in1=KV[d][:], op=ALU.add),
                         reads=["X%d" % d, "KV%d" % d], writes=["X%d" % d])
                    T.op("dve", lambda e, d=d, tt=tt: e.tensor_tensor(out=KV[d][:], in0=SA[d][:, 0:G * 64].rearrange("p (q s v) -> p q s v", q=npair, s=nseg),
                                                                      in1=col(bpT[d], tt), op=ALU.mult),
                         reads=[sak, "bpT%d" % d, "X%d" % d], writes=["KV%d" % d])
                    T.op("dve", lambda e, d=d: e.tensor_tensor(out=H[d][:], in0=X[d][:], in1=KV[d][:], op=ALU.add),
                         reads=["X%d" % d, "KV%d" % d], writes=[hk])
                    T.op("act", lambda e, d=d: e.activation(out=Hc[d][:], in_=H[d][:], func=AF.Copy), reads=[hk], writes=["Hc%d" % d])
            for d in range(2):
                emit_y(L - 1, d)
            if not sample:
                for d in range(2):
                    for qi, p in enumerate(pairs):
                        for si in range(nseg):
                            T.op("pe", lambda e, d=d, qi=qi, si=si: e.transpose(self.ps[6][0:64, 0:128], H[d][:, qi, si, :], self.ident_f[:]),
                                 reads=["H%d" % d, "ident_f"], writes=["ps6"], same=False)
                            T.op("dve", lambda e: e.tensor_copy(out=stg[:], in_=self.ps[6][0:64, 0:128]), reads=["ps6"], writes=["ststg"])
                            T.dma("sp", self.o_nst[si, l, d, 2 * p:2 * p + 2].rearrange("h v k -> v h k"), stg[:].rearrange("v (h k) -> v h k", h=2),
                                  reads=["ststg"], writes=["nst%d_%d_%d_%d" % (l, d, p, si)])
            T.barrier()
        wsc.close()
        with ExitStack() as fes:
            yb = self.sb("ybf", [128, 512], BF16, es=fes)
            ysq = self.sb("ysq", [128, 512], BF16, es=fes)
            mean = self.sb("gmean", [128, 512], F32, es=fes)
            rstd = self.sb("grstd", [128, 512], F32, es=fes)
            tq = self.sb("gtq", [128, 512], F32, es=fes)
            gneps = self.sb("gneps", [128, 1], F32, es=fes)
            T.op("dve", lambda e: e.memset(gneps[:], GN_EPS), writes=["gneps"])
            gT = self.sb("gT", [128, npair, TU], BF16, es=fes)
            cbT = self.sb("cbT", [128, npair, TU], BF16, es=fes)
            zraw = self.sb("zraw", [128, TU], F32, es=fes)
            tmp = self.sb("rwtmp", [128, TU], F32, es=fes)
            tmpb = self.sb("rwtmpb", [128, TU], BF16, es=fes)
            B["zraw"], B["tmp"] = zraw, tmp
            conv_chunk(14, tmpb[:], "rwtmpb", AF.Sigmoid)
            for qi, p in enumerate(pairs):
                for ti in range(TU // 512):
                    bank, bk = pbank()
                    self.mmg(bank[:], bk, [(R["gup"][:, p * 128:(p + 1) * 128], tmpb[:, ti * 512:(ti + 1) * 512])], reads=["gupw", "rwtmpb"])
                    T.op("act", lambda e, bank=bank, qi=qi, ti=ti: e.activation(out=gT[:, qi, ti * 512:(ti + 1) * 512], in_=bank[:], func=AF.Copy),
                         reads=[bk], writes=["gT"])
            for qi, p in enumerate(pairs):
                for d in range(2):
                    T.op("dve", lambda e, d=d, qi=qi, p=p: e.scalar_tensor_tensor(out=tmpb[:], in0=kdT[d][:, qi, :], scalar=vec[:, 24 + p:25 + p],
                                                                                  in1=rT[:, qi, :], op0=ALU.mult, op1=ALU.mult),
                         reads=["kdT%d" % d, "rT", "rwvecs"], writes=["rwtmpb"])
                    for ti in range(TU // 512):
                        sl = slice(ti * 512, (ti + 1) * 512)
                        bank, bk = pbank()
                        self.mmg(bank[:], bk, [(self.bones_bf[:], tmpb[:, sl])], reads=["bones_bf", "rwtmpb"])
                        if d == 0:
                            T.op("act", lambda e, bank=bank, sl=sl, qi=qi: e.activation(out=cbT[:, qi, sl], in_=bank[:], func=AF.Copy),
                                 reads=[bk], writes=["cbT"])
                        else:
                            T.op("dve", lambda e, bank=bank, sl=sl, qi=qi: e.tensor_tensor(out=cbT[:, qi, sl], in0=bank[:], in1=cbT[:, qi, sl], op=ALU.add),
                                 reads=[bk, "cbT"], writes=["cbT"])
            ocT = kapT
            for qi, p in enumerate(pairs):
                for ti in range(TU // 512):
                    sl = slice(ti * 512, (ti + 1) * 512)
                    y = yT[:, qi, sl]
                    T.op("act", lambda e, y=y: e.activation(out=yb[:], in_=y, func=AF.Copy), reads=["yT"], writes=["ybf"])
                    T.op("pool", lambda e, y=y: e.tensor_tensor(out=ysq[:], in0=y, in1=y, op=ALU.mult), reads=["yT"], writes=["ysq"])
                    self.mmg(self.ps[0][:], "ps0", [(self.bones_bf[:], yb[:])], reads=["bones_bf", "ybf"])
                    self.mmg(self.ps[1][:], "ps1", [(self.bones_bf[:], ysq[:])], reads=["bones_bf", "ysq"])
                    T.op("act", lambda e: e.activation(out=mean[:], in_=self.ps[0][:], func=AF.Copy, scale=1.0 / 64), reads=["ps0"], writes=["gmean"])
                    T.op("dve", lambda e: e.tensor_tensor(out=tq[:], in0=mean[:], in1=mean[:], op=ALU.mult), reads=["gmean"], writes=["gtq"])
                    T.op("dve", lambda e: e.scalar_tensor_tensor(out=rstd[:], in0=self.ps[1][:], scalar=1.0 / 64, in1=tq[:], op0=ALU.mult, op1=ALU.subtract),
                         reads=["ps1", "gtq"], writes=["grstd"])
                    T.op("act", lambda e: e.activation(out=rstd[:], in_=rstd[:], func=AF.Sqrt, bias=gneps[:, 0:1], scale=1.0), reads=["grstd", "gneps"], writes=["grstd"])
                    T.op("dve", lambda e: e.reciprocal(out=rstd[:], in_=rstd[:]), reads=["grstd"], writes=["grstd"])
                    T.op("dve", lambda e, y=y: e.tensor_tensor(out=tq[:], in0=y, in1=mean[:], op=ALU.subtract), reads=["yT", "gmean"], writes=["gtq"])
                    T.op("dve", lambda e: e.tensor_tensor(out=tq[:], in0=tq[:], in1=rstd[:], op=ALU.mult), reads=["gtq", "grstd"], writes=["gtq"])
                    T.op("dve", lambda e, p=p: e.tensor_scalar(out=tq[:], in0=tq[:], scalar1=vec[:, 28 + p:29 + p], scalar2=vec[:, 32 + p:33 + p], op0=ALU.mult, op1=ALU.add),
                         reads=["gtq", "rwvecs"], writes=["gtq"])
                    T.op("pool", lambda e, qi=qi, sl=sl: e.tensor_tensor(out=mean[:], in0=cbT[:, qi, sl], in1=vT[:, qi, sl], op=ALU.mult),
                         reads=["cbT", "vT"], writes=["gmean"])
                    T.op("dve", lambda e: e.tensor_tensor(out=tq[:], in0=tq[:], in1=mean[:], op=ALU.add), reads=["gtq", "gmean"], writes=["gtq"])
                    T.op("dve", lambda e, qi=qi, sl=sl: e.tensor_tensor(out=ocT[:, qi, sl], in0=tq[:], in1=gT[:, qi, sl], op=ALU.mult),
                         reads=["gtq", "gT"], writes=["kapT"])
            self._merge_branch(l, 2, [(lambda tt0, n, qi=qi: ocT[:, qi, tt0 - t00:tt0 - t00 + n]) for qi in range(npair)], "kapT",
                               pairs[0] * 128, t00, TU, first=False)
            T.barrier()


for _f in (_rw_declare, _rw_layer_consts, _rwkv, _rw_unit):
    setattr(Builder, _f.__name__, _f)
_old_declare3 = Builder._declare


def _declare3(self):
    _old_declare3(self)
    self._rw_declare()


Builder._declare = _declare3


def _prep_inputs3(inputs):
    maps = _prep_inputs2(inputs)
    shared = {}
    for k in ("w_shift", "decay_w0", "iclr_a0", "gate_up", "k_k", "k_a", "gn_g", "gn_b"):
        shared[k] = np.ascontiguousarray(inputs[k], dtype=np.float32)
    shared["r_k"] = np.ascontiguousarray(np.asarray(inputs["r_k"], np.float32).reshape(DEPTH, 512))
    shared["decay_up"] = np.ascontiguousarray(np.asarray(inputs["decay_up"], np.float32).reshape(DEPTH, 128, 512))
    shared["iclr_up"] = np.ascontiguousarray(np.asarray(inputs["iclr_up"], np.float32).reshape(DEPTH, 128, 512))
    idh = np.zeros((128, 64), np.float32)
    idh[np.arange(128), np.arange(128) % 64] = 1.0
    shared["c_identh"] = idh
    for core, m in enumerate(maps):
        m.update(shared)
        m["state_rwkv"] = np.ascontiguousarray(inputs["state_rwkv"][core], dtype=np.float32)
    return maps


def kernel(**inputs):
    b = Builder()
    nc = b.build()
    maps = _prep_inputs3(inputs)
    maps = [{k: v for k, v in m.items() if k in b.dram_in} for m in maps]
    res = run_bass_kernel_spmd(nc, maps, core_ids=list(range(8)))
    outs = res.results
    f32 = np.float32
    y_p = np.stack([outs[c]["y_tok"][:512].reshape(2, 256, D) for c in range(8)], 0).reshape(16, 256, D).astype(f32)
    y_s = np.stack([outs[c]["y_tok"][512:] for c in range(8)], 0).astype(f32)
    nak = np.concatenate([outs[c]["o_nak"] for c in range(8)], 0).reshape(16, DEPTH, 256, 2, 64).astype(f32)
    nav = np.concatenate([outs[c]["o_nav"] for c in range(8)], 0).reshape(16, DEPTH, 256, 2, 64).astype(f32)
    nbk = np.concatenate([outs[c]["o_nbk"] for c in range(8)], 0).reshape(16, DEPTH, 256, 8, 64).astype(f32)
    nbv = np.concatenate([outs[c]["o_nbv"] for c in range(8)], 0).reshape(16, DEPTH, 256, 8, 64).astype(f32)
    nst = np.concatenate([outs[c]["o_nst"] for c in range(8)], 0).reshape(16, DEPTH, 2, 8, 64, 64).astype(f32)
    return (y_p, y_s, nak, nav, nbk, nbv, nst)
```

```python
import numpy as np
from contextlib import ExitStack
import concourse.bass as bass
import concourse.mybir as mybir
from concourse.bass_utils import run_bass_kernel_spmd

F32 = mybir.dt.float32
BF16 = mybir.dt.bfloat16
AF = mybir.ActivationFunctionType
ALU = mybir.AluOpType
AX = mybir.AxisListType

D = 1024
NCH = 8
DEPTH = 2
DFF = 2816
NJ = 22
NTOK = 1536
TILES = [(0, 512), (512, 512), (1024, 512)]
ALPHA = (2 * DEPTH) ** 0.25
LN_EPS = 1e-5
HD = 64
SCALE = HD ** -0.5
IN_COLS = 7296
A_OFF = 0
B_OFF = 768
C_OFF = 2304
G_OFF = 4224
GN_EPS = 64e-5


class Trk:
    SEM_LIMIT = 60000

    def __init__(self, nc, es):
        self.nc = nc
        self.es = es
        self.eng = {}
        for name, obj in (("pe", nc.tensor), ("act", nc.scalar), ("dve", nc.vector),
                          ("pool", nc.gpsimd), ("sp", nc.sync)):
            sem = es.enter_context(nc.semaphore("sem_" + name))
            self.eng[name] = dict(obj=obj, sem=sem, cnt=0, waited={}, id=name, name=name, epoch=0, total=0)
        self.dsems = [es.enter_context(nc.semaphore("dsem%d" % i)) for i in range(40)]
        self.dcnt = [0] * len(self.dsems)
        self.drr = 0
        self.last_w = {}
        self.readers = {}
        self.n_wait = 0

    def _wait(self, e, ev):
        sem, val, sid = ev
        if e["waited"].get(sid, 0) < val:
            e["obj"].wait_ge(sem, val)
            e["waited"][sid] = val
            self.n_wait += 1

    def _deps(self, e, reads, writes, same=True):
        evs = []
        for r in reads:
            if r in self.last_w:
                evs.append(self.last_w[r])
        for w in writes:
            if w in self.last_w:
                evs.append(self.last_w[w])
            evs.extend(self.readers.get(w, ()))
        for ev in evs:
            if (not same) and ev[2].split("_")[0] == e["name"]:
                continue
            self._wait(e, ev)

    def _commit(self, ev, reads, writes):
        for w in writes:
            self.last_w[w] = ev
            self.readers[w] = []
        for r in reads:
            if r in writes:
                continue
            lst = self.readers.setdefault(r, [])
            lst[:] = [x for x in lst if x[2] != ev[2]]
            lst.append(ev)

    def op(self, ename, fn, reads=(), writes=(), same=True):
        e = self.eng[ename]
        if ename != "pe":
            psr = [r for r in reads if r.startswith("ps") and r[2:].isdigit() and r not in writes]
            if psr:
                writes = list(writes) + psr
        self._deps(e, reads, writes, same)
        if e["cnt"] >= self.SEM_LIMIT:
            e["epoch"] += 1
            e["sem"] = self.es.enter_context(self.nc.semaphore("sem_%s_%d" % (e["name"], e["epoch"])))
            e["id"] = "%s_%d" % (e["name"], e["epoch"])
            e["cnt"] = 0
        inst = fn(e["obj"])
        e["cnt"] += 1
        e["total"] += 1
        inst.then_inc(e["sem"], 1)
        ev = (e["sem"], e["cnt"], e["id"])
        self._commit(ev, reads, writes)
        return ev

    def dma(self, qname, out, in_, reads=(), writes=()):
        e = self.eng[qname]
        self._deps(e, reads, writes, True)
        i = self.drr
        self.drr = (self.drr + 1) % len(self.dsems)
        outs = out if isinstance(out, (list, tuple)) else [out]
        ins = in_ if isinstance(in_, (list, tuple)) else [in_]
        for o_, i_ in zip(outs, ins):
            inst = e["obj"].dma_start(out=o_, in_=i_)
            self.dcnt[i] += 16
            inst.then_inc(self.dsems[i], 16)
        ev = (self.dsems[i], self.dcnt[i], "d%d" % i)
        self._commit(ev, reads, writes)
        return ev

    def barrier(self):
        evs = [(e["sem"], e["cnt"], e["id"]) for e in self.eng.values() if e["cnt"] > 0]
        evs += [(self.dsems[i], self.dcnt[i], "d%d" % i) for i in range(len(self.dsems)) if self.dcnt[i] > 0]
        for e in self.eng.values():
            for ev in evs:
                self._wait(e, ev)

    def wait_all(self, ename):
        e = self.eng[ename]
        for k, ev in list(self.last_w.items()):
            self._wait(e, ev)
        for k, lst in list(self.readers.items()):
            for ev in lst:
                self._wait(e, ev)


def host_consts():
    c = {}
    c["ident"] = np.eye(128, dtype=np.float32)
    bo = np.zeros((128, 128), np.float32)
    bo[:64, :64] = 1.0
    bo[64:, 64:] = 1.0
    c["blockones"] = bo
    c["ones"] = np.ones((128, 128), np.float32)
    return c


class Builder:
    def __init__(self, dbg=None, nlayers=DEPTH, stop_after=None):
        self.dbg = dbg or []
        self.nlayers = nlayers
        self.stop_after = stop_after
        self.nc = bass.Bass("TRN2", target_bir_lowering=False)
        self.es = ExitStack()
        self.dram_in = {}
        self.dram_out = {}

    def din(self, name, shape, dt=F32):
        t = self.nc.dram_tensor(name, list(shape), dt, kind="ExternalInput").ap()
        self.dram_in[name] = t
        return t

    def dout(self, name, shape, dt=F32):
        t = self.nc.dram_tensor(name, list(shape), dt, kind="ExternalOutput").ap()
        self.dram_out[name] = t
        return t

    def sb(self, name, shape, dt=F32, es=None):
        self._uid = getattr(self, "_uid", 0) + 1
        return (es or self.es).enter_context(self.nc.sbuf_tensor("%s_%d" % (name, self._uid), list(shape), dt))

    def build(self):
        nc, es = self.nc, self.es
        with es:
            self.T = Trk(nc, es)
            self._declare()
            self._globals()
            for l in range(self.nlayers):
                self._layer(l)
                if self.stop_after is not None and self.stop_after[0] == l:
                    break
            self._finish()
        return nc

    def _declare(self):
        L = DEPTH
        self.x_in = self.din("x_tok", [NTOK, D])
        self.cvec = self.din("cvec", [2, D])
        self.w_ada = self.din("w_ada", [L, D, 9 * D])
        self.b_ada = self.din("b_ada", [L, 9 * D])
        self.ffn_w_in = [self.din("ffn1_w_in", [L, D, 2 * DFF]), self.din("ffn2_w_in", [L, D, 2 * DFF])]
        self.ffn_w_out = [self.din("ffn1_w_out", [L, DFF, D]), self.din("ffn2_w_out", [L, DFF, D])]
        self.ln_g = self.din("ln_g", [L, 3, D])
        self.ln_b = self.din("ln_b", [L, 3, D])
        self.c_ones = self.din("c_ones", [128, 128])
        self.c_ident = self.din("c_ident", [128, 128])
        self.c_blockones = self.din("c_blockones", [128, 128])
        self.y_out = self.dout("y_tok", [NTOK, D])
        for name, shape in self.dbg:
            self.dout(name, shape)

    def _globals(self):
        nc, T = self.nc, self.T
        self.xT = self.sb("xT", [128, NCH, NTOK], F32)
        self.hT = self.sb("hT", [128, NCH, NTOK], BF16)
        self.ps = [self.es.enter_context(nc.psum_tensor("ps%d" % i, [128, 512], F32)) for i in range(8)]
        self.NSLOT = 3
        self.wring = self.sb("wring", [128, self.NSLOT, 4096], BF16)
        self.wslot = 0
        self.ones_bf = self.sb("ones_bf", [128, 128], BF16)
        self.ident_f = self.sb("ident_f", [128, 128], F32)
        self.bones_bf = self.sb("bones_bf", [128, 128], BF16)
        self.scT = self.sb("scT", [128, NCH, 2], BF16)
        self.cT = self.sb("cT", [128, NCH, 2], F32)
        self.modT = self.sb("modT", [128, 2, 9, NCH], F32)
        self.badaT = self.sb("badaT", [128, 9, NCH], F32)
        self.lngT = self.sb("lngT", [128, 3, NCH], F32)
        self.lnbT = self.sb("lnbT", [128, 3, NCH], F32)
        self.gsT = self.sb("gsT", [128, 2, NCH], F32)
        self.ghT = self.sb("ghT", [128, 2, NCH], F32)
        self.bhT = self.sb("bhT", [128, 2, NCH], F32)
        self.s1T = self.sb("s1T", [128, 2, NCH], F32)
        self.eps_t = self.sb("eps_t", [128, 1], F32)

        T.dma("pool", self.ones_bf[:], self.c_ones, writes=["ones_bf"])
        T.dma("pool", self.bones_bf[:], self.c_blockones, writes=["bones_bf"])
        T.dma("sp", self.ident_f[:], self.c_ident, writes=["ident_f"])
        T.op("dve", lambda e: e.memset(self.eps_t[:], LN_EPS / (ALPHA * ALPHA)), writes=["eps_t"])
        self._load_x()
        self.vecstage = self.sb("vecstage", [72, 128], F32)
        self._load_vecT(self.cvec.rearrange("w (c p) -> (w c) p", p=128), 16, "cT_raw")
        T.op("dve", lambda e: e.tensor_copy(out=self.cT[:], in_=self.ps[7][:, 0:16].rearrange("p (w c) -> p c w", w=2)),
             reads=["ps7"], writes=["cT"])
        T.op("act", lambda e: e.activation(out=self.scT[:], in_=self.cT[:], func=AF.Silu),
             reads=["cT"], writes=["scT"])

    def _load_x(self):
        nc, T = self.nc, self.T
        with ExitStack() as les:
            xs = [self.sb("xstage%d" % i, [128, D], F32, es=les) for i in range(2)]
            for tt in range(NTOK // 128):
                s = xs[tt % 2]
                key = "xstage%d" % (tt % 2)
                T.dma("sp", s[:], self.x_in[tt * 128:(tt + 1) * 128, :], writes=[key])
                for half in range(2):
                    bank = self.ps[(tt * 2 + half) % 4]
                    bkey = "ps%d" % ((tt * 2 + half) % 4)
                    for q in range(4):
                        c = half * 4 + q
                        T.op("pe", lambda e, c=c, q=q, bank=bank, s=s: e.transpose(
                            bank[:, q * 128:(q + 1) * 128], s[:, c * 128:(c + 1) * 128], self.ident_f[:]),
                            reads=[key, "ident_f"], writes=[bkey] if q in (0, 3) else [], same=False)
                    T.op("dve" if half == 0 else "act",
                         (lambda e, bank=bank, half=half, tt=tt: e.tensor_copy(
                             out=self.xT[:, half * 4:(half + 1) * 4, tt * 128:(tt + 1) * 128],
                             in_=bank[:].rearrange("p (q t) -> p q t", q=4))) if half == 0 else
                         (lambda e, bank=bank, half=half, tt=tt: e.activation(
                             out=self.xT[:, half * 4:(half + 1) * 4, tt * 128:(tt + 1) * 128],
                             in_=bank[:].rearrange("p (q t) -> p q t", q=4), func=AF.Copy)),
                         reads=[bkey], writes=["xT"])
            self.T.barrier()


    def _load_vecT(self, src_rows, nrows, tag):
        T = self.T
        T.dma("sp", self.vecstage[0:nrows, :], src_rows, writes=["vecstage"])
        T.op("pe", lambda e: e.transpose(self.ps[7][:, 0:nrows], self.vecstage[0:nrows, :], self.ident_f[0:nrows, 0:nrows]),
             reads=["vecstage", "ident_f"], writes=["ps7"], same=False)

    def wload(self, view_fn, src_ap, split=None):
        s = self.wslot
        self.wslot = (self.wslot + 1) % self.NSLOT
        key = "wring%d" % s
        dst = view_fn(self.wring[:, s, :])
        if split:
            self.T.dma("pool", [dst[:, :, a, :] for a in range(split)], [src_ap[:, :, a, :] for a in range(split)], writes=[key])
        else:
            self.T.dma("pool", dst, src_ap, writes=[key])
        return dst, key


    def mmg(self, out_ap, okey, terms, reads):
        T = self.T
        n = len(terms)
        for i, (lt, rh) in enumerate(terms):
            T.op("pe", lambda e, lt=lt, rh=rh, i=i: e.matmul(out_ap, lt, rh, start=(i == 0), stop=(i == n - 1)),
                 reads=reads, writes=[okey] if (i == 0 or i == n - 1) else [], same=False)

    def _layer(self, l):
        self._mods(l)
        self._ffn(l, 0)
        if self.stop_after == (l, 0):
            return
        if hasattr(self, "_mixer"):
            self._mixer(l)
            if self.stop_after == (l, 1):
                return
        self._ffn(l, 1)

    def _mods(self, l):
        nc, T = self.nc, self.T
        self._load_vecT(self.b_ada[l].rearrange("(ic p) -> ic p", p=128), 72, "bada")
        T.op("dve", lambda e: e.tensor_copy(out=self.badaT[:].rearrange("p i c -> p (i c)"), in_=self.ps[7][:, 0:72]),
             reads=["ps7"], writes=["badaT"])
        self._load_vecT(self.ln_g[l].rearrange("s (c p) -> (s c) p", p=128), 24, "lng")
        T.op("dve", lambda e: e.tensor_copy(out=self.lngT[:].rearrange("p s c -> p (s c)"), in_=self.ps[7][:, 0:24]),
             reads=["ps7"], writes=["lngT"])
        self._load_vecT(self.ln_b[l].rearrange("s (c p) -> (s c) p", p=128), 24, "lnb")
        T.op("dve", lambda e: e.tensor_copy(out=self.lnbT[:].rearrange("p s c -> p (s c)"), in_=self.ps[7][:, 0:24]),
             reads=["ps7"], writes=["lnbT"])
        wv = self.w_ada[l].rearrange("(kc p) n -> p kc n", p=128)
        for piece in range(18):
            dst, key = self.wload(lambda s: s.rearrange("p (kc n) -> p kc n", kc=NCH), wv[:, :, piece * 512:(piece + 1) * 512])
            bank = self.ps[piece % 2]
            bkey = "ps%d" % (piece % 2)
            for q in range(4):
                for kc in range(NCH):
                    T.op("pe", lambda e, q=q, kc=kc, bank=bank, dst=dst: e.matmul(
                        bank[:, q * 2:q * 2 + 2], dst[:, kc, q * 128:(q + 1) * 128], self.scT[:, kc, :],
                        start=(kc == 0), stop=(kc == NCH - 1)),
                        reads=[key, "scT"], writes=[bkey] if ((q == 0 and kc == 0) or (q == 3 and kc == NCH - 1)) else [], same=False)
            i, c0 = divmod(piece * 4, NCH)
            T.op("dve", lambda e, bank=bank, i=i, c0=c0: e.tensor_tensor(
                out=self.modT[:, :, i, c0:c0 + 4],
                in0=bank[:, 0:8].rearrange("p (q w) -> p w q", w=2),
                in1=self.badaT[:, i, c0:c0 + 4].unsqueeze(1).broadcast_to([128, 2, 4]), op=ALU.add),
                reads=[bkey, "badaT"], writes=["modT"])
        T.op("dve", lambda e: e.tensor_scalar_add(out=self.s1T[:], in0=self.modT[:, :, 1, :], scalar1=1.0),
             reads=["modT"], writes=["s1T"])
        self._modulate_all(self.s1T, lambda w: self.modT[:, w, 0, :])

    def _modulate_all(self, scaleT, shift_fn):
        T = self.T
        for c in range(NCH):
            for (w, t0, n) in ((0, 0, 512), (1, 512, 1024)):
                T.op("act", lambda e, c=c, w=w, t0=t0, n=n: e.activation(
                    out=self.hT[:, c, t0:t0 + n], in_=self.xT[:, c, t0:t0 + n], func=AF.Identity,
                    scale=scaleT[:, w, c:c + 1], bias=shift_fn(w)[:, c:c + 1]),
                    reads=["xT", "modT", "s1T", "ghT", "bhT"], writes=["hT"])

    def _sub_scalars(self, l, sub, gate_i, gate_mul, nxt):
        T = self.T
        T.op("dve", lambda e: e.tensor_scalar_mul(out=self.gsT[:], in0=self.modT[:, :, gate_i, :], scalar1=gate_mul / ALPHA),
             reads=["modT"], writes=["gsT"])
        if nxt is not None:
            sh_i, sc_i = nxt
            T.op("dve", lambda e: e.tensor_scalar_add(out=self.ghT[:], in0=self.modT[:, :, sc_i, :], scalar1=1.0),
                 reads=["modT"], writes=["ghT"])
            T.op("dve", lambda e: e.tensor_tensor(out=self.bhT[:], in0=self.ghT[:],
                                                  in1=self.lnbT[:, sub, :].unsqueeze(1).broadcast_to([128, 2, NCH]), op=ALU.mult),
                 reads=["ghT", "lnbT"], writes=["bhT"])
            T.op("dve", lambda e: e.tensor_tensor(out=self.bhT[:], in0=self.bhT[:], in1=self.modT[:, :, sh_i, :], op=ALU.add),
                 reads=["modT"], writes=["bhT"])
            T.op("dve", lambda e: e.tensor_tensor(out=self.ghT[:], in0=self.ghT[:],
                                                  in1=self.lngT[:, sub, :].unsqueeze(1).broadcast_to([128, 2, NCH]), op=ALU.mult),
                 reads=["lngT"], writes=["ghT"])

    def _ln_tile(self, l, sub, ti, has_next, les):
        T = self.T
        t0, n = TILES[ti]
        w = 0 if ti == 0 else 1
        mean, rstd, tmp = self.ln_mean, self.ln_rstd, self.ln_tmp
        T.op("act", lambda e: e.activation(out=mean[:], in_=self.ps[6][:], func=AF.Copy, scale=1.0 / D),
             reads=["ps6"], writes=["ln_mean"])
        T.op("dve", lambda e: e.tensor_tensor(out=tmp[:], in0=mean[:], in1=mean[:], op=ALU.mult),
             reads=["ln_mean"], writes=["ln_tmp"])
        T.op("dve", lambda e: e.scalar_tensor_tensor(out=rstd[:], in0=self.ps[7][:], scalar=1.0 / D, in1=tmp[:],
                                                     op0=ALU.mult, op1=ALU.subtract),
             reads=["ps7", "ln_tmp"], writes=["ln_rstd"])
        T.op("act", lambda e: e.activation(out=rstd[:], in_=rstd[:], func=AF.Sqrt, bias=self.eps_t[:, 0:1], scale=1.0),
             reads=["ln_rstd", "eps_t"], writes=["ln_rstd"])
        T.op("dve", lambda e: e.reciprocal(out=rstd[:], in_=rstd[:]), reads=["ln_rstd"], writes=["ln_rstd"])
        for c in range(NCH):
            tb = self.ln_t[c % 2]
            tk = "ln_t%d" % (c % 2)
            T.op("dve", lambda e, c=c, tb=tb: e.tensor_tensor(out=tb[:], in0=self.xT[:, c, t0:t0 + n], in1=mean[:], op=ALU.subtract),
                 reads=["xT", "ln_mean"], writes=[tk])
            T.op("pool", lambda e, tb=tb: e.tensor_tensor(out=tb[:], in0=tb[:], in1=rstd[:], op=ALU.mult),
                 reads=["ln_rstd", tk], writes=[tk])
            T.op("act", lambda e, c=c, tb=tb: e.activation(out=self.xT[:, c, t0:t0 + n], in_=tb[:], func=AF.Identity,
                                                           scale=self.lngT[:, sub, c:c + 1], bias=self.lnbT[:, sub, c:c + 1]),
                 reads=[tk, "lngT", "lnbT"], writes=["xT"])
            if has_next:
                T.op("dve", lambda e, c=c, tb=tb: e.tensor_scalar(out=self.hT[:, c, t0:t0 + n], in0=tb[:],
                                                                  scalar1=self.ghT[:, w, c:c + 1], scalar2=self.bhT[:, w, c:c + 1],
                                                                  op0=ALU.mult, op1=ALU.add),
                     reads=[tk, "ghT", "bhT"], writes=["hT"])

    def _ffn(self, l, which):
        nc, T = self.nc, self.T
        sub = 0 if which == 0 else 2
        gate_i = 2 if which == 0 else 8
        nxt = (3, 4) if which == 0 else None
        has_next = which == 0
        self._sub_scalars(l, sub, gate_i, 0.5, nxt)
        w_in = self.ffn_w_in[which][l].rearrange("(kc p) (ag n) -> p kc ag n", p=128, ag=2)
        w_out = self.ffn_w_out[which][l].rearrange("(j p) n -> p j n", p=128)
        with ExitStack() as les:
            uT = self.sb("uT", [128, NJ, NTOK], BF16, es=les)
            sl = [self.sb("silu%d" % i, [128, 512], F32, es=les) for i in range(2)]
            self._ln_alloc(les)
            cnt = 0
            for jp in range(NJ // 2):
                dst, key = self.wload(lambda s_: s_.rearrange("p (ag kc n) -> p kc ag n", kc=NCH, ag=2),
                                      w_in[:, :, :, jp * 256:(jp + 1) * 256], split=2)
                for jj in range(2):
                    j = jp * 2 + jj
                    for ti, (t0, n) in enumerate(TILES):
                        ba, bg = self.ps[(cnt % 2) * 2], self.ps[(cnt % 2) * 2 + 1]
                        ka, kg = "ps%d" % ((cnt % 2) * 2), "ps%d" % ((cnt % 2) * 2 + 1)
                        for (bank, bk, ag) in ((ba, ka, 0), (bg, kg, 1)):
                            self.mmg(bank[:, 0:n], bk,
                                     [(dst[:, kc, ag, jj * 128:(jj + 1) * 128], self.hT[:, kc, t0:t0 + n]) for kc in range(NCH)],
                                     reads=[key, "hT"])
                        sb_ = sl[cnt % 2]
                        sk = "silu%d" % (cnt % 2)
                        T.op("act", lambda e, ba=ba, sb_=sb_, n=n: e.activation(out=sb_[:, 0:n], in_=ba[:, 0:n], func=AF.Silu),
                             reads=[ka], writes=[sk])
                        T.op("dve", lambda e, bg=bg, sb_=sb_, j=j, t0=t0, n=n: e.tensor_tensor(
                            out=uT[:, j, t0:t0 + n], in0=bg[:, 0:n], in1=sb_[:, 0:n], op=ALU.mult),
                            reads=[kg, sk], writes=["uT"])
                        cnt += 1
            for mp in range(4):
                pcs = []
                for jh in range(2):
                    pcs.append(self.wload(lambda s_: s_[:, 0:11 * 256].rearrange("p (j n) -> p j n", j=11),
                                          w_out[:, jh * 11:(jh + 1) * 11, mp * 256:(mp + 1) * 256]))
                for ti, (t0, n) in enumerate(TILES):
                    for mm_ in range(2):
                        c = mp * 2 + mm_
                        bi = (ti * 2 + mm_) % 6
                        bank, bk = self.ps[bi], "ps%d" % bi
                        self.mmg(bank[:, 0:n], bk,
                                 [(pcs[j // 11][0][:, j % 11, mm_ * 128:(mm_ + 1) * 128], uT[:, j, t0:t0 + n]) for j in range(NJ)],
                                 reads=[pcs[0][1], pcs[1][1], "uT"])
                        w = 0 if ti == 0 else 1
                        T.op("dve", lambda e, bank=bank, c=c, t0=t0, n=n, w=w: e.scalar_tensor_tensor(
                            out=self.xT[:, c, t0:t0 + n], in0=bank[:, 0:n], scalar=self.gsT[:, w, c:c + 1],
                            in1=self.xT[:, c, t0:t0 + n], op0=ALU.mult, op1=ALU.add),
                            reads=[bk, "gsT", "xT"], writes=["xT"])
            self._ln_all(l, sub, has_next)
            T.barrier()

    def _ln_alloc(self, les):
        self.ln_zb = [self.sb("ln_zb%d" % i, [128, 512], BF16, es=les) for i in range(2)]
        self.ln_zq = [self.sb("ln_zq%d" % i, [128, 512], BF16, es=les) for i in range(2)]
        self.ln_t = [self.sb("ln_t%d" % i, [128, 512], F32, es=les) for i in range(2)]
        self.ln_mean = self.sb("ln_mean", [128, 512], F32, es=les)
        self.ln_rstd = self.sb("ln_rstd", [128, 512], F32, es=les)
        self.ln_tmp = self.sb("ln_tmp", [128, 512], F32, es=les)

    def _ln_all(self, l, sub, has_next):
        T = self.T
        for ti, (t0, n) in enumerate(TILES):
            for c in range(NCH):
                zb = self.ln_zb[c % 2]
                zq = self.ln_zq[c % 2]
                kb, kq = "ln_zb%d" % (c % 2), "ln_zq%d" % (c % 2)
                T.op("act", lambda e, c=c, zb=zb: e.activation(out=zb[:], in_=self.xT[:, c, t0:t0 + n], func=AF.Copy),
                     reads=["xT"], writes=[kb])
                T.op("pool", lambda e, c=c, zq=zq: e.tensor_tensor(out=zq[:], in0=self.xT[:, c, t0:t0 + n],
                                                                   in1=self.xT[:, c, t0:t0 + n], op=ALU.mult),
                     reads=["xT"], writes=[kq])
                T.op("pe", lambda e, c=c, zb=zb: e.matmul(self.ps[6][:], self.ones_bf[:], zb[:], start=(c == 0), stop=(c == NCH - 1)),
                     reads=[kb, "ones_bf"], writes=["ps6"], same=False)
                T.op("pe", lambda e, c=c, zq=zq: e.matmul(self.ps[7][:], self.ones_bf[:], zq[:], start=(c == 0), stop=(c == NCH - 1)),
                     reads=[kq, "ones_bf"], writes=["ps7"], same=False)
            self._ln_tile(l, sub, ti, has_next, None)

    def _finish(self):
        nc, T = self.nc, self.T
        with ExitStack() as les:
            ys = [self.sb("ystage%d" % i, [128, D], F32, es=les) for i in range(2)]
            import os
            for tt in range(int(os.environ.get("DBG_NT", NTOK // 128))):
                s = ys[tt % 2]
                key = "ystage%d" % (tt % 2)
                for half in range(2):
                    bi = (tt * 2 + half) % 4
                    bank, bkey = self.ps[bi], "ps%d" % bi
                    for q in range(0 if os.environ.get("DBG_NOTR") else 4):
                        c = half * 4 + q
                        T.op("pe", lambda e, c=c, q=q, bank=bank, tt=tt: e.transpose(
                            bank[:, q * 128:(q + 1) * 128], self.xT[:, c, tt * 128:(tt + 1) * 128], self.ident_f[:]),
                            reads=["xT", "ident_f"], writes=[bkey] if q in (0, 3) else [], same=False)
                    if half == 0 or os.environ.get("DBG_NOACT"):
                        T.op("dve", lambda e, bank=bank, s=s, half=half: e.tensor_copy(out=s[:, half * 512:(half + 1) * 512], in_=bank[:]),
                             reads=[bkey], writes=[key if half == 0 else key + "h"])
                    else:
                        T.op("act", lambda e, bank=bank, s=s: e.activation(out=s[:, 512:1024], in_=bank[:], func=AF.Copy),
                             reads=[bkey], writes=[key + "h"])
                T.dma("sp", self.y_out[tt * 128:(tt + 1) * 128, :], s[:], reads=[key, key + "h"], writes=["y_out%d" % tt])
            T.wait_all("sp")


def _prep_inputs(inputs):
    consts = host_consts()
    shared = {}
    for k in ("w_ada", "b_ada", "ffn1_w_in", "ffn1_w_out", "ffn2_w_in", "ffn2_w_out", "ln_g", "ln_b"):
        shared[k] = np.ascontiguousarray(inputs[k], dtype=np.float32)
    shared["c_ones"] = consts["ones"]
    shared["c_ident"] = consts["ident"]
    shared["c_blockones"] = consts["blockones"]
    maps = []
    for core in range(8):
        m = dict(shared)
        xp = inputs["x_prompt"][2 * core:2 * core + 2].reshape(512, D)
        xs = inputs["x_sample"][core]
        m["x_tok"] = np.ascontiguousarray(np.concatenate([xp, xs], axis=0), dtype=np.float32)
        m["cvec"] = np.ascontiguousarray(np.stack([inputs["c_ctx"], inputs["c"][core]], axis=0), dtype=np.float32)
        maps.append(m)
    return maps


PROMPT_SEQS = [(0, 256), (256, 256)]
SAMPLE = (512, 1024)
NA_R = {0: (0, 5), 1: (0, 7), 2: (0, 9), 3: (0, 11), 4: (5, 15), 5: (7, 15), 6: (9, 15), 7: (11, 15)}
NVA, NVB = 12, 11


def _na_tables_idx():
    flat = np.zeros((128, NVA + NVB, 64), np.int64)
    mask = np.zeros((128, NVA + NVB, 64), np.float32)
    qc = np.arange(64)
    cs = np.clip(qc - 8, 0, 48)
    for half in range(2):
        for kc in range(64):
            p = half * 64 + kc
            colv = (kc >= cs) & (kc < cs + 16)
            dc = np.clip(kc - qc + 15, 0, 30)
            for v in range(NVA + NVB):
                idx = (v - 6) if v < NVA else (v - NVA - 3)
                dr = half - idx + 7
                rowv = True
                if v < NVA and idx == 5 and half == 0:
                    rowv = False
                if v >= NVA and idx == -3 and half == 1:
                    rowv = False
                drc = min(max(dr, 0), 14)
                flat[p, v] = drc * 31 + dc
                mask[p, v] = (colv & rowv).astype(np.float32)
    return flat, mask


def _rope_tables():
    t = np.arange(1024)
    n_freq = 16
    inv = 10000.0 ** (-np.arange(n_freq, dtype=np.float32) / n_freq)
    rows = (t // 64).astype(np.float32)
    cols = (t % 64).astype(np.float32)
    ang = np.concatenate([rows[:, None] * inv, cols[:, None] * inv], axis=-1)
    cos32, sin32 = np.cos(ang).astype(np.float32), np.sin(ang).astype(np.float32)
    cosT = np.zeros((128, 1024), np.float32)
    sinT = np.zeros((128, 1024), np.float32)
    for p in range(128):
        d = p % 64
        cosT[p] = cos32[:, d % 32]
        sinT[p] = sin32[:, d % 32] * (-1.0 if d < 32 else 1.0)
    return cosT, sinT


def _mixer_consts():
    c = {}
    a = np.arange(128)
    c["c_mlow"] = (a[None, :] <= a[:, None]).astype(np.float32)
    c["c_mup"] = (a[:, None] <= a[None, :]).astype(np.float32)
    pm = np.zeros((128, 128), np.float32)
    for m in range(128):
        k = (m & 64) | ((m + 32) & 63)
        pm[k, m] = 1.0
    c["c_swap"] = pm
    c["c_ropecos"], c["c_ropesin"] = _rope_tables()
    return c


def _mx_declare(self):
    L = DEPTH
    self.w_in = self.din("w_in", [L, D, IN_COLS])
    self.proj = [self.din("proj_a", [L, 512, D]), self.din("proj_b", [L, 512, D]), self.din("proj_c", [L, 512, D])]
    self.w_out = self.din("w_out", [L, D, D])
    self.sink = self.din("attn_sink", [L, 8])
    self.cak = self.din("cache_attn_k", [L, 256, 128])
    self.cav = self.din("cache_attn_v", [L, 256, 128])
    self.cbk = self.din("cache_na_k", [L, 256, 512])
    self.cbv = self.din("cache_na_v", [L, 256, 512])
    self.natab = self.din("na_tab", [L, 8, 128, (NVA + NVB) * 64])
    for nm in ("c_mlow", "c_mup", "c_swap"):
        setattr(self, nm, self.din(nm, [128, 128]))
    self.c_ropecos = self.din("c_ropecos", [128, 1024])
    self.c_ropesin = self.din("c_ropesin", [128, 1024])
    self.o_nak = self.dout("o_nak", [2, L, 256, 128])
    self.o_nav = self.dout("o_nav", [2, L, 256, 128])
    self.o_nbk = self.dout("o_nbk", [2, L, 256, 512])
    self.o_nbv = self.dout("o_nbv", [2, L, 256, 512])


def _mx_globals(self):
    T = self.T
    self.mlow = self.sb("mlow", [128, 128], BF16)
    self.mup = self.sb("mup", [128, 128], BF16)
    self.swapm = self.sb("swapm", [128, 128], BF16)
    T.dma("pool", self.mlow[:], self.c_mlow, writes=["mlow"])
    T.dma("pool", self.mup[:], self.c_mup, writes=["mup"])
    T.dma("pool", self.swapm[:], self.c_swap, writes=["swapm"])


def _mixer(self, l):
    T = self.T
    self._sub_scalars(l, 1, 5, 1.0, (6, 7))
    with ExitStack() as mes:
        self.mergedT = self.sb("mergedT", [128, NCH, NTOK], BF16, es=mes)
        self.sgb = [self.sb("sgb%d" % i, [128, 512], F32, es=mes) for i in range(2)]
        self.mtmp = [self.sb("mtmp%d" % i, [128, 512], F32, es=mes) for i in range(2)]
        self._attn(l, 0)
        self._attn(l, 1)
        if hasattr(self, "_rwkv"):
            self._rwkv(l)
        self._mix_out(l)
        T.barrier()


def _merge_branch(self, l, g, o_chunks, okey, prow0, t0, ntok, first):
    T = self.T
    nk = len(o_chunks)
    wg = self.w_in[l].rearrange("(kc p) n -> p kc n", p=128)
    wp = self.proj[g][l].rearrange("(kc p) n -> p kc n", p=128)
    k0 = prow0 // 128
    tiles = [(t0 + a, min(512, ntok - a)) for a in range(0, ntok, 512)]
    cnt = getattr(self, "_mb_cnt", 0)
    for m in range(NCH):
        s = self.wslot
        self.wslot = (self.wslot + 1) % self.NSLOT
        key = "wring%d" % s
        gv = self.wring[:, s, 0:1024].rearrange("p (kc n) -> p kc n", kc=NCH)
        pv = self.wring[:, s, 1024:1024 + nk * 128].rearrange("p (kc n) -> p kc n", kc=nk)
        gc0 = G_OFF + g * 1024 + m * 128
        T.dma("pool", [gv, pv], [wg[:, :, gc0:gc0 + 128], wp[:, k0:k0 + nk, m * 128:(m + 1) * 128]], writes=[key])
        for (tt0, n) in tiles:
            ba, bp = self.ps[(cnt % 2) * 2], self.ps[(cnt % 2) * 2 + 1]
            ka, kp = "ps%d" % ((cnt % 2) * 2), "ps%d" % ((cnt % 2) * 2 + 1)
            self.mmg(ba[:, 0:n], ka, [(gv[:, kc, :], self.hT[:, kc, tt0:tt0 + n]) for kc in range(NCH)], reads=[key, "hT"])
            self.mmg(bp[:, 0:n], kp, [(pv[:, kc, :], o_chunks[kc](tt0, n)) for kc in range(nk)], reads=[key, okey])
            sg = self.sgb[cnt % 2]
            sk = "sgb%d" % (cnt % 2)
            T.op("act", lambda e, ba=ba, sg=sg, n=n: e.activation(out=sg[:, 0:n], in_=ba[:, 0:n], func=AF.Sigmoid),
                 reads=[ka], writes=[sk])
            if first:
                T.op("dve", lambda e, bp=bp, sg=sg, m=m, tt0=tt0, n=n: e.tensor_tensor(
                    out=self.mergedT[:, m, tt0:tt0 + n], in0=bp[:, 0:n], in1=sg[:, 0:n], op=ALU.mult),
                    reads=[kp, sk], writes=["mergedT"])
            else:
                mt = self.mtmp[cnt % 2]
                mk = "mtmp%d" % (cnt % 2)
                T.op("dve", lambda e, bp=bp, sg=sg, mt=mt, n=n: e.tensor_tensor(out=mt[:, 0:n], in0=bp[:, 0:n], in1=sg[:, 0:n], op=ALU.mult),
                     reads=[kp, sk], writes=[mk])
                T.op("pool", lambda e, mt=mt, m=m, tt0=tt0, n=n: e.tensor_tensor(
                    out=self.mergedT[:, m, tt0:tt0 + n], in0=self.mergedT[:, m, tt0:tt0 + n], in1=mt[:, 0:n], op=ALU.add),
                    reads=[mk, "mergedT"], writes=["mergedT"])
            cnt += 1
    self._mb_cnt = cnt


def _mix_out(self, l):
    T = self.T
    wv = self.w_out[l].rearrange("(kc p) n -> p kc n", p=128)
    with ExitStack() as les:
        self._ln_alloc(les)
        cnt = 0
        for piece in range(2):
            dst, key = self.wload(lambda s_: s_.rearrange("p (kc n) -> p kc n", kc=NCH), wv[:, :, piece * 512:(piece + 1) * 512])
            for ti, (t0, n) in enumerate(TILES):
                w = 0 if ti == 0 else 1
                for q in range(4):
                    c = piece * 4 + q
                    bank, bk = self.ps[cnt % 4], "ps%d" % (cnt % 4)
                    self.mmg(bank[:, 0:n], bk, [(dst[:, kc, q * 128:(q + 1) * 128], self.mergedT[:, kc, t0:t0 + n]) for kc in range(NCH)],
                             reads=[key, "mergedT"])
                    T.op("dve", lambda e, bank=bank, c=c, t0=t0, n=n, w=w: e.scalar_tensor_tensor(
                        out=self.xT[:, c, t0:t0 + n], in0=bank[:, 0:n], scalar=self.gsT[:, w, c:c + 1],
                        in1=self.xT[:, c, t0:t0 + n], op0=ALU.mult, op1=ALU.add),
                        reads=[bk, "gsT", "xT"], writes=["xT"])
                    cnt += 1
        self._ln_all(l, 1, True)
        T.barrier()


def _attn_head(self, specs, q_fn, ncols, out_ap, rows, sink_ap, okey):
    T = self.T
    hc = self._ah_cnt
    self._ah_cnt += 1
    O, Ok = self.ps[2 + (hc % 2) * 2], "ps%d" % (2 + (hc % 2) * 2)
    Dn, Dk = self.ps[3 + (hc % 2) * 2], "ps%d" % (3 + (hc % 2) * 2)
    r0, r1 = rows
    n = len(specs)
    pend = None
    for idx in range(n + 1):
        if idx < n:
            sp = specs[idx]
            sc = self._as_cnt
            self._as_cnt += 1
            sbk, sk = self.ps[sc % 2], "ps%d" % (sc % 2)
            w = sp["c1"] - sp["c0"]
            self.mmg(sbk[:, 0:w], sk, [(sp["kT"], q_fn(sp["c0"], sp["c1"]))], reads=sp["keys"])
            pt, pk = self.ptb[sc % 4], "ptb%d" % (sc % 4)
            T.op("act", lambda e, sbk=sbk, pt=pt, w=w: e.activation(out=pt[:, 0:w], in_=sbk[:, 0:w], func=AF.Exp, scale=SCALE),
                 reads=[sk], writes=[pk])
            for (a, b, mk_ap, mkey) in sp.get("masks", ()):
                T.op("dve", lambda e, pt=pt, a=a, b=b, mk_ap=mk_ap: e.tensor_tensor(out=pt[:, a:b], in0=pt[:, a:b], in1=mk_ap, op=ALU.mult),
                     reads=[pk, mkey], writes=[pk])
            cur = (sp, pt, pk, w, idx)
        else:
            cur = None
        if pend is not None:
            sp, pt, pk, w, i = pend
            first, last = (i == 0), (i == n - 1)
            for (bank, bk, lt) in ((O, Ok, sp["v"]), (Dn, Dk, self.ones_bf[:])):
                T.op("pe", lambda e, bank=bank, lt=lt, pt=pt, w=w, sp=sp, first=first, last=last: e.matmul(
                    bank[:, sp["c0"]:sp["c1"]], lt, pt[:, 0:w], start=first, stop=last),
                    reads=[pk, "ones_bf"] + sp["keys"], writes=[bk] if (first or last) else [], same=False)
        pend = cur
    rc, rk = self.rcb[hc % 2], "rcb%d" % (hc % 2)
    import os
    if "dbg_misc" in self.dram_out and os.environ.get("DBG_HEAD") and int(os.environ["DBG_HEAD"]) == hc and not getattr(self, "_dbg_done", False):
        self._dbg_done = True
        T.op("dve", lambda e: e.tensor_copy(out=self.mtmp[0][:, 0:ncols], in_=Dn[:, 0:ncols]), reads=[Dk], writes=["mtmp0"])
        T.dma("sp", self.dram_out["dbg_misc"][:, 1024:1024 + ncols], self.mtmp[0][:, 0:ncols], reads=["mtmp0"], writes=["dbgm2"])
        T.op("dve", lambda e: e.tensor_copy(out=self.mtmp[1][:, 0:ncols], in_=O[:, 0:ncols]), reads=[Ok], writes=["mtmp1"])
        T.dma("sp", self.dram_out["dbg_misc"][:, 1536:1536 + ncols], self.mtmp[1][:, 0:ncols], reads=["mtmp1"], writes=["dbgm3"])
    if sink_ap is not None:
        T.op("dve", lambda e: e.tensor_scalar(out=rc[r0:r1, 0:ncols], in0=Dn[r0:r1, 0:ncols], scalar1=sink_ap, scalar2=None, op0=ALU.add),
             reads=[Dk, "esink"], writes=[rk])
        T.op("dve", lambda e: e.reciprocal(out=rc[r0:r1, 0:ncols], in_=rc[r0:r1, 0:ncols]), reads=[rk], writes=[rk])
    else:
        T.op("dve", lambda e: e.reciprocal(out=rc[r0:r1, 0:ncols], in_=Dn[r0:r1, 0:ncols]), reads=[Dk], writes=[rk])
    T.op("dve", lambda e: e.tensor_tensor(out=out_ap, in0=O[r0:r1, 0:ncols], in1=rc[r0:r1, 0:ncols], op=ALU.mult),
         reads=[Ok, rk], writes=[okey])


for _f in (_mx_declare, _mx_globals, _mixer, _merge_branch, _mix_out, _attn_head):
    setattr(Builder, _f.__name__, _f)


def _attn(self, l, which):
    nc, T = self.nc, self.T
    A = (which == 0)
    nkc = 2 if A else 4
    qoff = A_OFF if A else B_OFF
    wv = self.w_in[l].rearrange("(kc p) n -> p kc n", p=128)
    self._ah_cnt = 0
    self._as_cnt = 0
    with ExitStack() as aes:
        qT = self.sb("qT", [128, 4, NTOK], BF16, es=aes)
        kT = self.sb("kT", [128, nkc, NTOK], BF16, es=aes)
        VW = 256 if A else 512
        vtm = self.sb("vtm", [128, 12, VW], BF16, es=aes)
        ckT = self.sb("ckT", [128, nkc, 256], BF16, es=aes)
        cv = self.sb("cv", [128, 2, VW], BF16, es=aes)
        oT = self.sb("oT", [128, 4, NTOK], BF16, es=aes)
        self.ptb = [self.sb("ptb%d" % i, [128, 512], BF16, es=aes) for i in range(4)]
        self.rcb = [self.sb("rcb%d" % i, [128, 512], F32, es=aes) for i in range(2)]
        ostg = [self.sb("ostg%d" % i, [128, 512], F32, es=aes) for i in range(2)]
        if A:
            rcos = self.sb("rcos", [128, 1024], BF16, es=aes)
            rsin = self.sb("rsin", [128, 1024], BF16, es=aes)
            esink = self.sb("esink", [128, 8], F32, es=aes)
            T.dma("pool", rcos[:], self.c_ropecos, writes=["rcos"])
            T.dma("pool", rsin[:], self.c_ropesin, writes=["rsin"])
            T.dma("sp", esink[:], self.sink[l].partition_broadcast(128), writes=["esink"])
            T.op("act", lambda e: e.activation(out=esink[:], in_=esink[:], func=AF.Exp), reads=["esink"], writes=["esink"])
        else:
            etab = self.sb("etab", [128, (NVA + NVB) * 64], BF16, es=aes)
        pcnt = [0]

        def bankof():
            i = pcnt[0] % 2
            pcnt[0] += 1
            return self.ps[i], "ps%d" % i

        def evac(i, out_ap, in_ap, reads, writes):
            if i % 2 == 0:
                T.op("act", lambda e: e.activation(out=out_ap, in_=in_ap, func=AF.Copy), reads=reads, writes=writes)
            else:
                T.op("dve", lambda e: e.tensor_copy(out=out_ap, in_=in_ap), reads=reads, writes=writes)

        dst, key = self.wload(lambda s_: s_.rearrange("p (kc n) -> p kc n", kc=NCH), wv[:, :, qoff:qoff + 512])
        for c in range(4):
            for (t0, n) in TILES:
                bank, bk = bankof()
                self.mmg(bank[:, 0:n], bk, [(dst[:, kc, c * 128:(c + 1) * 128], self.hT[:, kc, t0:t0 + n]) for kc in range(NCH)], reads=[key, "hT"])
                evac(pcnt[0], qT[:, c, t0:t0 + n], bank[:, 0:n], [bk], ["qT"])
        import os
        stopat = os.environ.get("DBG_STOP", "")
        if stopat == "q":
            T.op("dve", lambda e: e.memset(oT[:], 0.0), writes=["oT"])
            self._merge_branch(l, which, [(lambda t0, n, c=c: oT[:, c, t0:t0 + n]) for c in range(4)], "oT", 0, 0, NTOK, first=A)
            return
        if A:
            s = self.wslot
            self.wslot = (self.wslot + 1) % self.NSLOT
            key = "wring%d" % s
            dst = self.wring[:, s, 0:NCH * 256].rearrange("p (kc kv dup d) -> p kc kv dup d", kc=NCH, kv=2, dup=2)
            T.dma("pool", [dst[:, :, kv, dup, :] for kv in range(2) for dup in range(2)],
                  [wv[:, :, 512 + kv * 64:512 + (kv + 1) * 64] for kv in range(2) for dup in range(2)], writes=[key])
            kw = lambda kc, c: dst[:, kc, c, :, :]
        else:
            dst, key = self.wload(lambda s_: s_.rearrange("p (kc n) -> p kc n", kc=NCH), wv[:, :, B_OFF + 512:B_OFF + 1024])
            kw = lambda kc, c: dst[:, kc, c * 128:(c + 1) * 128]
        for c in range(nkc):
            for (t0, n) in TILES:
                bank, bk = bankof()
                self.mmg(bank[:, 0:n], bk, [(kw(kc, c), self.hT[:, kc, t0:t0 + n]) for kc in range(NCH)], reads=[key, "hT"])
                evac(pcnt[0], kT[:, c, t0:t0 + n], bank[:, 0:n], [bk], ["kT"])
        if stopat == "k":
            T.op("dve", lambda e: e.memset(oT[:], 0.0), writes=["oT"])
            self._merge_branch(l, which, [(lambda t0, n, c=c: oT[:, c, t0:t0 + n]) for c in range(4)], "oT", 0, 0, NTOK, first=A)
            return
        ocnt = [0]

        def out_rows(dram_ap, st, width, bank):
            if os.environ.get("DBG_NOOUTROWS"):
                return
            og, ogk = ostg[ocnt[0] % 2], "ostg%d" % (ocnt[0] % 2)
            ocnt[0] += 1
            T.op("dve", lambda e: e.tensor_copy(out=og[:, 0:width], in_=bank), reads=[bk_cur[0]], writes=[ogk])
            sq, tl = st // 2, (st % 2) * 128
            if os.environ.get("DBG_NOOUTDMA"):
                return
            T.dma("sp", dram_ap[sq, l, tl:tl + 128, :], og[:, 0:width], reads=[ogk], writes=["outrows%d" % ocnt[0]])

        bk_cur = [None]
        if A:
            dst, key = self.wload(lambda s_: s_[:, 0:NCH * 256].rearrange("p (kc n) -> p kc n", kc=NCH), wv[:, :, 512:768])
            for st in range(12):
                bank, bk = bankof()
                bk_cur[0] = bk
                self.mmg(bank[:, 0:256], bk, [(self.hT[:, kc, st * 128:(st + 1) * 128], dst[:, kc, :]) for kc in range(NCH)], reads=[key, "hT"])
                if st < 4:
                    out_rows(self.o_nak, st, 128, bank[:, 0:128])
                    out_rows(self.o_nav, st, 128, bank[:, 128:256])
                for dup in range(2):
                    T.op("act", lambda e, bank=bank, st=st, dup=dup: e.activation(
                        out=vtm[:, st, :].rearrange("p (kv dup d) -> p kv dup d", kv=2, dup=2)[:, :, dup, :],
                        in_=bank[:, 128:256].rearrange("p (kv d) -> p kv d", kv=2), func=AF.Copy),
                        reads=[bk], writes=["vtm"])
        else:
            dstk, keyk = self.wload(lambda s_: s_.rearrange("p (kc n) -> p kc n", kc=NCH), wv[:, :, B_OFF + 512:B_OFF + 1024])
            for st in range(4):
                bank, bk = bankof()
                bk_cur[0] = bk
                self.mmg(bank[:, 0:512], bk, [(self.hT[:, kc, st * 128:(st + 1) * 128], dstk[:, kc, :]) for kc in range(NCH)], reads=[keyk, "hT"])
                out_rows(self.o_nbk, st, 512, bank[:, 0:512])
            dst, key = self.wload(lambda s_: s_.rearrange("p (kc n) -> p kc n", kc=NCH), wv[:, :, B_OFF + 1024:B_OFF + 1536])
            for st in range(12):
                bank, bk = bankof()
                bk_cur[0] = bk
                self.mmg(bank[:, 0:512], bk, [(self.hT[:, kc, st * 128:(st + 1) * 128], dst[:, kc, :]) for kc in range(NCH)], reads=[key, "hT"])
                if st < 4:
                    out_rows(self.o_nbv, st, 512, bank[:, 0:512])
                T.op("act", lambda e, bank=bank, st=st: e.activation(out=vtm[:, st, :], in_=bank[:, 0:512], func=AF.Copy),
                     reads=[bk], writes=["vtm"])
        import os
        if os.environ.get("DBG_NOCACHE"):
            pass
        elif A:
            for ct in range(2):
                T.dma("pool", [cv[:, ct, :].rearrange("p (kv dup d) -> p kv dup d", kv=2, dup=2)[:, :, dup, :] for dup in range(2)],
                      [self.cav[l, ct * 128:(ct + 1) * 128, :].rearrange("t (kv d) -> t kv d", kv=2) for dup in range(2)], writes=["cv"])
                og, ogk = ostg[ct], "ostg%d" % ct
                T.dma("sp", [og[:, 0:256].rearrange("p (kv dup d) -> p kv dup d", kv=2, dup=2)[:, :, dup, :] for dup in range(2)],
                      [self.cak[l, ct * 128:(ct + 1) * 128, :].rearrange("t (kv d) -> t kv d", kv=2) for dup in range(2)], writes=[ogk])
                for kv in range(2):
                    bank, bk = self.ps[6 + kv], "ps%d" % (6 + kv)
                    T.op("pe", lambda e, bank=bank, og=og, kv=kv: e.transpose(bank[:, 0:128], og[:, kv * 128:(kv + 1) * 128], self.ident_f[:]),
                         reads=[ogk, "ident_f"], writes=[bk], same=False)
                    T.op("dve", lambda e, bank=bank, kv=kv, ct=ct: e.tensor_copy(out=ckT[:, kv, ct * 128:(ct + 1) * 128], in_=bank[:, 0:128]),
                         reads=[bk], writes=["ckT"])
        else:
            for ct in range(2):
                T.dma("pool", cv[:, ct, :], self.cbv[l, ct * 128:(ct + 1) * 128, :], writes=["cv"])
                og, ogk = ostg[ct], "ostg%d" % ct
                T.dma("sp", og[:], self.cbk[l, ct * 128:(ct + 1) * 128, :], writes=[ogk])
                bank, bk = self.ps[6 + ct], "ps%d" % (6 + ct)
                for c in range(4):
                    T.op("pe", lambda e, bank=bank, og=og, c=c: e.transpose(bank[:, c * 128:(c + 1) * 128], og[:, c * 128:(c + 1) * 128], self.ident_f[:]),
                         reads=[ogk, "ident_f"], writes=[bk] if c in (0, 3) else [], same=False)
                T.op("dve", lambda e, bank=bank, ct=ct: e.tensor_copy(out=ckT[:, :, ct * 128:(ct + 1) * 128],
                                                                      in_=bank[:].rearrange("p (c t) -> p c t", c=4)),
                     reads=[bk], writes=["ckT"])
        if "dbg_misc" in self.dram_out and l == 0 and A and os.environ.get("DBG_DUMPM"):
            T.op("dve", lambda e: e.tensor_copy(out=self.rcb[0][:, 0:128], in_=self.mlow[:]), reads=["mlow"], writes=["rcb0"])
            T.op("dve", lambda e: e.tensor_copy(out=self.rcb[0][:, 128:256], in_=self.mup[:]), reads=["mup"], writes=["rcb0"])
            T.op("dve", lambda e: e.tensor_copy(out=self.rcb[0][:, 256:384], in_=self.swapm[:]), reads=["swapm"], writes=["rcb0"])
            T.op("dve", lambda e: e.tensor_copy(out=self.rcb[0][:, 384:512], in_=rcos[:, 0:128]), reads=["rcos"], writes=["rcb0"])
            T.dma("sp", self.dram_out["dbg_misc"][:, 0:512], self.rcb[0][:], reads=["rcb0"], writes=["dbgm0"])
        elif "dbg_misc" in self.dram_out and l == 0 and A:
            T.op("dve", lambda e: e.tensor_copy(out=self.rcb[0][:], in_=ckT[:].rearrange("p a b -> p (a b)")), reads=["ckT"], writes=["rcb0"])
            T.dma("sp", self.dram_out["dbg_misc"][:, 0:512], self.rcb[0][:], reads=["rcb0"], writes=["dbgm0"])
            T.op("dve", lambda e: e.tensor_copy(out=self.rcb[1][:], in_=cv[:].rearrange("p a b -> p (a b)")), reads=["cv"], writes=["rcb1"])
            T.dma("sp", self.dram_out["dbg_misc"][:, 512:1024], self.rcb[1][:], reads=["rcb1"], writes=["dbgm1"])
        import os
        if A and not os.environ.get("DBG_NOROPE"):
            rc = 0
            for (arr, akey, nchunk) in ((qT, "qT", 4), (kT, "kT", 2)):
                for c in range(nchunk):
                    for qt in range(2):
                        t0 = 512 + qt * 512
                        bank, bk = self.ps[6 + rc % 2], "ps%d" % (6 + rc % 2)
                        rc += 1
                        x = arr[:, c, t0:t0 + 512]
                        self.mmg(bank[:], bk, [(self.swapm[:], x)], reads=[akey, "swapm"])
                        T.op("dve", lambda e, x=x, qt=qt: e.tensor_tensor(out=self.mtmp[0][:], in0=x, in1=rcos[:, qt * 512:(qt + 1) * 512], op=ALU.mult),
                             reads=[akey, "rcos"], writes=["mtmp0"])
                        T.op("dve", lambda e, bank=bank, qt=qt: e.tensor_tensor(out=self.mtmp[1][:], in0=bank[:], in1=rsin[:, qt * 512:(qt + 1) * 512], op=ALU.mult),
                             reads=[bk, "rsin"], writes=["mtmp1"])
                        T.op("pool", lambda e, x=x: e.tensor_tensor(out=x, in0=self.mtmp[0][:], in1=self.mtmp[1][:], op=ALU.add),
                             reads=["mtmp0", "mtmp1"], writes=[akey])
        import os
        if os.environ.get("DBG_NOHEADS"):
            T.op("dve", lambda e: e.memset(oT[:], 0.0), writes=["oT"])
        for hp in range(0 if os.environ.get("DBG_NOHEADS") else 4):
            for par in range(2):
                h = hp * 2 + par
                if not A:
                    T.dma("pool", etab[:], self.natab[l, h], writes=["etab"])
                    T.op("act", lambda e: e.activation(out=etab[:], in_=etab[:], func=AF.Exp), reads=["etab"], writes=["etab"])
                rows = (par * 64, par * 64 + 64)
                kc_ = (h // 4) if A else hp
                vsl = (lambda a: a[:, kc_ * 128:(kc_ + 1) * 128])
                sink_ap = esink[rows[0]:rows[1], h:h + 1] if A else None
                for sq in range(2):
                    b0 = sq * 256
                    specs = [dict(kT=kT[rows[0]:rows[1], kc_, b0 + kt * 128:b0 + (kt + 1) * 128], v=vsl(vtm[:, sq * 2 + kt, :]),
                                  c0=0, c1=256, keys=["kT", "vtm", "qT"]) for kt in range(2)]
                    self._attn_head(specs, lambda c0, c1, b0=b0: qT[rows[0]:rows[1], hp, b0 + c0:b0 + c1], 256,
                                    oT[rows[0]:rows[1], hp, b0:b0 + 256], rows, sink_ap, "oT")
                for qt in range(2):
                    b0 = 512 + qt * 512
                    specs = [dict(kT=ckT[rows[0]:rows[1], kc_, ct * 128:(ct + 1) * 128], v=vsl(cv[:, ct, :]), c0=0, c1=512,
                                  keys=["ckT", "cv", "qT"]) for ct in range(2)]
                    if A and os.environ.get("DBG_ANOLOCAL"):
                        pass
                    elif A:
                        for j in range(4 * qt - 1, 4 * qt + 5):
                            if j < 0 or j > 7:
                                continue
                            ilo, ihi = max(j - 1, 4 * qt), min(j + 1, 4 * qt + 3)
                            masks = []
                            for i in range(ilo, ihi + 1):
                                a = (i - ilo) * 128
                                if i == j + 1:
                                    masks.append((a, a + 128, self.mlow[:], "mlow"))
                                elif i == j - 1:
                                    masks.append((a, a + 128, self.mup[:], "mup"))
                            specs.append(dict(kT=kT[rows[0]:rows[1], kc_, 512 + j * 128:512 + (j + 1) * 128], v=vsl(vtm[:, 4 + j, :]),
                                              c0=(ilo - 4 * qt) * 128, c1=(ihi - 4 * qt + 1) * 128, masks=masks, keys=["kT", "vtm", "qT"]))
                    else:
                        for j in range(8):
                            ra, rb = NA_R[j]
                            lo, hi = max(ra, 8 * qt), min(rb, 8 * qt + 7)
                            if lo > hi:
                                continue
                            c0, c1 = (lo - 8 * qt) * 64, (hi - 8 * qt + 1) * 64
                            v0 = (lo - 2 * j + 6) if j <= 3 else (NVA + lo - 2 * j + 3)
                            masks = [(0, c1 - c0, etab[:, v0 * 64:v0 * 64 + (c1 - c0)], "etab")]
                            specs.append(dict(kT=kT[rows[0]:rows[1], kc_, 512 + j * 128:512 + (j + 1) * 128], v=vsl(vtm[:, 4 + j, :]),
                                              c0=c0, c1=c1, masks=masks, keys=["kT", "vtm", "qT"]))
                    self._attn_head(specs, lambda c0, c1, b0=b0: qT[rows[0]:rows[1], hp, b0 + c0:b0 + c1], 512,
                                    oT[rows[0]:rows[1], hp, b0:b0 + 512], rows, sink_ap, "oT")
        if "dbg_oT" in self.dram_out and l == 0:
            for c in range(4):
                for (t0, n) in TILES:
                    og, ogk = ostg[c % 2], "ostg%d" % (c % 2)
                    T.op("dve", lambda e, og=og, c=c, t0=t0, n=n: e.tensor_copy(out=og[:, 0:n], in_=oT[:, c, t0:t0 + n]), reads=["oT"], writes=[ogk])
                    T.dma("sp", self.dram_out["dbg_oT"][which, c, :, t0:t0 + n], og[:, 0:n], reads=[ogk], writes=["dbgo%d_%d_%d" % (which, c, t0)])
        self._merge_branch(l, which, [(lambda t0, n, c=c: oT[:, c, t0:t0 + n]) for c in range(4)], "oT", 0, 0, NTOK, first=A)
        T.barrier()


Builder._attn = _attn
_old_declare = Builder._declare
_old_globals = Builder._globals


def _declare2(self):
    _old_declare(self)
    self._mx_declare()


def _globals2(self):
    _old_globals(self)
    self._mx_globals()


Builder._declare = _declare2
Builder._globals = _globals2


def _prep_inputs2(inputs):
    maps = _prep_inputs(inputs)
    mc = _mixer_consts()
    flat, mask = _na_tables_idx()
    flat = np.where(mask > 0, flat, 15 * 31)
    rpb = np.asarray(inputs["na_rpb"], np.float32).reshape(DEPTH, 8, 15 * 31)
    rpb = np.concatenate([rpb, np.full((DEPTH, 8, 1), -1.0e4, np.float32)], axis=-1)
    na_tab = rpb[:, :, flat.reshape(-1)].reshape(DEPTH, 8, 128, (NVA + NVB) * 64)
    shared = dict(mc)
    shared["na_tab"] = np.ascontiguousarray(na_tab)
    for k in ("w_in", "proj_a", "proj_b", "proj_c", "w_out", "attn_sink"):
        shared[k] = np.ascontiguousarray(inputs[k], dtype=np.float32)
    for core, m in enumerate(maps):
        m.update(shared)
        m["cache_attn_k"] = np.ascontiguousarray(inputs["cache_attn_k"][core].reshape(DEPTH, 256, 128))
        m["cache_attn_v"] = np.ascontiguousarray(inputs["cache_attn_v"][core].reshape(DEPTH, 256, 128))
        m["cache_na_k"] = np.ascontiguousarray(inputs["cache_na_k"][core].reshape(DEPTH, 256, 512))
        m["cache_na_v"] = np.ascontiguousarray(inputs["cache_na_v"][core].reshape(DEPTH, 256, 512))
    return maps


def _rw_declare(self):
    L = DEPTH
    self.w_shift = self.din("w_shift", [L, 3, 1920])
    self.decay_w0 = self.din("decay_w0", [L, 2, 512])
    self.decay_up = self.din("decay_up", [L, 128, 512])
    self.iclr_a0 = self.din("iclr_a0", [L, 2, 512])
    self.iclr_up = self.din("iclr_up", [L, 128, 512])
    self.gate_up = self.din("gate_up", [L, 128, 512])
    self.vec512 = {k: self.din(k, [L, 512]) for k in ("k_k", "k_a", "r_k", "gn_g", "gn_b")}
    self.st_in = self.din("state_rwkv", [L, 2, 8, 64, 64])
    self.o_nst = self.dout("o_nst", [2, L, 2, 8, 64, 64])
    self.c_identh = self.din("c_identh", [128, 64])


def _rw_layer_consts(self, l, es):
    T = self.T
    R = {}
    R["shT"] = self.sb("shT", [128, 3, 15], F32, es=es)
    self._load_vecT(self.w_shift[l].rearrange("s (c p) -> (s c) p", p=128), 45, "sh")
    T.op("dve", lambda e: e.tensor_copy(out=R["shT"][:].rearrange("p s c -> p (s c)"), in_=self.ps[7][:, 0:45]), reads=["ps7"], writes=["shT"])
    R["vecs"] = self.sb("rwvecs", [128, 36], F32, es=es)
    srcs = [self.decay_w0[l].rearrange("d (c p) -> (d c) p", p=128), self.iclr_a0[l].rearrange("d (c p) -> (d c) p", p=128)]
    srcs += [self.vec512[k][l].rearrange("(c p) -> c p", p=128) for k in ("k_k", "k_a", "r_k", "gn_g", "gn_b")]
    off = 0
    for sap, nr in zip(srcs, (8, 8, 4, 4, 4, 4, 4)):
        self._load_vecT(sap, nr, "rwv")
        T.op("dve", lambda e, off=off, nr=nr: e.tensor_copy(out=R["vecs"][:, off:off + nr], in_=self.ps[7][:, 0:nr]), reads=["ps7"], writes=["rwvecs"])
        off += nr
    R["omka"] = self.sb("omka", [128, 4], F32, es=es)
    T.op("dve", lambda e: e.tensor_scalar(out=R["omka"][:], in0=R["vecs"][:, 20:24], scalar1=-1.0, scalar2=1.0, op0=ALU.mult, op1=ALU.add),
         reads=["rwvecs"], writes=["omka"])
    R["dup"] = self.sb("dupw", [128, 512], BF16, es=es)
    R["iup"] = self.sb("iupw", [128, 512], BF16, es=es)
    R["gup"] = self.sb("gupw", [128, 512], BF16, es=es)
    T.dma("pool", R["dup"][:], self.decay_up[l], writes=["dupw"])
    T.dma("pool", R["iup"][:], self.iclr_up[l], writes=["iupw"])
    T.dma("pool", R["gup"][:], self.gate_up[l], writes=["gupw"])
    R["identh"] = self.sb("identh", [128, 64], BF16, es=es)
    T.dma("pool", R["identh"][:], self.c_identh, writes=["identh"])
    return R


def _rwkv(self, l):
    T = self.T
    with ExitStack() as res_:
        R = self._rw_layer_consts(l, res_)
        units = [dict(segs=[(0, 256), (256, 256)], pairs=[0, 1, 2, 3], sample=False),
                 dict(segs=[(512, 1024)], pairs=[0, 1], sample=True),
                 dict(segs=[(512, 1024)], pairs=[2, 3], sample=True)]
        import os
        if os.environ.get("DBG_UNITS"):
            units = [units[int(c)] for c in os.environ["DBG_UNITS"]]
        for u in units:
            self._rw_unit(l, R, u)
        T.barrier()


def _rw_unit(self, l, R, u):
    nc, T = self.nc, self.T
    segs, pairs, sample = u["segs"], u["pairs"], u["sample"]
    nseg, npair = len(segs), len(pairs)
    t00 = segs[0][0]
    L = segs[0][1]
    TU = nseg * L
    tiles = [(t00 + a, 512) for a in range(0, TU, 512)]
    G = npair * nseg
    wv = self.w_in[l].rearrange("(kc p) n -> p kc n", p=128)
    vec = R["vecs"]
    with ExitStack() as ues:
        kdT = [self.sb("kdT%d" % d, [128, npair, TU], BF16, es=ues) for d in range(2)]
        kapT = self.sb("kapT", [128, npair, TU], BF16, es=ues)
        rT = self.sb("rT", [128, npair, TU], BF16, es=ues)
        vT = self.sb("vT", [128, npair, TU], BF16, es=ues)
        B = {}
        pc = [0]

        def pbank():
            i = pc[0] % 2
            pc[0] += 1
            return self.ps[i], "ps%d" % i

        def conv_chunk(ci, out_ap, okey, post=None):
            zraw, tmp = B["zraw"], B["tmp"]
            c0 = C_OFF + ci * 128
            s = self.wslot
            self.wslot = (self.wslot + 1) % self.NSLOT
            key = "wring%d" % s
            dst = self.wring[:, s, 0:1024].rearrange("p (kc n) -> p kc n", kc=NCH)
            T.dma("pool", dst, wv[:, :, c0:c0 + 128], writes=[key])
            for ti, (tt0, n) in enumerate(tiles):
                bank, bk = pbank()
                self.mmg(bank[:, 0:n], bk, [(dst[:, kc_, :], self.hT[:, kc_, tt0:tt0 + n]) for kc_ in range(NCH)], reads=[key, "hT"])
                T.op("act", lambda e, bank=bank, ti=ti, n=n: e.activation(out=zraw[:, ti * 512:ti * 512 + n], in_=bank[:, 0:n], func=AF.Copy),
                     reads=[bk], writes=["zraw"])
            sh = R["shT"]
            T.op("dve", lambda e: e.tensor_scalar(out=tmp[:], in0=zraw[:], scalar1=sh[:, 1, ci:ci + 1], scalar2=None, op0=ALU.mult),
                 reads=["zraw", "shT"], writes=["rwtmp"])
            for si in range(nseg):
                a, b = si * L, (si + 1) * L
                T.op("dve", lambda e, a=a, b=b: e.scalar_tensor_tensor(out=tmp[:, a + 1:b], in0=zraw[:, a:b - 1], scalar=sh[:, 0, ci:ci + 1],
                                                                       in1=tmp[:, a + 1:b], op0=ALU.mult, op1=ALU.add),
                     reads=["zraw", "shT", "rwtmp"], writes=["rwtmp"])
                T.op("dve", lambda e, a=a, b=b: e.scalar_tensor_tensor(out=tmp[:, a:b - 1], in0=zraw[:, a + 1:b], scalar=sh[:, 2, ci:ci + 1],
                                                                       in1=tmp[:, a:b - 1], op0=ALU.mult, op1=ALU.add),
                     reads=["zraw", "shT", "rwtmp"], writes=["rwtmp"])
            T.op("act", lambda e: e.activation(out=out_ap, in_=tmp[:], func=(post or AF.Copy)), reads=["rwtmp"], writes=[okey])
        yT = self.sb("yT", [128, npair, TU], F32, es=ues)
        T.op("pool", lambda e: e.memset(yT[:], 0.0), writes=["yT"])
        wsc = ExitStack()
        wT = [self.sb("wT%d" % d, [128, npair, TU], F32, es=wsc) for d in range(2)]
        bpT = [self.sb("bpT%d" % d, [128, npair, TU], BF16, es=wsc) for d in range(2)]
        with ExitStack() as pes:
            zraw = self.sb("zraw", [128, TU], F32, es=pes)
            kc = self.sb("kcv", [128, TU], F32, es=pes)
            av = zraw
            tmp = self.sb("rwtmp", [128, TU], F32, es=pes)
            tmpb = self.sb("rwtmpb", [128, TU], BF16, es=pes)
            twlo = self.sb("twlo", [128, TU], BF16, es=pes)
            talo = self.sb("talo", [128, TU], BF16, es=pes)
            B["zraw"], B["tmp"] = zraw, tmp
            conv_chunk(12, twlo[:], "twlo", AF.Tanh)
            conv_chunk(13, talo[:], "talo")
            for qi, p in enumerate(pairs):
                conv_chunk(p, rT[:, qi, :], "rT")
                conv_chunk(8 + p, vT[:, qi, :], "vT")
                conv_chunk(4 + p, kc[:], "kcv")
                T.op("dve", lambda e, p=p: e.tensor_scalar(out=av[:], in0=kc[:], scalar1=vec[:, 16 + p:17 + p], scalar2=None, op0=ALU.mult),
                     reads=["kcv", "rwvecs"], writes=["zraw"])
                T.op("pool", lambda e: e.tensor_tensor(out=tmpb[:], in0=av[:], in1=av[:], op=ALU.mult), reads=["zraw"], writes=["rwtmpb"])
                for ti in range(TU // 512):
                    bank, bk = pbank()
                    sl = slice(ti * 512, (ti + 1) * 512)
                    self.mmg(bank[:], bk, [(self.bones_bf[:], tmpb[:, sl])], reads=["bones_bf", "rwtmpb"])
                    T.op("act", lambda e, bank=bank, sl=sl: e.activation(out=tmp[:, sl], in_=bank[:], func=AF.Sqrt), reads=[bk], writes=["rwtmp"])
                T.op("dve", lambda e: e.tensor_scalar(out=tmp[:], in0=tmp[:], scalar1=1e-12, scalar2=None, op0=ALU.max), reads=["rwtmp"], writes=["rwtmp"])
                T.op("dve", lambda e: e.reciprocal(out=tmp[:], in_=tmp[:]), reads=["rwtmp"], writes=["rwtmp"])
                T.op("dve", lambda e, qi=qi: e.tensor_tensor(out=kapT[:, qi, :], in0=av[:], in1=tmp[:], op=ALU.mult), reads=["zraw", "rwtmp"], writes=["kapT"])
                for d in range(2):
                    hs = slice(d * 64, (d + 1) * 64)
                    for ti in range(TU // 512):
                        sl = slice(ti * 512, (ti + 1) * 512)
                        bank, bk = pbank()
                        self.mmg(bank[:], bk, [(R["dup"][hs, p * 128:(p + 1) * 128], twlo[hs, sl])], reads=["dupw", "twlo"])
                        T.op("act", lambda e, bank=bank, sl=sl, d=d, p=p: e.activation(out=tmp[:, sl], in_=bank[:], func=AF.Sigmoid,
                                                                                      bias=vec[:, d * 4 + p:d * 4 + p + 1], scale=1.0),
                             reads=[bk, "rwvecs"], writes=["rwtmp"])
                        bank2, bk2 = pbank()
                        self.mmg(bank2[:], bk2, [(R["iup"][hs, p * 128:(p + 1) * 128], talo[hs, sl])], reads=["iupw", "talo"])
                        T.op("act", lambda e, bank2=bank2, sl=sl, d=d, p=p: e.activation(out=av[:, sl], in_=bank2[:], func=AF.Sigmoid,
                                                                                        bias=vec[:, 8 + d * 4 + p:8 + d * 4 + p + 1], scale=1.0),
                             reads=[bk2, "rwvecs"], writes=["zraw"])
                    T.op("act", lambda e, d=d, qi=qi: e.activation(out=wT[d][:, qi, :], in_=tmp[:], func=AF.Exp, scale=-float(np.exp(-0.5))),
                         reads=["rwtmp"], writes=["wT%d" % d])
                    T.op("dve", lambda e, d=d, qi=qi: e.scalar_tensor_tensor(out=bpT[d][:, qi, :], in0=kapT[:, qi, :], scalar=-1.0, in1=av[:],
                                                                             op0=ALU.mult, op1=ALU.mult),
                         reads=["kapT", "zraw"], writes=["bpT%d" % d])
                    T.op("dve", lambda e, p=p: e.tensor_scalar(out=tmp[:], in0=av[:], scalar1=vec[:, 20 + p:21 + p], scalar2=R["omka"][:, p:p + 1],
                                                               op0=ALU.mult, op1=ALU.add),
                         reads=["zraw", "rwvecs", "omka"], writes=["rwtmp"])
                    T.op("dve", lambda e, d=d, qi=qi: e.tensor_tensor(out=kdT[d][:, qi, :], in0=kc[:], in1=tmp[:], op=ALU.mult),
                         reads=["kcv", "rwtmp"], writes=["kdT%d" % d])
            T.barrier()
        with ExitStack() as ses:
            H = [self.sb("H%d" % d, [128, npair, nseg, 64], F32, es=ses) for d in range(2)]
            Hk = [self.sb("Hk%d" % d, [128, npair, nseg, 64], BF16, es=ses) for d in range(2)]
            Hc = [self.sb("Hc%d" % d, [128, npair, nseg, 64], BF16, es=ses) for d in range(2)]
            Vd = [self.sb("Vd%d" % d, [128, npair, nseg, 64], BF16, es=ses) for d in range(2)]
            KV = [self.sb("KV%d" % d, [128, npair, nseg, 64], F32, es=ses) for d in range(2)]
            stg = self.sb("ststg", [64, 128], F32, es=ses)
            for d in range(2):
                if not sample:
                    T.op("pool", lambda e, d=d: e.memset(H[d][:], 0.0), writes=["H%d" % d])
                else:
                    for qi, p in enumerate(pairs):
                        T.dma("sp", stg[:].rearrange("v (h k) -> v h k", h=2), self.st_in[l, d, 2 * p:2 * p + 2].rearrange("h v k -> v h k"), writes=["ststg"])
                        T.op("pe", lambda e: e.transpose(self.ps[6][:, 0:64], stg[:], self.ident_f[0:64, 0:64]), reads=["ststg", "ident_f"], writes=["ps6"], same=False)
                        T.op("dve", lambda e, d=d, qi=qi: e.tensor_copy(out=H[d][:, qi, 0, :], in_=self.ps[6][:, 0:64]), reads=["ps6"], writes=["H%d" % d])
            NB = 512 // G
            SA = [self.ps[0], self.ps[1]]
            VB = [self.ps[2], self.ps[3]]
            YP = [self.ps[4], self.ps[5]]
            ypv = [YP[d][:, 0:G * NB].rearrange("p (q s n) -> p q s n", q=npair, s=nseg) for d in range(2)]
            sh4 = [128, npair, nseg, 64]

            def col(arr, tt):
                return arr[:].rearrange("p q (s t) -> p q s t", s=nseg)[:, :, :, tt].unsqueeze(3).broadcast_to(sh4)

            idb = R["identh"][:].unsqueeze(1).unsqueeze(1).broadcast_to(sh4)
            fl = lambda a: a[:].rearrange("p q s v -> p (q s v)")
            X = [self.sb("X%d" % d, sh4, F32, es=ses) for d in range(2)]

            def emit_y(i, d):
                tt = i if d == 0 else L - 1 - i
                ypk = "ps%d" % (4 + d)
                cidx = (i % NB) if d == 0 else (NB - 1 - (i % NB))
                for qi in range(npair):
                    for si in range(nseg):
                        for par in range(2):
                            hs = slice(par * 64, (par + 1) * 64)
                            T.op("pe", lambda e, d=d, qi=qi, si=si, hs=hs, tt=tt, cidx=cidx: e.matmul(
                                ypv[d][hs, qi, si, cidx:cidx + 1], Hc[d][hs, qi, si, :], rT[hs, qi, si * L + tt:si * L + tt + 1], start=True, stop=True),
                                reads=["Hc%d" % d, "rT"], writes=[ypk], same=False)
                if (i % NB == NB - 1) or i == L - 1:
                    i0 = (i // NB) * NB
                    nb = i - i0 + 1
                    for si in range(nseg):
                        if d == 0:
                            ta, ca = si * L + i0, 0
                        else:
                            ta, ca = si * L + (L - 1 - i), NB - nb
                        T.op("dve", lambda e, d=d, si=si, ta=ta, ca=ca, nb=nb: e.tensor_tensor(
                            out=yT[:, :, ta:ta + nb], in0=ypv[d][:, :, si, ca:ca + nb], in1=yT[:, :, ta:ta + nb], op=ALU.add),
                            reads=[ypk, "yT"], writes=["yT"])

            for i in range(L):
                for d in range(2):
                    tt = i if d == 0 else L - 1 - i
                    hk, sak, vbk = "H%d" % d, "ps%d" % d, "ps%d" % (2 + d)
                    T.op("pool", lambda e, d=d, tt=tt: e.tensor_tensor(out=Vd[d][:], in0=idb, in1=col(vT, tt), op=ALU.mult),
                         reads=["vT", "identh"], writes=["Vd%d" % d])
                    T.op("pe", lambda e, d=d: e.matmul(VB[d][:, 0:G * 64], self.bones_bf[:], fl(Vd[d]), start=True, stop=True),
                         reads=["Vd%d" % d, "bones_bf"], writes=[vbk], same=False)
                    T.op("pool", lambda e, d=d, tt=tt: e.tensor_tensor(out=Hk[d][:], in0=H[d][:], in1=col(kapT, tt), op=ALU.mult),
                         reads=[hk, "kapT"], writes=["Hk%d" % d])
                    T.op("pe", lambda e, d=d: e.matmul(SA[d][:, 0:G * 64], self.bones_bf[:], fl(Hk[d]), start=True, stop=True),
                         reads=["Hk%d" % d, "bones_bf"], writes=[sak], same=False)
                    if i > 0:
                        emit_y(i - 1, d)
                    T.op("dve", lambda e, d=d, tt=tt: e.tensor_tensor(out=KV[d][:], in0=VB[d][:, 0:G * 64].rearrange("p (q s v) -> p q s v", q=npair, s=nseg),
                                                                      in1=col(kdT[d], tt), op=ALU.mult),
                         reads=[vbk, "kdT%d" % d], writes=["KV%d" % d], same=False)
                    T.op("dve", lambda e, d=d, tt=tt: e.tensor_tensor(out=X[d][:], in0=H[d][:], in1=col(wT[d], tt), op=ALU.mult),
                         reads=[hk, "wT%d" % d], writes=["X%d" % d], same=False)
                    T.op("dve", lambda e, d=d: e.tensor_tensor(out=X[d][:], in0=X[d][:], in1=KV[d][:], op=ALU.add),
                         reads=["X%d" % d, "KV%d" % d], writes=["X%d" % d], same=False)
                    T.op("dve", lambda e, d=d, tt=tt: e.tensor_tensor(out=KV[d][:], in0=SA[d][:, 0:G * 64].rearrange("p (q s v) -> p q s v", q=npair, s=nseg),
                                                                      in1=col(bpT[d], tt), op=ALU.mult),
                         reads=[sak, "bpT%d" % d, "X%d" % d], writes=["KV%d" % d], same=False)
                    T.op("dve", lambda e, d=d: e.tensor_tensor(out=H[d][:], in0=X[d][:], in1=KV[d][:], op=ALU.add),
                         reads=["X%d" % d, "KV%d" % d], writes=[hk], same=False)
                    T.op("act", lambda e, d=d: e.activation(out=Hc[d][:], in_=H[d][:], func=AF.Copy), reads=[hk], writes=["Hc%d" % d])
            for d in range(2):
                emit_y(L - 1, d)
            if not sample:
                for d in range(2):
                    for qi, p in enumerate(pairs):
                        for si in range(nseg):
                            T.op("pe", lambda e, d=d, qi=qi, si=si: e.transpose(self.ps[6][0:64, 0:128], H[d][:, qi, si, :], self.ident_f[:]),
                                 reads=["H%d" % d, "ident_f"], writes=["ps6"], same=False)
                            T.op("dve", lambda e: e.tensor_copy(out=stg[:], in_=self.ps[6][0:64, 0:128]), reads=["ps6"], writes=["ststg"])
                            T.dma("sp", self.o_nst[si, l, d, 2 * p:2 * p + 2].rearrange("h v k -> v h k"), stg[:].rearrange("v (h k) -> v h k", h=2),
                                  reads=["ststg"], writes=["nst%d_%d_%d_%d" % (l, d, p, si)])
            T.barrier()
        wsc.close()
        with ExitStack() as fes:
            yb = self.sb("ybf", [128, 512], BF16, es=fes)
            ysq = self.sb("ysq", [128, 512], BF16, es=fes)
            mean = self.sb("gmean", [128, 512], F32, es=fes)
            rstd = self.sb("grstd", [128, 512], F32, es=fes)
            tq = self.sb("gtq", [128, 512], F32, es=fes)
            gneps = self.sb("gneps", [128, 1], F32, es=fes)
            T.op("dve", lambda e: e.memset(gneps[:], GN_EPS), writes=["gneps"])
            gT = self.sb("gT", [128, npair, TU], BF16, es=fes)
            cbT = self.sb("cbT", [128, npair, TU], BF16, es=fes)
            zraw = self.sb("zraw", [128, TU], F32, es=fes)
            tmp = self.sb("rwtmp", [128, TU], F32, es=fes)
            tmpb = self.sb("rwtmpb", [128, TU], BF16, es=fes)
            B["zraw"], B["tmp"] = zraw, tmp
            conv_chunk(14, tmpb[:], "rwtmpb", AF.Sigmoid)
            for qi, p in enumerate(pairs):
                for ti in range(TU // 512):
                    bank, bk = pbank()
                    self.mmg(bank[:], bk, [(R["gup"][:, p * 128:(p + 1) * 128], tmpb[:, ti * 512:(ti + 1) * 512])], reads=["gupw", "rwtmpb"])
                    T.op("act", lambda e, bank=bank, qi=qi, ti=ti: e.activation(out=gT[:, qi, ti * 512:(ti + 1) * 512], in_=bank[:], func=AF.Copy),
                         reads=[bk], writes=["gT"])
            for qi, p in enumerate(pairs):
                for d in range(2):
                    T.op("dve", lambda e, d=d, qi=qi, p=p: e.scalar_tensor_tensor(out=tmpb[:], in0=kdT[d][:, qi, :], scalar=vec[:, 24 + p:25 + p],
                                                                                  in1=rT[:, qi, :], op0=ALU.mult, op1=ALU.mult),
                         reads=["kdT%d" % d, "rT", "rwvecs"], writes=["rwtmpb"])
                    for ti in range(TU // 512):
                        sl = slice(ti * 512, (ti + 1) * 512)
                        bank, bk = pbank()
                        self.mmg(bank[:], bk, [(self.bones_bf[:], tmpb[:, sl])], reads=["bones_bf", "rwtmpb"])
                        if d == 0:
                            T.op("act", lambda e, bank=bank, sl=sl, qi=qi: e.activation(out=cbT[:, qi, sl], in_=bank[:], func=AF.Copy),
                                 reads=[bk], writes=["cbT"])
                        else:
                            T.op("dve", lambda e, bank=bank, sl=sl, qi=qi: e.tensor_tensor(out=cbT[:, qi, sl], in0=bank[:], in1=cbT[:, qi, sl], op=ALU.add),
                                 reads=[bk, "cbT"], writes=["cbT"])
            ocT = kapT
            for qi, p in enumerate(pairs):
                for ti in range(TU // 512):
                    sl = slice(ti * 512, (ti + 1) * 512)
                    y = yT[:, qi, sl]
                    T.op("act", lambda e, y=y: e.activation(out=yb[:], in_=y, func=AF.Copy), reads=["yT"], writes=["ybf"])
                    T.op("pool", lambda e, y=y: e.tensor_tensor(out=ysq[:], in0=y, in1=y, op=ALU.mult), reads=["yT"], writes=["ysq"])
                    self.mmg(self.ps[0][:], "ps0", [(self.bones_bf[:], yb[:])], reads=["bones_bf", "ybf"])
                    self.mmg(self.ps[1][:], "ps1", [(self.bones_bf[:], ysq[:])], reads=["bones_bf", "ysq"])
                    T.op("act", lambda e: e.activation(out=mean[:], in_=self.ps[0][:], func=AF.Copy, scale=1.0 / 64), reads=["ps0"], writes=["gmean"])
                    T.op("dve", lambda e: e.tensor_tensor(out=tq[:], in0=mean[:], in1=mean[:], op=ALU.mult), reads=["gmean"], writes=["gtq"])
                    T.op("dve", lambda e: e.scalar_tensor_tensor(out=rstd[:], in0=self.ps[1][:], scalar=1.0 / 64, in1=tq[:], op0=ALU.mult, op1=ALU.subtract),
                         reads=["ps1", "gtq"], writes=["grstd"])
                    T.op("act", lambda e: e.activation(out=rstd[:], in_=rstd[:], func=AF.Sqrt, bias=gneps[:, 0:1], scale=1.0), reads=["grstd", "gneps"], writes=["grstd"])
                    T.op("dve", lambda e: e.reciprocal(out=rstd[:], in_=rstd[:]), reads=["grstd"], writes=["grstd"])
                    T.op("dve", lambda e, y=y: e.tensor_tensor(out=tq[:], in0=y, in1=mean[:], op=ALU.subtract), reads=["yT", "gmean"], writes=["gtq"])
                    T.op("dve", lambda e: e.tensor_tensor(out=tq[:], in0=tq[:], in1=rstd[:], op=ALU.mult), reads=["gtq", "grstd"], writes=["gtq"])
                    T.op("dve", lambda e, p=p: e.tensor_scalar(out=tq[:], in0=tq[:], scalar1=vec[:, 28 + p:29 + p], scalar2=vec[:, 32 + p:33 + p], op0=ALU.mult, op1=ALU.add),
                         reads=["gtq", "rwvecs"], writes=["gtq"])
                    T.op("pool", lambda e, qi=qi, sl=sl: e.tensor_tensor(out=mean[:], in0=cbT[:, qi, sl], in1=vT[:, qi, sl], op=ALU.mult),
                         reads=["cbT", "vT"], writes=["gmean"])
                    T.op("dve", lambda e: e.tensor_tensor(out=tq[:], in0=tq[:], in1=mean[:], op=ALU.add), reads=["gtq", "gmean"], writes=["gtq"])
                    T.op("dve", lambda e, qi=qi, sl=sl: e.tensor_tensor(out=ocT[:, qi, sl], in0=tq[:], in1=gT[:, qi, sl], op=ALU.mult),
                         reads=["gtq", "gT"], writes=["kapT"])
            self._merge_branch(l, 2, [(lambda tt0, n, qi=qi: ocT[:, qi, tt0 - t00:tt0 - t00 + n]) for qi in range(npair)], "kapT",
                               pairs[0] * 128, t00, TU, first=False)
            T.barrier()


for _f in (_rw_declare, _rw_layer_consts, _rwkv, _rw_unit):
    setattr(Builder, _f.__name__, _f)
_old_declare3 = Builder._declare


def _declare3(self):
    _old_declare3(self)
    self._rw_declare()


Builder._declare = _declare3


def _prep_inputs3(inputs):
    maps = _prep_inputs2(inputs)
    shared = {}
    for k in ("w_shift", "decay_w0", "iclr_a0", "gate_up", "k_k", "k_a", "gn_g", "gn_b"):
        shared[k] = np.ascontiguousarray(inputs[k], dtype=np.float32)
    shared["r_k"] = np.ascontiguousarray(np.asarray(inputs["r_k"], np.float32).reshape(DEPTH, 512))
    shared["decay_up"] = np.ascontiguousarray(np.asarray(inputs["decay_up"], np.float32).reshape(DEPTH, 128, 512))
    shared["iclr_up"] = np.ascontiguousarray(np.asarray(inputs["iclr_up"], np.float32).reshape(DEPTH, 128, 512))
    idh = np.zeros((128, 64), np.float32)
    idh[np.arange(128), np.arange(128) % 64] = 1.0
    shared["c_identh"] = idh
    for core, m in enumerate(maps):
        m.update(shared)
        m["state_rwkv"] = np.ascontiguousarray(inputs["state_rwkv"][core], dtype=np.float32)
    return maps


def kernel(**inputs):
    b = Builder()
    nc = b.build()
    maps = _prep_inputs3(inputs)
    maps = [{k: v for k, v in m.items() if k in b.dram_in} for m in maps]
    res = run_bass_kernel_spmd(nc, maps, core_ids=list(range(8)))
    outs = res.results
    f32 = np.float32
    y_p = np.stack([outs[c]["y_tok"][:512].reshape(2, 256, D) for c in range(8)], 0).reshape(16, 256, D).astype(f32)
    y_s = np.stack([outs[c]["y_tok"][512:] for c in range(8)], 0).astype(f32)
    nak = np.concatenate([outs[c]["o_nak"] for c in range(8)], 0).reshape(16, DEPTH, 256, 2, 64).astype(f32)
    nav = np.concatenate([outs[c]["o_nav"] for c in range(8)], 0).reshape(16, DEPTH, 256, 2, 64).astype(f32)
    nbk = np.concatenate([outs[c]["o_nbk"] for c in range(8)], 0).reshape(16, DEPTH, 256, 8, 64).astype(f32)
    nbv = np.concatenate([outs[c]["o_nbv"] for c in range(8)], 0).reshape(16, DEPTH, 256, 8, 64).astype(f32)
    nst = np.concatenate([outs[c]["o_nst"] for c in range(8)], 0).reshape(16, DEPTH, 2, 8, 64, 64).astype(f32)
    return (y_p, y_s, nak, nav, nbk, nbv, nst)
```

```python
import numpy as np
from contextlib import ExitStack
import concourse.bass as bass
import concourse.mybir as mybir
from concourse.bass_utils import run_bass_kernel_spmd

F32 = mybir.dt.float32
BF16 = mybir.dt.bfloat16
AF = mybir.ActivationFunctionType
ALU = mybir.AluOpType
AX = mybir.AxisListType

D = 1024
NCH = 8
DEPTH = 2
DFF = 2816
NJ = 22
NTOK = 1536
TILES = [(0, 512), (512, 512), (1024, 512)]
ALPHA = (2 * DEPTH) ** 0.25
LN_EPS = 1e-5
HD = 64
SCALE = HD ** -0.5
IN_COLS = 7296
A_OFF = 0
B_OFF = 768
C_OFF = 2304
G_OFF = 4224
GN_EPS = 64e-5


class Trk:
    SEM_LIMIT = 60000

    def __init__(self, nc, es):
        self.nc = nc
        self.es = es
        self.eng = {}
        for name, obj in (("pe", nc.tensor), ("act", nc.scalar), ("dve", nc.vector),
                          ("pool", nc.gpsimd), ("sp", nc.sync)):
            sem = es.enter_context(nc.semaphore("sem_" + name))
            self.eng[name] = dict(obj=obj, sem=sem, cnt=0, waited={}, id=name, name=name, epoch=0, total=0)
        self.dsems = [es.enter_context(nc.semaphore("dsem%d" % i)) for i in range(40)]
        self.dcnt = [0] * len(self.dsems)
        self.drr = 0
        self.last_w = {}
        self.readers = {}
        self.n_wait = 0

    def _wait(self, e, ev):
        sem, val, sid = ev
        if e["waited"].get(sid, 0) < val:
            e["obj"].wait_ge(sem, val)
            e["waited"][sid] = val
            self.n_wait += 1

    def _deps(self, e, reads, writes, same=True):
        evs = []
        for r in reads:
            if r in self.last_w:
                evs.append(self.last_w[r])
        for w in writes:
            if w in self.last_w:
                evs.append(self.last_w[w])
            evs.extend(self.readers.get(w, ()))
        for ev in evs:
            if (not same) and ev[2].split("_")[0] == e["name"]:
                continue
            self._wait(e, ev)

    def _commit(self, ev, reads, writes):
        for w in writes:
            self.last_w[w] = ev
            self.readers[w] = []
        for r in reads:
            if r in writes:
                continue
            lst = self.readers.setdefault(r, [])
            lst[:] = [x for x in lst if x[2] != ev[2]]
            lst.append(ev)

    def op(self, ename, fn, reads=(), writes=(), same=True):
        e = self.eng[ename]
        if ename != "pe":
            psr = [r for r in reads if r.startswith("ps") and r[2:].isdigit() and r not in writes]
            if psr:
                writes = list(writes) + psr
        self._deps(e, reads, writes, same)
        if e["cnt"] >= self.SEM_LIMIT:
            e["epoch"] += 1
            e["sem"] = self.es.enter_context(self.nc.semaphore("sem_%s_%d" % (e["name"], e["epoch"])))
            e["id"] = "%s_%d" % (e["name"], e["epoch"])
            e["cnt"] = 0
        inst = fn(e["obj"])
        e["cnt"] += 1
        e["total"] += 1
        inst.then_inc(e["sem"], 1)
        ev = (e["sem"], e["cnt"], e["id"])
        self._commit(ev, reads, writes)
        return ev

    def dma(self, qname, out, in_, reads=(), writes=()):
        e = self.eng[qname]
        self._deps(e, reads, writes, True)
        i = self.drr
        self.drr = (self.drr + 1) % len(self.dsems)
        outs = out if isinstance(out, (list, tuple)) else [out]
        ins = in_ if isinstance(in_, (list, tuple)) else [in_]
        for o_, i_ in zip(outs, ins):
            inst = e["obj"].dma_start(out=o_, in_=i_)
            self.dcnt[i] += 16
            inst.then_inc(self.dsems[i], 16)
        ev = (self.dsems[i], self.dcnt[i], "d%d" % i)
        self._commit(ev, reads, writes)
        return ev

    def barrier(self):
        evs = [(e["sem"], e["cnt"], e["id"]) for e in self.eng.values() if e["cnt"] > 0]
        evs += [(self.dsems[i], self.dcnt[i], "d%d" % i) for i in range(len(self.dsems)) if self.dcnt[i] > 0]
        for e in self.eng.values():
            for ev in evs:
                self._wait(e, ev)

    def wait_all(self, ename):
        e = self.eng[ename]
        for k, ev in list(self.last_w.items()):
            self._wait(e, ev)
        for k, lst in list(self.readers.items()):
            for ev in lst:
                self._wait(e, ev)


def host_consts():
    c = {}
    c["ident"] = np.eye(128, dtype=np.float32)
    bo = np.zeros((128, 128), np.float32)
    bo[:64, :64] = 1.0
    bo[64:, 64:] = 1.0
    c["blockones"] = bo
    c["ones"] = np.ones((128, 128), np.float32)
    return c


class Builder:
    def __init__(self, dbg=None, nlayers=DEPTH, stop_after=None):
        self.dbg = dbg or []
        self.nlayers = nlayers
        self.stop_after = stop_after
        self.nc = bass.Bass("TRN2", target_bir_lowering=False)
        self.es = ExitStack()
        self.dram_in = {}
        self.dram_out = {}

    def din(self, name, shape, dt=F32):
        t = self.nc.dram_tensor(name, list(shape), dt, kind="ExternalInput").ap()
        self.dram_in[name] = t
        return t

    def dout(self, name, shape, dt=F32):
        t = self.nc.dram_tensor(name, list(shape), dt, kind="ExternalOutput").ap()
        self.dram_out[name] = t
        return t

    def sb(self, name, shape, dt=F32, es=None):
        self._uid = getattr(self, "_uid", 0) + 1
        return (es or self.es).enter_context(self.nc.sbuf_tensor("%s_%d" % (name, self._uid), list(shape), dt))

    def build(self):
        nc, es = self.nc, self.es
        with es:
            self.T = Trk(nc, es)
            self._declare()
            self._globals()
            for l in range(self.nlayers):
                self._layer(l)
                if self.stop_after is not None and self.stop_after[0] == l:
                    break
            self._finish()
        return nc

    def _declare(self):
        L = DEPTH
        self.x_in = self.din("x_tok", [NTOK, D])
        self.cvec = self.din("cvec", [2, D])
        self.w_ada = self.din("w_ada", [L, D, 9 * D])
        self.b_ada = self.din("b_ada", [L, 9 * D])
        self.ffn_w_in = [self.din("ffn1_w_in", [L, D, 2 * DFF]), self.din("ffn2_w_in", [L, D, 2 * DFF])]
        self.ffn_w_out = [self.din("ffn1_w_out", [L, DFF, D]), self.din("ffn2_w_out", [L, DFF, D])]
        self.ln_g = self.din("ln_g", [L, 3, D])
        self.ln_b = self.din("ln_b", [L, 3, D])
        self.c_ones = self.din("c_ones", [128, 128])
        self.c_ident = self.din("c_ident", [128, 128])
        self.c_blockones = self.din("c_blockones", [128, 128])
        self.y_out = self.dout("y_tok", [NTOK, D])
        for name, shape in self.dbg:
            self.dout(name, shape)

    def _globals(self):
        nc, T = self.nc, self.T
        self.xT = self.sb("xT", [128, NCH, NTOK], F32)
        self.hT = self.sb("hT", [128, NCH, NTOK], BF16)
        self.ps = [self.es.enter_context(nc.psum_tensor("ps%d" % i, [128, 512], F32)) for i in range(8)]
        self.NSLOT = 3
        self.wring = self.sb("wring", [128, self.NSLOT, 4096], BF16)
        self.wslot = 0
        self.ones_bf = self.sb("ones_bf", [128, 128], BF16)
        self.ident_f = self.sb("ident_f", [128, 128], F32)
        self.bones_bf = self.sb("bones_bf", [128, 128], BF16)
        self.scT = self.sb("scT", [128, NCH, 2], BF16)
        self.cT = self.sb("cT", [128, NCH, 2], F32)
        self.modT = self.sb("modT", [128, 2, 9, NCH], F32)
        self.badaT = self.sb("badaT", [128, 9, NCH], F32)
        self.lngT = self.sb("lngT", [128, 3, NCH], F32)
        self.lnbT = self.sb("lnbT", [128, 3, NCH], F32)
        self.gsT = self.sb("gsT", [128, 2, NCH], F32)
        self.ghT = self.sb("ghT", [128, 2, NCH], F32)
        self.bhT = self.sb("bhT", [128, 2, NCH], F32)
        self.s1T = self.sb("s1T", [128, 2, NCH], F32)
        self.eps_t = self.sb("eps_t", [128, 1], F32)

        T.dma("pool", self.ones_bf[:], self.c_ones, writes=["ones_bf"])
        T.dma("pool", self.bones_bf[:], self.c_blockones, writes=["bones_bf"])
        T.dma("sp", self.ident_f[:], self.c_ident, writes=["ident_f"])
        T.op("dve", lambda e: e.memset(self.eps_t[:], LN_EPS / (ALPHA * ALPHA)), writes=["eps_t"])
        self._load_x()
        self.vecstage = self.sb("vecstage", [72, 128], F32)
        self._load_vecT(self.cvec.rearrange("w (c p) -> (w c) p", p=128), 16, "cT_raw")
        T.op("dve", lambda e: e.tensor_copy(out=self.cT[:], in_=self.ps[7][:, 0:16].rearrange("p (w c) -> p c w", w=2)),
             reads=["ps7"], writes=["cT"])
        T.op("act", lambda e: e.activation(out=self.scT[:], in_=self.cT[:], func=AF.Silu),
             reads=["cT"], writes=["scT"])

    def _load_x(self):
        nc, T = self.nc, self.T
        with ExitStack() as les:
            xs = [self.sb("xstage%d" % i, [128, D], F32, es=les) for i in range(2)]
            for tt in range(NTOK // 128):
                s = xs[tt % 2]
                key = "xstage%d" % (tt % 2)
                T.dma("sp", s[:], self.x_in[tt * 128:(tt + 1) * 128, :], writes=[key])
                for half in range(2):
                    bank = self.ps[(tt * 2 + half) % 4]
                    bkey = "ps%d" % ((tt * 2 + half) % 4)
                    for q in range(4):
                        c = half * 4 + q
                        T.op("pe", lambda e, c=c, q=q, bank=bank, s=s: e.transpose(
                            bank[:, q * 128:(q + 1) * 128], s[:, c * 128:(c + 1) * 128], self.ident_f[:]),
                            reads=[key, "ident_f"], writes=[bkey] if q in (0, 3) else [], same=False)
                    T.op("dve" if half == 0 else "act",
                         (lambda e, bank=bank, half=half, tt=tt: e.tensor_copy(
                             out=self.xT[:, half * 4:(half + 1) * 4, tt * 128:(tt + 1) * 128],
                             in_=bank[:].rearrange("p (q t) -> p q t", q=4))) if half == 0 else
                         (lambda e, bank=bank, half=half, tt=tt: e.activation(
                             out=self.xT[:, half * 4:(half + 1) * 4, tt * 128:(tt + 1) * 128],
                             in_=bank[:].rearrange("p (q t) -> p q t", q=4), func=AF.Copy)),
                         reads=[bkey], writes=["xT"])
            self.T.barrier()


    def _load_vecT(self, src_rows, nrows, tag):
        T = self.T
        T.dma("sp", self.vecstage[0:nrows, :], src_rows, writes=["vecstage"])
        T.op("pe", lambda e: e.transpose(self.ps[7][:, 0:nrows], self.vecstage[0:nrows, :], self.ident_f[0:nrows, 0:nrows]),
             reads=["vecstage", "ident_f"], writes=["ps7"], same=False)

    def wload(self, view_fn, src_ap, split=None):
        s = self.wslot
        self.wslot = (self.wslot + 1) % self.NSLOT
        key = "wring%d" % s
        dst = view_fn(self.wring[:, s, :])
        if split:
            self.T.dma("pool", [dst[:, :, a, :] for a in range(split)], [src_ap[:, :, a, :] for a in range(split)], writes=[key])
        else:
            self.T.dma("pool", dst, src_ap, writes=[key])
        return dst, key


    def mmg(self, out_ap, okey, terms, reads):
        T = self.T
        n = len(terms)
        for i, (lt, rh) in enumerate(terms):
            T.op("pe", lambda e, lt=lt, rh=rh, i=i: e.matmul(out_ap, lt, rh, start=(i == 0), stop=(i == n - 1)),
                 reads=reads, writes=[okey] if (i == 0 or i == n - 1) else [], same=False)

    def _layer(self, l):
        self._mods(l)
        self._ffn(l, 0)
        if self.stop_after == (l, 0):
            return
        if hasattr(self, "_mixer"):
            self._mixer(l)
            if self.stop_after == (l, 1):
                return
        self._ffn(l, 1)

    def _mods(self, l):
        nc, T = self.nc, self.T
        self._load_vecT(self.b_ada[l].rearrange("(ic p) -> ic p", p=128), 72, "bada")
        T.op("dve", lambda e: e.tensor_copy(out=self.badaT[:].rearrange("p i c -> p (i c)"), in_=self.ps[7][:, 0:72]),
             reads=["ps7"], writes=["badaT"])
        self._load_vecT(self.ln_g[l].rearrange("s (c p) -> (s c) p", p=128), 24, "lng")
        T.op("dve", lambda e: e.tensor_copy(out=self.lngT[:].rearrange("p s c -> p (s c)"), in_=self.ps[7][:, 0:24]),
             reads=["ps7"], writes=["lngT"])
        self._load_vecT(self.ln_b[l].rearrange("s (c p) -> (s c) p", p=128), 24, "lnb")
        T.op("dve", lambda e: e.tensor_copy(out=self.lnbT[:].rearrange("p s c -> p (s c)"), in_=self.ps[7][:, 0:24]),
             reads=["ps7"], writes=["lnbT"])
        wv = self.w_ada[l].rearrange("(kc p) n -> p kc n", p=128)
        for piece in range(18):
            dst, key = self.wload(lambda s: s.rearrange("p (kc n) -> p kc n", kc=NCH), wv[:, :, piece * 512:(piece + 1) * 512])
            bank = self.ps[piece % 2]
            bkey = "ps%d" % (piece % 2)
            for q in range(4):
                for kc in range(NCH):
                    T.op("pe", lambda e, q=q, kc=kc, bank=bank, dst=dst: e.matmul(
                        bank[:, q * 2:q * 2 + 2], dst[:, kc, q * 128:(q + 1) * 128], self.scT[:, kc, :],
                        start=(kc == 0), stop=(kc == NCH - 1)),
                        reads=[key, "scT"], writes=[bkey] if ((q == 0 and kc == 0) or (q == 3 and kc == NCH - 1)) else [], same=False)
            i, c0 = divmod(piece * 4, NCH)
            T.op("dve", lambda e, bank=bank, i=i, c0=c0: e.tensor_tensor(
                out=self.modT[:, :, i, c0:c0 + 4],
                in0=bank[:, 0:8].rearrange("p (q w) -> p w q", w=2),
                in1=self.badaT[:, i, c0:c0 + 4].unsqueeze(1).broadcast_to([128, 2, 4]), op=ALU.add),
                reads=[bkey, "badaT"], writes=["modT"])
        T.op("dve", lambda e: e.tensor_scalar_add(out=self.s1T[:], in0=self.modT[:, :, 1, :], scalar1=1.0),
             reads=["modT"], writes=["s1T"])
        self._modulate_all(self.s1T, lambda w: self.modT[:, w, 0, :])

    def _modulate_all(self, scaleT, shift_fn):
        T = self.T
        for c in range(NCH):
            for (w, t0, n) in ((0, 0, 512), (1, 512, 1024)):
                T.op("act", lambda e, c=c, w=w, t0=t0, n=n: e.activation(
                    out=self.hT[:, c, t0:t0 + n], in_=self.xT[:, c, t0:t0 + n], func=AF.Identity,
                    scale=scaleT[:, w, c:c + 1], bias=shift_fn(w)[:, c:c + 1]),
                    reads=["xT", "modT", "s1T", "ghT", "bhT"], writes=["hT"])

    def _sub_scalars(self, l, sub, gate_i, gate_mul, nxt):
        T = self.T
        T.op("dve", lambda e: e.tensor_scalar_mul(out=self.gsT[:], in0=self.modT[:, :, gate_i, :], scalar1=gate_mul / ALPHA),
             reads=["modT"], writes=["gsT"])
        if nxt is not None:
            sh_i, sc_i = nxt
            T.op("dve", lambda e: e.tensor_scalar_add(out=self.ghT[:], in0=self.modT[:, :, sc_i, :], scalar1=1.0),
                 reads=["modT"], writes=["ghT"])
            T.op("dve", lambda e: e.tensor_tensor(out=self.bhT[:], in0=self.ghT[:],
                                                  in1=self.lnbT[:, sub, :].unsqueeze(1).broadcast_to([128, 2, NCH]), op=ALU.mult),
                 reads=["ghT", "lnbT"], writes=["bhT"])
            T.op("dve", lambda e: e.tensor_tensor(out=self.bhT[:], in0=self.bhT[:], in1=self.modT[:, :, sh_i, :], op=ALU.add),
                 reads=["modT"], writes=["bhT"])
            T.op("dve", lambda e: e.tensor_tensor(out=self.ghT[:], in0=self.ghT[:],
                                                  in1=self.lngT[:, sub, :].unsqueeze(1).broadcast_to([128, 2, NCH]), op=ALU.mult),
                 reads=["lngT"], writes=["ghT"])

    def _ln_tile(self, l, sub, ti, has_next, les):
        T = self.T
        t0, n = TILES[ti]
        w = 0 if ti == 0 else 1
        mean, rstd, tmp = self.ln_mean, self.ln_rstd, self.ln_tmp
        T.op("act", lambda e: e.activation(out=mean[:], in_=self.ps[6][:], func=AF.Copy, scale=1.0 / D),
             reads=["ps6"], writes=["ln_mean"])
        T.op("dve", lambda e: e.tensor_tensor(out=tmp[:], in0=mean[:], in1=mean[:], op=ALU.mult),
             reads=["ln_mean"], writes=["ln_tmp"])
        T.op("dve", lambda e: e.scalar_tensor_tensor(out=rstd[:], in0=self.ps[7][:], scalar=1.0 / D, in1=tmp[:],
                                                     op0=ALU.mult, op1=ALU.subtract),
             reads=["ps7", "ln_tmp"], writes=["ln_rstd"])
        T.op("act", lambda e: e.activation(out=rstd[:], in_=rstd[:], func=AF.Sqrt, bias=self.eps_t[:, 0:1], scale=1.0),
             reads=["ln_rstd", "eps_t"], writes=["ln_rstd"])
        T.op("dve", lambda e: e.reciprocal(out=rstd[:], in_=rstd[:]), reads=["ln_rstd"], writes=["ln_rstd"])
        for c in range(NCH):
            tb = self.ln_t[c % 2]
            tk = "ln_t%d" % (c % 2)
            T.op("dve", lambda e, c=c, tb=tb: e.tensor_tensor(out=tb[:], in0=self.xT[:, c, t0:t0 + n], in1=mean[:], op=ALU.subtract),
                 reads=["xT", "ln_mean"], writes=[tk])
            T.op("pool", lambda e, tb=tb: e.tensor_tensor(out=tb[:], in0=tb[:], in1=rstd[:], op=ALU.mult),
                 reads=["ln_rstd", tk], writes=[tk])
            T.op("act", lambda e, c=c, tb=tb: e.activation(out=self.xT[:, c, t0:t0 + n], in_=tb[:], func=AF.Identity,
                                                           scale=self.lngT[:, sub, c:c + 1], bias=self.lnbT[:, sub, c:c + 1]),
                 reads=[tk, "lngT", "lnbT"], writes=["xT"])
            if has_next:
                T.op("dve", lambda e, c=c, tb=tb: e.tensor_scalar(out=self.hT[:, c, t0:t0 + n], in0=tb[:],
                                                                  scalar1=self.ghT[:, w, c:c + 1], scalar2=self.bhT[:, w, c:c + 1],
                                                                  op0=ALU.mult, op1=ALU.add),
                     reads=[tk, "ghT", "bhT"], writes=["hT"])

    def _ffn(self, l, which):
        nc, T = self.nc, self.T
        sub = 0 if which == 0 else 2
        gate_i = 2 if which == 0 else 8
        nxt = (3, 4) if which == 0 else None
        has_next = which == 0
        self._sub_scalars(l, sub, gate_i, 0.5, nxt)
        w_in = self.ffn_w_in[which][l].rearrange("(kc p) (ag n) -> p kc ag n", p=128, ag=2)
        w_out = self.ffn_w_out[which][l].rearrange("(j p) n -> p j n", p=128)
        with ExitStack() as les:
            uT = self.sb("uT", [128, NJ, NTOK], BF16, es=les)
            sl = [self.sb("silu%d" % i, [128, 512], F32, es=les) for i in range(2)]
            self._ln_alloc(les)
            cnt = 0
            for jp in range(NJ // 2):
                dst, key = self.wload(lambda s_: s_.rearrange("p (ag kc n) -> p kc ag n", kc=NCH, ag=2),
                                      w_in[:, :, :, jp * 256:(jp + 1) * 256], split=2)
                for jj in range(2):
                    j = jp * 2 + jj
                    for ti, (t0, n) in enumerate(TILES):
                        ba, bg = self.ps[(cnt % 2) * 2], self.ps[(cnt % 2) * 2 + 1]
                        ka, kg = "ps%d" % ((cnt % 2) * 2), "ps%d" % ((cnt % 2) * 2 + 1)
                        for (bank, bk, ag) in ((ba, ka, 0), (bg, kg, 1)):
                            self.mmg(bank[:, 0:n], bk,
                                     [(dst[:, kc, ag, jj * 128:(jj + 1) * 128], self.hT[:, kc, t0:t0 + n]) for kc in range(NCH)],
                                     reads=[key, "hT"])
                        sb_ = sl[cnt % 2]
                        sk = "silu%d" % (cnt % 2)
                        T.op("act", lambda e, ba=ba, sb_=sb_, n=n: e.activation(out=sb_[:, 0:n], in_=ba[:, 0:n], func=AF.Silu),
                             reads=[ka], writes=[sk])
                        T.op("dve", lambda e, bg=bg, sb_=sb_, j=j, t0=t0, n=n: e.tensor_tensor(
                            out=uT[:, j, t0:t0 + n], in0=bg[:, 0:n], in1=sb_[:, 0:n], op=ALU.mult),
                            reads=[kg, sk], writes=["uT"])
                        cnt += 1
            for mp in range(4):
                pcs = []
                for jh in range(2):
                    pcs.append(self.wload(lambda s_: s_[:, 0:11 * 256].rearrange("p (j n) -> p j n", j=11),
                                          w_out[:, jh * 11:(jh + 1) * 11, mp * 256:(mp + 1) * 256]))
                for ti, (t0, n) in enumerate(TILES):
                    for mm_ in range(2):
                        c = mp * 2 + mm_
                        bi = (ti * 2 + mm_) % 6
                        bank, bk = self.ps[bi], "ps%d" % bi
                        self.mmg(bank[:, 0:n], bk,
                                 [(pcs[j // 11][0][:, j % 11, mm_ * 128:(mm_ + 1) * 128], uT[:, j, t0:t0 + n]) for j in range(NJ)],
                                 reads=[pcs[0][1], pcs[1][1], "uT"])
                        w = 0 if ti == 0 else 1
                        T.op("dve", lambda e, bank=bank, c=c, t0=t0, n=n, w=w: e.scalar_tensor_tensor(
                            out=self.xT[:, c, t0:t0 + n], in0=bank[:, 0:n], scalar=self.gsT[:, w, c:c + 1],
                            in1=self.xT[:, c, t0:t0 + n], op0=ALU.mult, op1=ALU.add),
                            reads=[bk, "gsT", "xT"], writes=["xT"])
            self._ln_all(l, sub, has_next)
            T.barrier()

    def _ln_alloc(self, les):
        self.ln_zb = [self.sb("ln_zb%d" % i, [128, 512], BF16, es=les) for i in range(2)]
        self.ln_zq = [self.sb("ln_zq%d" % i, [128, 512], BF16, es=les) for i in range(2)]
        self.ln_t = [self.sb("ln_t%d" % i, [128, 512], F32, es=les) for i in range(2)]
        self.ln_mean = self.sb("ln_mean", [128, 512], F32, es=les)
        self.ln_rstd = self.sb("ln_rstd", [128, 512], F32, es=les)
        self.ln_tmp = self.sb("ln_tmp", [128, 512], F32, es=les)

    def _ln_all(self, l, sub, has_next):
        T = self.T
        for ti, (t0, n) in enumerate(TILES):
            for c in range(NCH):
                zb = self.ln_zb[c % 2]
                zq = self.ln_zq[c % 2]
                kb, kq = "ln_zb%d" % (c % 2), "ln_zq%d" % (c % 2)
                T.op("act", lambda e, c=c, zb=zb: e.activation(out=zb[:], in_=self.xT[:, c, t0:t0 + n], func=AF.Copy),
                     reads=["xT"], writes=[kb])
                T.op("pool", lambda e, c=c, zq=zq: e.tensor_tensor(out=zq[:], in0=self.xT[:, c, t0:t0 + n],
                                                                   in1=self.xT[:, c, t0:t0 + n], op=ALU.mult),
                     reads=["xT"], writes=[kq])
                T.op("pe", lambda e, c=c, zb=zb: e.matmul(self.ps[6][:], self.ones_bf[:], zb[:], start=(c == 0), stop=(c == NCH - 1)),
                     reads=[kb, "ones_bf"], writes=["ps6"], same=False)
                T.op("pe", lambda e, c=c, zq=zq: e.matmul(self.ps[7][:], self.ones_bf[:], zq[:], start=(c == 0), stop=(c == NCH - 1)),
                     reads=[kq, "ones_bf"], writes=["ps7"], same=False)
            self._ln_tile(l, sub, ti, has_next, None)

    def _finish(self):
        nc, T = self.nc, self.T
        with ExitStack() as les:
            ys = [self.sb("ystage%d" % i, [128, D], F32, es=les) for i in range(2)]
            import os
            for tt in range(int(os.environ.get("DBG_NT", NTOK // 128))):
                s = ys[tt % 2]
                key = "ystage%d" % (tt % 2)
                for half in range(2):
                    bi = (tt * 2 + half) % 4
                    bank, bkey = self.ps[bi], "ps%d" % bi
                    for q in range(0 if os.environ.get("DBG_NOTR") else 4):
                        c = half * 4 + q
                        T.op("pe", lambda e, c=c, q=q, bank=bank, tt=tt: e.transpose(
                            bank[:, q * 128:(q + 1) * 128], self.xT[:, c, tt * 128:(tt + 1) * 128], self.ident_f[:]),
                            reads=["xT", "ident_f"], writes=[bkey] if q in (0, 3) else [], same=False)
                    if half == 0 or os.environ.get("DBG_NOACT"):
                        T.op("dve", lambda e, bank=bank, s=s, half=half: e.tensor_copy(out=s[:, half * 512:(half + 1) * 512], in_=bank[:]),
                             reads=[bkey], writes=[key if half == 0 else key + "h"])
                    else:
                        T.op("act", lambda e, bank=bank, s=s: e.activation(out=s[:, 512:1024], in_=bank[:], func=AF.Copy),
                             reads=[bkey], writes=[key + "h"])
                T.dma("sp", self.y_out[tt * 128:(tt + 1) * 128, :], s[:], reads=[key, key + "h"], writes=["y_out%d" % tt])
            T.wait_all("sp")


def _prep_inputs(inputs):
    consts = host_consts()
    shared = {}
    for k in ("w_ada", "b_ada", "ffn1_w_in", "ffn1_w_out", "ffn2_w_in", "ffn2_w_out", "ln_g", "ln_b"):
        shared[k] = np.ascontiguousarray(inputs[k], dtype=np.float32)
    shared["c_ones"] = consts["ones"]
    shared["c_ident"] = consts["ident"]
    shared["c_blockones"] = consts["blockones"]
    maps = []
    for core in range(8):
        m = dict(shared)
        xp = inputs["x_prompt"][2 * core:2 * core + 2].reshape(512, D)
        xs = inputs["x_sample"][core]
        m["x_tok"] = np.ascontiguousarray(np.concatenate([xp, xs], axis=0), dtype=np.float32)
        m["cvec"] = np.ascontiguousarray(np.stack([inputs["c_ctx"], inputs["c"][core]], axis=0), dtype=np.float32)
        maps.append(m)
    return maps


PROMPT_SEQS = [(0, 256), (256, 256)]
SAMPLE = (512, 1024)
NA_R = {0: (0, 5), 1: (0, 7), 2: (0, 9), 3: (0, 11), 4: (5, 15), 5: (7, 15), 6: (9, 15), 7: (11, 15)}
NVA, NVB = 12, 11


def _na_tables_idx():
    flat = np.zeros((128, NVA + NVB, 64), np.int64)
    mask = np.zeros((128, NVA + NVB, 64), np.float32)
    qc = np.arange(64)
    cs = np.clip(qc - 8, 0, 48)
    for half in range(2):
        for kc in range(64):
            p = half * 64 + kc
            colv = (kc >= cs) & (kc < cs + 16)
            dc = np.clip(kc - qc + 15, 0, 30)
            for v in range(NVA + NVB):
                idx = (v - 6) if v < NVA else (v - NVA - 3)
                dr = half - idx + 7
                rowv = True
                if v < NVA and idx == 5 and half == 0:
                    rowv = False
                if v >= NVA and idx == -3 and half == 1:
                    rowv = False
                drc = min(max(dr, 0), 14)
                flat[p, v] = drc * 31 + dc
                mask[p, v] = (colv & rowv).astype(np.float32)
    return flat, mask


def _rope_tables():
    t = np.arange(1024)
    n_freq = 16
    inv = 10000.0 ** (-np.arange(n_freq, dtype=np.float32) / n_freq)
    rows = (t // 64).astype(np.float32)
    cols = (t % 64).astype(np.float32)
    ang = np.concatenate([rows[:, None] * inv, cols[:, None] * inv], axis=-1)
    cos32, sin32 = np.cos(ang).astype(np.float32), np.sin(ang).astype(np.float32)
    cosT = np.zeros((128, 1024), np.float32)
    sinT = np.zeros((128, 1024), np.float32)
    for p in range(128):
        d = p % 64
        cosT[p] = cos32[:, d % 32]
        sinT[p] = sin32[:, d % 32] * (-1.0 if d < 32 else 1.0)
    return cosT, sinT


def _mixer_consts():
    c = {}
    a = np.arange(128)
    c["c_mlow"] = (a[None, :] <= a[:, None]).astype(np.float32)
    c["c_mup"] = (a[:, None] <= a[None, :]).astype(np.float32)
    pm = np.zeros((128, 128), np.float32)
    for m in range(128):
        k = (m & 64) | ((m + 32) & 63)
        pm[k, m] = 1.0
    c["c_swap"] = pm
    c["c_ropecos"], c["c_ropesin"] = _rope_tables()
    return c


def _mx_declare(self):
    L = DEPTH
    self.w_in = self.din("w_in", [L, D, IN_COLS])
    self.proj = [self.din("proj_a", [L, 512, D]), self.din("proj_b", [L, 512, D]), self.din("proj_c", [L, 512, D])]
    self.w_out = self.din("w_out", [L, D, D])
    self.sink = self.din("attn_sink", [L, 8])
    self.cak = self.din("cache_attn_k", [L, 256, 128])
    self.cav = self.din("cache_attn_v", [L, 256, 128])
    self.cbk = self.din("cache_na_k", [L, 256, 512])
    self.cbv = self.din("cache_na_v", [L, 256, 512])
    self.natab = self.din("na_tab", [L, 8, 128, (NVA + NVB) * 64])
    for nm in ("c_mlow", "c_mup", "c_swap"):
        setattr(self, nm, self.din(nm, [128, 128]))
    self.c_ropecos = self.din("c_ropecos", [128, 1024])
    self.c_ropesin = self.din("c_ropesin", [128, 1024])
    self.o_nak = self.dout("o_nak", [2, L, 256, 128])
    self.o_nav = self.dout("o_nav", [2, L, 256, 128])
    self.o_nbk = self.dout("o_nbk", [2, L, 256, 512])
    self.o_nbv = self.dout("o_nbv", [2, L, 256, 512])


def _mx_globals(self):
    T = self.T
    self.mlow = self.sb("mlow", [128, 128], BF16)
    self.mup = self.sb("mup", [128, 128], BF16)
    self.swapm = self.sb("swapm", [128, 128], BF16)
    T.dma("pool", self.mlow[:], self.c_mlow, writes=["mlow"])
    T.dma("pool", self.mup[:], self.c_mup, writes=["mup"])
    T.dma("pool", self.swapm[:], self.c_swap, writes=["swapm"])


def _mixer(self, l):
    T = self.T
    self._sub_scalars(l, 1, 5, 1.0, (6, 7))
    with ExitStack() as mes:
        self.mergedT = self.sb("mergedT", [128, NCH, NTOK], BF16, es=mes)
        self.sgb = [self.sb("sgb%d" % i, [128, 512], F32, es=mes) for i in range(2)]
        self.mtmp = [self.sb("mtmp%d" % i, [128, 512], F32, es=mes) for i in range(2)]
        self._attn(l, 0)
        self._attn(l, 1)
        if hasattr(self, "_rwkv"):
            self._rwkv(l)
        self._mix_out(l)
        T.barrier()


def _merge_branch(self, l, g, o_chunks, okey, prow0, t0, ntok, first):
    T = self.T
    nk = len(o_chunks)
    wg = self.w_in[l].rearrange("(kc p) n -> p kc n", p=128)
    wp = self.proj[g][l].rearrange("(kc p) n -> p kc n", p=128)
    k0 = prow0 // 128
    tiles = [(t0 + a, min(512, ntok - a)) for a in range(0, ntok, 512)]
    cnt = getattr(self, "_mb_cnt", 0)
    for m in range(NCH):
        s = self.wslot
        self.wslot = (self.wslot + 1) % self.NSLOT
        key = "wring%d" % s
        gv = self.wring[:, s, 0:1024].rearrange("p (kc n) -> p kc n", kc=NCH)
        pv = self.wring[:, s, 1024:1024 + nk * 128].rearrange("p (kc n) -> p kc n", kc=nk)
        gc0 = G_OFF + g * 1024 + m * 128
        T.dma("pool", [gv, pv], [wg[:, :, gc0:gc0 + 128], wp[:, k0:k0 + nk, m * 128:(m + 1) * 128]], writes=[key])
        for (tt0, n) in tiles:
            ba, bp = self.ps[(cnt % 2) * 2], self.ps[(cnt % 2) * 2 + 1]
            ka, kp = "ps%d" % ((cnt % 2) * 2), "ps%d" % ((cnt % 2) * 2 + 1)
            self.mmg(ba[:, 0:n], ka, [(gv[:, kc, :], self.hT[:, kc, tt0:tt0 + n]) for kc in range(NCH)], reads=[key, "hT"])
            self.mmg(bp[:, 0:n], kp, [(pv[:, kc, :], o_chunks[kc](tt0, n)) for kc in range(nk)], reads=[key, okey])
            sg = self.sgb[cnt % 2]
            sk = "sgb%d" % (cnt % 2)
            T.op("act", lambda e, ba=ba, sg=sg, n=n: e.activation(out=sg[:, 0:n], in_=ba[:, 0:n], func=AF.Sigmoid),
                 reads=[ka], writes=[sk])
            if first:
                T.op("dve", lambda e, bp=bp, sg=sg, m=m, tt0=tt0, n=n: e.tensor_tensor(
                    out=self.mergedT[:, m, tt0:tt0 + n], in0=bp[:, 0:n], in1=sg[:, 0:n], op=ALU.mult),
                    reads=[kp, sk], writes=["mergedT"])
            else:
                mt = self.mtmp[cnt % 2]
                mk = "mtmp%d" % (cnt % 2)
                T.op("dve", lambda e, bp=bp, sg=sg, mt=mt, n=n: e.tensor_tensor(out=mt[:, 0:n], in0=bp[:, 0:n], in1=sg[:, 0:n], op=ALU.mult),
                     reads=[kp, sk], writes=[mk])
                T.op("pool", lambda e, mt=mt, m=m, tt0=tt0, n=n: e.tensor_tensor(
                    out=self.mergedT[:, m, tt0:tt0 + n], in0=self.mergedT[:, m, tt0:tt0 + n], in1=mt[:, 0:n], op=ALU.add),
                    reads=[mk, "mergedT"], writes=["mergedT"])
            cnt += 1
    self._mb_cnt = cnt


def _mix_out(self, l):
    T = self.T
    wv = self.w_out[l].rearrange("(kc p) n -> p kc n", p=128)
    with ExitStack() as les:
        self._ln_alloc(les)
        cnt = 0
        for piece in range(2):
            dst, key = self.wload(lambda s_: s_.rearrange("p (kc n) -> p kc n", kc=NCH), wv[:, :, piece * 512:(piece + 1) * 512])
            for ti, (t0, n) in enumerate(TILES):
                w = 0 if ti == 0 else 1
                for q in range(4):
                    c = piece * 4 + q
                    bank, bk = self.ps[cnt % 4], "ps%d" % (cnt % 4)
                    self.mmg(bank[:, 0:n], bk, [(dst[:, kc, q * 128:(q + 1) * 128], self.mergedT[:, kc, t0:t0 + n]) for kc in range(NCH)],
                             reads=[key, "mergedT"])
                    T.op("dve", lambda e, bank=bank, c=c, t0=t0, n=n, w=w: e.scalar_tensor_tensor(
                        out=self.xT[:, c, t0:t0 + n], in0=bank[:, 0:n], scalar=self.gsT[:, w, c:c + 1],
                        in1=self.xT[:, c, t0:t0 + n], op0=ALU.mult, op1=ALU.add),
                        reads=[bk, "gsT", "xT"], writes=["xT"])
                    cnt += 1
        self._ln_all(l, 1, True)
        T.barrier()


def _attn_head(self, specs, q_fn, ncols, out_ap, rows, sink_ap, okey):
    T = self.T
    hc = self._ah_cnt
    self._ah_cnt += 1
    O, Ok = self.ps[2 + (hc % 2) * 2], "ps%d" % (2 + (hc % 2) * 2)
    Dn, Dk = self.ps[3 + (hc % 2) * 2], "ps%d" % (3 + (hc % 2) * 2)
    r0, r1 = rows
    n = len(specs)
    pend = None
    for idx in range(n + 1):
        if idx < n:
            sp = specs[idx]
            sc = self._as_cnt
            self._as_cnt += 1
            sbk, sk = self.ps[sc % 2], "ps%d" % (sc % 2)
            w = sp["c1"] - sp["c0"]
            self.mmg(sbk[:, 0:w], sk, [(sp["kT"], q_fn(sp["c0"], sp["c1"]))], reads=sp["keys"])
            pt, pk = self.ptb[sc % 4], "ptb%d" % (sc % 4)
            T.op("act", lambda e, sbk=sbk, pt=pt, w=w: e.activation(out=pt[:, 0:w], in_=sbk[:, 0:w], func=AF.Exp, scale=SCALE),
                 reads=[sk], writes=[pk])
            for (a, b, mk_ap, mkey) in sp.get("masks", ()):
                T.op("dve", lambda e, pt=pt, a=a, b=b, mk_ap=mk_ap: e.tensor_tensor(out=pt[:, a:b], in0=pt[:, a:b], in1=mk_ap, op=ALU.mult),
                     reads=[pk, mkey], writes=[pk])
            cur = (sp, pt, pk, w, idx)
        else:
            cur = None
        if pend is not None:
            sp, pt, pk, w, i = pend
            first, last = (i == 0), (i == n - 1)
            for (bank, bk, lt) in ((O, Ok, sp["v"]), (Dn, Dk, self.ones_bf[:])):
                T.op("pe", lambda e, bank=bank, lt=lt, pt=pt, w=w, sp=sp, first=first, last=last: e.matmul(
                    bank[:, sp["c0"]:sp["c1"]], lt, pt[:, 0:w], start=first, stop=last),
                    reads=[pk, "ones_bf"] + sp["keys"], writes=[bk] if (first or last) else [], same=False)
        pend = cur
    rc, rk = self.rcb[hc % 2], "rcb%d" % (hc % 2)
    import os
    if "dbg_misc" in self.dram_out and os.environ.get("DBG_HEAD") and int(os.environ["DBG_HEAD"]) == hc and not getattr(self, "_dbg_done", False):
        self._dbg_done = True
        T.op("dve", lambda e: e.tensor_copy(out=self.mtmp[0][:, 0:ncols], in_=Dn[:, 0:ncols]), reads=[Dk], writes=["mtmp0"])
        T.dma("sp", self.dram_out["dbg_misc"][:, 1024:1024 + ncols], self.mtmp[0][:, 0:ncols], reads=["mtmp0"], writes=["dbgm2"])
        T.op("dve", lambda e: e.tensor_copy(out=self.mtmp[1][:, 0:ncols], in_=O[:, 0:ncols]), reads=[Ok], writes=["mtmp1"])
        T.dma("sp", self.dram_out["dbg_misc"][:, 1536:1536 + ncols], self.mtmp[1][:, 0:ncols], reads=["mtmp1"], writes=["dbgm3"])
    if sink_ap is not None:
        T.op("dve", lambda e: e.tensor_scalar(out=rc[r0:r1, 0:ncols], in0=Dn[r0:r1, 0:ncols], scalar1=sink_ap, scalar2=None, op0=ALU.add),
             reads=[Dk, "esink"], writes=[rk])
        T.op("dve", lambda e: e.reciprocal(out=rc[r0:r1, 0:ncols], in_=rc[r0:r1, 0:ncols]), reads=[rk], writes=[rk])
    else:
        T.op("dve", lambda e: e.reciprocal(out=rc[r0:r1, 0:ncols], in_=Dn[r0:r1, 0:ncols]), reads=[Dk], writes=[rk])
    T.op("dve", lambda e: e.tensor_tensor(out=out_ap, in0=O[r0:r1, 0:ncols], in1=rc[r0:r1, 0:ncols], op=ALU.mult),
         reads=[Ok, rk], writes=[okey])


for _f in (_mx_declare, _mx_globals, _mixer, _merge_branch, _mix_out, _attn_head):
    setattr(Builder, _f.__name__, _f)


def _attn(self, l, which):
    nc, T = self.nc, self.T
    A = (which == 0)
    nkc = 2 if A else 4
    qoff = A_OFF if A else B_OFF
    wv = self.w_in[l].rearrange("(kc p) n -> p kc n", p=128)
    self._ah_cnt = 0
    self._as_cnt = 0
    with ExitStack() as aes:
        qT = self.sb("qT", [128, 4, NTOK], BF16, es=aes)
        kT = self.sb("kT", [128, nkc, NTOK], BF16, es=aes)
        VW = 256 if A else 512
        vtm = self.sb("vtm", [128, 12, VW], BF16, es=aes)
        ckT = self.sb("ckT", [128, nkc, 256], BF16, es=aes)
        cv = self.sb("cv", [128, 2, VW], BF16, es=aes)
        oT = self.sb("oT", [128, 4, NTOK], BF16, es=aes)
        self.ptb = [self.sb("ptb%d" % i, [128, 512], BF16, es=aes) for i in range(4)]
        self.rcb = [self.sb("rcb%d" % i, [128, 512], F32, es=aes) for i in range(2)]
        ostg = [self.sb("ostg%d" % i, [128, 512], F32, es=aes) for i in range(2)]
        if A:
            rcos = self.sb("rcos", [128, 1024], BF16, es=aes)
            rsin = self.sb("rsin", [128, 1024], BF16, es=aes)
            esink = self.sb("esink", [128, 8], F32, es=aes)
            T.dma("pool", rcos[:], self.c_ropecos, writes=["rcos"])
            T.dma("pool", rsin[:], self.c_ropesin, writes=["rsin"])
            T.dma("sp", esink[:], self.sink[l].partition_broadcast(128), writes=["esink"])
            T.op("act", lambda e: e.activation(out=esink[:], in_=esink[:], func=AF.Exp), reads=["esink"], writes=["esink"])
        else:
            etab = self.sb("etab", [128, (NVA + NVB) * 64], BF16, es=aes)
        pcnt = [0]

        def bankof():
            i = pcnt[0] % 2
            pcnt[0] += 1
            return self.ps[i], "ps%d" % i

        def evac(i, out_ap, in_ap, reads, writes):
            if i % 2 == 0:
                T.op("act", lambda e: e.activation(out=out_ap, in_=in_ap, func=AF.Copy), reads=reads, writes=writes)
            else:
                T.op("dve", lambda e: e.tensor_copy(out=out_ap, in_=in_ap), reads=reads, writes=writes)

        dst, key = self.wload(lambda s_: s_.rearrange("p (kc n) -> p kc n", kc=NCH), wv[:, :, qoff:qoff + 512])
        for c in range(4):
            for (t0, n) in TILES:
                bank, bk = bankof()
                self.mmg(bank[:, 0:n], bk, [(dst[:, kc, c * 128:(c + 1) * 128], self.hT[:, kc, t0:t0 + n]) for kc in range(NCH)], reads=[key, "hT"])
                evac(pcnt[0], qT[:, c, t0:t0 + n], bank[:, 0:n], [bk], ["qT"])
        import os
        stopat = os.environ.get("DBG_STOP", "")
        if stopat == "q":
            T.op("dve", lambda e: e.memset(oT[:], 0.0), writes=["oT"])
            self._merge_branch(l, which, [(lambda t0, n, c=c: oT[:, c, t0:t0 + n]) for c in range(4)], "oT", 0, 0, NTOK, first=A)
            return
        if A:
            s = self.wslot
            self.wslot = (self.wslot + 1) % self.NSLOT
            key = "wring%d" % s
            dst = self.wring[:, s, 0:NCH * 256].rearrange("p (kc kv dup d) -> p kc kv dup d", kc=NCH, kv=2, dup=2)
            T.dma("pool", [dst[:, :, kv, dup, :] for kv in range(2) for dup in range(2)],
                  [wv[:, :, 512 + kv * 64:512 + (kv + 1) * 64] for kv in range(2) for dup in range(2)], writes=[key])
            kw = lambda kc, c: dst[:, kc, c, :, :]
        else:
            dst, key = self.wload(lambda s_: s_.rearrange("p (kc n) -> p kc n", kc=NCH), wv[:, :, B_OFF + 512:B_OFF + 1024])
            kw = lambda kc, c: dst[:, kc, c * 128:(c + 1) * 128]
        for c in range(nkc):
            for (t0, n) in TILES:
                bank, bk = bankof()
                self.mmg(bank[:, 0:n], bk, [(kw(kc, c), self.hT[:, kc, t0:t0 + n]) for kc in range(NCH)], reads=[key, "hT"])
                evac(pcnt[0], kT[:, c, t0:t0 + n], bank[:, 0:n], [bk], ["kT"])
        if stopat == "k":
            T.op("dve", lambda e: e.memset(oT[:], 0.0), writes=["oT"])
            self._merge_branch(l, which, [(lambda t0, n, c=c: oT[:, c, t0:t0 + n]) for c in range(4)], "oT", 0, 0, NTOK, first=A)
            return
        ocnt = [0]

        def out_rows(dram_ap, st, width, bank):
            if os.environ.get("DBG_NOOUTROWS"):
                return
            og, ogk = ostg[ocnt[0] % 2], "ostg%d" % (ocnt[0] % 2)
            ocnt[0] += 1
            T.op("dve", lambda e: e.tensor_copy(out=og[:, 0:width], in_=bank), reads=[bk_cur[0]], writes=[ogk])
            sq, tl = st // 2, (st % 2) * 128
            if os.environ.get("DBG_NOOUTDMA"):
                return
            T.dma("sp", dram_ap[sq, l, tl:tl + 128, :], og[:, 0:width], reads=[ogk], writes=["outrows%d" % ocnt[0]])

        bk_cur = [None]
        if A:
            dst, key = self.wload(lambda s_: s_[:, 0:NCH * 256].rearrange("p (kc n) -> p kc n", kc=NCH), wv[:, :, 512:768])
            for st in range(12):
                bank, bk = bankof()
                bk_cur[0] = bk
                self.mmg(bank[:, 0:256], bk, [(self.hT[:, kc, st * 128:(st + 1) * 128], dst[:, kc, :]) for kc in range(NCH)], reads=[key, "hT"])
                if st < 4:
                    out_rows(self.o_nak, st, 128, bank[:, 0:128])
                    out_rows(self.o_nav, st, 128, bank[:, 128:256])
                for dup in range(2):
                    T.op("act", lambda e, bank=bank, st=st, dup=dup: e.activation(
                        out=vtm[:, st, :].rearrange("p (kv dup d) -> p kv dup d", kv=2, dup=2)[:, :, dup, :],
                        in_=bank[:, 128:256].rearrange("p (kv d) -> p kv d", kv=2), func=AF.Copy),
                        reads=[bk], writes=["vtm"])
        else:
            dstk, keyk = self.wload(lambda s_: s_.rearrange("p (kc n) -> p kc n", kc=NCH), wv[:, :, B_OFF + 512:B_OFF + 1024])
            for st in range(4):
                bank, bk = bankof()
                bk_cur[0] = bk
                self.mmg(bank[:, 0:512], bk, [(self.hT[:, kc, st * 128:(st + 1) * 128], dstk[:, kc, :]) for kc in range(NCH)], reads=[keyk, "hT"])
                out_rows(self.o_nbk, st, 512, bank[:, 0:512])
            dst, key = self.wload(lambda s_: s_.rearrange("p (kc n) -> p kc n", kc=NCH), wv[:, :, B_OFF + 1024:B_OFF + 1536])
            for st in range(12):
                bank, bk = bankof()
                bk_cur[0] = bk
                self.mmg(bank[:, 0:512], bk, [(self.hT[:, kc, st * 128:(st + 1) * 128], dst[:, kc, :]) for kc in range(NCH)], reads=[key, "hT"])
                if st < 4:
                    out_rows(self.o_nbv, st, 512, bank[:, 0:512])
                T.op("act", lambda e, bank=bank, st=st: e.activation(out=vtm[:, st, :], in_=bank[:, 0:512], func=AF.Copy),
                     reads=[bk], writes=["vtm"])
        import os
        if os.environ.get("DBG_NOCACHE"):
            pass
        elif A:
            for ct in range(2):
                T.dma("pool", [cv[:, ct, :].rearrange("p (kv dup d) -> p kv dup d", kv=2, dup=2)[:, :, dup, :] for dup in range(2)],
                      [self.cav[l, ct * 128:(ct + 1) * 128, :].rearrange("t (kv d) -> t kv d", kv=2) for dup in range(2)], writes=["cv"])
                og, ogk = ostg[ct], "ostg%d" % ct
                T.dma("sp", [og[:, 0:256].rearrange("p (kv dup d) -> p kv dup d", kv=2, dup=2)[:, :, dup, :] for dup in range(2)],
                      [self.cak[l, ct * 128:(ct + 1) * 128, :].rearrange("t (kv d) -> t kv d", kv=2) for dup in range(2)], writes=[ogk])
                for kv in range(2):
                    bank, bk = self.ps[6 + kv], "ps%d" % (6 + kv)
                    T.op("pe", lambda e, bank=bank, og=og, kv=kv: e.transpose(bank[:, 0:128], og[:, kv * 128:(kv + 1) * 128], self.ident_f[:]),
                         reads=[ogk, "ident_f"], writes=[bk], same=False)
                    T.op("dve", lambda e, bank=bank, kv=kv, ct=ct: e.tensor_copy(out=ckT[:, kv, ct * 128:(ct + 1) * 128], in_=bank[:, 0:128]),
                         reads=[bk], writes=["ckT"])
        else:
            for ct in range(2):
                T.dma("pool", cv[:, ct, :], self.cbv[l, ct * 128:(ct + 1) * 128, :], writes=["cv"])
                og, ogk = ostg[ct], "ostg%d" % ct
                T.dma("sp", og[:], self.cbk[l, ct * 128:(ct + 1) * 128, :], writes=[ogk])
                bank, bk = self.ps[6 + ct], "ps%d" % (6 + ct)
                for c in range(4):
                    T.op("pe", lambda e, bank=bank, og=og, c=c: e.transpose(bank[:, c * 128:(c + 1) * 128], og[:, c * 128:(c + 1) * 128], self.ident_f[:]),
                         reads=[ogk, "ident_f"], writes=[bk] if c in (0, 3) else [], same=False)
                T.op("dve", lambda e, bank=bank, ct=ct: e.tensor_copy(out=ckT[:, :, ct * 128:(ct + 1) * 128],
                                                                      in_=bank[:].rearrange("p (c t) -> p c t", c=4)),
                     reads=[bk], writes=["ckT"])
        if "dbg_misc" in self.dram_out and l == 0 and A and os.environ.get("DBG_DUMPM"):
            T.op("dve", lambda e: e.tensor_copy(out=self.rcb[0][:, 0:128], in_=self.mlow[:]), reads=["mlow"], writes=["rcb0"])
            T.op("dve", lambda e: e.tensor_copy(out=self.rcb[0][:, 128:256], in_=self.mup[:]), reads=["mup"], writes=["rcb0"])
            T.op("dve", lambda e: e.tensor_copy(out=self.rcb[0][:, 256:384], in_=self.swapm[:]), reads=["swapm"], writes=["rcb0"])
            T.op("dve", lambda e: e.tensor_copy(out=self.rcb[0][:, 384:512], in_=rcos[:, 0:128]), reads=["rcos"], writes=["rcb0"])
            T.dma("sp", self.dram_out["dbg_misc"][:, 0:512], self.rcb[0][:], reads=["rcb0"], writes=["dbgm0"])
        elif "dbg_misc" in self.dram_out and l == 0 and A:
            T.op("dve", lambda e: e.tensor_copy(out=self.rcb[0][:], in_=ckT[:].rearrange("p a b -> p (a b)")), reads=["ckT"], writes=["rcb0"])
            T.dma("sp", self.dram_out["dbg_misc"][:, 0:512], self.rcb[0][:], reads=["rcb0"], writes=["dbgm0"])
            T.op("dve", lambda e: e.tensor_copy(out=self.rcb[1][:], in_=cv[:].rearrange("p a b -> p (a b)")), reads=["cv"], writes=["rcb1"])
            T.dma("sp", self.dram_out["dbg_misc"][:, 512:1024], self.rcb[1][:], reads=["rcb1"], writes=["dbgm1"])
        import os
        if A and not os.environ.get("DBG_NOROPE"):
            rc = 0
            for (arr, akey, nchunk) in ((qT, "qT", 4), (kT, "kT", 2)):
                for c in range(nchunk):
                    for qt in range(2):
                        t0 = 512 + qt * 512
                        bank, bk = self.ps[6 + rc % 2], "ps%d" % (6 + rc % 2)
                        rc += 1
                        x = arr[:, c, t0:t0 + 512]
                        self.mmg(bank[:], bk, [(self.swapm[:], x)], reads=[akey, "swapm"])
                        T.op("dve", lambda e, x=x, qt=qt: e.tensor_tensor(out=self.mtmp[0][:], in0=x, in1=rcos[:, qt * 512:(qt + 1) * 512], op=ALU.mult),
                             reads=[akey, "rcos"], writes=["mtmp0"])
                        T.op("dve", lambda e, bank=bank, qt=qt: e.tensor_tensor(out=self.mtmp[1][:], in0=bank[:], in1=rsin[:, qt * 512:(qt + 1) * 512], op=ALU.mult),
                             reads=[bk, "rsin"], writes=["mtmp1"])
                        T.op("pool", lambda e, x=x: e.tensor_tensor(out=x, in0=self.mtmp[0][:], in1=self.mtmp[1][:], op=ALU.add),
                             reads=["mtmp0", "mtmp1"], writes=[akey])
        import os
        if os.environ.get("DBG_NOHEADS"):
            T.op("dve", lambda e: e.memset(oT[:], 0.0), writes=["oT"])
        for hp in range(0 if os.environ.get("DBG_NOHEADS") else 4):
            for par in range(2):
                h = hp * 2 + par
                if not A:
                    T.dma("pool", etab[:], self.natab[l, h], writes=["etab"])
                    T.op("act", lambda e: e.activation(out=etab[:], in_=etab[:], func=AF.Exp), reads=["etab"], writes=["etab"])
                rows = (par * 64, par * 64 + 64)
                kc_ = (h // 4) if A else hp
                vsl = (lambda a: a[:, kc_ * 128:(kc_ + 1) * 128])
                sink_ap = esink[rows[0]:rows[1], h:h + 1] if A else None
                for sq in range(2):
                    b0 = sq * 256
                    specs = [dict(kT=kT[rows[0]:rows[1], kc_, b0 + kt * 128:b0 + (kt + 1) * 128], v=vsl(vtm[:, sq * 2 + kt, :]),
                                  c0=0, c1=256, keys=["kT", "vtm", "qT"]) for kt in range(2)]
                    self._attn_head(specs, lambda c0, c1, b0=b0: qT[rows[0]:rows[1], hp, b0 + c0:b0 + c1], 256,
                                    oT[rows[0]:rows[1], hp, b0:b0 + 256], rows, sink_ap, "oT")
                for qt in range(2):
                    b0 = 512 + qt * 512
                    specs = [dict(kT=ckT[rows[0]:rows[1], kc_, ct * 128:(ct + 1) * 128], v=vsl(cv[:, ct, :]), c0=0, c1=512,
                                  keys=["ckT", "cv", "qT"]) for ct in range(2)]
                    if A and os.environ.get("DBG_ANOLOCAL"):
                        pass
                    elif A:
                        for j in range(4 * qt - 1, 4 * qt + 5):
                            if j < 0 or j > 7:
                                continue
                            ilo, ihi = max(j - 1, 4 * qt), min(j + 1, 4 * qt + 3)
                            masks = []
                            for i in range(ilo, ihi + 1):
                                a = (i - ilo) * 128
                                if i == j + 1:
                                    masks.append((a, a + 128, self.mlow[:], "mlow"))
                                elif i == j - 1:
                                    masks.append((a, a + 128, self.mup[:], "mup"))
                            specs.append(dict(kT=kT[rows[0]:rows[1], kc_, 512 + j * 128:512 + (j + 1) * 128], v=vsl(vtm[:, 4 + j, :]),
                                              c0=(ilo - 4 * qt) * 128, c1=(ihi - 4 * qt + 1) * 128, masks=masks, keys=["kT", "vtm", "qT"]))
                    else:
                        for j in range(8):
                            ra, rb = NA_R[j]
                            lo, hi = max(ra, 8 * qt), min(rb, 8 * qt + 7)
                            if lo > hi:
                                continue
                            c0, c1 = (lo - 8 * qt) * 64, (hi - 8 * qt + 1) * 64
                            v0 = (lo - 2 * j + 6) if j <= 3 else (NVA + lo - 2 * j + 3)
                            masks = [(0, c1 - c0, etab[:, v0 * 64:v0 * 64 + (c1 - c0)], "etab")]
                            specs.append(dict(kT=kT[rows[0]:rows[1], kc_, 512 + j * 128:512 + (j + 1) * 128], v=vsl(vtm[:, 4 + j, :]),
                                              c0=c0, c1=c1, masks=masks, keys=["kT", "vtm", "qT"]))
                    self._attn_head(specs, lambda c0, c1, b0=b0: qT[rows[0]:rows[1], hp, b0 + c0:b0 + c1], 512,
                                    oT[rows[0]:rows[1], hp, b0:b0 + 512], rows, sink_ap, "oT")
        if "dbg_oT" in self.dram_out and l == 0:
            for c in range(4):
                for (t0, n) in TILES:
                    og, ogk = ostg[c % 2], "ostg%d" % (c % 2)
                    T.op("dve", lambda e, og=og, c=c, t0=t0, n=n: e.tensor_copy(out=og[:, 0:n], in_=oT[:, c, t0:t0 + n]), reads=["oT"], writes=[ogk])
                    T.dma("sp", self.dram_out["dbg_oT"][which, c, :, t0:t0 + n], og[:, 0:n], reads=[ogk], writes=["dbgo%d_%d_%d" % (which, c, t0)])
        self._merge_branch(l, which, [(lambda t0, n, c=c: oT[:, c, t0:t0 + n]) for c in range(4)], "oT", 0, 0, NTOK, first=A)
        T.barrier()


Builder._attn = _attn
_old_declare = Builder._declare
_old_globals = Builder._globals


def _declare2(self):
    _old_declare(self)
    self._mx_declare()


def _globals2(self):
    _old_globals(self)
    self._mx_globals()


Builder._declare = _declare2
Builder._globals = _globals2


def _prep_inputs2(inputs):
    maps = _prep_inputs(inputs)
    mc = _mixer_consts()
    flat, mask = _na_tables_idx()
    flat = np.where(mask > 0, flat, 15 * 31)
    rpb = np.asarray(inputs["na_rpb"], np.float32).reshape(DEPTH, 8, 15 * 31)
    rpb = np.concatenate([rpb, np.full((DEPTH, 8, 1), -1.0e4, np.float32)], axis=-1)
    na_tab = rpb[:, :, flat.reshape(-1)].reshape(DEPTH, 8, 128, (NVA + NVB) * 64)
    shared = dict(mc)
    shared["na_tab"] = np.ascontiguousarray(na_tab)
    for k in ("w_in", "proj_a", "proj_b", "proj_c", "w_out", "attn_sink"):
        shared[k] = np.ascontiguousarray(inputs[k], dtype=np.float32)
    for core, m in enumerate(maps):
        m.update(shared)
        m["cache_attn_k"] = np.ascontiguousarray(inputs["cache_attn_k"][core].reshape(DEPTH, 256, 128))
        m["cache_attn_v"] = np.ascontiguousarray(inputs["cache_attn_v"][core].reshape(DEPTH, 256, 128))
        m["cache_na_k"] = np.ascontiguousarray(inputs["cache_na_k"][core].reshape(DEPTH, 256, 512))
        m["cache_na_v"] = np.ascontiguousarray(inputs["cache_na_v"][core].reshape(DEPTH, 256, 512))
    return maps


def _rw_declare(self):
    L = DEPTH
    self.w_shift = self.din("w_shift", [L, 3, 1920])
    self.decay_w0 = self.din("decay_w0", [L, 2, 512])
    self.decay_up = self.din("decay_up", [L, 128, 512])
    self.iclr_a0 = self.din("iclr_a0", [L, 2, 512])
    self.iclr_up = self.din("iclr_up", [L, 128, 512])
    self.gate_up = self.din("gate_up", [L, 128, 512])
    self.vec512 = {k: self.din(k, [L, 512]) for k in ("k_k", "k_a", "r_k", "gn_g", "gn_b")}
    self.st_in = self.din("state_rwkv", [L, 2, 8, 64, 64])
    self.o_nst = self.dout("o_nst", [2, L, 2, 8, 64, 64])
    self.c_identh = self.din("c_identh", [128, 64])


def _rw_layer_consts(self, l, es):
    T = self.T
    R = {}
    R["shT"] = self.sb("shT", [128, 3, 15], F32, es=es)
    self._load_vecT(self.w_shift[l].rearrange("s (c p) -> (s c) p", p=128), 45, "sh")
    T.op("dve", lambda e: e.tensor_copy(out=R["shT"][:].rearrange("p s c -> p (s c)"), in_=self.ps[7][:, 0:45]), reads=["ps7"], writes=["shT"])
    R["vecs"] = self.sb("rwvecs", [128, 36], F32, es=es)
    srcs = [self.decay_w0[l].rearrange("d (c p) -> (d c) p", p=128), self.iclr_a0[l].rearrange("d (c p) -> (d c) p", p=128)]
    srcs += [self.vec512[k][l].rearrange("(c p) -> c p", p=128) for k in ("k_k", "k_a", "r_k", "gn_g", "gn_b")]
    off = 0
    for sap, nr in zip(srcs, (8, 8, 4, 4, 4, 4, 4)):
        self._load_vecT(sap, nr, "rwv")
        T.op("dve", lambda e, off=off, nr=nr: e.tensor_copy(out=R["vecs"][:, off:off + nr], in_=self.ps[7][:, 0:nr]), reads=["ps7"], writes=["rwvecs"])
        off += nr
    R["omka"] = self.sb("omka", [128, 4], F32, es=es)
    T.op("dve", lambda e: e.tensor_scalar(out=R["omka"][:], in0=R["vecs"][:, 20:24], scalar1=-1.0, scalar2=1.0, op0=ALU.mult, op1=ALU.add),
         reads=["rwvecs"], writes=["omka"])
    R["dup"] = self.sb("dupw", [128, 512], BF16, es=es)
    R["iup"] = self.sb("iupw", [128, 512], BF16, es=es)
    R["gup"] = self.sb("gupw", [128, 512], BF16, es=es)
    T.dma("pool", R["dup"][:], self.decay_up[l], writes=["dupw"])
    T.dma("pool", R["iup"][:], self.iclr_up[l], writes=["iupw"])
    T.dma("pool", R["gup"][:], self.gate_up[l], writes=["gupw"])
    R["identh"] = self.sb("identh", [128, 64], BF16, es=es)
    T.dma("pool", R["identh"][:], self.c_identh, writes=["identh"])
    return R


def _rwkv(self, l):
    T = self.T
    with ExitStack() as res_:
        R = self._rw_layer_consts(l, res_)
        units = [dict(segs=[(0, 256), (256, 256)], pairs=[0, 1, 2, 3], sample=False),
                 dict(segs=[(512, 1024)], pairs=[0, 1], sample=True),
                 dict(segs=[(512, 1024)], pairs=[2, 3], sample=True)]
        import os
        if os.environ.get("DBG_UNITS"):
            units = [units[int(c)] for c in os.environ["DBG_UNITS"]]
        for u in units:
            self._rw_unit(l, R, u)
        T.barrier()


def _rw_unit(self, l, R, u):
    nc, T = self.nc, self.T
    segs, pairs, sample = u["segs"], u["pairs"], u["sample"]
    nseg, npair = len(segs), len(pairs)
    t00 = segs[0][0]
    L = segs[0][1]
    TU = nseg * L
    tiles = [(t00 + a, 512) for a in range(0, TU, 512)]
    G = npair * nseg
    wv = self.w_in[l].rearrange("(kc p) n -> p kc n", p=128)
    vec = R["vecs"]
    with ExitStack() as ues:
        kdT = [self.sb("kdT%d" % d, [128, npair, TU], BF16, es=ues) for d in range(2)]
        kapT = self.sb("kapT", [128, npair, TU], BF16, es=ues)
        rT = self.sb("rT", [128, npair, TU], BF16, es=ues)
        vT = self.sb("vT", [128, npair, TU], BF16, es=ues)
        B = {}
        pc = [0]

        def pbank():
            i = pc[0] % 2
            pc[0] += 1
            return self.ps[i], "ps%d" % i

        def conv_chunk(ci, out_ap, okey, post=None):
            zraw, tmp = B["zraw"], B["tmp"]
            c0 = C_OFF + ci * 128
            s = self.wslot
            self.wslot = (self.wslot + 1) % self.NSLOT
            key = "wring%d" % s
            dst = self.wring[:, s, 0:1024].rearrange("p (kc n) -> p kc n", kc=NCH)
            T.dma("pool", dst, wv[:, :, c0:c0 + 128], writes=[key])
            for ti, (tt0, n) in enumerate(tiles):
                bank, bk = pbank()
                self.mmg(bank[:, 0:n], bk, [(dst[:, kc_, :], self.hT[:, kc_, tt0:tt0 + n]) for kc_ in range(NCH)], reads=[key, "hT"])
                T.op("act", lambda e, bank=bank, ti=ti, n=n: e.activation(out=zraw[:, ti * 512:ti * 512 + n], in_=bank[:, 0:n], func=AF.Copy),
                     reads=[bk], writes=["zraw"])
            sh = R["shT"]
            T.op("dve", lambda e: e.tensor_scalar(out=tmp[:], in0=zraw[:], scalar1=sh[:, 1, ci:ci + 1], scalar2=None, op0=ALU.mult),
                 reads=["zraw", "shT"], writes=["rwtmp"])
            for si in range(nseg):
                a, b = si * L, (si + 1) * L
                T.op("dve", lambda e, a=a, b=b: e.scalar_tensor_tensor(out=tmp[:, a + 1:b], in0=zraw[:, a:b - 1], scalar=sh[:, 0, ci:ci + 1],
                                                                       in1=tmp[:, a + 1:b], op0=ALU.mult, op1=ALU.add),
                     reads=["zraw", "shT", "rwtmp"], writes=["rwtmp"])
                T.op("dve", lambda e, a=a, b=b: e.scalar_tensor_tensor(out=tmp[:, a:b - 1], in0=zraw[:, a + 1:b], scalar=sh[:, 2, ci:ci + 1],
                                                                       in1=tmp[:, a:b - 1], op0=ALU.mult, op1=ALU.add),
                     reads=["zraw", "shT", "rwtmp"], writes=["rwtmp"])
            T.op("act", lambda e: e.activation(out=out_ap, in_=tmp[:], func=(post or AF.Copy)), reads=["rwtmp"], writes=[okey])
        yT = self.sb("yT", [128, npair, TU], F32, es=ues)
        T.op("pool", lambda e: e.memset(yT[:], 0.0), writes=["yT"])
        wsc = ExitStack()
        wT = [self.sb("wT%d" % d, [128, npair, TU], F32, es=wsc) for d in range(2)]
        bpT = [self.sb("bpT%d" % d, [128, npair, TU], BF16, es=wsc) for d in range(2)]
        with ExitStack() as pes:
            zraw = self.sb("zraw", [128, TU], F32, es=pes)
            kc = self.sb("kcv", [128, TU], F32, es=pes)
            av = zraw
            tmp = self.sb("rwtmp", [128, TU], F32, es=pes)
            tmpb = self.sb("rwtmpb", [128, TU], BF16, es=pes)
            twlo = self.sb("twlo", [128, TU], BF16, es=pes)
            talo = self.sb("talo", [128, TU], BF16, es=pes)
            B["zraw"], B["tmp"] = zraw, tmp
            conv_chunk(12, twlo[:], "twlo", AF.Tanh)
            conv_chunk(13, talo[:], "talo")
            for qi, p in enumerate(pairs):
                conv_chunk(p, rT[:, qi, :], "rT")
                conv_chunk(8 + p, vT[:, qi, :], "vT")
                conv_chunk(4 + p, kc[:], "kcv")
                T.op("dve", lambda e, p=p: e.tensor_scalar(out=av[:], in0=kc[:], scalar1=vec[:, 16 + p:17 + p], scalar2=None, op0=ALU.mult),
                     reads=["kcv", "rwvecs"], writes=["zraw"])
                T.op("pool", lambda e: e.tensor_tensor(out=tmpb[:], in0=av[:], in1=av[:], op=ALU.mult), reads=["zraw"], writes=["rwtmpb"])
                for ti in range(TU // 512):
                    bank, bk = pbank()
                    sl = slice(ti * 512, (ti + 1) * 512)
                    self.mmg(bank[:], bk, [(self.bones_bf[:], tmpb[:, sl])], reads=["bones_bf", "rwtmpb"])
                    T.op("act", lambda e, bank=bank, sl=sl: e.activation(out=tmp[:, sl], in_=bank[:], func=AF.Sqrt), reads=[bk], writes=["rwtmp"])
                T.op("dve", lambda e: e.tensor_scalar(out=tmp[:], in0=tmp[:], scalar1=1e-12, scalar2=None, op0=ALU.max), reads=["rwtmp"], writes=["rwtmp"])
                T.op("dve", lambda e: e.reciprocal(out=tmp[:], in_=tmp[:]), reads=["rwtmp"], writes=["rwtmp"])
                T.op("dve", lambda e, qi=qi: e.tensor_tensor(out=kapT[:, qi, :], in0=av[:], in1=tmp[:], op=ALU.mult), reads=["zraw", "rwtmp"], writes=["kapT"])
                for d in range(2):
                    hs = slice(d * 64, (d + 1) * 64)
                    for ti in range(TU // 512):
                        sl = slice(ti * 512, (ti + 1) * 512)
                        bank, bk = pbank()
                        self.mmg(bank[:], bk, [(R["dup"][hs, p * 128:(p + 1) * 128], twlo[hs, sl])], reads=["dupw", "twlo"])
                        T.op("act", lambda e, bank=bank, sl=sl, d=d, p=p: e.activation(out=tmp[:, sl], in_=bank[:], func=AF.Sigmoid,
                                                                                      bias=vec[:, d * 4 + p:d * 4 + p + 1], scale=1.0),
                             reads=[bk, "rwvecs"], writes=["rwtmp"])
                        bank2, bk2 = pbank()
                        self.mmg(bank2[:], bk2, [(R["iup"][hs, p * 128:(p + 1) * 128], talo[hs, sl])], reads=["iupw", "talo"])
                        T.op("act", lambda e, bank2=bank2, sl=sl, d=d, p=p: e.activation(out=av[:, sl], in_=bank2[:], func=AF.Sigmoid,
                                                                                        bias=vec[:, 8 + d * 4 + p:8 + d * 4 + p + 1], scale=1.0),
                             reads=[bk2, "rwvecs"], writes=["zraw"])
                    T.op("act", lambda e, d=d, qi=qi: e.activation(out=wT[d][:, qi, :], in_=tmp[:], func=AF.Exp, scale=-float(np.exp(-0.5))),
                         reads=["rwtmp"], writes=["wT%d" % d])
                    T.op("dve", lambda e, d=d, qi=qi: e.scalar_tensor_tensor(out=bpT[d][:, qi, :], in0=kapT[:, qi, :], scalar=-1.0, in1=av[:],
                                                                             op0=ALU.mult, op1=ALU.mult),
                         reads=["kapT", "zraw"], writes=["bpT%d" % d])
                    T.op("dve", lambda e, p=p: e.tensor_scalar(out=tmp[:], in0=av[:], scalar1=vec[:, 20 + p:21 + p], scalar2=R["omka"][:, p:p + 1],
                                                               op0=ALU.mult, op1=ALU.add),
                         reads=["zraw", "rwvecs", "omka"], writes=["rwtmp"])
                    T.op("dve", lambda e, d=d, qi=qi: e.tensor_tensor(out=kdT[d][:, qi, :], in0=kc[:], in1=tmp[:], op=ALU.mult),
                         reads=["kcv", "rwtmp"], writes=["kdT%d" % d])
            T.barrier()
        with ExitStack() as ses:
            H = [self.sb("H%d" % d, [128, npair, nseg, 64], F32, es=ses) for d in range(2)]
            Hk = [self.sb("Hk%d" % d, [128, npair, nseg, 64], BF16, es=ses) for d in range(2)]
            Hc = [self.sb("Hc%d" % d, [128, npair, nseg, 64], BF16, es=ses) for d in range(2)]
            Vd = [self.sb("Vd%d" % d, [128, npair, nseg, 64], BF16, es=ses) for d in range(2)]
            KV = [self.sb("KV%d" % d, [128, npair, nseg, 64], F32, es=ses) for d in range(2)]
            stg = self.sb("ststg", [64, 128], F32, es=ses)
            for d in range(2):
                if not sample:
                    T.op("pool", lambda e, d=d: e.memset(H[d][:], 0.0), writes=["H%d" % d])
                else:
                    for qi, p in enumerate(pairs):
                        T.dma("sp", stg[:].rearrange("v (h k) -> v h k", h=2), self.st_in[l, d, 2 * p:2 * p + 2].rearrange("h v k -> v h k"), writes=["ststg"])
                        T.op("pe", lambda e: e.transpose(self.ps[6][:, 0:64], stg[:], self.ident_f[0:64, 0:64]), reads=["ststg", "ident_f"], writes=["ps6"], same=False)
                        T.op("dve", lambda e, d=d, qi=qi: e.tensor_copy(out=H[d][:, qi, 0, :], in_=self.ps[6][:, 0:64]), reads=["ps6"], writes=["H%d" % d])
            NB = 512 // G
            SA = [self.ps[0], self.ps[1]]
            VB = [self.ps[2], self.ps[3]]
            YP = [self.ps[4], self.ps[5]]
            ypv = [YP[d][:, 0:G * NB].rearrange("p (q s n) -> p q s n", q=npair, s=nseg) for d in range(2)]
            sh4 = [128, npair, nseg, 64]

            def col(arr, tt):
                return arr[:].rearrange("p q (s t) -> p q s t", s=nseg)[:, :, :, tt].unsqueeze(3).broadcast_to(sh4)

            idb = R["identh"][:].unsqueeze(1).unsqueeze(1).broadcast_to(sh4)
            fl = lambda a: a[:].rearrange("p q s v -> p (q s v)")
            X = [self.sb("X%d" % d, sh4, F32, es=ses) for d in range(2)]

            def emit_y(i, d):
                tt = i if d == 0 else L - 1 - i
                ypk = "ps%d" % (4 + d)
                cidx = (i % NB) if d == 0 else (NB - 1 - (i % NB))
                for qi in range(npair):
                    for si in range(nseg):
                        for par in range(2):
                            hs = slice(par * 64, (par + 1) * 64)
                            T.op("pe", lambda e, d=d, qi=qi, si=si, hs=hs, tt=tt, cidx=cidx: e.matmul(
                                ypv[d][hs, qi, si, cidx:cidx + 1], Hc[d][hs, qi, si, :], rT[hs, qi, si * L + tt:si * L + tt + 1], start=True, stop=True),
                                reads=["Hc%d" % d, "rT"], writes=[ypk], same=False)
                if (i % NB == NB - 1) or i == L - 1:
                    i0 = (i // NB) * NB
                    nb = i - i0 + 1
                    for si in range(nseg):
                        if d == 0:
                            ta, ca = si * L + i0, 0
                        else:
                            ta, ca = si * L + (L - 1 - i), NB - nb
                        T.op("dve", lambda e, d=d, si=si, ta=ta, ca=ca, nb=nb: e.tensor_tensor(
                            out=yT[:, :, ta:ta + nb], in0=ypv[d][:, :, si, ca:ca + nb], in1=yT[:, :, ta:ta + nb], op=ALU.add),
                            reads=[ypk, "yT"], writes=["yT"])

            for i in range(L):
                for d in range(2):
                    tt = i if d == 0 else L - 1 - i
                    hk, sak, vbk = "H%d" % d, "ps%d" % d, "ps%d" % (2 + d)
                    T.op("pool", lambda e, d=d, tt=tt: e.tensor_tensor(out=Vd[d][:], in0=idb, in1=col(vT, tt), op=ALU.mult),
                         reads=["vT", "identh"], writes=["Vd%d" % d])
                    T.op("pe", lambda e, d=d: e.matmul(VB[d][:, 0:G * 64], self.bones_bf[:], fl(Vd[d]), start=True, stop=True),
                         reads=["Vd%d" % d, "bones_bf"], writes=[vbk], same=False)
                    T.op("pool", lambda e, d=d, tt=tt: e.tensor_tensor(out=Hk[d][:], in0=H[d][:], in1=col(kapT, tt), op=ALU.mult),
                         reads=[hk, "kapT"], writes=["Hk%d" % d])
                    T.op("pe", lambda e, d=d: e.matmul(SA[d][:, 0:G * 64], self.bones_bf[:], fl(Hk[d]), start=True, stop=True),
                         reads=["Hk%d" % d, "bones_bf"], writes=[sak], same=False)
                    if i > 0:
                        emit_y(i - 1, d)
                    T.op("dve", lambda e, d=d, tt=tt: e.tensor_tensor(out=KV[d][:], in0=VB[d][:, 0:G * 64].rearrange("p (q s v) -> p q s v", q=npair, s=nseg),
                                                                      in1=col(kdT[d], tt), op=ALU.mult),
                         reads=[vbk, "kdT%d" % d], writes=["KV%d" % d], same=False)
                    if G > 2:
                        T.op("pool", lambda e, d=d, tt=tt: e.tensor_tensor(out=X[d][:], in0=H[d][:], in1=col(wT[d], tt), op=ALU.mult),
                             reads=[hk, "wT%d" % d], writes=["X%d" % d])
                    else:
                        T.op("dve", lambda e, d=d, tt=tt: e.tensor_tensor(out=X[d][:], in0=H[d][:], in1=col(wT[d], tt), op=ALU.mult),
                             reads=[hk, "wT%d" % d], writes=["X%d" % d], same=False)
                    T.op("dve", lambda e, d=d: e.tensor_tensor(out=X[d][:], in0=X[d][:], in1=KV[d][:], op=ALU.add),
                         reads=["X%d" % d, "KV%d" % d], writes=["X%d" % d], same=False)
                    T.op("dve", lambda e, d=d, tt=tt: e.tensor_tensor(out=KV[d][:], in0=SA[d][:, 0:G * 64].rearrange("p (q s v) -> p q s v", q=npair, s=nseg),
                                                                      in1=col(bpT[d], tt), op=ALU.mult),
                         reads=[sak, "bpT%d" % d, "X%d" % d], writes=["KV%d" % d], same=False)
                    T.op("dve", lambda e, d=d: e.tensor_tensor(out=H[d][:], in0=X[d][:], in1=KV[d][:], op=ALU.add),
                         reads=["X%d" % d, "KV%d" % d], writes=[hk], same=False)
                    T.op("act", lambda e, d=d: e.activation(out=Hc[d][:], in_=H[d][:], func=AF.Copy), reads=[hk], writes=["Hc%d" % d])
            for d in range(2):
                emit_y(L - 1, d)
            if not sample:
                for d in range(2):
                    for qi, p in enumerate(pairs):
                        for si in range(nseg):
                            T.op("pe", lambda e, d=d, qi=qi, si=si: e.transpose(self.ps[6][0:64, 0:128], H[d][:, qi, si, :], self.ident_f[:]),
                                 reads=["H%d" % d, "ident_f"], writes=["ps6"], same=False)
                            T.op("dve", lambda e: e.tensor_copy(out=stg[:], in_=self.ps[6][0:64, 0:128]), reads=["ps6"], writes=["ststg"])
                            T.dma("sp", self.o_nst[si, l, d, 2 * p:2 * p + 2].rearrange("h v k -> v h k"), stg[:].rearrange("v (h k) -> v h k", h=2),
                                  reads=["ststg"], writes=["nst%d_%d_%d_%d" % (l, d, p, si)])
            T.barrier()
        wsc.close()
        with ExitStack() as fes:
            yb = self.sb("ybf", [128, 512], BF16, es=fes)
            ysq = self.sb("ysq", [128, 512], BF16, es=fes)
            mean = self.sb("gmean", [128, 512], F32, es=fes)
            rstd = self.sb("grstd", [128, 512], F32, es=fes)
            tq = self.sb("gtq", [128, 512], F32, es=fes)
            gneps = self.sb("gneps", [128, 1], F32, es=fes)
            T.op("dve", lambda e: e.memset(gneps[:], GN_EPS), writes=["gneps"])
            gT = self.sb("gT", [128, npair, TU], BF16, es=fes)
            cbT = self.sb("cbT", [128, npair, TU], BF16, es=fes)
            zraw = self.sb("zraw", [128, TU], F32, es=fes)
            tmp = self.sb("rwtmp", [128, TU], F32, es=fes)
            tmpb = self.sb("rwtmpb", [128, TU], BF16, es=fes)
            B["zraw"], B["tmp"] = zraw, tmp
            conv_chunk(14, tmpb[:], "rwtmpb", AF.Sigmoid)
            for qi, p in enumerate(pairs):
                for ti in range(TU // 512):
                    bank, bk = pbank()
                    self.mmg(bank[:], bk, [(R["gup"][:, p * 128:(p + 1) * 128], tmpb[:, ti * 512:(ti + 1) * 512])], reads=["gupw", "rwtmpb"])
                    T.op("act", lambda e, bank=bank, qi=qi, ti=ti: e.activation(out=gT[:, qi, ti * 512:(ti + 1) * 512], in_=bank[:], func=AF.Copy),
                         reads=[bk], writes=["gT"])
            for qi, p in enumerate(pairs):
                for d in range(2):
                    T.op("dve", lambda e, d=d, qi=qi, p=p: e.scalar_tensor_tensor(out=tmpb[:], in0=kdT[d][:, qi, :], scalar=vec[:, 24 + p:25 + p],
                                                                                  in1=rT[:, qi, :], op0=ALU.mult, op1=ALU.mult),
                         reads=["kdT%d" % d, "rT", "rwvecs"], writes=["rwtmpb"])
                    for ti in range(TU // 512):
                        sl = slice(ti * 512, (ti + 1) * 512)
                        bank, bk = pbank()
                        self.mmg(bank[:], bk, [(self.bones_bf[:], tmpb[:, sl])], reads=["bones_bf", "rwtmpb"])
                        if d == 0:
                            T.op("act", lambda e, bank=bank, sl=sl, qi=qi: e.activation(out=cbT[:, qi, sl], in_=bank[:], func=AF.Copy),
                                 reads=[bk], writes=["cbT"])
                        else:
                            T.op("dve", lambda e, bank=bank, sl=sl, qi=qi: e.tensor_tensor(out=cbT[:, qi, sl], in0=bank[:], in1=cbT[:, qi, sl], op=ALU.add),
                                 reads=[bk, "cbT"], writes=["cbT"])
            ocT = kapT
            for qi, p in enumerate(pairs):
                for ti in range(TU // 512):
                    sl = slice(ti * 512, (ti + 1) * 512)
                    y = yT[:, qi, sl]
                    T.op("act", lambda e, y=y: e.activation(out=yb[:], in_=y, func=AF.Copy), reads=["yT"], writes=["ybf"])
                    T.op("pool", lambda e, y=y: e.tensor_tensor(out=ysq[:], in0=y, in1=y, op=ALU.mult), reads=["yT"], writes=["ysq"])
                    self.mmg(self.ps[0][:], "ps0", [(self.bones_bf[:], yb[:])], reads=["bones_bf", "ybf"])
                    self.mmg(self.ps[1][:], "ps1", [(self.bones_bf[:], ysq[:])], reads=["bones_bf", "ysq"])
                    T.op("act", lambda e: e.activation(out=mean[:], in_=self.ps[0][:], func=AF.Copy, scale=1.0 / 64), reads=["ps0"], writes=["gmean"])
                    T.op("dve", lambda e: e.tensor_tensor(out=tq[:], in0=mean[:], in1=mean[:], op=ALU.mult), reads=["gmean"], writes=["gtq"])
                    T.op("dve", lambda e: e.scalar_tensor_tensor(out=rstd[:], in0=self.ps[1][:], scalar=1.0 / 64, in1=tq[:], op0=ALU.mult, op1=ALU.subtract),
                         reads=["ps1", "gtq"], writes=["grstd"])
                    T.op("act", lambda e: e.activation(out=rstd[:], in_=rstd[:], func=AF.Sqrt, bias=gneps[:, 0:1], scale=1.0), reads=["grstd", "gneps"], writes=["grstd"])
                    T.op("dve", lambda e: e.reciprocal(out=rstd[:], in_=rstd[:]), reads=["grstd"], writes=["grstd"])
                    T.op("dve", lambda e, y=y: e.tensor_tensor(out=tq[:], in0=y, in1=mean[:], op=ALU.subtract), reads=["yT", "gmean"], writes=["gtq"])
                    T.op("dve", lambda e: e.tensor_tensor(out=tq[:], in0=tq[:], in1=rstd[:], op=ALU.mult), reads=["gtq", "grstd"], writes=["gtq"])
                    T.op("dve", lambda e, p=p: e.tensor_scalar(out=tq[:], in0=tq[:], scalar1=vec[:, 28 + p:29 + p], scalar2=vec[:, 32 + p:33 + p], op0=ALU.mult, op1=ALU.add),
                         reads=["gtq", "rwvecs"], writes=["gtq"])
                    T.op("pool", lambda e, qi=qi, sl=sl: e.tensor_tensor(out=mean[:], in0=cbT[:, qi, sl], in1=vT[:, qi, sl], op=ALU.mult),
                         reads=["cbT", "vT"], writes=["gmean"])
                    T.op("dve", lambda e: e.tensor_tensor(out=tq[:], in0=tq[:], in1=mean[:], op=ALU.add), reads=["gtq", "gmean"], writes=["gtq"])
                    T.op("dve", lambda e, qi=qi, sl=sl: e.tensor_tensor(out=ocT[:, qi, sl], in0=tq[:], in1=gT[:, qi, sl], op=ALU.mult),
                         reads=["gtq", "gT"], writes=["kapT"])
            self._merge_branch(l, 2, [(lambda tt0, n, qi=qi: ocT[:, qi, tt0 - t00:tt0 - t00 + n]) for qi in range(npair)], "kapT",
                               pairs[0] * 128, t00, TU, first=False)
            T.barrier()


for _f in (_rw_declare, _rw_layer_consts, _rwkv, _rw_unit):
    setattr(Builder, _f.__name__, _f)
_old_declare3 = Builder._declare


def _declare3(self):
    _old_declare3(self)
    self._rw_declare()


Builder._declare = _declare3


def _prep_inputs3(inputs):
    maps = _prep_inputs2(inputs)
    shared = {}
    for k in ("w_shift", "decay_w0", "iclr_a0", "gate_up", "k_k", "k_a", "gn_g", "gn_b"):
        shared[k] = np.ascontiguousarray(inputs[k], dtype=np.float32)
    shared["r_k"] = np.ascontiguousarray(np.asarray(inputs["r_k"], np.float32).reshape(DEPTH, 512))
    shared["decay_up"] = np.ascontiguousarray(np.asarray(inputs["decay_up"], np.float32).reshape(DEPTH, 128, 512))
    shared["iclr_up"] = np.ascontiguousarray(np.asarray(inputs["iclr_up"], np.float32).reshape(DEPTH, 128, 512))
    idh = np.zeros((128, 64), np.float32)
    idh[np.arange(128), np.arange(128) % 64] = 1.0
    shared["c_identh"] = idh
    for core, m in enumerate(maps):
        m.update(shared)
        m["state_rwkv"] = np.ascontiguousarray(inputs["state_rwkv"][core], dtype=np.float32)
    return maps


def kernel(**inputs):
    b = Builder()
    nc = b.build()
    maps = _prep_inputs3(inputs)
    maps = [{k: v for k, v in m.items() if k in b.dram_in} for m in maps]
    res = run_bass_kernel_spmd(nc, maps, core_ids=list(range(8)))
    outs = res.results
    f32 = np.float32
    y_p = np.stack([outs[c]["y_tok"][:512].reshape(2, 256, D) for c in range(8)], 0).reshape(16, 256, D).astype(f32)
    y_s = np.stack([outs[c]["y_tok"][512:] for c in range(8)], 0).astype(f32)
    nak = np.concatenate([outs[c]["o_nak"] for c in range(8)], 0).reshape(16, DEPTH, 256, 2, 64).astype(f32)
    nav = np.concatenate([outs[c]["o_nav"] for c in range(8)], 0).reshape(16, DEPTH, 256, 2, 64).astype(f32)
    nbk = np.concatenate([outs[c]["o_nbk"] for c in range(8)], 0).reshape(16, DEPTH, 256, 8, 64).astype(f32)
    nbv = np.concatenate([outs[c]["o_nbv"] for c in range(8)], 0).reshape(16, DEPTH, 256, 8, 64).astype(f32)
    nst = np.concatenate([outs[c]["o_nst"] for c in range(8)], 0).reshape(16, DEPTH, 2, 8, 64, 64).astype(f32)
    return (y_p, y_s, nak, nav, nbk, nbv, nst)
```

```python
import numpy as np
from contextlib import ExitStack
import concourse.bass as bass
import concourse.mybir as mybir
from concourse.bass_utils import run_bass_kernel_spmd

F32 = mybir.dt.float32
BF16 = mybir.dt.bfloat16
AF = mybir.ActivationFunctionType
ALU = mybir.AluOpType
AX = mybir.AxisListType

D = 1024
NCH = 8
DEPTH = 2
DFF = 2816
NJ = 22
NTOK = 1536
TILES = [(0, 512), (512, 512), (1024, 512)]
ALPHA = (2 * DEPTH) ** 0.25
LN_EPS = 1e-5
HD = 64
SCALE = HD ** -0.5
IN_COLS = 7296
A_OFF = 0
B_OFF = 768
C_OFF = 2304
G_OFF = 4224
GN_EPS = 64e-5


class Trk:
    SEM_LIMIT = 60000

    def __init__(self, nc, es):
        self.nc = nc
        self.es = es
        self.eng = {}
        for name, obj in (("pe", nc.tensor), ("act", nc.scalar), ("dve", nc.vector),
                          ("pool", nc.gpsimd), ("sp", nc.sync)):
            sem = es.enter_context(nc.semaphore("sem_" + name))
            self.eng[name] = dict(obj=obj, sem=sem, cnt=0, waited={}, id=name, name=name, epoch=0, total=0)
        self.dsems = [es.enter_context(nc.semaphore("dsem%d" % i)) for i in range(40)]
        self.dcnt = [0] * len(self.dsems)
        self.drr = 0
        self.last_w = {}
        self.readers = {}
        self.n_wait = 0

    def _wait(self, e, ev):
        sem, val, sid = ev
        if e["waited"].get(sid, 0) < val:
            e["obj"].wait_ge(sem, val)
            e["waited"][sid] = val
            self.n_wait += 1

    def _deps(self, e, reads, writes, same=True):
        evs = []
        for r in reads:
            if r in self.last_w:
                evs.append(self.last_w[r])
        for w in writes:
            if w in self.last_w:
                evs.append(self.last_w[w])
            evs.extend(self.readers.get(w, ()))
        for ev in evs:
            if (not same) and ev[2].split("_")[0] == e["name"]:
                continue
            self._wait(e, ev)

    def _commit(self, ev, reads, writes):
        for w in writes:
            self.last_w[w] = ev
            self.readers[w] = []
        for r in reads:
            if r in writes:
                continue
            lst = self.readers.setdefault(r, [])
            lst[:] = [x for x in lst if x[2] != ev[2]]
            lst.append(ev)

    def op(self, ename, fn, reads=(), writes=(), same=True):
        e = self.eng[ename]
        if ename != "pe":
            psr = [r for r in reads if r.startswith("ps") and r[2:].isdigit() and r not in writes]
            if psr:
                writes = list(writes) + psr
        self._deps(e, reads, writes, same)
        if e["cnt"] >= self.SEM_LIMIT:
            e["epoch"] += 1
            e["sem"] = self.es.enter_context(self.nc.semaphore("sem_%s_%d" % (e["name"], e["epoch"])))
            e["id"] = "%s_%d" % (e["name"], e["epoch"])
            e["cnt"] = 0
        inst = fn(e["obj"])
        e["cnt"] += 1
        e["total"] += 1
        inst.then_inc(e["sem"], 1)
        ev = (e["sem"], e["cnt"], e["id"])
        self._commit(ev, reads, writes)
        return ev

    def dma(self, qname, out, in_, reads=(), writes=()):
        e = self.eng[qname]
        self._deps(e, reads, writes, True)
        i = self.drr
        self.drr = (self.drr + 1) % len(self.dsems)
        outs = out if isinstance(out, (list, tuple)) else [out]
        ins = in_ if isinstance(in_, (list, tuple)) else [in_]
        for o_, i_ in zip(outs, ins):
            inst = e["obj"].dma_start(out=o_, in_=i_)
            self.dcnt[i] += 16
            inst.then_inc(self.dsems[i], 16)
        ev = (self.dsems[i], self.dcnt[i], "d%d" % i)
        self._commit(ev, reads, writes)
        return ev

    def barrier(self):
        evs = [(e["sem"], e["cnt"], e["id"]) for e in self.eng.values() if e["cnt"] > 0]
        evs += [(self.dsems[i], self.dcnt[i], "d%d" % i) for i in range(len(self.dsems)) if self.dcnt[i] > 0]
        for e in self.eng.values():
            for ev in evs:
                self._wait(e, ev)

    def wait_all(self, ename):
        e = self.eng[ename]
        for k, ev in list(self.last_w.items()):
            self._wait(e, ev)
        for k, lst in list(self.readers.items()):
            for ev in lst:
                self._wait(e, ev)


def host_consts():
    c = {}
    c["ident"] = np.eye(128, dtype=np.float32)
    bo = np.zeros((128, 128), np.float32)
    bo[:64, :64] = 1.0
    bo[64:, 64:] = 1.0
    c["blockones"] = bo
    c["ones"] = np.ones((128, 128), np.float32)
    return c


class Builder:
    def __init__(self, dbg=None, nlayers=DEPTH, stop_after=None):
        self.dbg = dbg or []
        self.nlayers = nlayers
        self.stop_after = stop_after
        self.nc = bass.Bass("TRN2", target_bir_lowering=False)
        self.es = ExitStack()
        self.dram_in = {}
        self.dram_out = {}

    def din(self, name, shape, dt=F32):
        t = self.nc.dram_tensor(name, list(shape), dt, kind="ExternalInput").ap()
        self.dram_in[name] = t
        return t

    def dout(self, name, shape, dt=F32):
        t = self.nc.dram_tensor(name, list(shape), dt, kind="ExternalOutput").ap()
        self.dram_out[name] = t
        return t

    def sb(self, name, shape, dt=F32, es=None):
        self._uid = getattr(self, "_uid", 0) + 1
        return (es or self.es).enter_context(self.nc.sbuf_tensor("%s_%d" % (name, self._uid), list(shape), dt))

    def build(self):
        nc, es = self.nc, self.es
        with es:
            self.T = Trk(nc, es)
            self._declare()
            self._globals()
            for l in range(self.nlayers):
                self._layer(l)
                if self.stop_after is not None and self.stop_after[0] == l:
                    break
            self._finish()
        return nc

    def _declare(self):
        L = DEPTH
        self.x_in = self.din("x_tok", [NTOK, D])
        self.cvec = self.din("cvec", [2, D])
        self.w_ada = self.din("w_ada", [L, D, 9 * D])
        self.b_ada = self.din("b_ada", [L, 9 * D])
        self.ffn_w_in = [self.din("ffn1_w_in", [L, D, 2 * DFF]), self.din("ffn2_w_in", [L, D, 2 * DFF])]
        self.ffn_w_out = [self.din("ffn1_w_out", [L, DFF, D]), self.din("ffn2_w_out", [L, DFF, D])]
        self.ln_g = self.din("ln_g", [L, 3, D])
        self.ln_b = self.din("ln_b", [L, 3, D])
        self.c_ones = self.din("c_ones", [128, 128])
        self.c_ident = self.din("c_ident", [128, 128])
        self.c_blockones = self.din("c_blockones", [128, 128])
        self.y_out = self.dout("y_tok", [NTOK, D])
        for name, shape in self.dbg:
            self.dout(name, shape)

    def _globals(self):
        nc, T = self.nc, self.T
        self.xT = self.sb("xT", [128, NCH, NTOK], F32)
        self.hT = self.sb("hT", [128, NCH, NTOK], BF16)
        self.ps = [self.es.enter_context(nc.psum_tensor("ps%d" % i, [128, 512], F32)) for i in range(8)]
        self.NSLOT = 3
        self.wring = self.sb("wring", [128, self.NSLOT, 4096], BF16)
        self.wslot = 0
        self.ones_bf = self.sb("ones_bf", [128, 128], BF16)
        self.ident_f = self.sb("ident_f", [128, 128], F32)
        self.bones_bf = self.sb("bones_bf", [128, 128], BF16)
        self.scT = self.sb("scT", [128, NCH, 2], BF16)
        self.cT = self.sb("cT", [128, NCH, 2], F32)
        self.modT = self.sb("modT", [128, 2, 9, NCH], F32)
        self.badaT = self.sb("badaT", [128, 9, NCH], F32)
        self.lngT = self.sb("lngT", [128, 3, NCH], F32)
        self.lnbT = self.sb("lnbT", [128, 3, NCH], F32)
        self.gsT = self.sb("gsT", [128, 2, NCH], F32)
        self.ghT = self.sb("ghT", [128, 2, NCH], F32)
        self.bhT = self.sb("bhT", [128, 2, NCH], F32)
        self.s1T = self.sb("s1T", [128, 2, NCH], F32)
        self.eps_t = self.sb("eps_t", [128, 1], F32)

        T.dma("pool", self.ones_bf[:], self.c_ones, writes=["ones_bf"])
        T.dma("pool", self.bones_bf[:], self.c_blockones, writes=["bones_bf"])
        T.dma("sp", self.ident_f[:], self.c_ident, writes=["ident_f"])
        T.op("dve", lambda e: e.memset(self.eps_t[:], LN_EPS / (ALPHA * ALPHA)), writes=["eps_t"])
        self._load_x()
        self.vecstage = self.sb("vecstage", [72, 128], F32)
        self._load_vecT(self.cvec.rearrange("w (c p) -> (w c) p", p=128), 16, "cT_raw")
        T.op("dve", lambda e: e.tensor_copy(out=self.cT[:], in_=self.ps[7][:, 0:16].rearrange("p (w c) -> p c w", w=2)),
             reads=["ps7"], writes=["cT"])
        T.op("act", lambda e: e.activation(out=self.scT[:], in_=self.cT[:], func=AF.Silu),
             reads=["cT"], writes=["scT"])

    def _load_x(self):
        nc, T = self.nc, self.T
        with ExitStack() as les:
            xs = [self.sb("xstage%d" % i, [128, D], F32, es=les) for i in range(2)]
            for tt in range(NTOK // 128):
                s = xs[tt % 2]
                key = "xstage%d" % (tt % 2)
                T.dma("sp", s[:], self.x_in[tt * 128:(tt + 1) * 128, :], writes=[key])
                for half in range(2):
                    bank = self.ps[(tt * 2 + half) % 4]
                    bkey = "ps%d" % ((tt * 2 + half) % 4)
                    for q in range(4):
                        c = half * 4 + q
                        T.op("pe", lambda e, c=c, q=q, bank=bank, s=s: e.transpose(
                            bank[:, q * 128:(q + 1) * 128], s[:, c * 128:(c + 1) * 128], self.ident_f[:]),
                            reads=[key, "ident_f"], writes=[bkey] if q in (0, 3) else [], same=False)
                    T.op("dve" if half == 0 else "act",
                         (lambda e, bank=bank, half=half, tt=tt: e.tensor_copy(
                             out=self.xT[:, half * 4:(half + 1) * 4, tt * 128:(tt + 1) * 128],
                             in_=bank[:].rearrange("p (q t) -> p q t", q=4))) if half == 0 else
                         (lambda e, bank=bank, half=half, tt=tt: e.activation(
                             out=self.xT[:, half * 4:(half + 1) * 4, tt * 128:(tt + 1) * 128],
                             in_=bank[:].rearrange("p (q t) -> p q t", q=4), func=AF.Copy)),
                         reads=[bkey], writes=["xT"])
            self.T.barrier()


    def _load_vecT(self, src_rows, nrows, tag):
        T = self.T
        T.dma("sp", self.vecstage[0:nrows, :], src_rows, writes=["vecstage"])
        T.op("pe", lambda e: e.transpose(self.ps[7][:, 0:nrows], self.vecstage[0:nrows, :], self.ident_f[0:nrows, 0:nrows]),
             reads=["vecstage", "ident_f"], writes=["ps7"], same=False)

    def wload(self, view_fn, src_ap, split=None):
        s = self.wslot
        self.wslot = (self.wslot + 1) % self.NSLOT
        key = "wring%d" % s
        dst = view_fn(self.wring[:, s, :])
        if split:
            self.T.dma("pool", [dst[:, :, a, :] for a in range(split)], [src_ap[:, :, a, :] for a in range(split)], writes=[key])
        else:
            self.T.dma("pool", dst, src_ap, writes=[key])
        return dst, key


    def mmg(self, out_ap, okey, terms, reads):
        T = self.T
        n = len(terms)
        for i, (lt, rh) in enumerate(terms):
            T.op("pe", lambda e, lt=lt, rh=rh, i=i: e.matmul(out_ap, lt, rh, start=(i == 0), stop=(i == n - 1)),
                 reads=reads, writes=[okey] if (i == 0 or i == n - 1) else [], same=False)

    def _layer(self, l):
        self._mods(l)
        self._ffn(l, 0)
        if self.stop_after == (l, 0):
            return
        if hasattr(self, "_mixer"):
            self._mixer(l)
            if self.stop_after == (l, 1):
                return
        self._ffn(l, 1)

    def _mods(self, l):
        nc, T = self.nc, self.T
        self._load_vecT(self.b_ada[l].rearrange("(ic p) -> ic p", p=128), 72, "bada")
        T.op("dve", lambda e: e.tensor_copy(out=self.badaT[:].rearrange("p i c -> p (i c)"), in_=self.ps[7][:, 0:72]),
             reads=["ps7"], writes=["badaT"])
        self._load_vecT(self.ln_g[l].rearrange("s (c p) -> (s c) p", p=128), 24, "lng")
        T.op("dve", lambda e: e.tensor_copy(out=self.lngT[:].rearrange("p s c -> p (s c)"), in_=self.ps[7][:, 0:24]),
             reads=["ps7"], writes=["lngT"])
        self._load_vecT(self.ln_b[l].rearrange("s (c p) -> (s c) p", p=128), 24, "lnb")
        T.op("dve", lambda e: e.tensor_copy(out=self.lnbT[:].rearrange("p s c -> p (s c)"), in_=self.ps[7][:, 0:24]),
             reads=["ps7"], writes=["lnbT"])
        wv = self.w_ada[l].rearrange("(kc p) n -> p kc n", p=128)
        for piece in range(18):
            dst, key = self.wload(lambda s: s.rearrange("p (kc n) -> p kc n", kc=NCH), wv[:, :, piece * 512:(piece + 1) * 512])
            bank = self.ps[piece % 2]
            bkey = "ps%d" % (piece % 2)
            for q in range(4):
                for kc in range(NCH):
                    T.op("pe", lambda e, q=q, kc=kc, bank=bank, dst=dst: e.matmul(
                        bank[:, q * 2:q * 2 + 2], dst[:, kc, q * 128:(q + 1) * 128], self.scT[:, kc, :],
                        start=(kc == 0), stop=(kc == NCH - 1)),
                        reads=[key, "scT"], writes=[bkey] if ((q == 0 and kc == 0) or (q == 3 and kc == NCH - 1)) else [], same=False)
            i, c0 = divmod(piece * 4, NCH)
            T.op("dve", lambda e, bank=bank, i=i, c0=c0: e.tensor_tensor(
                out=self.modT[:, :, i, c0:c0 + 4],
                in0=bank[:, 0:8].rearrange("p (q w) -> p w q", w=2),
                in1=self.badaT[:, i, c0:c0 + 4].unsqueeze(1).broadcast_to([128, 2, 4]), op=ALU.add),
                reads=[bkey, "badaT"], writes=["modT"])
        T.op("dve", lambda e: e.tensor_scalar_add(out=self.s1T[:], in0=self.modT[:, :, 1, :], scalar1=1.0),
             reads=["modT"], writes=["s1T"])
        self._modulate_all(self.s1T, lambda w: self.modT[:, w, 0, :])

    def _modulate_all(self, scaleT, shift_fn):
        T = self.T
        for c in range(NCH):
            for (w, t0, n) in ((0, 0, 512), (1, 512, 1024)):
                T.op("act", lambda e, c=c, w=w, t0=t0, n=n: e.activation(
                    out=self.hT[:, c, t0:t0 + n], in_=self.xT[:, c, t0:t0 + n], func=AF.Identity,
                    scale=scaleT[:, w, c:c + 1], bias=shift_fn(w)[:, c:c + 1]),
                    reads=["xT", "modT", "s1T", "ghT", "bhT"], writes=["hT"])

    def _sub_scalars(self, l, sub, gate_i, gate_mul, nxt):
        T = self.T
        T.op("dve", lambda e: e.tensor_scalar_mul(out=self.gsT[:], in0=self.modT[:, :, gate_i, :], scalar1=gate_mul / ALPHA),
             reads=["modT"], writes=["gsT"])
        if nxt is not None:
            sh_i, sc_i = nxt
            T.op("dve", lambda e: e.tensor_scalar_add(out=self.ghT[:], in0=self.modT[:, :, sc_i, :], scalar1=1.0),
                 reads=["modT"], writes=["ghT"])
            T.op("dve", lambda e: e.tensor_tensor(out=self.bhT[:], in0=self.ghT[:],
                                                  in1=self.lnbT[:, sub, :].unsqueeze(1).broadcast_to([128, 2, NCH]), op=ALU.mult),
                 reads=["ghT", "lnbT"], writes=["bhT"])
            T.op("dve", lambda e: e.tensor_tensor(out=self.bhT[:], in0=self.bhT[:], in1=self.modT[:, :, sh_i, :], op=ALU.add),
                 reads=["modT"], writes=["bhT"])
            T.op("dve", lambda e: e.tensor_tensor(out=self.ghT[:], in0=self.ghT[:],
                                                  in1=self.lngT[:, sub, :].unsqueeze(1).broadcast_to([128, 2, NCH]), op=ALU.mult),
                 reads=["lngT"], writes=["ghT"])

    def _ln_tile(self, l, sub, ti, has_next, les):
        T = self.T
        t0, n = TILES[ti]
        w = 0 if ti == 0 else 1
        mean, rstd, tmp = self.ln_mean, self.ln_rstd, self.ln_tmp
        T.op("act", lambda e: e.activation(out=mean[:], in_=self.ps[6][:], func=AF.Copy, scale=1.0 / D),
             reads=["ps6"], writes=["ln_mean"])
        T.op("dve", lambda e: e.tensor_tensor(out=tmp[:], in0=mean[:], in1=mean[:], op=ALU.mult),
             reads=["ln_mean"], writes=["ln_tmp"])
        T.op("dve", lambda e: e.scalar_tensor_tensor(out=rstd[:], in0=self.ps[7][:], scalar=1.0 / D, in1=tmp[:],
                                                     op0=ALU.mult, op1=ALU.subtract),
             reads=["ps7", "ln_tmp"], writes=["ln_rstd"])
        T.op("act", lambda e: e.activation(out=rstd[:], in_=rstd[:], func=AF.Sqrt, bias=self.eps_t[:, 0:1], scale=1.0),
             reads=["ln_rstd", "eps_t"], writes=["ln_rstd"])
        T.op("dve", lambda e: e.reciprocal(out=rstd[:], in_=rstd[:]), reads=["ln_rstd"], writes=["ln_rstd"])
        for c in range(NCH):
            tb = self.ln_t[c % 2]
            tk = "ln_t%d" % (c % 2)
            T.op("dve", lambda e, c=c, tb=tb: e.tensor_tensor(out=tb[:], in0=self.xT[:, c, t0:t0 + n], in1=mean[:], op=ALU.subtract),
                 reads=["xT", "ln_mean"], writes=[tk])
            T.op("pool", lambda e, tb=tb: e.tensor_tensor(out=tb[:], in0=tb[:], in1=rstd[:], op=ALU.mult),
                 reads=["ln_rstd", tk], writes=[tk])
            T.op("act", lambda e, c=c, tb=tb: e.activation(out=self.xT[:, c, t0:t0 + n], in_=tb[:], func=AF.Identity,
                                                           scale=self.lngT[:, sub, c:c + 1], bias=self.lnbT[:, sub, c:c + 1]),
                 reads=[tk, "lngT", "lnbT"], writes=["xT"])
            if has_next:
                T.op("dve", lambda e, c=c, tb=tb: e.tensor_scalar(out=self.hT[:, c, t0:t0 + n], in0=tb[:],
                                                                  scalar1=self.ghT[:, w, c:c + 1], scalar2=self.bhT[:, w, c:c + 1],
                                                                  op0=ALU.mult, op1=ALU.add),
                     reads=[tk, "ghT", "bhT"], writes=["hT"])

    def _ffn(self, l, which):
        nc, T = self.nc, self.T
        sub = 0 if which == 0 else 2
        gate_i = 2 if which == 0 else 8
        nxt = (3, 4) if which == 0 else None
        has_next = which == 0
        self._sub_scalars(l, sub, gate_i, 0.5, nxt)
        w_in = self.ffn_w_in[which][l].rearrange("(kc p) (ag n) -> p kc ag n", p=128, ag=2)
        w_out = self.ffn_w_out[which][l].rearrange("(j p) n -> p j n", p=128)
        with ExitStack() as les:
            uT = self.sb("uT", [128, NJ, NTOK], BF16, es=les)
            sl = [self.sb("silu%d" % i, [128, 512], F32, es=les) for i in range(2)]
            self._ln_alloc(les)
            cnt = 0
            for jp in range(NJ // 2):
                dst, key = self.wload(lambda s_: s_.rearrange("p (ag kc n) -> p kc ag n", kc=NCH, ag=2),
                                      w_in[:, :, :, jp * 256:(jp + 1) * 256], split=2)
                for jj in range(2):
                    j = jp * 2 + jj
                    for ti, (t0, n) in enumerate(TILES):
                        ba, bg = self.ps[(cnt % 2) * 2], self.ps[(cnt % 2) * 2 + 1]
                        ka, kg = "ps%d" % ((cnt % 2) * 2), "ps%d" % ((cnt % 2) * 2 + 1)
                        for (bank, bk, ag) in ((ba, ka, 0), (bg, kg, 1)):
                            self.mmg(bank[:, 0:n], bk,
                                     [(dst[:, kc, ag, jj * 128:(jj + 1) * 128], self.hT[:, kc, t0:t0 + n]) for kc in range(NCH)],
                                     reads=[key, "hT"])
                        sb_ = sl[cnt % 2]
                        sk = "silu%d" % (cnt % 2)
                        T.op("act", lambda e, ba=ba, sb_=sb_, n=n: e.activation(out=sb_[:, 0:n], in_=ba[:, 0:n], func=AF.Silu),
                             reads=[ka], writes=[sk])
                        T.op("dve", lambda e, bg=bg, sb_=sb_, j=j, t0=t0, n=n: e.tensor_tensor(
                            out=uT[:, j, t0:t0 + n], in0=bg[:, 0:n], in1=sb_[:, 0:n], op=ALU.mult),
                            reads=[kg, sk], writes=["uT"])
                        cnt += 1
            for mp in range(4):
                pcs = []
                for jh in range(2):
                    pcs.append(self.wload(lambda s_: s_[:, 0:11 * 256].rearrange("p (j n) -> p j n", j=11),
                                          w_out[:, jh * 11:(jh + 1) * 11, mp * 256:(mp + 1) * 256]))
                for ti, (t0, n) in enumerate(TILES):
                    for mm_ in range(2):
                        c = mp * 2 + mm_
                        bi = (ti * 2 + mm_) % 6
                        bank, bk = self.ps[bi], "ps%d" % bi
                        self.mmg(bank[:, 0:n], bk,
                                 [(pcs[j // 11][0][:, j % 11, mm_ * 128:(mm_ + 1) * 128], uT[:, j, t0:t0 + n]) for j in range(NJ)],
                                 reads=[pcs[0][1], pcs[1][1], "uT"])
                        w = 0 if ti == 0 else 1
                        T.op("dve", lambda e, bank=bank, c=c, t0=t0, n=n, w=w: e.scalar_tensor_tensor(
                            out=self.xT[:, c, t0:t0 + n], in0=bank[:, 0:n], scalar=self.gsT[:, w, c:c + 1],
                            in1=self.xT[:, c, t0:t0 + n], op0=ALU.mult, op1=ALU.add),
                            reads=[bk, "gsT", "xT"], writes=["xT"])
            self._ln_all(l, sub, has_next)
            T.barrier()

    def _ln_alloc(self, les):
        self.ln_zb = [self.sb("ln_zb%d" % i, [128, 512], BF16, es=les) for i in range(2)]
        self.ln_zq = [self.sb("ln_zq%d" % i, [128, 512], BF16, es=les) for i in range(2)]
        self.ln_t = [self.sb("ln_t%d" % i, [128, 512], F32, es=les) for i in range(2)]
        self.ln_mean = self.sb("ln_mean", [128, 512], F32, es=les)
        self.ln_rstd = self.sb("ln_rstd", [128, 512], F32, es=les)
        self.ln_tmp = self.sb("ln_tmp", [128, 512], F32, es=les)

    def _ln_all(self, l, sub, has_next):
        T = self.T
        for ti, (t0, n) in enumerate(TILES):
            for c in range(NCH):
                zb = self.ln_zb[c % 2]
                zq = self.ln_zq[c % 2]
                kb, kq = "ln_zb%d" % (c % 2), "ln_zq%d" % (c % 2)
                T.op("act", lambda e, c=c, zb=zb: e.activation(out=zb[:], in_=self.xT[:, c, t0:t0 + n], func=AF.Copy),
                     reads=["xT"], writes=[kb])
                T.op("pool", lambda e, c=c, zq=zq: e.tensor_tensor(out=zq[:], in0=self.xT[:, c, t0:t0 + n],
                                                                   in1=self.xT[:, c, t0:t0 + n], op=ALU.mult),
                     reads=["xT"], writes=[kq])
                T.op("pe", lambda e, c=c, zb=zb: e.matmul(self.ps[6][:], self.ones_bf[:], zb[:], start=(c == 0), stop=(c == NCH - 1)),
                     reads=[kb, "ones_bf"], writes=["ps6"], same=False)
                T.op("pe", lambda e, c=c, zq=zq: e.matmul(self.ps[7][:], self.ones_bf[:], zq[:], start=(c == 0), stop=(c == NCH - 1)),
                     reads=[kq, "ones_bf"], writes=["ps7"], same=False)
            self._ln_tile(l, sub, ti, has_next, None)

    def _finish(self):
        nc, T = self.nc, self.T
        with ExitStack() as les:
            ys = [self.sb("ystage%d" % i, [128, D], F32, es=les) for i in range(2)]
            import os
            for tt in range(int(os.environ.get("DBG_NT", NTOK // 128))):
                s = ys[tt % 2]
                key = "ystage%d" % (tt % 2)
                for half in range(2):
                    bi = (tt * 2 + half) % 4
                    bank, bkey = self.ps[bi], "ps%d" % bi
                    for q in range(0 if os.environ.get("DBG_NOTR") else 4):
                        c = half * 4 + q
                        T.op("pe", lambda e, c=c, q=q, bank=bank, tt=tt: e.transpose(
                            bank[:, q * 128:(q + 1) * 128], self.xT[:, c, tt * 128:(tt + 1) * 128], self.ident_f[:]),
                            reads=["xT", "ident_f"], writes=[bkey] if q in (0, 3) else [], same=False)
                    if half == 0 or os.environ.get("DBG_NOACT"):
                        T.op("dve", lambda e, bank=bank, s=s, half=half: e.tensor_copy(out=s[:, half * 512:(half + 1) * 512], in_=bank[:]),
                             reads=[bkey], writes=[key if half == 0 else key + "h"])
                    else:
                        T.op("act", lambda e, bank=bank, s=s: e.activation(out=s[:, 512:1024], in_=bank[:], func=AF.Copy),
                             reads=[bkey], writes=[key + "h"])
                T.dma("sp", self.y_out[tt * 128:(tt + 1) * 128, :], s[:], reads=[key, key + "h"], writes=["y_out%d" % tt])
            T.wait_all("sp")


def _prep_inputs(inputs):
    consts = host_consts()
    shared = {}
    for k in ("w_ada", "b_ada", "ffn1_w_in", "ffn1_w_out", "ffn2_w_in", "ffn2_w_out", "ln_g", "ln_b"):
        shared[k] = np.ascontiguousarray(inputs[k], dtype=np.float32)
    shared["c_ones"] = consts["ones"]
    shared["c_ident"] = consts["ident"]
    shared["c_blockones"] = consts["blockones"]
    maps = []
    for core in range(8):
        m = dict(shared)
        xp = inputs["x_prompt"][2 * core:2 * core + 2].reshape(512, D)
        xs = inputs["x_sample"][core]
        m["x_tok"] = np.ascontiguousarray(np.concatenate([xp, xs], axis=0), dtype=np.float32)
        m["cvec"] = np.ascontiguousarray(np.stack([inputs["c_ctx"], inputs["c"][core]], axis=0), dtype=np.float32)
        maps.append(m)
    return maps


PROMPT_SEQS = [(0, 256), (256, 256)]
SAMPLE = (512, 1024)
NA_R = {0: (0, 5), 1: (0, 7), 2: (0, 9), 3: (0, 11), 4: (5, 15), 5: (7, 15), 6: (9, 15), 7: (11, 15)}
NVA, NVB = 12, 11


def _na_tables_idx():
    flat = np.zeros((128, NVA + NVB, 64), np.int64)
    mask = np.zeros((128, NVA + NVB, 64), np.float32)
    qc = np.arange(64)
    cs = np.clip(qc - 8, 0, 48)
    for half in range(2):
        for kc in range(64):
            p = half * 64 + kc
            colv = (kc >= cs) & (kc < cs + 16)
            dc = np.clip(kc - qc + 15, 0, 30)
            for v in range(NVA + NVB):
                idx = (v - 6) if v < NVA else (v - NVA - 3)
                dr = half - idx + 7
                rowv = True
                if v < NVA and idx == 5 and half == 0:
                    rowv = False
                if v >= NVA and idx == -3 and half == 1:
                    rowv = False
                drc = min(max(dr, 0), 14)
                flat[p, v] = drc * 31 + dc
                mask[p, v] = (colv & rowv).astype(np.float32)
    return flat, mask


def _rope_tables():
    t = np.arange(1024)
    n_freq = 16
    inv = 10000.0 ** (-np.arange(n_freq, dtype=np.float32) / n_freq)
    rows = (t // 64).astype(np.float32)
    cols = (t % 64).astype(np.float32)
    ang = np.concatenate([rows[:, None] * inv, cols[:, None] * inv], axis=-1)
    cos32, sin32 = np.cos(ang).astype(np.float32), np.sin(ang).astype(np.float32)
    cosT = np.zeros((128, 1024), np.float32)
    sinT = np.zeros((128, 1024), np.float32)
    for p in range(128):
        d = p % 64
        cosT[p] = cos32[:, d % 32]
        sinT[p] = sin32[:, d % 32] * (-1.0 if d < 32 else 1.0)
    return cosT, sinT


def _mixer_consts():
    c = {}
    a = np.arange(128)
    c["c_mlow"] = (a[None, :] <= a[:, None]).astype(np.float32)
    c["c_mup"] = (a[:, None] <= a[None, :]).astype(np.float32)
    pm = np.zeros((128, 128), np.float32)
    for m in range(128):
        k = (m & 64) | ((m + 32) & 63)
        pm[k, m] = 1.0
    c["c_swap"] = pm
    c["c_ropecos"], c["c_ropesin"] = _rope_tables()
    return c


def _mx_declare(self):
    L = DEPTH
    self.w_in = self.din("w_in", [L, D, IN_COLS])
    self.proj = [self.din("proj_a", [L, 512, D]), self.din("proj_b", [L, 512, D]), self.din("proj_c", [L, 512, D])]
    self.w_out = self.din("w_out", [L, D, D])
    self.sink = self.din("attn_sink", [L, 8])
    self.cak = self.din("cache_attn_k", [L, 256, 128])
    self.cav = self.din("cache_attn_v", [L, 256, 128])
    self.cbk = self.din("cache_na_k", [L, 256, 512])
    self.cbv = self.din("cache_na_v", [L, 256, 512])
    self.natab = self.din("na_tab", [L, 8, 128, (NVA + NVB) * 64])
    for nm in ("c_mlow", "c_mup", "c_swap"):
        setattr(self, nm, self.din(nm, [128, 128]))
    self.c_ropecos = self.din("c_ropecos", [128, 1024])
    self.c_ropesin = self.din("c_ropesin", [128, 1024])
    self.o_nak = self.dout("o_nak", [2, L, 256, 128])
    self.o_nav = self.dout("o_nav", [2, L, 256, 128])
    self.o_nbk = self.dout("o_nbk", [2, L, 256, 512])
    self.o_nbv = self.dout("o_nbv", [2, L, 256, 512])


def _mx_globals(self):
    T = self.T
    self.mlow = self.sb("mlow", [128, 128], BF16)
    self.mup = self.sb("mup", [128, 128], BF16)
    self.swapm = self.sb("swapm", [128, 128], BF16)
    T.dma("pool", self.mlow[:], self.c_mlow, writes=["mlow"])
    T.dma("pool", self.mup[:], self.c_mup, writes=["mup"])
    T.dma("pool", self.swapm[:], self.c_swap, writes=["swapm"])


def _mixer(self, l):
    T = self.T
    self._sub_scalars(l, 1, 5, 1.0, (6, 7))
    with ExitStack() as mes:
        self.mergedT = self.sb("mergedT", [128, NCH, NTOK], BF16, es=mes)
        self.sgb = [self.sb("sgb%d" % i, [128, 512], F32, es=mes) for i in range(2)]
        self.mtmp = [self.sb("mtmp%d" % i, [128, 512], F32, es=mes) for i in range(2)]
        self._attn(l, 0)
        self._attn(l, 1)
        if hasattr(self, "_rwkv"):
            self._rwkv(l)
        self._mix_out(l)
        T.barrier()


def _merge_branch(self, l, g, o_chunks, okey, prow0, t0, ntok, first):
    T = self.T
    nk = len(o_chunks)
    wg = self.w_in[l].rearrange("(kc p) n -> p kc n", p=128)
    wp = self.proj[g][l].rearrange("(kc p) n -> p kc n", p=128)
    k0 = prow0 // 128
    tiles = [(t0 + a, min(512, ntok - a)) for a in range(0, ntok, 512)]
    cnt = getattr(self, "_mb_cnt", 0)
    for m in range(NCH):
        s = self.wslot
        self.wslot = (self.wslot + 1) % self.NSLOT
        key = "wring%d" % s
        gv = self.wring[:, s, 0:1024].rearrange("p (kc n) -> p kc n", kc=NCH)
        pv = self.wring[:, s, 1024:1024 + nk * 128].rearrange("p (kc n) -> p kc n", kc=nk)
        gc0 = G_OFF + g * 1024 + m * 128
        T.dma("pool", [gv, pv], [wg[:, :, gc0:gc0 + 128], wp[:, k0:k0 + nk, m * 128:(m + 1) * 128]], writes=[key])
        for (tt0, n) in tiles:
            ba, bp = self.ps[(cnt % 2) * 2], self.ps[(cnt % 2) * 2 + 1]
            ka, kp = "ps%d" % ((cnt % 2) * 2), "ps%d" % ((cnt % 2) * 2 + 1)
            self.mmg(ba[:, 0:n], ka, [(gv[:, kc, :], self.hT[:, kc, tt0:tt0 + n]) for kc in range(NCH)], reads=[key, "hT"])
            self.mmg(bp[:, 0:n], kp, [(pv[:, kc, :], o_chunks[kc](tt0, n)) for kc in range(nk)], reads=[key, okey])
            sg = self.sgb[cnt % 2]
            sk = "sgb%d" % (cnt % 2)
            T.op("act", lambda e, ba=ba, sg=sg, n=n: e.activation(out=sg[:, 0:n], in_=ba[:, 0:n], func=AF.Sigmoid),
                 reads=[ka], writes=[sk])
            if first:
                T.op("dve", lambda e, bp=bp, sg=sg, m=m, tt0=tt0, n=n: e.tensor_tensor(
                    out=self.mergedT[:, m, tt0:tt0 + n], in0=bp[:, 0:n], in1=sg[:, 0:n], op=ALU.mult),
                    reads=[kp, sk], writes=["mergedT"])
            else:
                mt = self.mtmp[cnt % 2]
                mk = "mtmp%d" % (cnt % 2)
                T.op("dve", lambda e, bp=bp, sg=sg, mt=mt, n=n: e.tensor_tensor(out=mt[:, 0:n], in0=bp[:, 0:n], in1=sg[:, 0:n], op=ALU.mult),
                     reads=[kp, sk], writes=[mk])
                T.op("pool", lambda e, mt=mt, m=m, tt0=tt0, n=n: e.tensor_tensor(
                    out=self.mergedT[:, m, tt0:tt0 + n], in0=self.mergedT[:, m, tt0:tt0 + n], in1=mt[:, 0:n], op=ALU.add),
                    reads=[mk, "mergedT"], writes=["mergedT"])
            cnt += 1
    self._mb_cnt = cnt


def _mix_out(self, l):
    T = self.T
    wv = self.w_out[l].rearrange("(kc p) n -> p kc n", p=128)
    with ExitStack() as les:
        self._ln_alloc(les)
        cnt = 0
        for piece in range(2):
            dst, key = self.wload(lambda s_: s_.rearrange("p (kc n) -> p kc n", kc=NCH), wv[:, :, piece * 512:(piece + 1) * 512])
            for ti, (t0, n) in enumerate(TILES):
                w = 0 if ti == 0 else 1
                for q in range(4):
                    c = piece * 4 + q
                    bank, bk = self.ps[cnt % 4], "ps%d" % (cnt % 4)
                    self.mmg(bank[:, 0:n], bk, [(dst[:, kc, q * 128:(q + 1) * 128], self.mergedT[:, kc, t0:t0 + n]) for kc in range(NCH)],
                             reads=[key, "mergedT"])
                    T.op("dve", lambda e, bank=bank, c=c, t0=t0, n=n, w=w: e.scalar_tensor_tensor(
                        out=self.xT[:, c, t0:t0 + n], in0=bank[:, 0:n], scalar=self.gsT[:, w, c:c + 1],
                        in1=self.xT[:, c, t0:t0 + n], op0=ALU.mult, op1=ALU.add),
                        reads=[bk, "gsT", "xT"], writes=["xT"])
                    cnt += 1
        self._ln_all(l, 1, True)
        T.barrier()


def _attn_head(self, specs, q_fn, ncols, out_ap, rows, sink_ap, okey):
    T = self.T
    hc = self._ah_cnt
    self._ah_cnt += 1
    O, Ok = self.ps[2 + (hc % 2) * 2], "ps%d" % (2 + (hc % 2) * 2)
    Dn, Dk = self.ps[3 + (hc % 2) * 2], "ps%d" % (3 + (hc % 2) * 2)
    r0, r1 = rows
    n = len(specs)
    pend = None
    for idx in range(n + 1):
        if idx < n:
            sp = specs[idx]
            sc = self._as_cnt
            self._as_cnt += 1
            sbk, sk = self.ps[sc % 2], "ps%d" % (sc % 2)
            w = sp["c1"] - sp["c0"]
            self.mmg(sbk[:, 0:w], sk, [(sp["kT"], q_fn(sp["c0"], sp["c1"]))], reads=sp["keys"])
            pt, pk = self.ptb[sc % 4], "ptb%d" % (sc % 4)
            T.op("act", lambda e, sbk=sbk, pt=pt, w=w: e.activation(out=pt[:, 0:w], in_=sbk[:, 0:w], func=AF.Exp, scale=SCALE),
                 reads=[sk], writes=[pk])
            for (a, b, mk_ap, mkey) in sp.get("masks", ()):
                T.op("dve", lambda e, pt=pt, a=a, b=b, mk_ap=mk_ap: e.tensor_tensor(out=pt[:, a:b], in0=pt[:, a:b], in1=mk_ap, op=ALU.mult),
                     reads=[pk, mkey], writes=[pk])
            cur = (sp, pt, pk, w, idx)
        else:
            cur = None
        if pend is not None:
            sp, pt, pk, w, i = pend
            first, last = (i == 0), (i == n - 1)
            for (bank, bk, lt) in ((O, Ok, sp["v"]), (Dn, Dk, self.ones_bf[:])):
                T.op("pe", lambda e, bank=bank, lt=lt, pt=pt, w=w, sp=sp, first=first, last=last: e.matmul(
                    bank[:, sp["c0"]:sp["c1"]], lt, pt[:, 0:w], start=first, stop=last),
                    reads=[pk, "ones_bf"] + sp["keys"], writes=[bk] if (first or last) else [], same=False)
        pend = cur
    rc, rk = self.rcb[hc % 2], "rcb%d" % (hc % 2)
    import os
    if "dbg_misc" in self.dram_out and os.environ.get("DBG_HEAD") and int(os.environ["DBG_HEAD"]) == hc and not getattr(self, "_dbg_done", False):
        self._dbg_done = True
        T.op("dve", lambda e: e.tensor_copy(out=self.mtmp[0][:, 0:ncols], in_=Dn[:, 0:ncols]), reads=[Dk], writes=["mtmp0"])
        T.dma("sp", self.dram_out["dbg_misc"][:, 1024:1024 + ncols], self.mtmp[0][:, 0:ncols], reads=["mtmp0"], writes=["dbgm2"])
        T.op("dve", lambda e: e.tensor_copy(out=self.mtmp[1][:, 0:ncols], in_=O[:, 0:ncols]), reads=[Ok], writes=["mtmp1"])
        T.dma("sp", self.dram_out["dbg_misc"][:, 1536:1536 + ncols], self.mtmp[1][:, 0:ncols], reads=["mtmp1"], writes=["dbgm3"])
    if sink_ap is not None:
        T.op("dve", lambda e: e.tensor_scalar(out=rc[r0:r1, 0:ncols], in0=Dn[r0:r1, 0:ncols], scalar1=sink_ap, scalar2=None, op0=ALU.add),
             reads=[Dk, "esink"], writes=[rk])
        T.op("dve", lambda e: e.reciprocal(out=rc[r0:r1, 0:ncols], in_=rc[r0:r1, 0:ncols]), reads=[rk], writes=[rk])
    else:
        T.op("dve", lambda e: e.reciprocal(out=rc[r0:r1, 0:ncols], in_=Dn[r0:r1, 0:ncols]), reads=[Dk], writes=[rk])
    T.op("dve", lambda e: e.tensor_tensor(out=out_ap, in0=O[r0:r1, 0:ncols], in1=rc[r0:r1, 0:ncols], op=ALU.mult),
         reads=[Ok, rk], writes=[okey])


for _f in (_mx_declare, _mx_globals, _mixer, _merge_branch, _mix_out, _attn_head):
    setattr(Builder, _f.__name__, _f)


def _attn(self, l, which):
    nc, T = self.nc, self.T
    A = (which == 0)
    nkc = 2 if A else 4
    qoff = A_OFF if A else B_OFF
    wv = self.w_in[l].rearrange("(kc p) n -> p kc n", p=128)
    self._ah_cnt = 0
    self._as_cnt = 0
    with ExitStack() as aes:
        qT = self.sb("qT", [128, 4, NTOK], BF16, es=aes)
        kT = self.sb("kT", [128, nkc, NTOK], BF16, es=aes)
        VW = 256 if A else 512
        vtm = self.sb("vtm", [128, 12, VW], BF16, es=aes)
        ckT = self.sb("ckT", [128, nkc, 256], BF16, es=aes)
        cv = self.sb("cv", [128, 2, VW], BF16, es=aes)
        oT = self.sb("oT", [128, 4, NTOK], BF16, es=aes)
        self.ptb = [self.sb("ptb%d" % i, [128, 512], BF16, es=aes) for i in range(4)]
        self.rcb = [self.sb("rcb%d" % i, [128, 512], F32, es=aes) for i in range(2)]
        ostg = [self.sb("ostg%d" % i, [128, 512], F32, es=aes) for i in range(2)]
        if A:
            rcos = self.sb("rcos", [128, 1024], BF16, es=aes)
            rsin = self.sb("rsin", [128, 1024], BF16, es=aes)
            esink = self.sb("esink", [128, 8], F32, es=aes)
            T.dma("pool", rcos[:], self.c_ropecos, writes=["rcos"])
            T.dma("pool", rsin[:], self.c_ropesin, writes=["rsin"])
            T.dma("sp", esink[:], self.sink[l].partition_broadcast(128), writes=["esink"])
            T.op("act", lambda e: e.activation(out=esink[:], in_=esink[:], func=AF.Exp), reads=["esink"], writes=["esink"])
        else:
            etab = self.sb("etab", [128, (NVA + NVB) * 64], BF16, es=aes)
        pcnt = [0]

        def bankof():
            i = pcnt[0] % 2
            pcnt[0] += 1
            return self.ps[i], "ps%d" % i

        def evac(i, out_ap, in_ap, reads, writes):
            if i % 2 == 0:
                T.op("act", lambda e: e.activation(out=out_ap, in_=in_ap, func=AF.Copy), reads=reads, writes=writes)
            else:
                T.op("dve", lambda e: e.tensor_copy(out=out_ap, in_=in_ap), reads=reads, writes=writes)

        dst, key = self.wload(lambda s_: s_.rearrange("p (kc n) -> p kc n", kc=NCH), wv[:, :, qoff:qoff + 512])
        for c in range(4):
            for (t0, n) in TILES:
                bank, bk = bankof()
                self.mmg(bank[:, 0:n], bk, [(dst[:, kc, c * 128:(c + 1) * 128], self.hT[:, kc, t0:t0 + n]) for kc in range(NCH)], reads=[key, "hT"])
                evac(pcnt[0], qT[:, c, t0:t0 + n], bank[:, 0:n], [bk], ["qT"])
        import os
        stopat = os.environ.get("DBG_STOP", "")
        if stopat == "q":
            T.op("dve", lambda e: e.memset(oT[:], 0.0), writes=["oT"])
            self._merge_branch(l, which, [(lambda t0, n, c=c: oT[:, c, t0:t0 + n]) for c in range(4)], "oT", 0, 0, NTOK, first=A)
            return
        if A:
            s = self.wslot
            self.wslot = (self.wslot + 1) % self.NSLOT
            key = "wring%d" % s
            dst = self.wring[:, s, 0:NCH * 256].rearrange("p (kc kv dup d) -> p kc kv dup d", kc=NCH, kv=2, dup=2)
            T.dma("pool", [dst[:, :, kv, dup, :] for kv in range(2) for dup in range(2)],
                  [wv[:, :, 512 + kv * 64:512 + (kv + 1) * 64] for kv in range(2) for dup in range(2)], writes=[key])
            kw = lambda kc, c: dst[:, kc, c, :, :]
        else:
            dst, key = self.wload(lambda s_: s_.rearrange("p (kc n) -> p kc n", kc=NCH), wv[:, :, B_OFF + 512:B_OFF + 1024])
            kw = lambda kc, c: dst[:, kc, c * 128:(c + 1) * 128]
        for c in range(nkc):
            for (t0, n) in TILES:
                bank, bk = bankof()
                self.mmg(bank[:, 0:n], bk, [(kw(kc, c), self.hT[:, kc, t0:t0 + n]) for kc in range(NCH)], reads=[key, "hT"])
                evac(pcnt[0], kT[:, c, t0:t0 + n], bank[:, 0:n], [bk], ["kT"])
        if stopat == "k":
            T.op("dve", lambda e: e.memset(oT[:], 0.0), writes=["oT"])
            self._merge_branch(l, which, [(lambda t0, n, c=c: oT[:, c, t0:t0 + n]) for c in range(4)], "oT", 0, 0, NTOK, first=A)
            return
        ocnt = [0]

        def out_rows(dram_ap, st, width, bank):
            if os.environ.get("DBG_NOOUTROWS"):
                return
            og, ogk = ostg[ocnt[0] % 2], "ostg%d" % (ocnt[0] % 2)
            ocnt[0] += 1
            T.op("dve", lambda e: e.tensor_copy(out=og[:, 0:width], in_=bank), reads=[bk_cur[0]], writes=[ogk])
            sq, tl = st // 2, (st % 2) * 128
            if os.environ.get("DBG_NOOUTDMA"):
                return
            T.dma("sp", dram_ap[sq, l, tl:tl + 128, :], og[:, 0:width], reads=[ogk], writes=["outrows%d" % ocnt[0]])

        bk_cur = [None]
        if A:
            dst, key = self.wload(lambda s_: s_[:, 0:NCH * 256].rearrange("p (kc n) -> p kc n", kc=NCH), wv[:, :, 512:768])
            for st in range(12):
                bank, bk = bankof()
                bk_cur[0] = bk
                self.mmg(bank[:, 0:256], bk, [(self.hT[:, kc, st * 128:(st + 1) * 128], dst[:, kc, :]) for kc in range(NCH)], reads=[key, "hT"])
                if st < 4:
                    out_rows(self.o_nak, st, 128, bank[:, 0:128])
                    out_rows(self.o_nav, st, 128, bank[:, 128:256])
                for dup in range(2):
                    T.op("act", lambda e, bank=bank, st=st, dup=dup: e.activation(
                        out=vtm[:, st, :].rearrange("p (kv dup d) -> p kv dup d", kv=2, dup=2)[:, :, dup, :],
                        in_=bank[:, 128:256].rearrange("p (kv d) -> p kv d", kv=2), func=AF.Copy),
                        reads=[bk], writes=["vtm"])
        else:
            dstk, keyk = self.wload(lambda s_: s_.rearrange("p (kc n) -> p kc n", kc=NCH), wv[:, :, B_OFF + 512:B_OFF + 1024])
            for st in range(4):
                bank, bk = bankof()
                bk_cur[0] = bk
                self.mmg(bank[:, 0:512], bk, [(self.hT[:, kc, st * 128:(st + 1) * 128], dstk[:, kc, :]) for kc in range(NCH)], reads=[keyk, "hT"])
                out_rows(self.o_nbk, st, 512, bank[:, 0:512])
            dst, key = self.wload(lambda s_: s_.rearrange("p (kc n) -> p kc n", kc=NCH), wv[:, :, B_OFF + 1024:B_OFF + 1536])
            for st in range(12):
                bank, bk = bankof()
                bk_cur[0] = bk
                self.mmg(bank[:, 0:512], bk, [(self.hT[:, kc, st * 128:(st + 1) * 128], dst[:, kc, :]) for kc in range(NCH)], reads=[key, "hT"])
                if st < 4:
                    out_rows(self.o_nbv, st, 512, bank[:, 0:512])
                T.op("act", lambda e, bank=bank, st=st: e.activation(out=vtm[:, st, :], in_=bank[:, 0:512], func=AF.Copy),
                     reads=[bk], writes=["vtm"])
        import os
        if os.environ.get("DBG_NOCACHE"):
            pass
        elif A:
            for ct in range(2):
                T.dma("pool", [cv[:, ct, :].rearrange("p (kv dup d) -> p kv dup d", kv=2, dup=2)[:, :, dup, :] for dup in range(2)],
                      [self.cav[l, ct * 128:(ct + 1) * 128, :].rearrange("t (kv d) -> t kv d", kv=2) for dup in range(2)], writes=["cv"])
                og, ogk = ostg[ct], "ostg%d" % ct
                T.dma("sp", [og[:, 0:256].rearrange("p (kv dup d) -> p kv dup d", kv=2, dup=2)[:, :, dup, :] for dup in range(2)],
                      [self.cak[l, ct * 128:(ct + 1) * 128, :].rearrange("t (kv d) -> t kv d", kv=2) for dup in range(2)], writes=[ogk])
                for kv in range(2):
                    bank, bk = self.ps[6 + kv], "ps%d" % (6 + kv)
                    T.op("pe", lambda e, bank=bank, og=og, kv=kv: e.transpose(bank[:, 0:128], og[:, kv * 128:(kv + 1) * 128], self.ident_f[:]),
                         reads=[ogk, "ident_f"], writes=[bk], same=False)
                    T.op("dve", lambda e, bank=bank, kv=kv, ct=ct: e.tensor_copy(out=ckT[:, kv, ct * 128:(ct + 1) * 128], in_=bank[:, 0:128]),
                         reads=[bk], writes=["ckT"])
        else:
            for ct in range(2):
                T.dma("pool", cv[:, ct, :], self.cbv[l, ct * 128:(ct + 1) * 128, :], writes=["cv"])
                og, ogk = ostg[ct], "ostg%d" % ct
                T.dma("sp", og[:], self.cbk[l, ct * 128:(ct + 1) * 128, :], writes=[ogk])
                bank, bk = self.ps[6 + ct], "ps%d" % (6 + ct)
                for c in range(4):
                    T.op("pe", lambda e, bank=bank, og=og, c=c: e.transpose(bank[:, c * 128:(c + 1) * 128], og[:, c * 128:(c + 1) * 128], self.ident_f[:]),
                         reads=[ogk, "ident_f"], writes=[bk] if c in (0, 3) else [], same=False)
                T.op("dve", lambda e, bank=bank, ct=ct: e.tensor_copy(out=ckT[:, :, ct * 128:(ct + 1) * 128],
                                                                      in_=bank[:].rearrange("p (c t) -> p c t", c=4)),
                     reads=[bk], writes=["ckT"])
        if "dbg_misc" in self.dram_out and l == 0 and A and os.environ.get("DBG_DUMPM"):
            T.op("dve", lambda e: e.tensor_copy(out=self.rcb[0][:, 0:128], in_=self.mlow[:]), reads=["mlow"], writes=["rcb0"])
            T.op("dve", lambda e: e.tensor_copy(out=self.rcb[0][:, 128:256], in_=self.mup[:]), reads=["mup"], writes=["rcb0"])
            T.op("dve", lambda e: e.tensor_copy(out=self.rcb[0][:, 256:384], in_=self.swapm[:]), reads=["swapm"], writes=["rcb0"])
            T.op("dve", lambda e: e.tensor_copy(out=self.rcb[0][:, 384:512], in_=rcos[:, 0:128]), reads=["rcos"], writes=["rcb0"])
            T.dma("sp", self.dram_out["dbg_misc"][:, 0:512], self.rcb[0][:], reads=["rcb0"], writes=["dbgm0"])
        elif "dbg_misc" in self.dram_out and l == 0 and A:
            T.op("dve", lambda e: e.tensor_copy(out=self.rcb[0][:], in_=ckT[:].rearrange("p a b -> p (a b)")), reads=["ckT"], writes=["rcb0"])
            T.dma("sp", self.dram_out["dbg_misc"][:, 0:512], self.rcb[0][:], reads=["rcb0"], writes=["dbgm0"])
            T.op("dve", lambda e: e.tensor_copy(out=self.rcb[1][:], in_=cv[:].rearrange("p a b -> p (a b)")), reads=["cv"], writes=["rcb1"])
            T.dma("sp", self.dram_out["dbg_misc"][:, 512:1024], self.rcb[1][:], reads=["rcb1"], writes=["dbgm1"])
        import os
        if A and not os.environ.get("DBG_NOROPE"):
            rc = 0
            for (arr, akey, nchunk) in ((qT, "qT", 4), (kT, "kT", 2)):
                for c in range(nchunk):
                    for qt in range(2):
                        t0 = 512 + qt * 512
                        bank, bk = self.ps[6 + rc % 2], "ps%d" % (6 + rc % 2)
                        rc += 1
                        x = arr[:, c, t0:t0 + 512]
                        self.mmg(bank[:], bk, [(self.swapm[:], x)], reads=[akey, "swapm"])
                        T.op("dve", lambda e, x=x, qt=qt: e.tensor_tensor(out=self.mtmp[0][:], in0=x, in1=rcos[:, qt * 512:(qt + 1) * 512], op=ALU.mult),
                             reads=[akey, "rcos"], writes=["mtmp0"])
                        T.op("dve", lambda e, bank=bank, qt=qt: e.tensor_tensor(out=self.mtmp[1][:], in0=bank[:], in1=rsin[:, qt * 512:(qt + 1) * 512], op=ALU.mult),
                             reads=[bk, "rsin"], writes=["mtmp1"])
                        T.op("pool", lambda e, x=x: e.tensor_tensor(out=x, in0=self.mtmp[0][:], in1=self.mtmp[1][:], op=ALU.add),
                             reads=["mtmp0", "mtmp1"], writes=[akey])
        import os
        if os.environ.get("DBG_NOHEADS"):
            T.op("dve", lambda e: e.memset(oT[:], 0.0), writes=["oT"])
        for hp in range(0 if os.environ.get("DBG_NOHEADS") else 4):
            for par in range(2):
                h = hp * 2 + par
                if not A:
                    T.dma("pool", etab[:], self.natab[l, h], writes=["etab"])
                    T.op("act", lambda e: e.activation(out=etab[:], in_=etab[:], func=AF.Exp), reads=["etab"], writes=["etab"])
                rows = (par * 64, par * 64 + 64)
                kc_ = (h // 4) if A else hp
                vsl = (lambda a: a[:, kc_ * 128:(kc_ + 1) * 128])
                sink_ap = esink[rows[0]:rows[1], h:h + 1] if A else None
                for sq in range(2):
                    b0 = sq * 256
                    specs = [dict(kT=kT[rows[0]:rows[1], kc_, b0 + kt * 128:b0 + (kt + 1) * 128], v=vsl(vtm[:, sq * 2 + kt, :]),
                                  c0=0, c1=256, keys=["kT", "vtm", "qT"]) for kt in range(2)]
                    self._attn_head(specs, lambda c0, c1, b0=b0: qT[rows[0]:rows[1], hp, b0 + c0:b0 + c1], 256,
                                    oT[rows[0]:rows[1], hp, b0:b0 + 256], rows, sink_ap, "oT")
                for qt in range(2):
                    b0 = 512 + qt * 512
                    specs = [dict(kT=ckT[rows[0]:rows[1], kc_, ct * 128:(ct + 1) * 128], v=vsl(cv[:, ct, :]), c0=0, c1=512,
                                  keys=["ckT", "cv", "qT"]) for ct in range(2)]
                    if A and os.environ.get("DBG_ANOLOCAL"):
                        pass
                    elif A:
                        for j in range(4 * qt - 1, 4 * qt + 5):
                            if j < 0 or j > 7:
                                continue
                            ilo, ihi = max(j - 1, 4 * qt), min(j + 1, 4 * qt + 3)
                            masks = []
                            for i in range(ilo, ihi + 1):
                                a = (i - ilo) * 128
                                if i == j + 1:
                                    masks.append((a, a + 128, self.mlow[:], "mlow"))
                                elif i == j - 1:
                                    masks.append((a, a + 128, self.mup[:], "mup"))
                            specs.append(dict(kT=kT[rows[0]:rows[1], kc_, 512 + j * 128:512 + (j + 1) * 128], v=vsl(vtm[:, 4 + j, :]),
                                              c0=(ilo - 4 * qt) * 128, c1=(ihi - 4 * qt + 1) * 128, masks=masks, keys=["kT", "vtm", "qT"]))
                    else:
                        for j in range(8):
                            ra, rb = NA_R[j]
                            lo, hi = max(ra, 8 * qt), min(rb, 8 * qt + 7)
                            if lo > hi:
                                continue
                            c0, c1 = (lo - 8 * qt) * 64, (hi - 8 * qt + 1) * 64
                            v0 = (lo - 2 * j + 6) if j <= 3 else (NVA + lo - 2 * j + 3)
                            masks = [(0, c1 - c0, etab[:, v0 * 64:v0 * 64 + (c1 - c0)], "etab")]
                            specs.append(dict(kT=kT[rows[0]:rows[1], kc_, 512 + j * 128:512 + (j + 1) * 128], v=vsl(vtm[:, 4 + j, :]),
                                              c0=c0, c1=c1, masks=masks, keys=["kT", "vtm", "qT"]))
                    self._attn_head(specs, lambda c0, c1, b0=b0: qT[rows[0]:rows[1], hp, b0 + c0:b0 + c1], 512,
                                    oT[rows[0]:rows[1], hp, b0:b0 + 512], rows, sink_ap, "oT")
        if "dbg_oT" in self.dram_out and l == 0:
            for c in range(4):
                for (t0, n) in TILES:
                    og, ogk = ostg[c % 2], "ostg%d" % (c % 2)
                    T.op("dve", lambda e, og=og, c=c, t0=t0, n=n: e.tensor_copy(out=og[:, 0:n], in_=oT[:, c, t0:t0 + n]), reads=["oT"], writes=[ogk])
                    T.dma("sp", self.dram_out["dbg_oT"][which, c, :, t0:t0 + n], og[:, 0:n], reads=[ogk], writes=["dbgo%d_%d_%d" % (which, c, t0)])
        self._merge_branch(l, which, [(lambda t0, n, c=c: oT[:, c, t0:t0 + n]) for c in range(4)], "oT", 0, 0, NTOK, first=A)
        T.barrier()


Builder._attn = _attn
_old_declare = Builder._declare
_old_globals = Builder._globals


def _declare2(self):
    _old_declare(self)
    self._mx_declare()


def _globals2(self):
    _old_globals(self)
    self._mx_globals()


Builder._declare = _declare2
Builder._globals = _globals2


def _prep_inputs2(inputs):
    maps = _prep_inputs(inputs)
    mc = _mixer_consts()
    flat, mask = _na_tables_idx()
    flat = np.where(mask > 0, flat, 15 * 31)
    rpb = np.asarray(inputs["na_rpb"], np.float32).reshape(DEPTH, 8, 15 * 31)
    rpb = np.concatenate([rpb, np.full((DEPTH, 8, 1), -1.0e4, np.float32)], axis=-1)
    na_tab = rpb[:, :, flat.reshape(-1)].reshape(DEPTH, 8, 128, (NVA + NVB) * 64)
    shared = dict(mc)
    shared["na_tab"] = np.ascontiguousarray(na_tab)
    for k in ("w_in", "proj_a", "proj_b", "proj_c", "w_out", "attn_sink"):
        shared[k] = np.ascontiguousarray(inputs[k], dtype=np.float32)
    for core, m in enumerate(maps):
        m.update(shared)
        m["cache_attn_k"] = np.ascontiguousarray(inputs["cache_attn_k"][core].reshape(DEPTH, 256, 128))
        m["cache_attn_v"] = np.ascontiguousarray(inputs["cache_attn_v"][core].reshape(DEPTH, 256, 128))
        m["cache_na_k"] = np.ascontiguousarray(inputs["cache_na_k"][core].reshape(DEPTH, 256, 512))
        m["cache_na_v"] = np.ascontiguousarray(inputs["cache_na_v"][core].reshape(DEPTH, 256, 512))
    return maps


def _rw_declare(self):
    L = DEPTH
    self.w_shift = self.din("w_shift", [L, 3, 1920])
    self.decay_w0 = self.din("decay_w0", [L, 2, 512])
    self.decay_up = self.din("decay_up", [L, 128, 512])
    self.iclr_a0 = self.din("iclr_a0", [L, 2, 512])
    self.iclr_up = self.din("iclr_up", [L, 128, 512])
    self.gate_up = self.din("gate_up", [L, 128, 512])
    self.vec512 = {k: self.din(k, [L, 512]) for k in ("k_k", "k_a", "r_k", "gn_g", "gn_b")}
    self.st_in = self.din("state_rwkv", [L, 2, 8, 64, 64])
    self.o_nst = self.dout("o_nst", [2, L, 2, 8, 64, 64])
    self.c_identh = self.din("c_identh", [128, 64])


def _rw_layer_consts(self, l, es):
    T = self.T
    R = {}
    R["shT"] = self.sb("shT", [128, 3, 15], F32, es=es)
    self._load_vecT(self.w_shift[l].rearrange("s (c p) -> (s c) p", p=128), 45, "sh")
    T.op("dve", lambda e: e.tensor_copy(out=R["shT"][:].rearrange("p s c -> p (s c)"), in_=self.ps[7][:, 0:45]), reads=["ps7"], writes=["shT"])
    R["vecs"] = self.sb("rwvecs", [128, 36], F32, es=es)
    srcs = [self.decay_w0[l].rearrange("d (c p) -> (d c) p", p=128), self.iclr_a0[l].rearrange("d (c p) -> (d c) p", p=128)]
    srcs += [self.vec512[k][l].rearrange("(c p) -> c p", p=128) for k in ("k_k", "k_a", "r_k", "gn_g", "gn_b")]
    off = 0
    for sap, nr in zip(srcs, (8, 8, 4, 4, 4, 4, 4)):
        self._load_vecT(sap, nr, "rwv")
        T.op("dve", lambda e, off=off, nr=nr: e.tensor_copy(out=R["vecs"][:, off:off + nr], in_=self.ps[7][:, 0:nr]), reads=["ps7"], writes=["rwvecs"])
        off += nr
    R["omka"] = self.sb("omka", [128, 4], F32, es=es)
    T.op("dve", lambda e: e.tensor_scalar(out=R["omka"][:], in0=R["vecs"][:, 20:24], scalar1=-1.0, scalar2=1.0, op0=ALU.mult, op1=ALU.add),
         reads=["rwvecs"], writes=["omka"])
    R["dup"] = self.sb("dupw", [128, 512], BF16, es=es)
    R["iup"] = self.sb("iupw", [128, 512], BF16, es=es)
    R["gup"] = self.sb("gupw", [128, 512], BF16, es=es)
    T.dma("pool", R["dup"][:], self.decay_up[l], writes=["dupw"])
    T.dma("pool", R["iup"][:], self.iclr_up[l], writes=["iupw"])
    T.dma("pool", R["gup"][:], self.gate_up[l], writes=["gupw"])
    R["identh"] = self.sb("identh", [128, 64], BF16, es=es)
    T.dma("pool", R["identh"][:], self.c_identh, writes=["identh"])
    return R


def _rwkv(self, l):
    T = self.T
    with ExitStack() as res_:
        R = self._rw_layer_consts(l, res_)
        units = [dict(segs=[(0, 256), (256, 256)], pairs=[0, 1, 2, 3], sample=False),
                 dict(segs=[(512, 1024)], pairs=[0, 1], sample=True),
                 dict(segs=[(512, 1024)], pairs=[2, 3], sample=True)]
        import os
        if os.environ.get("DBG_UNITS"):
            units = [units[int(c)] for c in os.environ["DBG_UNITS"]]
        for u in units:
            self._rw_unit(l, R, u)
        T.barrier()


def _rw_unit(self, l, R, u):
    nc, T = self.nc, self.T
    segs, pairs, sample = u["segs"], u["pairs"], u["sample"]
    nseg, npair = len(segs), len(pairs)
    t00 = segs[0][0]
    L = segs[0][1]
    TU = nseg * L
    tiles = [(t00 + a, 512) for a in range(0, TU, 512)]
    G = npair * nseg
    wv = self.w_in[l].rearrange("(kc p) n -> p kc n", p=128)
    vec = R["vecs"]
    with ExitStack() as ues:
        kdT = [self.sb("kdT%d" % d, [128, npair, TU], BF16, es=ues) for d in range(2)]
        kapT = self.sb("kapT", [128, npair, TU], BF16, es=ues)
        rT = self.sb("rT", [128, npair, TU], BF16, es=ues)
        vT = self.sb("vT", [128, npair, TU], BF16, es=ues)
        B = {}
        pc = [0]

        def pbank():
            i = pc[0] % 2
            pc[0] += 1
            return self.ps[i], "ps%d" % i

        def conv_chunk(ci, out_ap, okey, post=None):
            zraw, tmp = B["zraw"], B["tmp"]
            c0 = C_OFF + ci * 128
            s = self.wslot
            self.wslot = (self.wslot + 1) % self.NSLOT
            key = "wring%d" % s
            dst = self.wring[:, s, 0:1024].rearrange("p (kc n) -> p kc n", kc=NCH)
            T.dma("pool", dst, wv[:, :, c0:c0 + 128], writes=[key])
            for ti, (tt0, n) in enumerate(tiles):
                bank, bk = pbank()
                self.mmg(bank[:, 0:n], bk, [(dst[:, kc_, :], self.hT[:, kc_, tt0:tt0 + n]) for kc_ in range(NCH)], reads=[key, "hT"])
                T.op("act", lambda e, bank=bank, ti=ti, n=n: e.activation(out=zraw[:, ti * 512:ti * 512 + n], in_=bank[:, 0:n], func=AF.Copy),
                     reads=[bk], writes=["zraw"])
            sh = R["shT"]
            T.op("dve", lambda e: e.tensor_scalar(out=tmp[:], in0=zraw[:], scalar1=sh[:, 1, ci:ci + 1], scalar2=None, op0=ALU.mult),
                 reads=["zraw", "shT"], writes=["rwtmp"])
            for si in range(nseg):
                a, b = si * L, (si + 1) * L
                T.op("dve", lambda e, a=a, b=b: e.scalar_tensor_tensor(out=tmp[:, a + 1:b], in0=zraw[:, a:b - 1], scalar=sh[:, 0, ci:ci + 1],
                                                                       in1=tmp[:, a + 1:b], op0=ALU.mult, op1=ALU.add),
                     reads=["zraw", "shT", "rwtmp"], writes=["rwtmp"])
                T.op("dve", lambda e, a=a, b=b: e.scalar_tensor_tensor(out=tmp[:, a:b - 1], in0=zraw[:, a + 1:b], scalar=sh[:, 2, ci:ci + 1],
                                                                       in1=tmp[:, a:b - 1], op0=ALU.mult, op1=ALU.add),
                     reads=["zraw", "shT", "rwtmp"], writes=["rwtmp"])
            T.op("act", lambda e: e.activation(out=out_ap, in_=tmp[:], func=(post or AF.Copy)), reads=["rwtmp"], writes=[okey])
        yT = self.sb("yT", [128, npair, TU], F32, es=ues)
        T.op("pool", lambda e: e.memset(yT[:], 0.0), writes=["yT"])
        wsc = ExitStack()
        wT = [self.sb("wT%d" % d, [128, npair, TU], F32, es=wsc) for d in range(2)]
        bpT = [self.sb("bpT%d" % d, [128, npair, TU], BF16, es=wsc) for d in range(2)]
        with ExitStack() as pes:
            zraw = self.sb("zraw", [128, TU], F32, es=pes)
            kc = self.sb("kcv", [128, TU], F32, es=pes)
            av = zraw
            tmp = self.sb("rwtmp", [128, TU], F32, es=pes)
            tmpb = self.sb("rwtmpb", [128, TU], BF16, es=pes)
            twlo = self.sb("twlo", [128, TU], BF16, es=pes)
            talo = self.sb("talo", [128, TU], BF16, es=pes)
            B["zraw"], B["tmp"] = zraw, tmp
            conv_chunk(12, twlo[:], "twlo", AF.Tanh)
            conv_chunk(13, talo[:], "talo")
            for qi, p in enumerate(pairs):
                conv_chunk(p, rT[:, qi, :], "rT")
                conv_chunk(8 + p, vT[:, qi, :], "vT")
                conv_chunk(4 + p, kc[:], "kcv")
                T.op("dve", lambda e, p=p: e.tensor_scalar(out=av[:], in0=kc[:], scalar1=vec[:, 16 + p:17 + p], scalar2=None, op0=ALU.mult),
                     reads=["kcv", "rwvecs"], writes=["zraw"])
                T.op("pool", lambda e: e.tensor_tensor(out=tmpb[:], in0=av[:], in1=av[:], op=ALU.mult), reads=["zraw"], writes=["rwtmpb"])
                for ti in range(TU // 512):
                    bank, bk = pbank()
                    sl = slice(ti * 512, (ti + 1) * 512)
                    self.mmg(bank[:], bk, [(self.bones_bf[:], tmpb[:, sl])], reads=["bones_bf", "rwtmpb"])
                    T.op("act", lambda e, bank=bank, sl=sl: e.activation(out=tmp[:, sl], in_=bank[:], func=AF.Sqrt), reads=[bk], writes=["rwtmp"])
                T.op("dve", lambda e: e.tensor_scalar(out=tmp[:], in0=tmp[:], scalar1=1e-12, scalar2=None, op0=ALU.max), reads=["rwtmp"], writes=["rwtmp"])
                T.op("dve", lambda e: e.reciprocal(out=tmp[:], in_=tmp[:]), reads=["rwtmp"], writes=["rwtmp"])
                T.op("dve", lambda e, qi=qi: e.tensor_tensor(out=kapT[:, qi, :], in0=av[:], in1=tmp[:], op=ALU.mult), reads=["zraw", "rwtmp"], writes=["kapT"])
                for d in range(2):
                    hs = slice(d * 64, (d + 1) * 64)
                    for ti in range(TU // 512):
                        sl = slice(ti * 512, (ti + 1) * 512)
                        bank, bk = pbank()
                        self.mmg(bank[:], bk, [(R["dup"][hs, p * 128:(p + 1) * 128], twlo[hs, sl])], reads=["dupw", "twlo"])
                        T.op("act", lambda e, bank=bank, sl=sl, d=d, p=p: e.activation(out=tmp[:, sl], in_=bank[:], func=AF.Sigmoid,
                                                                                      bias=vec[:, d * 4 + p:d * 4 + p + 1], scale=1.0),
                             reads=[bk, "rwvecs"], writes=["rwtmp"])
                        bank2, bk2 = pbank()
                        self.mmg(bank2[:], bk2, [(R["iup"][hs, p * 128:(p + 1) * 128], talo[hs, sl])], reads=["iupw", "talo"])
                        T.op("act", lambda e, bank2=bank2, sl=sl, d=d, p=p: e.activation(out=av[:, sl], in_=bank2[:], func=AF.Sigmoid,
                                                                                        bias=vec[:, 8 + d * 4 + p:8 + d * 4 + p + 1], scale=1.0),
                             reads=[bk2, "rwvecs"], writes=["zraw"])
                    T.op("act", lambda e, d=d, qi=qi: e.activation(out=wT[d][:, qi, :], in_=tmp[:], func=AF.Exp, scale=-float(np.exp(-0.5))),
                         reads=["rwtmp"], writes=["wT%d" % d])
                    T.op("dve", lambda e, d=d, qi=qi: e.scalar_tensor_tensor(out=bpT[d][:, qi, :], in0=kapT[:, qi, :], scalar=-1.0, in1=av[:],
                                                                             op0=ALU.mult, op1=ALU.mult),
                         reads=["kapT", "zraw"], writes=["bpT%d" % d])
                    T.op("dve", lambda e, p=p: e.tensor_scalar(out=tmp[:], in0=av[:], scalar1=vec[:, 20 + p:21 + p], scalar2=R["omka"][:, p:p + 1],
                                                               op0=ALU.mult, op1=ALU.add),
                         reads=["zraw", "rwvecs", "omka"], writes=["rwtmp"])
                    T.op("dve", lambda e, d=d, qi=qi: e.tensor_tensor(out=kdT[d][:, qi, :], in0=kc[:], in1=tmp[:], op=ALU.mult),
                         reads=["kcv", "rwtmp"], writes=["kdT%d" % d])
            T.barrier()
        with ExitStack() as ses:
            H = [self.sb("H%d" % d, [128, npair, nseg, 64], F32, es=ses) for d in range(2)]
            Hk = [self.sb("Hk%d" % d, [128, npair, nseg, 64], BF16, es=ses) for d in range(2)]
            Hc = [self.sb("Hc%d" % d, [128, npair, nseg, 64], BF16, es=ses) for d in range(2)]
            Vd = [self.sb("Vd%d" % d, [128, npair, nseg, 64], BF16, es=ses) for d in range(2)]
            KV = [self.sb("KV%d" % d, [128, npair, nseg, 64], F32, es=ses) for d in range(2)]
            stg = self.sb("ststg", [64, 128], F32, es=ses)
            for d in range(2):
                if not sample:
                    T.op("pool", lambda e, d=d: e.memset(H[d][:], 0.0), writes=["H%d" % d])
                else:
                    for qi, p in enumerate(pairs):
                        T.dma("sp", stg[:].rearrange("v (h k) -> v h k", h=2), self.st_in[l, d, 2 * p:2 * p + 2].rearrange("h v k -> v h k"), writes=["ststg"])
                        T.op("pe", lambda e: e.transpose(self.ps[6][:, 0:64], stg[:], self.ident_f[0:64, 0:64]), reads=["ststg", "ident_f"], writes=["ps6"], same=False)
                        T.op("dve", lambda e, d=d, qi=qi: e.tensor_copy(out=H[d][:, qi, 0, :], in_=self.ps[6][:, 0:64]), reads=["ps6"], writes=["H%d" % d])
            NB = 512 // G
            SA = [self.ps[0], self.ps[1]]
            VB = [self.ps[2], self.ps[3]]
            YP = [self.ps[4], self.ps[5]]
            ypv = [YP[d][:, 0:G * NB].rearrange("p (q s n) -> p q s n", q=npair, s=nseg) for d in range(2)]
            sh4 = [128, npair, nseg, 64]

            def col(arr, tt):
                return arr[:].rearrange("p q (s t) -> p q s t", s=nseg)[:, :, :, tt].unsqueeze(3).broadcast_to(sh4)

            idb = R["identh"][:].unsqueeze(1).unsqueeze(1).broadcast_to(sh4)
            fl = lambda a: a[:].rearrange("p q s v -> p (q s v)")
            X = [self.sb("X%d" % d, sh4, F32, es=ses) for d in range(2)]

            def emit_y(i, d):
                tt = i if d == 0 else L - 1 - i
                ypk = "ps%d" % (4 + d)
                cidx = (i % NB) if d == 0 else (NB - 1 - (i % NB))
                for qi in range(npair):
                    for si in range(nseg):
                        for par in range(2):
                            hs = slice(par * 64, (par + 1) * 64)
                            T.op("pe", lambda e, d=d, qi=qi, si=si, hs=hs, tt=tt, cidx=cidx: e.matmul(
                                ypv[d][hs, qi, si, cidx:cidx + 1], Hc[d][hs, qi, si, :], rT[hs, qi, si * L + tt:si * L + tt + 1], start=True, stop=True),
                                reads=["Hc%d" % d, "rT"], writes=[ypk], same=False)
                if (i % NB == NB - 1) or i == L - 1:
                    i0 = (i // NB) * NB
                    nb = i - i0 + 1
                    for si in range(nseg):
                        if d == 0:
                            ta, ca = si * L + i0, 0
                        else:
                            ta, ca = si * L + (L - 1 - i), NB - nb
                        T.op("dve", lambda e, d=d, si=si, ta=ta, ca=ca, nb=nb: e.tensor_tensor(
                            out=yT[:, :, ta:ta + nb], in0=ypv[d][:, :, si, ca:ca + nb], in1=yT[:, :, ta:ta + nb], op=ALU.add),
                            reads=[ypk, "yT"], writes=["yT"])

            for i in range(L):
                for d in range(2):
                    tt = i if d == 0 else L - 1 - i
                    hk, sak, vbk = "H%d" % d, "ps%d" % d, "ps%d" % (2 + d)
                    T.op("pool", lambda e, d=d, tt=tt: e.tensor_tensor(out=Vd[d][:], in0=idb, in1=col(vT, tt), op=ALU.mult),
                         reads=["vT", "identh"], writes=["Vd%d" % d])
                    T.op("pe", lambda e, d=d: e.matmul(VB[d][:, 0:G * 64], self.bones_bf[:], fl(Vd[d]), start=True, stop=True),
                         reads=["Vd%d" % d, "bones_bf"], writes=[vbk], same=False)
                    T.op("pool", lambda e, d=d, tt=tt: e.tensor_tensor(out=Hk[d][:], in0=H[d][:], in1=col(kapT, tt), op=ALU.mult),
                         reads=[hk, "kapT"], writes=["Hk%d" % d])
                    T.op("pe", lambda e, d=d: e.matmul(SA[d][:, 0:G * 64], self.bones_bf[:], fl(Hk[d]), start=True, stop=True),
                         reads=["Hk%d" % d, "bones_bf"], writes=[sak], same=False)
                    if i > 0:
                        emit_y(i - 1, d)
                    vbv = VB[d][:, 0:G * 64].rearrange("p (q s v) -> p q s v", q=npair, s=nseg)
                    sav = SA[d][:, 0:G * 64].rearrange("p (q s v) -> p q s v", q=npair, s=nseg)
                    T.op("dve", lambda e, d=d, tt=tt, vbv=vbv: e.tensor_tensor(out=vbv, in0=vbv, in1=col(kdT[d], tt), op=ALU.mult),
                         reads=[vbk, "kdT%d" % d], writes=[vbk], same=False)
                    if G > 2:
                        T.op("pool", lambda e, d=d, tt=tt: e.tensor_tensor(out=X[d][:], in0=H[d][:], in1=col(wT[d], tt), op=ALU.mult),
                             reads=[hk, "wT%d" % d], writes=["X%d" % d])
                    else:
                        T.op("dve", lambda e, d=d, tt=tt: e.tensor_tensor(out=X[d][:], in0=H[d][:], in1=col(wT[d], tt), op=ALU.mult),
                             reads=[hk, "wT%d" % d], writes=["X%d" % d], same=False)
                    T.op("dve", lambda e, d=d, vbv=vbv: e.tensor_tensor(out=X[d][:], in0=vbv, in1=X[d][:], op=ALU.add),
                         reads=["X%d" % d, vbk], writes=["X%d" % d], same=False)
                    T.op("dve", lambda e, d=d, tt=tt, sav=sav: e.tensor_tensor(out=sav, in0=sav, in1=col(bpT[d], tt), op=ALU.mult),
                         reads=[sak, "bpT%d" % d], writes=[sak], same=False)
                    T.op("dve", lambda e, d=d, sav=sav: e.tensor_tensor(out=H[d][:], in0=sav, in1=X[d][:], op=ALU.add),
                         reads=["X%d" % d, sak], writes=[hk], same=False)
                    T.op("act", lambda e, d=d: e.activation(out=Hc[d][:], in_=H[d][:], func=AF.Copy), reads=[hk], writes=["Hc%d" % d])
            for d in range(2):
                emit_y(L - 1, d)
            if not sample:
                for d in range(2):
                    for qi, p in enumerate(pairs):
                        for si in range(nseg):
                            T.op("pe", lambda e, d=d, qi=qi, si=si: e.transpose(self.ps[6][0:64, 0:128], H[d][:, qi, si, :], self.ident_f[:]),
                                 reads=["H%d" % d, "ident_f"], writes=["ps6"], same=False)
                            T.op("dve", lambda e: e.tensor_copy(out=stg[:], in_=self.ps[6][0:64, 0:128]), reads=["ps6"], writes=["ststg"])
                            T.dma("sp", self.o_nst[si, l, d, 2 * p:2 * p + 2].rearrange("h v k -> v h k"), stg[:].rearrange("v (h k) -> v h k", h=2),
                                  reads=["ststg"], writes=["nst%d_%d_%d_%d" % (l, d, p, si)])
            T.barrier()
        wsc.close()
        with ExitStack() as fes:
            yb = self.sb("ybf", [128, 512], BF16, es=fes)
            ysq = self.sb("ysq", [128, 512], BF16, es=fes)
            mean = self.sb("gmean", [128, 512], F32, es=fes)
            rstd = self.sb("grstd", [128, 512], F32, es=fes)
            tq = self.sb("gtq", [128, 512], F32, es=fes)
            gneps = self.sb("gneps", [128, 1], F32, es=fes)
            T.op("dve", lambda e: e.memset(gneps[:], GN_EPS), writes=["gneps"])
            gT = self.sb("gT", [128, npair, TU], BF16, es=fes)
            cbT = self.sb("cbT", [128, npair, TU], BF16, es=fes)
            zraw = self.sb("zraw", [128, TU], F32, es=fes)
            tmp = self.sb("rwtmp", [128, TU], F32, es=fes)
            tmpb = self.sb("rwtmpb", [128, TU], BF16, es=fes)
            B["zraw"], B["tmp"] = zraw, tmp
            conv_chunk(14, tmpb[:], "rwtmpb", AF.Sigmoid)
            for qi, p in enumerate(pairs):
                for ti in range(TU // 512):
                    bank, bk = pbank()
                    self.mmg(bank[:], bk, [(R["gup"][:, p * 128:(p + 1) * 128], tmpb[:, ti * 512:(ti + 1) * 512])], reads=["gupw", "rwtmpb"])
                    T.op("act", lambda e, bank=bank, qi=qi, ti=ti: e.activation(out=gT[:, qi, ti * 512:(ti + 1) * 512], in_=bank[:], func=AF.Copy),
                         reads=[bk], writes=["gT"])
            for qi, p in enumerate(pairs):
                for d in range(2):
                    T.op("dve", lambda e, d=d, qi=qi, p=p: e.scalar_tensor_tensor(out=tmpb[:], in0=kdT[d][:, qi, :], scalar=vec[:, 24 + p:25 + p],
                                                                                  in1=rT[:, qi, :], op0=ALU.mult, op1=ALU.mult),
                         reads=["kdT%d" % d, "rT", "rwvecs"], writes=["rwtmpb"])
                    for ti in range(TU // 512):
                        sl = slice(ti * 512, (ti + 1) * 512)
                        bank, bk = pbank()
                        self.mmg(bank[:], bk, [(self.bones_bf[:], tmpb[:, sl])], reads=["bones_bf", "rwtmpb"])
                        if d == 0:
                            T.op("act", lambda e, bank=bank, sl=sl, qi=qi: e.activation(out=cbT[:, qi, sl], in_=bank[:], func=AF.Copy),
                                 reads=[bk], writes=["cbT"])
                        else:
                            T.op("dve", lambda e, bank=bank, sl=sl, qi=qi: e.tensor_tensor(out=cbT[:, qi, sl], in0=bank[:], in1=cbT[:, qi, sl], op=ALU.add),
                                 reads=[bk, "cbT"], writes=["cbT"])
            ocT = kapT
            for qi, p in enumerate(pairs):
                for ti in range(TU // 512):
                    sl = slice(ti * 512, (ti + 1) * 512)
                    y = yT[:, qi, sl]
                    T.op("act", lambda e, y=y: e.activation(out=yb[:], in_=y, func=AF.Copy), reads=["yT"], writes=["ybf"])
                    T.op("pool", lambda e, y=y: e.tensor_tensor(out=ysq[:], in0=y, in1=y, op=ALU.mult), reads=["yT"], writes=["ysq"])
                    self.mmg(self.ps[0][:], "ps0", [(self.bones_bf[:], yb[:])], reads=["bones_bf", "ybf"])
                    self.mmg(self.ps[1][:], "ps1", [(self.bones_bf[:], ysq[:])], reads=["bones_bf", "ysq"])
                    T.op("act", lambda e: e.activation(out=mean[:], in_=self.ps[0][:], func=AF.Copy, scale=1.0 / 64), reads=["ps0"], writes=["gmean"])
                    T.op("dve", lambda e: e.tensor_tensor(out=tq[:], in0=mean[:], in1=mean[:], op=ALU.mult), reads=["gmean"], writes=["gtq"])
                    T.op("dve", lambda e: e.scalar_tensor_tensor(out=rstd[:], in0=self.ps[1][:], scalar=1.0 / 64, in1=tq[:], op0=ALU.mult, op1=ALU.subtract),
                         reads=["ps1", "gtq"], writes=["grstd"])
                    T.op("act", lambda e: e.activation(out=rstd[:], in_=rstd[:], func=AF.Sqrt, bias=gneps[:, 0:1], scale=1.0), reads=["grstd", "gneps"], writes=["grstd"])
                    T.op("dve", lambda e: e.reciprocal(out=rstd[:], in_=rstd[:]), reads=["grstd"], writes=["grstd"])
                    T.op("dve", lambda e, y=y: e.tensor_tensor(out=tq[:], in0=y, in1=mean[:], op=ALU.subtract), reads=["yT", "gmean"], writes=["gtq"])
                    T.op("dve", lambda e: e.tensor_tensor(out=tq[:], in0=tq[:], in1=rstd[:], op=ALU.mult), reads=["gtq", "grstd"], writes=["gtq"])
                    T.op("dve", lambda e, p=p: e.tensor_scalar(out=tq[:], in0=tq[:], scalar1=vec[:, 28 + p:29 + p], scalar2=vec[:, 32 + p:33 + p], op0=ALU.mult, op1=ALU.add),
                         reads=["gtq", "rwvecs"], writes=["gtq"])
                    T.op("pool", lambda e, qi=qi, sl=sl: e.tensor_tensor(out=mean[:], in0=cbT[:, qi, sl], in1=vT[:, qi, sl], op=ALU.mult),
                         reads=["cbT", "vT"], writes=["gmean"])
                    T.op("dve", lambda e: e.tensor_tensor(out=tq[:], in0=tq[:], in1=mean[:], op=ALU.add), reads=["gtq", "gmean"], writes=["gtq"])
                    T.op("dve", lambda e, qi=qi, sl=sl: e.tensor_tensor(out=ocT[:, qi, sl], in0=tq[:], in1=gT[:, qi, sl], op=ALU.mult),
                         reads=["gtq", "gT"], writes=["kapT"])
            self._merge_branch(l, 2, [(lambda tt0, n, qi=qi: ocT[:, qi, tt0 - t00:tt0 - t00 + n]) for qi in range(npair)], "kapT",
                               pairs[0] * 128, t00, TU, first=False)
            T.barrier()


for _f in (_rw_declare, _rw_layer_consts, _rwkv, _rw_unit):
    setattr(Builder, _f.__name__, _f)
_old_declare3 = Builder._declare


def _declare3(self):
    _old_declare3(self)
    self._rw_declare()


Builder._declare = _declare3


def _prep_inputs3(inputs):
    maps = _prep_inputs2(inputs)
    shared = {}
    for k in ("w_shift", "decay_w0", "iclr_a0", "gate_up", "k_k", "k_a", "gn_g", "gn_b"):
        shared[k] = np.ascontiguousarray(inputs[k], dtype=np.float32)
    shared["r_k"] = np.ascontiguousarray(np.asarray(inputs["r_k"], np.float32).reshape(DEPTH, 512))
    shared["decay_up"] = np.ascontiguousarray(np.asarray(inputs["decay_up"], np.float32).reshape(DEPTH, 128, 512))
    shared["iclr_up"] = np.ascontiguousarray(np.asarray(inputs["iclr_up"], np.float32).reshape(DEPTH, 128, 512))
    idh = np.zeros((128, 64), np.float32)
    idh[np.arange(128), np.arange(128) % 64] = 1.0
    shared["c_identh"] = idh
    for core, m in enumerate(maps):
        m.update(shared)
        m["state_rwkv"] = np.ascontiguousarray(inputs["state_rwkv"][core], dtype=np.float32)
    return maps


def kernel(**inputs):
    b = Builder()
    nc = b.build()
    maps = _prep_inputs3(inputs)
    maps = [{k: v for k, v in m.items() if k in b.dram_in} for m in maps]
    res = run_bass_kernel_spmd(nc, maps, core_ids=list(range(8)))
    outs = res.results
    f32 = np.float32
    y_p = np.stack([outs[c]["y_tok"][:512].reshape(2, 256, D) for c in range(8)], 0).reshape(16, 256, D).astype(f32)
    y_s = np.stack([outs[c]["y_tok"][512:] for c in range(8)], 0).astype(f32)
    nak = np.concatenate([outs[c]["o_nak"] for c in range(8)], 0).reshape(16, DEPTH, 256, 2, 64).astype(f32)
    nav = np.concatenate([outs[c]["o_nav"] for c in range(8)], 0).reshape(16, DEPTH, 256, 2, 64).astype(f32)
    nbk = np.concatenate([outs[c]["o_nbk"] for c in range(8)], 0).reshape(16, DEPTH, 256, 8, 64).astype(f32)
    nbv = np.concatenate([outs[c]["o_nbv"] for c in range(8)], 0).reshape(16, DEPTH, 256, 8, 64).astype(f32)
    nst = np.concatenate([outs[c]["o_nst"] for c in range(8)], 0).reshape(16, DEPTH, 2, 8, 64, 64).astype(f32)
    return (y_p, y_s, nak, nav, nbk, nbv, nst)
```
